# Optimizing a Trainium2 kernel written in Bass

```python
import math
import jax, jax.numpy as jnp
from jax import lax
import numpy as np

D_MODEL = 1024
BATCH = 2
SEQ = 8192
DEPTH = 2

HEAD_DIM = 64
N_MIX_HEADS = D_MODEL // HEAD_DIM
MIX_WIDTH = N_MIX_HEADS * HEAD_DIM
FOX_HEADS = N_MIX_HEADS // 4
NSA_HEADS = N_MIX_HEADS // 2
DIFF_HEADS = N_MIX_HEADS - FOX_HEADS - NSA_HEADS
FOX_W = FOX_HEADS * HEAD_DIM
NSA_W = NSA_HEADS * HEAD_DIM
DIFF_QK_DIM = HEAD_DIM // 2
DIFF_V_DIM = HEAD_DIM
DIFF_W = DIFF_HEADS * DIFF_V_DIM
NSA_KV_GROUPS = 2
NSA_HPG = NSA_HEADS // NSA_KV_GROUPS
NSA_KV_W = NSA_KV_GROUPS * HEAD_DIM
CMP_LEN = 32
CMP_STRIDE = 16
CMP_HIDDEN = 2 * HEAD_DIM
SEL_BLOCK = 64
SEL_TOPK = 16
WINDOW = 512
Q_BLOCK = 128
ROPE_THETA = 10000.0
LN_EPS = 1e-5
RMS_EPS = 1e-5
NEG_INF = -1e30
FORCE_SCORE = 1e30
DEEPNORM_ALPHA = (2 * DEPTH) ** 0.25
DEEPNORM_BETA = (8 * DEPTH) ** -0.25

SPLITS = (
    ('fox_q', FOX_W, False), ('fox_k', FOX_W, False), ('fox_v', FOX_W, True),
    ('fox_f', FOX_HEADS, False), ('fox_z', FOX_W, False),
    ('nsa_q', NSA_W, False),
    ('nsa_k_cmp', NSA_KV_W, False), ('nsa_v_cmp', NSA_KV_W, True),
    ('nsa_k_sel', NSA_KV_W, False), ('nsa_v_sel', NSA_KV_W, True),
    ('nsa_k_win', NSA_KV_W, False), ('nsa_v_win', NSA_KV_W, True),
    ('nsa_gate', 3 * NSA_HEADS, False), ('nsa_z', NSA_W, False),
    ('diff_q', DIFF_HEADS * 2 * DIFF_QK_DIM, False), ('diff_k', DIFF_HEADS * 2 * DIFF_QK_DIM, False),
    ('diff_v', DIFF_W, True), ('diff_z', DIFF_W, False),
)
IN_WIDTH = sum(w for _, w, _ in SPLITS)

kernel_name = 'hymba_fox_nsa_diff_deepnorm'


def split_columns(proj):
    offsets, acc = [], 0
    for _, w, _ in SPLITS[:-1]:
        acc += w
        offsets.append(acc)
    parts = jnp.split(proj, offsets, axis=-1)
    return {name: part for (name, _, _), part in zip(SPLITS, parts)}


def masked_softmax(logits, mask):
    logits = jnp.where(mask, logits.astype(jnp.float32), NEG_INF)
    return jax.nn.softmax(logits, axis=-1) * mask


def rope(x):
    S, d = x.shape[1], x.shape[-1]
    half = d // 2
    inv = ROPE_THETA ** (-(jnp.arange(half, dtype=jnp.float32) * 2.0 / d))
    ang = jnp.arange(S, dtype=jnp.float32)[:, None] * inv[None, :]
    shape = (1, S) + (1,) * (x.ndim - 3) + (half,)
    cos, sin = jnp.cos(ang).reshape(shape), jnp.sin(ang).reshape(shape)
    x1 = x[..., :half].astype(jnp.float32)
    x2 = x[..., half:].astype(jnp.float32)
    return jnp.concatenate([x1 * cos - x2 * sin, x2 * cos + x1 * sin], axis=-1).astype(x.dtype)


def layer_norm(x, g, b):
    xf = x.astype(jnp.float32)
    mu = jnp.mean(xf, axis=-1, keepdims=True)
    var = jnp.mean(jnp.square(xf - mu), axis=-1, keepdims=True)
    return ((xf - mu) * lax.rsqrt(var + LN_EPS) * g + b).astype(x.dtype)


def rms_norm(x, g):
    xf = x.astype(jnp.float32)
    return (xf * lax.rsqrt(jnp.mean(jnp.square(xf), axis=-1, keepdims=True) + RMS_EPS) * g).astype(x.dtype)


def fox_attention(q, k, v, log_f):
    B, H, S, dk = q.shape
    c = jnp.cumsum(log_f, axis=-1)
    scale = dk ** -0.5
    key_pos = jnp.arange(S)

    def block(i):
        s0 = i * Q_BLOCK
        t = s0 + jnp.arange(Q_BLOCK)
        qb = lax.dynamic_slice_in_dim(q, s0, Q_BLOCK, axis=2)
        cq = lax.dynamic_slice_in_dim(c, s0, Q_BLOCK, axis=2)
        logits = (jnp.einsum('bhqd,bhkd->bhqk', qb, k).astype(jnp.float32) * scale
                  + cq[..., None] - c[:, :, None, :])
        p = masked_softmax(logits, key_pos[None, :] <= t[:, None]).astype(v.dtype)
        return jnp.einsum('bhqk,bhkd->bhqd', p, v)

    out = lax.map(block, jnp.arange(S // Q_BLOCK))
    return out.transpose(1, 0, 3, 2, 4).reshape(B, S, H * dk)


def diff_attention(q, k, v, lam):
    B, H, _, S, dq = q.shape
    scale = dq ** -0.5
    key_pos = jnp.arange(S)

    def block(i):
        s0 = i * Q_BLOCK
        t = s0 + jnp.arange(Q_BLOCK)
        qb = lax.dynamic_slice_in_dim(q, s0, Q_BLOCK, axis=3)
        logits = jnp.einsum('bhcqd,bhckd->bhcqk', qb, k).astype(jnp.float32) * scale
        p = masked_softmax(logits, key_pos[None, :] <= t[:, None])
        a = (p[:, :, 0] - lam * p[:, :, 1]).astype(v.dtype)
        return jnp.einsum('bhqk,bhkd->bhqd', a, v)

    out = lax.map(block, jnp.arange(S // Q_BLOCK))
    return out.transpose(1, 0, 3, 2, 4).reshape(B, S, H, v.shape[-1])


def compress(tok, pos_emb, w1, w2):
    B, S, G, dk = tok.shape
    n_cmp = (S - CMP_LEN) // CMP_STRIDE + 1
    idx = CMP_STRIDE * jnp.arange(n_cmp)[:, None] + jnp.arange(CMP_LEN)[None, :]
    blocks = tok[:, idx] + pos_emb[None, None, :, None, :]
    blocks = blocks.transpose(0, 3, 1, 2, 4).reshape(B, G, n_cmp, CMP_LEN * dk)
    return jax.nn.silu(blocks @ w1) @ w2


def gather_blocks(blocks, idx):
    return jax.vmap(jax.vmap(lambda blk, ix: blk[ix]))(blocks, idx)


def nsa_attention(q, k_cmp, v_cmp, k_sel, v_sel, k_win, v_win, gates,
                  pos_k, pos_v, w1_k, w2_k, w1_v, w2_v):
    B, S, H, dk = q.shape
    G, hpg = NSA_KV_GROUPS, NSA_HPG
    scale = dk ** -0.5
    qg = q.reshape(B, S, G, hpg, dk).transpose(0, 2, 3, 1, 4)
    gg = gates.reshape(B, S, G, hpg, 3).transpose(0, 2, 3, 1, 4)
    kc = compress(k_cmp, pos_k, w1_k, w2_k)
    vc = compress(v_cmp, pos_v, w1_v, w2_v)
    n_cmp = kc.shape[2]
    cmp_start = CMP_STRIDE * jnp.arange(n_cmp)
    cmp_end = cmp_start + CMP_LEN - 1
    n_slc = S // SEL_BLOCK
    n_sel = min(SEL_TOPK, n_slc)
    ksb = k_sel.reshape(B, n_slc, SEL_BLOCK, G, dk).transpose(0, 3, 1, 2, 4)
    vsb = v_sel.reshape(B, n_slc, SEL_BLOCK, G, dk).transpose(0, 3, 1, 2, 4)
    sel_start = SEL_BLOCK * jnp.arange(n_slc)
    overlap = (jnp.clip(jnp.minimum(cmp_start[:, None] + CMP_LEN, sel_start[None, :] + SEL_BLOCK)
                        - jnp.maximum(cmp_start[:, None], sel_start[None, :]), 0)
               .astype(jnp.float32) / CMP_LEN)
    j = jnp.arange(n_slc)
    pad = ((0, 0), (0, 0), (WINDOW, 0), (0, 0))
    kwp = jnp.pad(k_win.transpose(0, 2, 1, 3), pad)
    vwp = jnp.pad(v_win.transpose(0, 2, 1, 3), pad)

    def block(i):
        s0 = i * Q_BLOCK
        t = s0 + jnp.arange(Q_BLOCK)
        qb = lax.dynamic_slice_in_dim(qg, s0, Q_BLOCK, axis=3)
        gb = lax.dynamic_slice_in_dim(gg, s0, Q_BLOCK, axis=3)
        s_c = jnp.einsum('bghqd,bgnd->bghqn', qb, kc).astype(jnp.float32) * scale
        p_c = masked_softmax(s_c, cmp_end[None, :] <= t[:, None])
        o_c = jnp.einsum('bghqn,bgnd->bghqd', p_c.astype(vc.dtype), vc)
        imp = jnp.einsum('bghqn,nj->bgqj', p_c, overlap)
        cur = t // SEL_BLOCK
        valid = j[None, :] * SEL_BLOCK <= t[:, None]
        forced = (j[None, :] == 0) | (j[None, :] == cur[:, None]) | (j[None, :] == cur[:, None] - 1)
        score = jnp.where(valid, jnp.where(forced, FORCE_SCORE, imp), NEG_INF)
        _, idx = lax.top_k(score, n_sel)
        ks = gather_blocks(ksb, idx)
        vs = gather_blocks(vsb, idx)
        tok_pos = idx[..., None] * SEL_BLOCK + jnp.arange(SEL_BLOCK)
        m_s = (tok_pos <= t[None, None, :, None, None]).reshape(B, G, 1, Q_BLOCK, n_sel * SEL_BLOCK)
        s_s = (jnp.einsum('bghqd,bgqnld->bghqnl', qb, ks).astype(jnp.float32)
               .reshape(B, G, hpg, Q_BLOCK, n_sel * SEL_BLOCK) * scale)
        p_s = masked_softmax(s_s, m_s)
        o_s = jnp.einsum('bghqm,bgqmd->bghqd', p_s.astype(vs.dtype),
                         vs.reshape(B, G, Q_BLOCK, n_sel * SEL_BLOCK, dk))
        kw = lax.dynamic_slice_in_dim(kwp, s0, Q_BLOCK + WINDOW, axis=2)
        vw = lax.dynamic_slice_in_dim(vwp, s0, Q_BLOCK + WINDOW, axis=2)
        kpos = s0 - WINDOW + jnp.arange(Q_BLOCK + WINDOW)
        m_w = ((kpos[None, :] <= t[:, None]) & (t[:, None] - kpos[None, :] < WINDOW)
               & (kpos[None, :] >= 0))
        s_w = jnp.einsum('bghqd,bgkd->bghqk', qb, kw).astype(jnp.float32) * scale
        p_w = masked_softmax(s_w, m_w)
        o_w = jnp.einsum('bghqk,bgkd->bghqd', p_w.astype(vw.dtype), vw)
        return gb[..., 0:1] * o_c + gb[..., 1:2] * o_s + gb[..., 2:3] * o_w

    out = lax.map(block, jnp.arange(S // Q_BLOCK))
    return out.transpose(1, 0, 4, 2, 3, 5).reshape(B, S, H * dk)


def setup_inputs(seed: int = 0) -> dict:
    key = jax.random.key(seed)
    ks = jax.random.split(key, 20)
    nrm = lambda k, shape: jax.random.normal(k, shape, jnp.float32)
    col_scale = np.concatenate([np.full((w,), DEEPNORM_BETA if is_v else 1.0, np.float32)
                                for _, w, is_v in SPLITS])
    x = nrm(ks[0], (BATCH, SEQ, D_MODEL))
    w_in = nrm(ks[1], (DEPTH, D_MODEL, IN_WIDTH)) * (D_MODEL ** -0.5) * jnp.asarray(col_scale)
    b_fox_f = 4.0 + 0.1 * nrm(ks[2], (DEPTH, FOX_HEADS))
    cmp_pos_k = 0.1 * nrm(ks[3], (DEPTH, CMP_LEN, HEAD_DIM))
    cmp_pos_v = 0.1 * nrm(ks[4], (DEPTH, CMP_LEN, HEAD_DIM))
    cmp_w1_k = nrm(ks[5], (DEPTH, CMP_LEN * HEAD_DIM, CMP_HIDDEN)) * (CMP_LEN * HEAD_DIM) ** -0.5
    cmp_w2_k = nrm(ks[6], (DEPTH, CMP_HIDDEN, HEAD_DIM)) * CMP_HIDDEN ** -0.5
    cmp_w1_v = nrm(ks[7], (DEPTH, CMP_LEN * HEAD_DIM, CMP_HIDDEN)) * (CMP_LEN * HEAD_DIM) ** -0.5
    cmp_w2_v = nrm(ks[8], (DEPTH, CMP_HIDDEN, HEAD_DIM)) * CMP_HIDDEN ** -0.5
    lam_q1 = 0.1 * nrm(ks[9], (DEPTH, DIFF_QK_DIM))
    lam_k1 = 0.1 * nrm(ks[10], (DEPTH, DIFF_QK_DIM))
    lam_q2 = 0.1 * nrm(ks[11], (DEPTH, DIFF_QK_DIM))
    lam_k2 = 0.1 * nrm(ks[12], (DEPTH, DIFF_QK_DIM))
    diff_subln_g = 1.0 + 0.02 * nrm(ks[13], (DEPTH, DIFF_V_DIM))
    w_out = nrm(ks[14], (DEPTH, MIX_WIDTH, D_MODEL)) * (MIX_WIDTH ** -0.5) * DEEPNORM_BETA
    ln_g = 1.0 + 0.02 * nrm(ks[15], (DEPTH, D_MODEL))
    ln_b = 0.02 * nrm(ks[16], (DEPTH, D_MODEL))
    return {'x': x, 'w_in': w_in, 'b_fox_f': b_fox_f, 'cmp_pos_k': cmp_pos_k, 'cmp_pos_v': cmp_pos_v,
            'cmp_w1_k': cmp_w1_k, 'cmp_w2_k': cmp_w2_k, 'cmp_w1_v': cmp_w1_v, 'cmp_w2_v': cmp_w2_v,
            'lam_q1': lam_q1, 'lam_k1': lam_k1, 'lam_q2': lam_q2, 'lam_k2': lam_k2,
            'diff_subln_g': diff_subln_g, 'w_out': w_out, 'ln_g': ln_g, 'ln_b': ln_b}


def reference(x, w_in, b_fox_f, cmp_pos_k, cmp_pos_v, cmp_w1_k, cmp_w2_k, cmp_w1_v, cmp_w2_v,
              lam_q1, lam_k1, lam_q2, lam_k2, diff_subln_g, w_out, ln_g, ln_b):
    B, S, _ = x.shape
    G = NSA_KV_GROUPS
    for l in range(DEPTH):
        p = split_columns(x @ w_in[l])
        heads_f = lambda a: a.reshape(B, S, FOX_HEADS, HEAD_DIM).transpose(0, 2, 1, 3)
        log_f = jax.nn.log_sigmoid((p['fox_f'] + b_fox_f[l]).astype(jnp.float32)).transpose(0, 2, 1)
        o_fox = fox_attention(heads_f(p['fox_q']), heads_f(p['fox_k']), heads_f(p['fox_v']), log_f)
        o_fox = o_fox * jax.nn.silu(p['fox_z'])
        kv = lambda a: a.reshape(B, S, G, HEAD_DIM)
        o_nsa = nsa_attention(
            rope(p['nsa_q'].reshape(B, S, NSA_HEADS, HEAD_DIM)),
            rope(kv(p['nsa_k_cmp'])), kv(p['nsa_v_cmp']),
            rope(kv(p['nsa_k_sel'])), kv(p['nsa_v_sel']),
            rope(kv(p['nsa_k_win'])), kv(p['nsa_v_win']),
            jax.nn.sigmoid(p['nsa_gate'].reshape(B, S, NSA_HEADS, 3)),
            cmp_pos_k[l], cmp_pos_v[l], cmp_w1_k[l], cmp_w2_k[l], cmp_w1_v[l], cmp_w2_v[l])
        o_nsa = o_nsa * jax.nn.silu(p['nsa_z'])
        qd = rope(p['diff_q'].reshape(B, S, DIFF_HEADS, 2, DIFF_QK_DIM)).transpose(0, 2, 3, 1, 4)
        kd = rope(p['diff_k'].reshape(B, S, DIFF_HEADS, 2, DIFF_QK_DIM)).transpose(0, 2, 3, 1, 4)
        vd = p['diff_v'].reshape(B, S, DIFF_HEADS, DIFF_V_DIM).transpose(0, 2, 1, 3)
        lam_init = 0.8 - 0.6 * math.exp(-0.3 * l)
        lam = (jnp.exp(jnp.sum(lam_q1[l].astype(jnp.float32) * lam_k1[l].astype(jnp.float32)))
               - jnp.exp(jnp.sum(lam_q2[l].astype(jnp.float32) * lam_k2[l].astype(jnp.float32))) + lam_init)
        od = rms_norm(diff_attention(qd, kd, vd, lam), diff_subln_g[l]) * (1.0 - lam_init)
        o_diff = od.reshape(B, S, DIFF_W) * jax.nn.silu(p['diff_z'])
        mix = jnp.concatenate([o_fox, o_nsa, o_diff], axis=-1) @ w_out[l]
        x = layer_norm(DEEPNORM_ALPHA * x + mix, ln_g[l], ln_b[l])
    return x
```

```python
import math
import os
from contextlib import ExitStack
import numpy as np
import concourse.bass as bass
import concourse.mybir as mybir
from concourse.bass_utils import run_bass_kernel_spmd

F32 = mybir.dt.float32
BF16 = mybir.dt.bfloat16
ALU = mybir.AluOpType
AF = mybir.ActivationFunctionType
AX = mybir.AxisListType

D_MODEL = 1024
DEPTH = 2
HD = 64
IN_SPLITS = (('fox_q', 256), ('fox_k', 256), ('fox_v', 256), ('fox_f', 4), ('fox_z', 256), ('nsa_q', 512),
             ('nsa_k_cmp', 128), ('nsa_v_cmp', 128), ('nsa_k_sel', 128), ('nsa_v_sel', 128),
             ('nsa_k_win', 128), ('nsa_v_win', 128), ('nsa_gate', 24), ('nsa_z', 512),
             ('diff_q', 256), ('diff_k', 256), ('diff_v', 256), ('diff_z', 256))
OFF = {}
_a = 0
for _n, _w in IN_SPLITS:
    OFF[_n] = _a
    _a += _w
IN_WIDTH = _a
ALPHA = (2 * DEPTH) ** 0.25
NG = 16
BIG = 1.0e30


class Res:
    __slots__ = ("w", "r", "excl")

    def __init__(self, excl=False):
        self.excl = excl
        self.w = None
        self.r = []


class SemC:
    __slots__ = ("sem", "cnt", "key")
    _n = 0

    def __init__(self, sem):
        self.sem = sem
        self.cnt = 0
        SemC._n += 1
        self.key = SemC._n


class Eng:
    def __init__(self, h, semc, same_sync):
        self.h = h
        self.s = semc
        self.waited = {}
        self.same_sync = same_sync


class Ctx:
    def __init__(self, nc, stack):
        self.nc = nc
        self.stack = stack
        mk = lambda n: SemC(stack.enter_context(nc.semaphore(n)))
        self.pe = Eng(nc.tensor, mk("s_pe"), False)
        self.dve = Eng(nc.vector, mk("s_dve"), True)
        self.act = Eng(nc.scalar, mk("s_act"), True)
        self.pool = Eng(nc.gpsimd, mk("s_pool"), True)
        self.sp = Eng(nc.sync, mk("s_sp"), False)

    def res(self, excl=False):
        return Res(excl)

    def dsem(self, name):
        return SemC(self.stack.enter_context(self.nc.semaphore(name)))

    def sbuf(self, name, shape, dt):
        return self.stack.enter_context(self.nc.sbuf_tensor(name, list(shape), dt))

    def psum(self, name, shape, dt):
        return self.stack.enter_context(self.nc.psum_tensor(name, list(shape), dt))

    def _need(self, eng, reads, writes):
        need = {}

        def add(p):
            if p is None:
                return
            s, v = p
            if s is eng.s and not eng.same_sync:
                return
            if need.get(s.key, (None, -1))[1] < v:
                need[s.key] = (s, v)
        for r in reads:
            add(r.w)
            if r.excl:
                for p in r.r:
                    if p[0] is not eng.s:
                        add(p)
        for w in writes:
            add(w.w)
            for p in w.r:
                add(p)
        for k, (s, v) in need.items():
            if eng.waited.get(k, 0) < v:
                eng.h.wait_ge(s.sem, v)
                eng.waited[k] = v

    def op(self, eng, fn, reads=(), writes=()):
        self._need(eng, reads, writes)
        ins = fn()
        eng.s.cnt += 1
        ins.then_inc(eng.s.sem, 1)
        me = (eng.s, eng.s.cnt)
        for r in reads:
            r.r.append(me)
            if len(r.r) > 24:
                r.r = r.r[-24:] if False else _compact(r.r)
        for w in writes:
            w.w = me
            w.r = []
        return ins

    def dma(self, q, dsem, out, in_, reads=(), writes=()):
        self._need(q, reads, writes)
        ins = q.h.dma_start(out=out, in_=in_)
        dsem.cnt += 16
        ins.then_inc(dsem.sem, 16)
        me = (dsem, dsem.cnt)
        for r in reads:
            r.r.append(me)
        for w in writes:
            w.w = me
            w.r = []
        return ins

    def fork(self, parent, n):
        kids = [Res() for _ in range(n)]
        for k in kids:
            k.r = _compact(list(parent.r) + ([parent.w] if parent.w else []))
        return kids

    def join(self, parent, kids):
        for k in kids:
            parent.r = _compact(parent.r + k.r + ([k.w] if k.w else []))

    def wait_all(self, eng, semcs):
        for s in semcs:
            if s.cnt > 0 and eng.waited.get(s.key, 0) < s.cnt:
                eng.h.wait_ge(s.sem, s.cnt)
                eng.waited[s.key] = s.cnt


def _compact(lst):
    best = {}
    for s, v in lst:
        if best.get(s.key, (None, -1))[1] < v:
            best[s.key] = (s, v)
    return list(best.values())


def build_fused(S, depth=DEPTH):
    NT = S // 512
    NK = S // 128
    SQ = S // 4
    nc = bass.Bass("TRN2", target_bir_lowering=False)
    din = lambda n, sh, dt=F32: nc.dram_tensor(n, list(sh), dt, kind="ExternalInput").ap()
    xT0_d = din("xT", [1024, S])
    xq_d = din("xq", [SQ, 1024])
    wfm_d = din("wfm", [depth, 1024, NG * 128])
    wtm_d = din("wtm", [depth, 1024, 257])
    wout_d = din("wout", [depth, 256, 1024])
    w1_d = din("w1", [depth, 2, 2048, 128])
    w2_d = din("w2", [depth, 128, 128])
    post_d = din("post", [depth, 128, 32])
    vec_d = din("vecs", [depth, 128, 8])
    lng_d = din("lng", [depth, 128, 1024])
    lnb_d = din("lnb", [depth, 128, 1024])
    tab_d = din("tab", [128, 4, S])
    ind_d = din("ind", [64, S])
    ctri_d = din("c_tri", [128, 128])
    ctriu_d = din("c_triu", [128, 128])
    cid_d = din("c_ident", [128, 128])
    cmk_d = din("c_maskc", [128, 2048])
    cov_d = din("c_ov", [128, 4, 128])
    y_d = nc.dram_tensor("y", [SQ, 1024], F32, kind="ExternalOutput").ap()
    TPC = 1
    NCH = NT // TPC
    part_d = [nc.dram_tensor(f"part_i{c}", [TPC * 512, 1024], F32).ap() for c in range(NCH)]
    rs_d = [nc.dram_tensor(f"rs_i{c}", [TPC * 128, 1024], F32).ap() for c in range(NCH)]
    NTB = SQ // 128
    ytq_d = [nc.dram_tensor(f"ytq_i{i}", [1024, 256], BF16).ap() for i in range(NTB // 2)]
    xT1_d = [nc.dram_tensor(f"xT1_i{i}", [4 * 1024, 256], BF16).ap() for i in range(NTB // 2)]
    yq_d = nc.dram_tensor("yq_i", [SQ, 1024], F32).ap()
    RG = [[0, 1, 2, 3], [4, 5, 6, 7]]

    with ExitStack() as st:
        cx = Ctx(nc, st)
        pe, dve, act, pool, sp = cx.pe, cx.dve, cx.act, cx.pool, cx.sp
        V, G, T, A = nc.vector, nc.gpsimd, nc.tensor, nc.scalar
        sb = cx.sbuf
        WFM = sb("WFM", [128, 8, NG * 128], BF16); rWFM = cx.res()
        WTM = sb("WTM", [128, 8, 257], BF16); rWTM = cx.res()
        WOUT = sb("WOUT", [128, 2, 1024], BF16); rWOUT = cx.res()
        W1 = sb("W1", [128, 32, 128], BF16); rW1 = cx.res()
        W2 = sb("W2", [128, 128], BF16); rW2 = cx.res()
        POST = sb("POST", [128, 32], BF16); rPOST = cx.res()
        BH = sb("BH", [128, 2], F32); rBH = cx.res()
        KTA = sb("KTA", [128, S], BF16); rKTA = cx.res()
        KTS = sb("KTS", [128, S], BF16); rKTS = cx.res()
        KTW = sb("KTW", [128, 1024], BF16); rKTW = cx.res()
        VA = sb("VA", [128, NK, 192], BF16); rVA = cx.res()
        VN = sb("VN", [128, NK, 192], BF16); rVN = cx.res()
        VCA = sb("VCA", [128, 4, 128], BF16); rVCA = cx.res()
        KCT = sb("KCT", [128, 512], BF16); rKCT = cx.res()
        VCTF = sb("VCTF", [64, 512], F32); rVCTF = cx.res()
        CMPIN = sb("CMPIN", [128, 528], BF16); rCMPIN = cx.res()
        XB = sb("XB", [128, 8, 512], BF16); rXB = cx.res()
        TAB = sb("TAB", [128, 4, 512], F32); rTAB = cx.res()
        TRI = sb("TRI", [128, 128], BF16); TRIU = sb("TRIU", [128, 128], BF16)
        MASKC = sb("MASKC", [128, 2048], BF16); OV = sb("OV", [128, 4, 128], BF16)
        IDF = sb("IDF", [128, 128], F32); TRIF = sb("TRIF", [128, 128], F32); ONESF = sb("ONESF", [128, 128], F32)
        VEC = sb("VEC", [128, 8], F32)
        rC = cx.res()
        rC2 = cx.res()
        QF = sb("QF", [128, 512], BF16); QD0 = sb("QD0", [128, 512], BF16); QD1 = sb("QD1", [128, 512], BF16); rQA = cx.res()
        QN = sb("QN", [128, 4, 512], BF16); rQN = cx.res()
        QSA = [sb(f"QSA{i}", [128, 2, 256], BF16) for i in range(2)]; rQSA = [[cx.res(), cx.res()] for _ in range(2)]
        T0 = sb("T0", [128, 512], F32); rT0 = cx.res()
        T1 = sb("T1", [128, 512], F32); rT1 = cx.res()
        Z1 = sb("Z1", [128, 512], F32); rZ1 = cx.res()
        GZ = sb("GZ", [128, 3, 2, 512], BF16); rGZ = cx.res()
        NPT = 5
        ptc = [0]
        PT = [sb(f"PT{i}", [128, 512], BF16) for i in range(NPT)]; rPT = [cx.res() for _ in range(NPT)]
        FA = sb("FA", [128, 512], F32); rFA = cx.res()
        FB = sb("FB", [128, 512], F32); rFB = cx.res()
        CT0 = sb("CT0", [128, 512], BF16); rCT0 = cx.res()
        CT1 = sb("CT1", [128, 512], BF16); rCT1 = cx.res()
        OUTS = sb("OUTS", [128, 1024], F32); rOUTS = cx.res()
        CNEG = sb("CNEG", [128, NK], F32); rCNEG = cx.res()
        CARRY = sb("CARRY", [128, NK + 4], F32); rCARRY = cx.res()
        SPL = sb("SPL", [128, NK], F32); rSPL = cx.res()
        NB = sb("NB", [128, NK], F32); rNB = cx.res()
        LFRAW = sb("LFRAW", [128, NK], F32); rLFRAW = cx.res()
        SC = sb("SC", [128, 128], F32); rSC = cx.res()
        SC2 = sb("SC2", [128, 128], F32); rSC2 = cx.res()
        M8 = sb("M8", [128, 8], F32); M8b = sb("M8b", [128, 8], F32); rM8 = cx.res(); rM8b = cx.res()
        MNEG2 = [sb(f"MNEG{i}", [128, 128], F32) for i in range(2)]; rMNEG2 = [cx.res(), cx.res()]
        RS4 = sb("RS4", [128, 4], F32); rRS4 = cx.res()
        RI4 = sb("RI4", [128, 4], F32); rRI4 = cx.res()
        YA = [sb(f"YA{i}", [64, 256], F32)[:] for i in range(2)]; rYA = [cx.res(), cx.res()]
        YB = sb("YB", [64, 256], F32); rYB = cx.res()
        YC = sb("YC", [64, 256], F32); rYC = cx.res()
        FN = sb("FN", [128, 256], F32); rFN = cx.res()
        HS = sb("HS", [128, 64], BF16); rHS = cx.res()
        HSS = sb("HSS", [128, 64], F32); rHSS = cx.res()
        LAM = sb("LAM", [128, 4], F32); rLAM = cx.res()
        IMPB = cx.psum("IMPB", [128, 512], F32); rIMPB = cx.res(True)
        STB = [cx.psum(f"ST{i}", [128, 512], F32) for i in range(4)]; rST = [cx.res(True) for _ in range(4)]
        OB = [cx.psum(f"OB{i}", [128, 512], F32) for i in range(3)]; rOB = [cx.res(True) for _ in range(3)]
        ALLB = [(IMPB, rIMPB)] + list(zip(STB, rST)) + list(zip(OB, rOB))
        ld = cx.dsem("ld_const"); ldx = cx.dsem("ld_x"); ldt = cx.dsem("ld_tab"); sto = cx.dsem("st_out")
        pjc = [0]

        def pjx():
            i = pjc[0] % 8
            pjc[0] += 1
            return ALLB[i]

        pj = pjx

        ldw = cx.dsem("ld_w"); ccs = cx.dsem("cc_sem"); ldb = [cx.dsem(f"ld_b{i}") for i in range(4)]; sta = [cx.dsem(f"st_a{i}") for i in range(4)]; ldg = cx.dsem("ld_g")
        rYQ = cx.res()
        rYTQ = [cx.res() for _ in range(NTB // 2)]; rXT1 = [cx.res() for _ in range(NTB // 2)]
        rPart = [cx.res() for _ in range(NCH)]; rRSd = [cx.res() for _ in range(NCH)]
        styt2 = [cx.dsem(f"st_yt{i}") for i in range(4)]
        rsc = [cx.dsem(f"rs_sem{c}") for c in range(NCH)]
        cx.dma(pool, ld, KTS[64:128, :], ind_d, writes=[rKTS])
        cx.dma(pool, ld, TRI[:], ctri_d, writes=[rC])
        cx.dma(pool, ld, TRIU[:], ctriu_d, writes=[rC])
        cx.dma(pool, ld, MASKC[:], cmk_d, writes=[rC])
        cx.dma(pool, ld, OV[:], cov_d, writes=[rC])
        cx.dma(sp, ld, IDF[:], cid_d, writes=[rC])
        cx.dma(sp, ld, TRIF[:], ctri_d, writes=[rC])
        EPSC = sb("EPSC", [128, 1], F32)
        for r_ in (rKTS, rC):
            r_.w = (ld, ld.cnt)
        cx.op(dve, lambda: V.memset(ONESF[:], 1.0), writes=[rC2])
        cx.op(dve, lambda: V.memset(EPSC[:], 1e-5), writes=[rC2])
        rVEC = cx.res()
        if NK * 96 >= 5120:
            vaf = VA[:].rearrange("p a b -> p (a b)").bitcast(F32)
            vnf = VN[:].rearrange("p a b -> p (a b)").bitcast(F32)
            INB = [v_[:, o_:o_ + 2048].rearrange("p (j n) -> p j n", j=2) for v_ in (vaf, vnf) for o_ in (0, 2048)]
            kf = KTA[:].bitcast(F32)
            ACCB = [kf[:, o_:o_ + 1024] for o_ in (0, 1024, 2048, 3072)]
            tabb = TAB[:].rearrange("p a b -> p (a b)").bitcast(BF16)
            YTB2 = [tabb[:, o_:o_ + 1024].rearrange("p (c t) -> p c t", c=8) for o_ in (0, 1024, 2048, 3072)]
            xf = XB[:].rearrange("p a b -> p (a b)").bitcast(F32)
            GTB, BTB = xf[:, 0:1024], xf[:, 1024:2048]
            ALIAS = True
        else:
            ALIAS = False
            INB = [sb(f"INB{i}", [128, 2, 1024], F32)[:] for i in range(4)]; rINB = [cx.res() for _ in range(4)]
            ACCB = [sb(f"ACCB{i}", [128, 1024], F32)[:] for i in range(4)]; rACCB = [cx.res() for _ in range(4)]
            YTB2 = [sb(f"YTB{i}", [128, 8, 128], BF16)[:] for i in range(4)]; rYTB2 = [cx.res() for _ in range(4)]
            GTB = sb("GTB", [128, 1024], F32)[:]; BTB = sb("BTB", [128, 1024], F32)[:]; rGB = cx.res()
        STT = sb("STT", [128, 2, 6], F32); rSTT = cx.res()
        MV = sb("MV", [128, 4], F32); rMV = cx.res()
        def load_weights(L):
            cx.dma(pool, ldw, WFM[:], wfm_d[L].rearrange("(c p) n -> p c n", p=128), writes=[rWFM])
            cx.dma(pool, ldw, WTM[:], wtm_d[L].rearrange("(c p) n -> p c n", p=128), writes=[rWTM])
            cx.dma(pool, ldw, WOUT[:], wout_d[L].rearrange("(c p) n -> p c n", p=128), writes=[rWOUT])
            cx.dma(pool, ldw, W1[0:64, :, :], w1_d[L, 0].rearrange("(l d) h -> d l h", d=64), writes=[rW1])
            cx.dma(pool, ldw, W1[64:128, :, :], w1_d[L, 1].rearrange("(l d) h -> d l h", d=64), writes=[rW1])
            cx.dma(pool, ldw, W2[:], w2_d[L], writes=[rW2])
            cx.dma(pool, ldw, POST[:], post_d[L], writes=[rPOST])
            cx.dma(pool, ldw, VEC[:], vec_d[L], writes=[rVEC])
            for r_ in (rWFM, rWTM, rWOUT, rW1, rW2, rPOST, rVEC):
                r_.w = (ldw, ldw.cnt)

        load_weights(0)
        for L in range(depth):
          if True:
              cx.op(dve, lambda: V.memset(VA[:, :, 64:128], 1.0), writes=[rVA])
              cx.op(pool, lambda: G.memset(VN[:, :, 64:128], 1.0), writes=[rVN])
              cx.op(dve, lambda: V.memset(VCA[:, :, 0:64], 0.0), writes=[rVCA])
              cx.op(dve, lambda: V.memset(VCA[:, :, 64:128], 1.0), writes=[rVCA])
              cx.op(dve, lambda: V.memset(KCT[:], 0.0), writes=[rKCT])
              cx.op(pool, lambda: G.memset(KTW[:], 0.0), writes=[rKTW])
              cx.op(pool, lambda: G.memset(QN[64:128, :, :], 0.0), writes=[rQN])
              cx.op(dve, lambda: V.memset(QF[:], 0.0), writes=[rQA])
              cx.op(dve, lambda: V.memset(QD0[:], 0.0), writes=[rQA])
              cx.op(pool, lambda: G.memset(QD1[:], 0.0), writes=[rQA])
              cx.op(dve, lambda: V.memset(VCTF[:], 0.0), writes=[rVCTF])
              cx.op(pool, lambda: G.memset(CMPIN[:], 0.0), writes=[rCMPIN])
              cx.op(dve, lambda: V.memset(CARRY[:], 0.0), writes=[rCARRY])
              for s_ in range(2):
                  pb, rpb = pj()
                  lo = 64 * s_
                  for l in range(32):
                      cx.op(pe, lambda l=l, lo=lo, pb=pb: T.matmul(pb[:, 0:1], lhsT=W1[lo:lo + 64, l, :], rhs=POST[lo:lo + 64, l:l + 1],
                                                                    start=(l == 0), stop=(l == 31)),
                            reads=[rW1, rPOST], writes=[rpb])
                  cx.op(dve, lambda pb=pb, s_=s_: V.tensor_copy(out=BH[:, s_:s_ + 1], in_=pb[:, 0:1]), reads=[rpb], writes=[rBH])
              cx.op(dve, lambda: V.tensor_tensor(out=LAM[:, 0:1], in0=VEC[:, 2:3], in1=VEC[:, 3:4], op=ALU.mult), reads=[rC, rVEC], writes=[rLAM])
              cx.op(dve, lambda: V.tensor_tensor(out=LAM[:, 1:2], in0=VEC[:, 4:5], in1=VEC[:, 5:6], op=ALU.mult), reads=[rC, rC2, rVEC, rLAM], writes=[rLAM])
              pb, rpb = pj()
              cx.op(pe, lambda: T.matmul(pb[:, 0:2], lhsT=ONESF[:], rhs=LAM[:, 0:2], start=True, stop=True), reads=[rC, rC2, rLAM], writes=[rpb])
              cx.op(act, lambda: A.activation(out=LAM[:, 2:4], in_=pb[:, 0:2], func=AF.Exp), reads=[rpb], writes=[rLAM])
              cx.op(dve, lambda: V.tensor_tensor(out=LAM[:, 0:1], in0=LAM[:, 3:4], in1=LAM[:, 2:3], op=ALU.subtract), reads=[rLAM], writes=[rLAM])
              cx.op(dve, lambda: V.tensor_tensor(out=LAM[:, 0:1], in0=LAM[:, 0:1], in1=VEC[:, 6:7], op=ALU.add),
                    reads=[rLAM, rC, rC2, rVEC], writes=[rLAM])
              cx.op(dve, lambda: V.tensor_tensor(out=LAM[:, 1:2], in0=VEC[:, 1:2], in1=VEC[:, 7:8], op=ALU.mult),
                    reads=[rC, rC2, rVEC, rLAM], writes=[rLAM])
              NEGLAM = LAM[:, 0:1]
              GCOL = LAM[:, 1:2]

              def load_tile(Tq):
                  if L == 0:
                      cx.dma(pool, ldx, XB[:], xT0_d[:, Tq * 512:(Tq + 1) * 512].rearrange("(c p) t -> p c t", p=128), writes=[rXB])
                  else:
                      for pc in range(4):
                          row = (Tq % TPC) * 512 + pc * 128
                          rk, ib = row // (TPC * 128), (Tq // TPC) * TPC + (row % (TPC * 128)) // 128
                          cx.dma(pool, ldx, XB[:, :, pc * 128:(pc + 1) * 128],
                                 xT1_d[ib // 2][rk * 1024:(rk + 1) * 1024, (ib % 2) * 128:(ib % 2 + 1) * 128].rearrange("(c p) t -> p c t", p=128),
                                 reads=[rXT1[ib // 2]], writes=[rXB])
                      rXB.w = (ldx, ldx.cnt)
                  cx.dma(sp, ldt, TAB[:], tab_d[:, :, Tq * 512:(Tq + 1) * 512], writes=[rTAB])

              load_tile(0)

              def fm_group(gi):
                  pb, rpb = pjx()
                  for c in range(8):
                      cx.op(pe, lambda c=c, pb=pb: T.matmul(pb[:], lhsT=WFM[:, c, gi * 128:(gi + 1) * 128], rhs=XB[:, c, :],
                                                             start=(c == 0), stop=(c == 7)), reads=[rWFM, rXB], writes=[rpb])
                  return pb, rpb

              def rope(pa, rpa, ps, rps, rows_a, rows_s, tabi, outs):
                  (a0, a1), (s0, s1) = rows_a, rows_s
                  n = a1 - a0
                  cx.op(dve, lambda: V.tensor_tensor(out=T0[0:n, :], in0=pa[a0:a1, :], in1=TAB[0:n, tabi, :], op=ALU.mult),
                        reads=[rpa, rTAB], writes=[rT0])
                  cx.op(dve, lambda: V.tensor_tensor(out=T1[0:n, :], in0=ps[s0:s1, :], in1=TAB[0:n, tabi + 1, :], op=ALU.mult),
                        reads=[rps, rTAB], writes=[rT1])
                  for (r0_, r1_, dst, rdst) in outs:
                      cx.op(pool, lambda r0_=r0_, r1_=r1_, dst=dst: G.tensor_tensor(out=dst, in0=T0[r0_:r1_, :], in1=T1[r0_:r1_, :], op=ALU.add),
                            reads=[rT0, rT1], writes=[rdst])

              def bcast_h(ap2d, h):
                  return bass.AP(tensor=ap2d.tensor, offset=ap2d.offset, ap=[list(ap2d.ap[0]), [0, h], list(ap2d.ap[1])])

              SGT, rSGT, ZN, rZN = FA, rFA, FB, rFB
              obc = [0]
              stc = [0]
              DSK = int(os.environ.get("DBG_DSK", "3"))

              def ob_next():
                  k = obc[0] % 3
                  obc[0] += 1
                  return OB[k], rOB[k]

              def st_next():
                  k = stc[0] % 4
                  stc[0] += 1
                  return STB[k], rST[k]

              for Tq in range(NT):
                  c0t = Tq * 512
                  pA, rA_ = fm_group(0)
                  pS, rS_ = fm_group(2)
                  rope(pA, rA_, pS, rS_, (0, 64), (0, 64), 2, [(0, 32, QD0[0:32, :], rQA), (32, 64, QD1[32:64, :], rQA)])
                  cx.op(act, lambda: A.copy(out=QF[64:128, :], in_=pA[64:128, :]), reads=[rA_], writes=[rQA])
                  pK, rK_ = fm_group(1)
                  rope(pK, rK_, pS, rS_, (0, 64), (64, 128), 2, [(0, 64, KTA[0:64, c0t:c0t + 512], rKTA)])
                  cx.op(act, lambda: A.copy(out=KTA[64:128, c0t:c0t + 512], in_=pK[64:128, :]), reads=[rK_], writes=[rKTA])
                  for (ga, gs, hq) in ((3, 4, 0), (5, 6, 2)):
                      p1, r1 = fm_group(ga)
                      p2, r2 = fm_group(gs)
                      rope(p1, r1, p2, r2, (0, 128), (0, 128), 0, [(0, 64, QN[0:64, hq, :], rQN), (64, 128, QN[0:64, hq + 1, :], rQN)])
                  p1, r1 = fm_group(7)
                  p2, r2 = fm_group(8)
                  rope(p1, r1, p2, r2, (0, 128), (0, 128), 0, [(0, 64, CMPIN[0:64, 16:528], rCMPIN), (64, 128, KTS[0:64, c0t:c0t + 512], rKTS)])
                  p1, r1 = fm_group(9)
                  p2, r2 = fm_group(10)
                  wslot = (Tq % 2) * 512
                  rope(p1, r1, p2, r2, (0, 64), (0, 64), 0, [(0, 64, KTW[0:64, wslot:wslot + 512], rKTW)])
                  cx.op(act, lambda: A.copy(out=CMPIN[64:128, 16:528], in_=p1[64:128, :]), reads=[r1], writes=[rCMPIN])
                  p1, r1 = fm_group(11)
                  cx.op(act, lambda: A.activation(out=Z1[:], in_=p1[:], func=AF.Sigmoid), reads=[r1], writes=[rZ1])
                  cx.op(dve, lambda: V.tensor_tensor(out=Z1[:], in0=p1[:], in1=Z1[:], op=ALU.mult), reads=[r1, rZ1], writes=[rZ1])
                  p1, r1 = fm_group(12)
                  cx.op(act, lambda: A.activation(out=ZN[:], in_=p1[:], func=AF.Sigmoid), reads=[r1], writes=[rZN])
                  cx.op(dve, lambda: V.tensor_tensor(out=ZN[:], in0=p1[:], in1=ZN[:], op=ALU.mult), reads=[r1, rZN], writes=[rZN])
                  sgbufs = ((FA, rFA), (T0, rT0), (T1, rT1))
                  for br in range(3):
                      p1, r1 = fm_group(13 + br)
                      SGb, rSGb = sgbufs[br]
                      cx.op(act, lambda: A.activation(out=SGb[:], in_=p1[:], func=AF.Sigmoid), reads=[r1], writes=[rSGb])
                      ob = 0 if br == 2 else 64
                      for h in range(2):
                          eng_, E_ = ((pool, G), (dve, V))[(2 * br + h) % 2]
                          cx.op(eng_, lambda h=h, ob=ob, br=br, E_=E_: E_.tensor_tensor(out=GZ[ob:ob + 64, br, h, :], in0=SGb[64 * h:64 * h + 64, :],
                                                                                       in1=ZN[64 * h:64 * h + 64, :], op=ALU.mult),
                                reads=[rSGb, rZN], writes=[rGZ])
                  for st_ in range(4):
                      kt = Tq * 4 + st_
                      pb, rpb = pjx()
                      for c in range(8):
                          cx.op(pe, lambda c=c, pb=pb, st_=st_: T.matmul(pb[:, 0:257], lhsT=XB[:, c, st_ * 128:(st_ + 1) * 128], rhs=WTM[:, c, :],
                                                                          start=(c == 0), stop=(c == 7)), reads=[rXB, rWTM], writes=[rpb])
                      cx.op(act, lambda pb=pb, kt=kt: A.copy(out=VA[:, kt, :].rearrange("p (a b) -> p a b", a=3)[:, 0::2, :],
                                                             in_=pb[:, 0:128].rearrange("p (a b) -> p a b", a=2)), reads=[rpb], writes=[rVA])
                      cx.op(dve, lambda pb=pb, kt=kt: V.tensor_copy(out=VN[:, kt, :].rearrange("p (a b) -> p a b", a=3)[:, 0::2, :],
                                                                    in_=pb[:, 128:256].rearrange("p (a b) -> p a b", a=2)), reads=[rpb], writes=[rVN])
                      cx.op(dve, lambda pb=pb, kt=kt: V.tensor_copy(out=LFRAW[:, kt:kt + 1], in_=pb[:, 256:257]), reads=[rpb], writes=[rLFRAW])
                  if Tq + 1 < NT:
                      load_tile(Tq + 1)
                  deferred = []

                  def defer(n, fn, tag):
                      deferred.append([n, fn, tag])

                  def run_deferred(upto_tag=None, everything=False, pred=None):
                      if pred is None and upto_tag is not None:
                          pred = lambda t: t == upto_tag
                      while deferred:
                          n, fn, tag = deferred[0]
                          if everything or n <= 0 or (pred is not None and any(pred(d[2]) for d in deferred)):
                              deferred.pop(0)
                              fn()
                          else:
                              break

                  j0 = 1 if Tq == 0 else 0
                  nj = 32 - j0
                  n0 = 32 * Tq - 1 + j0
                  for s_ in range(2):
                      lo = 64 * s_
                      pb, rpb = pjx()
                      for l in range(32):
                          cx.op(pe, lambda l=l, lo=lo, pb=pb: T.matmul(pb[:, 0:nj], lhsT=W1[lo:lo + 64, l, :],
                                                                        rhs=CMPIN[lo:lo + 64, 16 * j0 + l:16 * j0 + l + 16 * (nj - 1) + 1:16],
                                                                        start=(l == 0), stop=(l == 31)), reads=[rW1, rCMPIN], writes=[rpb])
                      cx.op(act, lambda pb=pb, s_=s_: A.activation(out=HSS[:, 32 * s_:32 * s_ + nj], in_=pb[:, 0:nj], func=AF.Sigmoid, bias=BH[:, s_:s_ + 1]),
                            reads=[rpb, rBH], writes=[rHSS])
                      cx.op(dve, lambda pb=pb, s_=s_: V.scalar_tensor_tensor(out=HS[:, 32 * s_:32 * s_ + nj], in0=pb[:, 0:nj], scalar=BH[:, s_:s_ + 1],
                                                                             in1=HSS[:, 32 * s_:32 * s_ + nj], op0=ALU.add, op1=ALU.mult),
                            reads=[rpb, rBH, rHSS], writes=[rHS])
                  cx.op(pool, lambda: G.tensor_copy(out=CMPIN[:, 0:16], in_=CMPIN[:, 512:528]), reads=[rCMPIN], writes=[rCMPIN])

                  def kc_stage2():
                      pb, rpb = st_next()
                      cx.op(pe, lambda: T.matmul(pb[0:64, 0:nj], lhsT=W2[:, 0:64], rhs=HS[:, 0:nj], start=True, stop=True), reads=[rW2, rHS], writes=[rpb])
                      cx.op(pe, lambda: T.matmul(pb[0:64, 64:64 + nj], lhsT=W2[:, 64:128], rhs=HS[:, 32:32 + nj], start=False, stop=True, skip_group_check=True),
                            reads=[rW2, rHS], writes=[rpb])
                      cx.op(dve, lambda: V.tensor_copy(out=KCT[0:64, n0:n0 + nj], in_=pb[0:64, 0:nj]), reads=[rpb], writes=[rKCT])
                      cx.op(dve, lambda: V.tensor_copy(out=VCTF[0:64, n0:n0 + nj], in_=pb[0:64, 64:64 + nj]), reads=[rpb], writes=[rVCTF])
                      for a_ in sorted(set([max(n0, 0) // 128, (n0 + nj - 1) // 128])):
                          pb2, rpb2 = st_next()
                          cx.op(pe, lambda a_=a_, pb2=pb2: T.transpose(out=pb2[:, 0:64], in_=VCTF[0:64, a_ * 128:(a_ + 1) * 128], identity=IDF[0:64, 0:64]),
                                reads=[rVCTF, rC, rC2], writes=[rpb2])
                          cx.op(dve, lambda a_=a_, pb2=pb2: V.tensor_copy(out=VCA[:, a_, 0:64], in_=pb2[:, 0:64]), reads=[rpb2], writes=[rVCA])
                  defer(4, kc_stage2, ("kc", Tq))

                  k0 = Tq * 4
                  cx.op(dve, lambda: V.tensor_scalar(out=SPL[:, k0:k0 + 4], in0=LFRAW[:, k0:k0 + 4], scalar1=VEC[:, 0:1], scalar2=None, op0=ALU.add),
                        reads=[rLFRAW, rC, rC2, rVEC], writes=[rSPL])
                  cx.op(act, lambda: A.activation(out=SPL[:, k0:k0 + 4], in_=SPL[:, k0:k0 + 4], func=AF.Exp, scale=-1.0), reads=[rSPL], writes=[rSPL])
                  cx.op(act, lambda: A.activation(out=SPL[:, k0:k0 + 4], in_=SPL[:, k0:k0 + 4], func=AF.Ln, bias=1.0), reads=[rSPL], writes=[rSPL])

                  def nb_stage():
                      pb, rpb = st_next()
                      cx.op(pe, lambda: T.matmul(pb[:, 0:4], lhsT=TRIF[:], rhs=SPL[:, k0:k0 + 4], start=True, stop=True), reads=[rC, rC2, rSPL], writes=[rpb])
                      cx.op(pe, lambda: T.matmul(pb[:, 8:12], lhsT=ONESF[:], rhs=SPL[:, k0:k0 + 4], start=False, stop=True, skip_group_check=True), reads=[rC, rC2, rSPL], writes=[rpb])
                      for j in range(4):
                          cx.op(dve, lambda j=j: V.tensor_tensor(out=CARRY[:, k0 + j + 1:k0 + j + 2], in0=CARRY[:, k0 + j:k0 + j + 1],
                                                                 in1=pb[:, 8 + j:9 + j], op=ALU.add), reads=[rpb, rCARRY], writes=[rCARRY])
                      cx.op(dve, lambda: V.tensor_tensor(out=CNEG[:, k0:k0 + 4], in0=pb[:, 0:4], in1=CARRY[:, k0:k0 + 4], op=ALU.add),
                            reads=[rpb, rCARRY], writes=[rCNEG])
                      cx.op(dve, lambda: V.tensor_scalar(out=NB[:, 0:k0 + 4], in0=CNEG[:, 0:k0 + 4], scalar1=CARRY[:, k0:k0 + 1], scalar2=None,
                                                         op0=ALU.subtract), reads=[rCNEG, rCARRY], writes=[rNB])
                  defer(8, nb_stage, ("nb", Tq))
                  chain = []


                  gidc = [0, 0]

                  def begin_group():
                      gidc[0] += 1
                      gidc[1] = 0

                  def add_item(smm, expf, maskf, pvf, after=None, pre=None):
                      chain.append([smm, expf, maskf, pvf, after, pre, gidc[0], gidc[1] == 0])
                      gidc[1] += 1

                  def outproj(st_):
                      for dc in range(2):
                          pb, rpb = st_next()
                          cx.op(pe, lambda dc=dc, pb=pb: T.matmul(pb[:], lhsT=CT0[:, st_ * 128:(st_ + 1) * 128], rhs=WOUT[:, 0, dc * 512:(dc + 1) * 512],
                                                                 start=True, stop=False), reads=[rCT0, rWOUT], writes=[rpb])
                          cx.op(pe, lambda dc=dc, pb=pb: T.matmul(pb[:], lhsT=CT1[:, st_ * 128:(st_ + 1) * 128], rhs=WOUT[:, 1, dc * 512:(dc + 1) * 512],
                                                                 start=False, stop=True), reads=[rCT1, rWOUT], writes=[rpb])
                          cx.op(dve, lambda dc=dc, pb=pb: V.tensor_copy(out=OUTS[:, dc * 512:(dc + 1) * 512], in_=pb[:]), reads=[rpb], writes=[rOUTS])
                      ch = Tq // TPC
                      r0 = (Tq % TPC) * 512 + st_ * 128
                      cx.dma(sp, sto, part_d[ch][r0:r0 + 128, :], OUTS[:], reads=[rOUTS, rPart[ch]])

                  def branch_out(ob_, rob_, br, qc, o_low, Y, rY):
                      so, oo = (64, 0) if o_low else (0, 64)
                      cx.op(dve, lambda: V.tensor_scalar(out=FN[so:so + 64, :], in0=ob_[so:so + 64, 0:256], scalar1=1e-30, scalar2=None, op0=ALU.max),
                            reads=[rob_], writes=[rFN])
                      cx.op(dve, lambda: V.reciprocal(out=FN[so:so + 64, :], in_=FN[so:so + 64, :]), reads=[rFN], writes=[rFN])
                      cx.op(dve, lambda: V.tensor_tensor(out=FN[so:so + 64, :].rearrange("p (h q) -> p h q", h=2),
                                                         in0=FN[so:so + 64, :].rearrange("p (h q) -> p h q", h=2),
                                                         in1=GZ[so:so + 64, br, :, qc:qc + 128], op=ALU.mult), reads=[rFN, rGZ], writes=[rFN])
                      cx.op(dve, lambda: V.tensor_tensor(out=Y, in0=ob_[oo:oo + 64, 0:256], in1=FN[so:so + 64, :], op=ALU.mult),
                            reads=[rob_, rFN], writes=[rY])

                  def cmp_group(bi):
                      begin_group()
                      i = 4 * Tq + bi
                      qc = 128 * bi
                      a_max = i // 16
                      r16 = i % 16
                      par = i % 2
                      ob_, rob_ = ob_next()
                      for a_ in range(a_max + 1):
                          def s_c(sb_, rsb, a_=a_):
                              cx.op(pe, lambda: T.matmul(sb_[:], lhsT=KCT[:, a_ * 128:(a_ + 1) * 128], rhs=QN[:, :, qc:qc + 128],
                                                         start=True, stop=True), reads=[rKCT, rQN], writes=[rsb])

                          def e_c(sb_, rsb, pt, rpt, a_=a_):
                              cx.op(act, lambda: A.activation(out=pt[:], in_=sb_[:], func=AF.Exp, scale=0.125), reads=[rsb], writes=[rpt])

                          def m_c(pt, rpt, a_=a_):
                              cx.op(dve, lambda: V.tensor_tensor(out=pt[:].rearrange("p (h q) -> p h q", h=4),
                                                                 in0=pt[:].rearrange("p (h q) -> p h q", h=4),
                                                                 in1=bcast_h(MASKC[:, r16 * 128:(r16 + 1) * 128], 4),
                                                                 op=ALU.mult), reads=[rpt, rC, rC2], writes=[rpt])

                          def p_c(pt, rpt, a_=a_):
                              cx.op(pe, lambda: T.matmul(ob_[:, 0:256], lhsT=VCA[:, a_, :], rhs=pt[:, 0:256], start=(a_ == 0), stop=(a_ == a_max)),
                                    reads=[rVCA, rpt], writes=[rob_])
                              for h in range(4):
                                  cx.op(pe, lambda h=h: T.matmul(IMPB[:, h * 128:(h + 1) * 128], lhsT=pt[:, h * 128:(h + 1) * 128], rhs=OV[:, a_, :],
                                                                 start=(a_ == 0 and h == 0), stop=(a_ == a_max), skip_group_check=True),
                                        reads=[rpt, rC, rC2], writes=[rIMPB])

                          def epi_cmp():
                              branch_out(ob_, rob_, 0, qc, True, YA[par], rYA[par])

                              def stage2():
                                  cx.op(dve, lambda: V.tensor_reduce(out=RS4[:], in_=IMPB[:].rearrange("p (h j) -> p h j", h=4), axis=AX.X, op=ALU.add),
                                        reads=[rIMPB], writes=[rRS4])
                                  cx.op(dve, lambda: V.tensor_scalar(out=RS4[:], in0=RS4[:], scalar1=1e-30, scalar2=None, op0=ALU.max), reads=[rRS4], writes=[rRS4])
                                  cx.op(dve, lambda: V.reciprocal(out=RI4[:], in_=RS4[:]), reads=[rRS4], writes=[rRI4])
                                  cx.op(dve, lambda: V.tensor_scalar(out=SC[:], in0=IMPB[:, 0:128], scalar1=RI4[:, 0:1], scalar2=None, op0=ALU.mult),
                                        reads=[rIMPB, rRI4], writes=[rSC])
                                  for h in range(1, 4):
                                      cx.op(dve, lambda h=h: V.scalar_tensor_tensor(out=SC[:], in0=IMPB[:, h * 128:(h + 1) * 128], scalar=RI4[:, h:h + 1], in1=SC[:],
                                                                                    op0=ALU.mult, op1=ALU.add), reads=[rIMPB, rRI4, rSC], writes=[rSC])

                              def stage3():
                                  cx.op(dve, lambda: V.memset(SC[:, 0:1], BIG), reads=[], writes=[rSC])
                                  lo0 = max(2 * i - 1, 0)
                                  cx.op(dve, lambda: V.memset(SC[0:64, lo0:2 * i + 1], BIG), writes=[rSC])
                                  cx.op(dve, lambda: V.memset(SC[64:128, 2 * i:2 * i + 2], BIG), writes=[rSC])
                                  cx.op(dve, lambda: V.max(out=M8[:], in_=SC[:]), reads=[rSC], writes=[rM8])
                                  cx.op(dve, lambda: V.match_replace(out=SC2[:], in_to_replace=M8[:], in_values=SC[:], imm_value=-BIG),
                                        reads=[rSC, rM8], writes=[rSC2])
                                  cx.op(dve, lambda: V.max(out=M8b[:], in_=SC2[:]), reads=[rSC2], writes=[rM8b])
                                  cx.op(dve, lambda: V.tensor_scalar(out=MNEG2[par][:], in0=SC[:], scalar1=M8b[:, 7:8], scalar2=-30000.0, op0=ALU.is_lt, op1=ALU.mult),
                                        reads=[rSC, rM8b], writes=[rMNEG2[par]])
                                  cx.op(pool, lambda: G.tensor_copy(out=QSA[par][0:64, :, :].rearrange("p a (h q) -> p a h q", h=2),
                                                                    in_=bass.AP(tensor=QN.tensor if hasattr(QN, "tensor") else QN[0:64, 0:2, qc:qc + 128].tensor,
                                                                                offset=QN[0:64, 0:2, qc:qc + 128].offset,
                                                                                ap=[list(QN[0:64, 0:2, qc:qc + 128].ap[0]), [0, 2]] + [list(a) for a in QN[0:64, 0:2, qc:qc + 128].ap[1:]])),
                                        reads=[rQN], writes=[rQSA[par][0], rQSA[par][1]])

                              def stage_b():
                                  mt, rmt = st_next()
                                  cx.op(pe, lambda: T.transpose(out=mt[:, 0:128], in_=MNEG2[par][:], identity=IDF[:]), reads=[rMNEG2[par], rC, rC2], writes=[rmt])
                                  for half in range(2):
                                      cx.op(dve, lambda half=half: V.tensor_copy(out=QSA[par][64:128, half, :].rearrange("p (h q) -> p h q", h=2),
                                                                                 in_=bcast_h(mt[64 * half:64 * half + 64, 0:128], 2)),
                                            reads=[rmt], writes=[rQSA[par][half]])
                              defer(6, stage2, ("c", i))
                              defer(12, stage3, ("c", i))
                              defer(28, stage_b, ("q", i))
                          add_item(s_c, e_c, m_c if a_ == a_max else None, p_c, epi_cmp if a_ == a_max else None,
                                   (lambda: run_deferred(pred=lambda t: t[0] in ("c", "kc"))) if a_ == 0 else None)

                  def win_group(bi):
                      begin_group()
                      i = 4 * Tq + bi
                      qc = 128 * bi
                      ob_, rob_ = ob_next()
                      kts = list(range(max(0, i - 4), i + 1))
                      for kt in kts:
                          def s_w(sb_, rsb, kt=kt):
                              wl = (kt % 8) * 128
                              cx.op(pe, lambda: T.matmul(sb_[:, 0:256], lhsT=KTW[:, wl:wl + 128], rhs=QN[:, 0:2, qc:qc + 128], start=True, stop=True),
                                    reads=[rKTW, rQN], writes=[rsb])

                          def e_w(sb_, rsb, pt, rpt):
                              cx.op(act, lambda: A.activation(out=pt[:, 0:256], in_=sb_[:, 0:256], func=AF.Exp, scale=0.125), reads=[rsb], writes=[rpt])
                          mk = None
                          if kt == i or kt == i - 4:
                              def mk(pt, rpt, kt=kt):
                                  msk = TRI if kt == i else TRIU
                                  cx.op(dve, lambda: V.tensor_tensor(out=pt[:, 0:256].rearrange("p (h q) -> p h q", h=2),
                                                                     in0=pt[:, 0:256].rearrange("p (h q) -> p h q", h=2),
                                                                     in1=bcast_h(msk[:], 2), op=ALU.mult), reads=[rpt, rC, rC2], writes=[rpt])

                          def p_w(pt, rpt, kt=kt):
                              cx.op(pe, lambda: T.matmul(ob_[:, 0:256], lhsT=VN[:, kt, 64:192], rhs=pt[:, 0:256], start=(kt == kts[0]), stop=(kt == kts[-1])),
                                    reads=[rVN, rpt], writes=[rob_])
                          add_item(s_w, e_w, mk, p_w, (lambda: branch_out(ob_, rob_, 2, qc, False, YB[:], rYB)) if kt == kts[-1] else None)

                  def sel_group(bi):
                      begin_group()
                      i = 4 * Tq + bi
                      qc = 128 * bi
                      par = i % 2
                      ob_, rob_ = ob_next()
                      for kt in range(i + 1):
                          def s_s(sb_, rsb, kt=kt):
                              half = kt // 32
                              cx.op(pe, lambda: T.matmul(sb_[:, 0:256], lhsT=KTS[:, kt * 128:(kt + 1) * 128], rhs=QSA[par][:, half, :], start=True, stop=True),
                                    reads=[rKTS, rQSA[par][half]], writes=[rsb])

                          def e_s(sb_, rsb, pt, rpt):
                              cx.op(act, lambda: A.activation(out=pt[:, 0:256], in_=sb_[:, 0:256], func=AF.Exp, scale=0.125), reads=[rsb], writes=[rpt])
                          mk = None
                          if kt == i:
                              def mk(pt, rpt):
                                  cx.op(dve, lambda: V.tensor_tensor(out=pt[:, 0:256].rearrange("p (h q) -> p h q", h=2),
                                                                     in0=pt[:, 0:256].rearrange("p (h q) -> p h q", h=2),
                                                                     in1=bcast_h(TRI[:], 2), op=ALU.mult), reads=[rpt, rC, rC2], writes=[rpt])

                          def p_s(pt, rpt, kt=kt):
                              cx.op(pe, lambda: T.matmul(ob_[:, 0:256], lhsT=VN[:, kt, 0:128], rhs=pt[:, 0:256], start=(kt == 0), stop=(kt == i)),
                                    reads=[rVN, rpt], writes=[rob_])

                          def epi_sel():
                              branch_out(ob_, rob_, 1, qc, True, YC[:], rYC)
                              cx.op(pool, lambda: G.tensor_tensor(out=YB[:], in0=YA[par], in1=YB[:], op=ALU.add), reads=[rYA[par], rYB], writes=[rYB])
                              for h in range(2):
                                  cx.op(pool, lambda h=h: G.tensor_tensor(out=CT1[64 * h:64 * h + 64, qc:qc + 128], in0=YB[:, 128 * h:128 * h + 128],
                                                                          in1=YC[:, 128 * h:128 * h + 128], op=ALU.add), reads=[rYB, rYC], writes=[rCT1])
                              defer(8, lambda: outproj(bi), ("o", i))
                          add_item(s_s, e_s, mk, p_s, epi_sel if kt == i else None, (lambda: run_deferred(upto_tag=("q", i))) if kt == 0 else None)

                  nkt = 4 * Tq + 4

                  def dense_group(kind):
                      begin_group()
                      ob_, rob_ = ob_next()
                      for kt in range(nkt):
                          jd = kt - 4 * Tq
                          cc = 128 * jd if jd >= 0 else 0
                          first, last = (kt == 0), (kt == nkt - 1)

                          def smm(sb_, rsb, kt=kt, cc=cc):
                              qsrc = (QF, QD0, QD1)[kind]
                              cx.op(pe, lambda: T.matmul(sb_[:, cc:512], lhsT=KTA[:, kt * 128:(kt + 1) * 128], rhs=qsrc[:, cc:512],
                                                         start=True, stop=True), reads=[rKTA, rQA], writes=[rsb])

                          def expf(sb_, rsb, pt, rpt, kt=kt, cc=cc):
                              if kind == 0:
                                  cx.op(act, lambda: A.activation(out=pt[:, cc:512], in_=sb_[:, cc:512], func=AF.Exp, bias=NB[:, kt:kt + 1], scale=0.125),
                                        reads=[rsb, rNB], writes=[rpt])
                              else:
                                  cx.op(act, lambda: A.activation(out=pt[:, cc:512], in_=sb_[:, cc:512], func=AF.Exp, scale=float(32 ** -0.5)),
                                        reads=[rsb], writes=[rpt])

                          def m_diag(pt, rpt, cc=cc):
                              cx.op(pool, lambda: G.tensor_tensor(out=pt[:, cc:cc + 128], in0=pt[:, cc:cc + 128], in1=TRI[:], op=ALU.mult),
                                    reads=[rpt, rC, rC2], writes=[rpt])

                          def pvf(pt, rpt, kt=kt, cc=cc, first=first, last=last):
                              lhs = VA[:, kt, 0:128] if kind == 0 else VA[:, kt, 64:192]
                              cx.op(pe, lambda: T.matmul(ob_[:, cc:512], lhsT=lhs, rhs=pt[:, cc:512], start=first, stop=last),
                                    reads=[rVA, rpt], writes=[rob_])

                          def epi():
                              if kind == 0:
                                  cx.op(dve, lambda: V.reciprocal(out=FA[64:128, :], in_=ob_[64:128, :]), reads=[rob_], writes=[rFA])
                                  cx.op(dve, lambda: V.tensor_tensor(out=FA[64:128, :], in0=FA[64:128, :], in1=Z1[64:128, :], op=ALU.mult),
                                        reads=[rFA, rZ1], writes=[rFA])
                                  cx.op(dve, lambda: V.tensor_tensor(out=CT0[0:64, :], in0=ob_[0:64, :], in1=FA[64:128, :], op=ALU.mult),
                                        reads=[rob_, rFA], writes=[rCT0])
                              elif kind == 1:
                                  cx.op(dve, lambda: V.reciprocal(out=FB[0:64, :], in_=ob_[0:64, :]), reads=[rob_], writes=[rFB])
                                  cx.op(dve, lambda: V.tensor_tensor(out=T0[0:64, :], in0=ob_[64:128, :], in1=FB[0:64, :], op=ALU.mult),
                                        reads=[rob_, rFB], writes=[rT0])
                              else:
                                  cx.op(dve, lambda: V.reciprocal(out=FB[0:64, :], in_=ob_[0:64, :]), reads=[rob_, rFB], writes=[rFB])
                                  cx.op(dve, lambda: V.tensor_scalar(out=FB[0:64, :], in0=FB[0:64, :], scalar1=NEGLAM[0:64, :], scalar2=None, op0=ALU.mult),
                                        reads=[rFB, rLAM], writes=[rFB])
                                  cx.op(dve, lambda: V.tensor_tensor(out=T1[0:64, :], in0=ob_[64:128, :], in1=FB[0:64, :], op=ALU.mult),
                                        reads=[rob_, rFB], writes=[rT1])
                                  cx.op(pool, lambda: G.tensor_tensor(out=T0[0:64, :], in0=T0[0:64, :], in1=T1[0:64, :], op=ALU.add),
                                        reads=[rT0, rT1], writes=[rT0])
                                  cx.op(pool, lambda: G.tensor_tensor(out=T1[0:64, :], in0=T0[0:64, :], in1=T0[0:64, :], op=ALU.mult),
                                        reads=[rT0, rT1], writes=[rT1])

                                  def stage_b():
                                      pb, rpb = st_next()
                                      cx.op(pe, lambda: T.matmul(pb[0:64, :], lhsT=ONESF[0:64, 0:64], rhs=T1[0:64, :], start=True, stop=True),
                                            reads=[rC, rC2, rT1], writes=[rpb])
                                      cx.op(act, lambda: A.activation(out=FB[0:64, :], in_=pb[0:64, :], func=AF.Ln, scale=1.0 / 64.0, bias=EPSC[0:64, :]),
                                            reads=[rpb, rC, rC2], writes=[rFB])
                                      cx.op(act, lambda: A.activation(out=FB[0:64, :], in_=FB[0:64, :], func=AF.Exp, scale=-0.5), reads=[rFB], writes=[rFB])
                                      cx.op(dve, lambda: V.scalar_tensor_tensor(out=T0[0:64, :], in0=T0[0:64, :], scalar=GCOL[0:64, :], in1=FB[0:64, :],
                                                                                op0=ALU.mult, op1=ALU.mult), reads=[rT0, rFB, rLAM], writes=[rT0])
                                      cx.op(pool, lambda: G.tensor_tensor(out=CT0[64:128, :], in0=T0[0:64, :], in1=Z1[0:64, :], op=ALU.mult),
                                            reads=[rT0, rZ1], writes=[rCT0])
                                  defer(14, stage_b, ("d", Tq))
                          add_item(smm, expf, m_diag if jd >= 0 else None, pvf, epi if last else None,
                                   (lambda: run_deferred(pred=lambda t: t[0] == "nb")) if (kind == 0 and kt == 0) else None)

                  if os.environ.get("DBG_ORDER") == "old":
                      dense_group(0)
                      dense_group(1)
                      dense_group(2)
                      for bi in range(4):
                          cmp_group(bi)
                          win_group(bi)
                          sel_group(bi)
                  else:
                      dense_group(1)
                      dense_group(2)
                      cmp_group(0)
                      dense_group(0)
                      for bi in range(4):
                          if bi + 1 < 4:
                              cmp_group(bi + 1)
                          win_group(bi)
                          sel_group(bi)

                  pend = []

                  def pop_pair():
                      X = pend.pop(0)
                      Y = pend.pop(0) if pend else None
                      order = [X] if Y is None else ([X, Y] if (X[5] and X[4] == Y[4]) else [Y, X])
                      for it in order:
                          it[0](it[1], it[2])
                      for it in ([X] if Y is None else [X, Y]):
                          if it[3] is not None:
                              it[3]()
                  ci = 0
                  while ci < len(chain):
                      pair = chain[ci:ci + 2]
                      ci += 2
                      for it in pair:
                          for d_ in deferred:
                              d_[0] -= 1
                      run_deferred()
                      for it in pair:
                          if it[5] is not None:
                              it[5]()
                      slots = []
                      for it in pair:
                          sb_, rsb = st_next()
                          pi = ptc[0] % NPT
                          ptc[0] += 1
                          slots.append((sb_, rsb, PT[pi], rPT[pi]))
                      for it, sl in reversed(list(zip(pair, slots))):
                          it[0](sl[0], sl[1])
                      for it, sl in reversed(list(zip(pair, slots))):
                          it[1](sl[0], sl[1], sl[2], sl[3])
                          if it[2] is not None:
                              it[2](sl[2], sl[3])
                      for it, sl in zip(pair, slots):
                          pend.append((it[3], sl[2], sl[3], it[4], it[6], it[7]))
                      if len(pend) > 2:
                          pop_pair()
                  while pend:
                      pop_pair()
                  run_deferred(everything=True)


                  if Tq % TPC == TPC - 1:
                      ch = Tq // TPC
                      cx._need(pool, [], [rPart[ch], rRSd[ch]])
                      ins = G.collective_compute("ReduceScatter", ALU.add, replica_groups=RG, ins=[part_d[ch].opt()], outs=[rs_d[ch].opt()])
                      rsc[ch].cnt += 1
                      ins.then_inc(rsc[ch].sem)
                      rPart[ch].w = (rsc[ch], rsc[ch].cnt); rPart[ch].r = []
                      rRSd[ch].w = (rsc[ch], rsc[ch].cnt); rRSd[ch].r = []

              if L + 1 < depth:
                  load_weights(L + 1)
              if ALIAS:
                  rINB = cx.fork(rVA, 2) + cx.fork(rVN, 2)
                  rACCB = cx.fork(rKTA, 4)
                  rYTB2 = cx.fork(rTAB, 4)
                  (rGB,) = cx.fork(rXB, 1)
              cx.dma(sp, ldg, GTB, lng_d[L], writes=[rGB])
              cx.dma(sp, ldg, BTB, lnb_d[L], writes=[rGB])
              rGB.w = (ldg, ldg.cnt)
              xres_d = xq_d if L == 0 else yq_d
              ntb = SQ // 128
              last = (L == depth - 1)

              cx._need(sp, [], [rYQ])

              cx._need(act, [], [rYQ])

              def loadb(i):
                  k = i % 4
                  r0 = i * 128
                  ch, jj = i // TPC, i % TPC
                  cx.dma(sp, ldb[k], INB[k][:, 0, :], rs_d[ch][jj * 128:(jj + 1) * 128, :], reads=[rRSd[ch]], writes=[rINB[k]])
                  cx.dma(sp, ldb[k], INB[k][:, 1, :], xres_d[r0:r0 + 128, :], writes=[rINB[k]])
                  rINB[k].w = (ldb[k], ldb[k].cnt)

              for i_ in range(min(3, ntb)):
                  loadb(i_)
              for i in range(ntb):
                  k = i % 4
                  if i + 3 < ntb:
                      loadb(i + 3)
                  I_, Ac, rI, rAc = INB[k], ACCB[k], rINB[k], rACCB[k]
                  cx.op(dve, lambda: V.scalar_tensor_tensor(out=Ac, in0=I_[:, 1, :], scalar=float(ALPHA), in1=I_[:, 0, :], op0=ALU.mult, op1=ALU.add),
                        reads=[rI], writes=[rAc])
                  for h in range(2):
                      cx.op(dve, lambda h=h: V.bn_stats(out=STT[:, h, :], in_=Ac[:, h * 512:(h + 1) * 512]), reads=[rAc], writes=[rSTT])
                  cx.op(dve, lambda: V.bn_aggr(out=MV[:, 0:2], in_=STT[:].rearrange("p a b -> p (a b)")), reads=[rSTT], writes=[rMV])
                  cx.op(dve, lambda: V.tensor_scalar(out=MV[:, 2:3], in0=MV[:, 1:2], scalar1=1e-5, scalar2=None, op0=ALU.add), reads=[rMV], writes=[rMV])
                  cx.op(act, lambda: A.activation(out=MV[:, 2:3], in_=MV[:, 2:3], func=AF.Ln), reads=[rMV], writes=[rMV])
                  cx.op(act, lambda: A.activation(out=MV[:, 3:4], in_=MV[:, 2:3], func=AF.Exp, scale=-0.5), reads=[rMV], writes=[rMV])
                  cx.op(dve, lambda: V.tensor_scalar(out=Ac, in0=Ac, scalar1=MV[:, 0:1], scalar2=MV[:, 3:4], op0=ALU.subtract, op1=ALU.mult),
                        reads=[rAc, rMV], writes=[rAc])
                  cx.op(dve, lambda: V.tensor_tensor(out=Ac, in0=Ac, in1=GTB, op=ALU.mult), reads=[rAc, rGB], writes=[rAc])
                  cx.op(dve, lambda: V.tensor_tensor(out=Ac, in0=Ac, in1=BTB, op=ALU.add), reads=[rAc, rGB], writes=[rAc])
                  if last:
                      cx.dma(act, sta[k], y_d[i * 128:(i + 1) * 128, :], Ac, reads=[rAc])
                  else:
                      cx.dma(act, sta[k], yq_d[i * 128:(i + 1) * 128, :], Ac, reads=[rAc, rYQ])
                      for hb in range(2):
                          pb, rpb = pj()
                          for c4 in range(4):
                              c = 4 * hb + c4
                              cx.op(pe, lambda c=c, c4=c4, pb=pb: T.transpose(out=pb[:, c4 * 128:(c4 + 1) * 128], in_=Ac[:, c * 128:(c + 1) * 128], identity=IDF[:]),
                                    reads=[rAc, rC], writes=[rpb])
                          cx.op(act if hb == 0 else dve, lambda hb=hb, pb=pb: (A.copy if hb == 0 else V.tensor_copy)(out=YTB2[k][:, 4 * hb:4 * hb + 4, :], in_=pb[:].rearrange("p (c t) -> p c t", c=4)),
                                reads=[rpb], writes=[rYTB2[k]])
                      i2 = i // 2
                      for c in range(8):
                          cx.dma(act, styt2[k], ytq_d[i2][c * 128:(c + 1) * 128, (i % 2) * 128:(i % 2 + 1) * 128], YTB2[k][:, c, :],
                                 reads=[rYTB2[k], rYTQ[i2]])
                      if i % 2 == 1:
                          cx._need(pool, [], [rYTQ[i2], rXT1[i2]])
                          ins = G.collective_compute("AllGather", ALU.bypass, replica_groups=RG, ins=[ytq_d[i2].opt()], outs=[xT1_d[i2].opt()])
                          ccs.cnt += 1
                          ins.then_inc(ccs.sem)
                          rYTQ[i2].w = (ccs, ccs.cnt); rYTQ[i2].r = []
                          rXT1[i2].w = (ccs, ccs.cnt); rXT1[i2].r = []
              if ALIAS:
                  cx.join(rVA, rINB[0:2]); cx.join(rVN, rINB[2:4]); cx.join(rKTA, rACCB); cx.join(rTAB, rYTB2); cx.join(rXB, [rGB])
        cx.wait_all(sp, sta)
    return nc


def _consts(S):
    half = 32
    t = np.arange(S, dtype=np.float32)
    tab = np.zeros((128, 4, S), np.float32)
    inv = (10000.0 ** (-(np.arange(32, dtype=np.float32) * 2.0 / 64))).astype(np.float32)
    ang = t[None, :] * inv[:, None]
    cosn, sinn = np.cos(ang), np.sin(ang)
    for r in range(128):
        i = r % 64
        tab[r, 0] = cosn[i % 32]
        tab[r, 1] = -sinn[i % 32] if i < 32 else sinn[i % 32]
    invd = (10000.0 ** (-(np.arange(16, dtype=np.float32) * 2.0 / 32))).astype(np.float32)
    angd = t[None, :] * invd[:, None]
    cosd, sind = np.cos(angd), np.sin(angd)
    for r in range(64):
        w = r % 32
        tab[r, 2] = cosd[w % 16]
        tab[r, 3] = -sind[w % 16] if w < 16 else sind[w % 16]
    key = np.arange(S)
    ind = (((key[None, :] // 64) % 64) == np.arange(64)[:, None]).astype(np.float32)
    p = np.arange(128)
    c_tri = (p[:, None] <= p[None, :]).astype(np.float32)
    c_triu = (p[:, None] > p[None, :]).astype(np.float32)
    c_ident = np.eye(128, dtype=np.float32)
    u = np.arange(2048)
    c_maskc = ((16 * p[:, None] + 31) <= u[None, :]).astype(np.float32)
    n = np.arange(512)
    j = np.arange(128)
    ov = np.clip(np.minimum(16 * n[:, None] + 32, 64 * j[None, :] + 64) - np.maximum(16 * n[:, None], 64 * j[None, :]), 0, None)
    ov = (ov.astype(np.float32) / 32.0)
    ov[511] = 0.0
    c_ov = np.ascontiguousarray(ov.reshape(4, 128, 128).transpose(1, 0, 2))
    return dict(tab=tab, ind=ind, c_tri=c_tri, c_triu=c_triu, c_ident=c_ident, c_maskc=c_maskc, c_ov=c_ov)


def _core_cols(c):
    g, hp = c // 2, c % 2
    rng64 = np.arange(64)
    sw64 = (rng64 + 32) % 64
    d32 = np.arange(32)
    swd = np.concatenate([(d32 + 16) % 32, 32 + (d32 + 16) % 32])
    dq = OFF['diff_q'] + 64 * c + rng64
    dk = OFF['diff_k'] + 64 * c + rng64
    dq_sw = OFF['diff_q'] + 64 * c + swd
    dk_sw = OFF['diff_k'] + 64 * c + swd
    fq = OFF['fox_q'] + 64 * c + rng64
    fk = OFF['fox_k'] + 64 * c + rng64
    my = [4 * g + 2 * hp, 4 * g + 2 * hp + 1]
    oth = [4 * g + 2 * (1 - hp), 4 * g + 2 * (1 - hp) + 1]
    nq = lambda H, perm: OFF['nsa_q'] + 64 * H + perm
    kc = lambda perm: OFF['nsa_k_cmp'] + 64 * g + perm
    ks = lambda perm: OFF['nsa_k_sel'] + 64 * g + perm
    kw = lambda perm: OFF['nsa_k_win'] + 64 * g + perm
    vcmp = OFF['nsa_v_cmp'] + 64 * g + rng64
    gate = lambda H, br: np.full(64, OFF['nsa_gate'] + 3 * H + br)
    groups = [
        np.concatenate([dq, fq]), np.concatenate([dk, fk]), np.concatenate([dq_sw, dk_sw]),
        np.concatenate([nq(my[0], rng64), nq(my[1], rng64)]), np.concatenate([nq(my[0], sw64), nq(my[1], sw64)]),
        np.concatenate([nq(oth[0], rng64), nq(oth[1], rng64)]), np.concatenate([nq(oth[0], sw64), nq(oth[1], sw64)]),
        np.concatenate([kc(rng64), ks(rng64)]), np.concatenate([kc(sw64), ks(sw64)]),
        np.concatenate([kw(rng64), vcmp]), np.concatenate([kw(sw64), vcmp]),
        np.concatenate([OFF['diff_z'] + 64 * c + rng64, OFF['fox_z'] + 64 * c + rng64]),
        np.concatenate([OFF['nsa_z'] + 64 * my[0] + rng64, OFF['nsa_z'] + 64 * my[1] + rng64]),
        np.concatenate([gate(my[0], 0), gate(my[1], 0)]), np.concatenate([gate(my[0], 1), gate(my[1], 1)]),
        np.concatenate([gate(my[0], 2), gate(my[1], 2)]),
    ]
    fm = np.concatenate(groups)
    tm = np.concatenate([OFF['fox_v'] + 64 * c + rng64, OFF['diff_v'] + 64 * c + rng64,
                         OFF['nsa_v_sel'] + 64 * g + rng64, OFF['nsa_v_win'] + 64 * g + rng64,
                         np.array([OFF['fox_f'] + c])])
    wo = np.concatenate([64 * c + rng64, 256 + 512 + 64 * c + rng64, 256 + 64 * my[0] + rng64, 256 + 64 * my[1] + rng64])
    return fm, tm, wo


_PROG = {}


def _get(S, depth):
    k = (S, depth)
    if k not in _PROG:
        _PROG[k] = build_fused(S, depth)
    return _PROG[k]


def kernel(x, w_in, b_fox_f, cmp_pos_k, cmp_pos_v, cmp_w1_k, cmp_w2_k, cmp_w1_v, cmp_w2_v,
           lam_q1, lam_k1, lam_q2, lam_k2, diff_subln_g, w_out, ln_g, ln_b):
    f32 = lambda a: np.ascontiguousarray(np.asarray(a, dtype=np.float32))
    x = f32(x)
    B, S, D = x.shape
    depth = w_in.shape[0]
    SQ = S // 4
    w_in, w_out = f32(w_in), f32(w_out)
    cst = _consts(S)
    cols = [_core_cols(c) for c in range(4)]
    w1 = np.ascontiguousarray(np.stack([f32(cmp_w1_k), f32(cmp_w1_v)], axis=1))
    w2 = np.ascontiguousarray(np.concatenate([f32(cmp_w2_k), f32(cmp_w2_v)], axis=2))
    post = np.ascontiguousarray(np.concatenate([f32(cmp_pos_k).transpose(0, 2, 1), f32(cmp_pos_v).transpose(0, 2, 1)], axis=1))
    lng = np.ascontiguousarray(np.broadcast_to(f32(ln_g)[:, None, :], (depth, 128, D)))
    lnb = np.ascontiguousarray(np.broadcast_to(f32(ln_b)[:, None, :], (depth, 128, D)))
    NT = S // 512
    TPC = 1

    def tokmap(r):
        idx = []
        for i in range(SQ // 128):
            base = (i // TPC) * TPC * 512 + r * TPC * 128 + (i % TPC) * 128
            idx.append(np.arange(base, base + 128))
        return np.concatenate(idx)
    tmaps = [tokmap(r) for r in range(4)]
    xT = [np.ascontiguousarray(x[b].T) for b in range(B)]
    in_maps = []
    for core in range(8):
        b, c = core // 4, core % 4
        fm, tm, wo = cols[c]
        vec = np.zeros((depth, 128, 8), np.float32)
        for l in range(depth):
            lam_init = 0.8 - 0.6 * math.exp(-0.3 * l)
            vec[l, :, 0] = f32(b_fox_f)[l, c]
            vec[l, :, 1] = f32(diff_subln_g)[l][np.arange(128) % 64]
            vec[l, 0:32, 2] = f32(lam_q1)[l]; vec[l, 0:32, 3] = f32(lam_k1)[l]
            vec[l, 0:32, 4] = f32(lam_q2)[l]; vec[l, 0:32, 5] = f32(lam_k2)[l]
            vec[l, :, 6] = -lam_init
            vec[l, :, 7] = 1.0 - lam_init
        m = dict(xT=xT[b], xq=np.ascontiguousarray(x[b, tmaps[c]]),
                 wfm=np.ascontiguousarray(w_in[:, :, fm]), wtm=np.ascontiguousarray(w_in[:, :, tm]),
                 wout=np.ascontiguousarray(w_out[:, wo, :]), w1=w1, w2=w2, post=post, vecs=vec, lng=lng, lnb=lnb)
        m.update(cst)
        in_maps.append(m)
    res = run_bass_kernel_spmd(_get(S, depth), in_maps, core_ids=list(range(8)))
    out = np.empty((B, S, D), np.float32)
    for core in range(8):
        b, c = core // 4, core % 4
        out[b, tmaps[c]] = res.results[core]["y"]
    return out
```

```python
import math
import os
from contextlib import ExitStack
import numpy as np
import concourse.bass as bass
import concourse.mybir as mybir
from concourse.bass_utils import run_bass_kernel_spmd

F32 = mybir.dt.float32
BF16 = mybir.dt.bfloat16
ALU = mybir.AluOpType
AF = mybir.ActivationFunctionType
AX = mybir.AxisListType

D_MODEL = 1024
DEPTH = 2
HD = 64
IN_SPLITS = (('fox_q', 256), ('fox_k', 256), ('fox_v', 256), ('fox_f', 4), ('fox_z', 256), ('nsa_q', 512),
             ('nsa_k_cmp', 128), ('nsa_v_cmp', 128), ('nsa_k_sel', 128), ('nsa_v_sel', 128),
             ('nsa_k_win', 128), ('nsa_v_win', 128), ('nsa_gate', 24), ('nsa_z', 512),
             ('diff_q', 256), ('diff_k', 256), ('diff_v', 256), ('diff_z', 256))
OFF = {}
_a = 0
for _n, _w in IN_SPLITS:
    OFF[_n] = _a
    _a += _w
IN_WIDTH = _a
ALPHA = (2 * DEPTH) ** 0.25
NG = 16
BIG = 1.0e30


class Res:
    __slots__ = ("w", "r", "excl")

    def __init__(self, excl=False):
        self.excl = excl
        self.w = None
        self.r = []


class SemC:
    __slots__ = ("sem", "cnt", "key")
    _n = 0

    def __init__(self, sem):
        self.sem = sem
        self.cnt = 0
        SemC._n += 1
        self.key = SemC._n


class Eng:
    def __init__(self, h, semc, same_sync):
        self.h = h
        self.s = semc
        self.waited = {}
        self.same_sync = same_sync


class Ctx:
    def __init__(self, nc, stack):
        self.nc = nc
        self.stack = stack
        mk = lambda n: SemC(stack.enter_context(nc.semaphore(n)))
        self.pe = Eng(nc.tensor, mk("s_pe"), False)
        self.dve = Eng(nc.vector, mk("s_dve"), True)
        self.act = Eng(nc.scalar, mk("s_act"), True)
        self.pool = Eng(nc.gpsimd, mk("s_pool"), True)
        self.sp = Eng(nc.sync, mk("s_sp"), False)

    def res(self, excl=False):
        return Res(excl)

    def dsem(self, name):
        return SemC(self.stack.enter_context(self.nc.semaphore(name)))

    def sbuf(self, name, shape, dt):
        return self.stack.enter_context(self.nc.sbuf_tensor(name, list(shape), dt))

    def psum(self, name, shape, dt):
        return self.stack.enter_context(self.nc.psum_tensor(name, list(shape), dt))

    def _need(self, eng, reads, writes):
        need = {}

        def add(p):
            if p is None:
                return
            s, v = p
            if s is eng.s and not eng.same_sync:
                return
            if need.get(s.key, (None, -1))[1] < v:
                need[s.key] = (s, v)
        for r in reads:
            add(r.w)
            if r.excl:
                for p in r.r:
                    if p[0] is not eng.s:
                        add(p)
        for w in writes:
            add(w.w)
            for p in w.r:
                add(p)
        for k, (s, v) in need.items():
            if eng.waited.get(k, 0) < v:
                eng.h.wait_ge(s.sem, v)
                eng.waited[k] = v

    def op(self, eng, fn, reads=(), writes=()):
        self._need(eng, reads, writes)
        ins = fn()
        eng.s.cnt += 1
        ins.then_inc(eng.s.sem, 1)
        me = (eng.s, eng.s.cnt)
        for r in reads:
            r.r.append(me)
            if len(r.r) > 24:
                r.r = r.r[-24:] if False else _compact(r.r)
        for w in writes:
            w.w = me
            w.r = []
        return ins

    def dma(self, q, dsem, out, in_, reads=(), writes=()):
        self._need(q, reads, writes)
        ins = q.h.dma_start(out=out, in_=in_)
        dsem.cnt += 16
        ins.then_inc(dsem.sem, 16)
        me = (dsem, dsem.cnt)
        for r in reads:
            r.r.append(me)
        for w in writes:
            w.w = me
            w.r = []
        return ins

    def fork(self, parent, n):
        kids = [Res() for _ in range(n)]
        for k in kids:
            k.r = _compact(list(parent.r) + ([parent.w] if parent.w else []))
        return kids

    def join(self, parent, kids):
        for k in kids:
            parent.r = _compact(parent.r + k.r + ([k.w] if k.w else []))

    def wait_all(self, eng, semcs):
        for s in semcs:
            if s.cnt > 0 and eng.waited.get(s.key, 0) < s.cnt:
                eng.h.wait_ge(s.sem, s.cnt)
                eng.waited[s.key] = s.cnt


def _compact(lst):
    best = {}
    for s, v in lst:
        if best.get(s.key, (None, -1))[1] < v:
            best[s.key] = (s, v)
    return list(best.values())


def build_fused(S, depth=DEPTH):
    NT = S // 512
    NK = S // 128
    SQ = S // 4
    nc = bass.Bass("TRN2", target_bir_lowering=False)
    din = lambda n, sh, dt=F32: nc.dram_tensor(n, list(sh), dt, kind="ExternalInput").ap()
    xT0_d = din("xT", [1024, S])
    xq_d = din("xq", [SQ, 1024])
    wfm_d = din("wfm", [depth, 1024, NG * 128])
    wtm_d = din("wtm", [depth, 1024, 257])
    wout_d = din("wout", [depth, 256, 1024])
    w1_d = din("w1", [depth, 2, 2048, 128])
    w2_d = din("w2", [depth, 128, 128])
    post_d = din("post", [depth, 128, 32])
    vec_d = din("vecs", [depth, 128, 8])
    lng_d = din("lng", [depth, 128, 1024])
    lnb_d = din("lnb", [depth, 128, 1024])
    tab_d = din("tab", [128, 4, S])
    ind_d = din("ind", [64, S])
    ctri_d = din("c_tri", [128, 128])
    ctriu_d = din("c_triu", [128, 128])
    cid_d = din("c_ident", [128, 128])
    cmk_d = din("c_maskc", [128, 2048])
    cov_d = din("c_ov", [128, 4, 128])
    y_d = nc.dram_tensor("y", [SQ, 1024], F32, kind="ExternalOutput").ap()
    TPC = 1
    NCH = NT // TPC
    part_d = [nc.dram_tensor(f"part_i{c}", [TPC * 512, 1024], F32).ap() for c in range(NCH)]
    rs_d = [nc.dram_tensor(f"rs_i{c}", [TPC * 128, 1024], F32).ap() for c in range(NCH)]
    NTB = SQ // 128
    ytq_d = [nc.dram_tensor(f"ytq_i{i}", [1024, 256], BF16).ap() for i in range(NTB // 2)]
    xT1_d = [nc.dram_tensor(f"xT1_i{i}", [4 * 1024, 256], BF16).ap() for i in range(NTB // 2)]
    yq_d = nc.dram_tensor("yq_i", [SQ, 1024], F32).ap()
    RG = [[0, 1, 2, 3], [4, 5, 6, 7]]

    with ExitStack() as st:
        cx = Ctx(nc, st)
        pe, dve, act, pool, sp = cx.pe, cx.dve, cx.act, cx.pool, cx.sp
        V, G, T, A = nc.vector, nc.gpsimd, nc.tensor, nc.scalar
        sb = cx.sbuf
        WFM = sb("WFM", [128, 8, NG * 128], BF16); rWFM = cx.res()
        WTM = sb("WTM", [128, 8, 257], BF16); rWTM = cx.res()
        WOUT = sb("WOUT", [128, 2, 1024], BF16); rWOUT = cx.res()
        W1 = sb("W1", [128, 32, 128], BF16); rW1 = cx.res()
        W2 = sb("W2", [128, 128], BF16); rW2 = cx.res()
        POST = sb("POST", [128, 32], BF16); rPOST = cx.res()
        BH = sb("BH", [128, 2], F32); rBH = cx.res()
        KTA = sb("KTA", [128, S], BF16); rKTA = cx.res()
        KTS = sb("KTS", [128, S], BF16); rKTS = cx.res()
        KTW = sb("KTW", [128, 1024], BF16); rKTW = cx.res()
        VA = sb("VA", [128, NK, 192], BF16); rVA = cx.res()
        VN = sb("VN", [128, NK, 192], BF16); rVN = cx.res()
        VCA = sb("VCA", [128, 4, 128], BF16); rVCA = cx.res()
        KCT = sb("KCT", [128, 512], BF16); rKCT = cx.res()
        VCTF = sb("VCTF", [64, 512], F32); rVCTF = cx.res()
        CMPIN = sb("CMPIN", [128, 528], BF16); rCMPIN = cx.res()
        XB = sb("XB", [128, 8, 512], BF16); rXB = cx.res()
        TAB = sb("TAB", [128, 4, 512], F32); rTAB = cx.res()
        TRI = sb("TRI", [128, 128], BF16); TRIU = sb("TRIU", [128, 128], BF16)
        MASKC = sb("MASKC", [128, 2048], BF16); OV = sb("OV", [128, 4, 128], BF16)
        IDF = sb("IDF", [128, 128], F32); TRIF = sb("TRIF", [128, 128], F32); ONESF = sb("ONESF", [128, 128], F32)
        VEC = sb("VEC", [128, 8], F32)
        rC = cx.res()
        rC2 = cx.res()
        QF = sb("QF", [128, 512], BF16); QD0 = sb("QD0", [128, 512], BF16); QD1 = sb("QD1", [128, 512], BF16); rQA = cx.res()
        QN = sb("QN", [128, 4, 512], BF16); rQN = cx.res()
        QSA = [sb(f"QSA{i}", [128, 2, 256], BF16) for i in range(2)]; rQSA = [[cx.res(), cx.res()] for _ in range(2)]
        T0 = sb("T0", [128, 512], F32); rT0 = cx.res()
        T1 = sb("T1", [128, 512], F32); rT1 = cx.res()
        Z1 = sb("Z1", [128, 512], F32); rZ1 = cx.res()
        GZ = sb("GZ", [128, 3, 2, 512], BF16); rGZ = cx.res()
        NPT = 5
        ptc = [0]
        PT = [sb(f"PT{i}", [128, 512], BF16) for i in range(NPT)]; rPT = [cx.res() for _ in range(NPT)]
        FA = sb("FA", [128, 512], F32); rFA = cx.res()
        FB = sb("FB", [128, 512], F32); rFB = cx.res()
        CT0 = sb("CT0", [128, 512], BF16); rCT0 = cx.res()
        CT1 = sb("CT1", [128, 512], BF16); rCT1 = cx.res()
        OUTS = sb("OUTS", [128, 1024], F32); rOUTS = cx.res()
        CNEG = sb("CNEG", [128, NK], F32); rCNEG = cx.res()
        CARRY = sb("CARRY", [128, NK + 4], F32); rCARRY = cx.res()
        SPL = sb("SPL", [128, NK], F32); rSPL = cx.res()
        NB = sb("NB", [128, NK], F32); rNB = cx.res()
        LFRAW = sb("LFRAW", [128, NK], F32); rLFRAW = cx.res()
        SC = sb("SC", [128, 128], F32); rSC = cx.res()
        SC2 = sb("SC2", [128, 128], F32); rSC2 = cx.res()
        M8 = sb("M8", [128, 8], F32); M8b = sb("M8b", [128, 8], F32); rM8 = cx.res(); rM8b = cx.res()
        MNEG2 = [sb(f"MNEG{i}", [128, 128], F32) for i in range(2)]; rMNEG2 = [cx.res(), cx.res()]
        RS4 = sb("RS4", [128, 4], F32); rRS4 = cx.res()
        RI4 = sb("RI4", [128, 4], F32); rRI4 = cx.res()
        YA = [sb(f"YA{i}", [64, 256], F32)[:] for i in range(2)]; rYA = [cx.res(), cx.res()]
        YB = sb("YB", [64, 256], F32); rYB = cx.res()
        YC = sb("YC", [64, 256], F32); rYC = cx.res()
        FN = sb("FN", [128, 256], F32); rFN = cx.res()
        HS = sb("HS", [128, 64], BF16); rHS = cx.res()
        HSS = sb("HSS", [128, 64], F32); rHSS = cx.res()
        LAM = sb("LAM", [128, 4], F32); rLAM = cx.res()
        IMPB = cx.psum("IMPB", [128, 512], F32); rIMPB = cx.res(True)
        STB = [cx.psum(f"ST{i}", [128, 512], F32) for i in range(4)]; rST = [cx.res(True) for _ in range(4)]
        OB = [cx.psum(f"OB{i}", [128, 512], F32) for i in range(3)]; rOB = [cx.res(True) for _ in range(3)]
        ALLB = [(IMPB, rIMPB)] + list(zip(STB, rST)) + list(zip(OB, rOB))
        ld = cx.dsem("ld_const"); ldx = cx.dsem("ld_x"); ldt = cx.dsem("ld_tab"); sto = cx.dsem("st_out")
        pjc = [0]

        def pjx():
            i = pjc[0] % 8
            pjc[0] += 1
            return ALLB[i]

        pj = pjx

        ldw = cx.dsem("ld_w"); ccs = cx.dsem("cc_sem"); ldb = [cx.dsem("ld_b0"), cx.dsem("ld_b1")]; sta = [cx.dsem("st_a0"), cx.dsem("st_a1")]; ldg = cx.dsem("ld_g")
        rYQ = cx.res()
        rYTQ = [cx.res() for _ in range(NTB // 2)]; rXT1 = [cx.res() for _ in range(NTB // 2)]
        rPart = [cx.res() for _ in range(NCH)]; rRSd = [cx.res() for _ in range(NCH)]
        styt2 = [cx.dsem("st_yt0"), cx.dsem("st_yt1")]
        rsc = [cx.dsem(f"rs_sem{c}") for c in range(NCH)]
        cx.dma(pool, ld, KTS[64:128, :], ind_d, writes=[rKTS])
        cx.dma(pool, ld, TRI[:], ctri_d, writes=[rC])
        cx.dma(pool, ld, TRIU[:], ctriu_d, writes=[rC])
        cx.dma(pool, ld, MASKC[:], cmk_d, writes=[rC])
        cx.dma(pool, ld, OV[:], cov_d, writes=[rC])
        cx.dma(sp, ld, IDF[:], cid_d, writes=[rC])
        cx.dma(sp, ld, TRIF[:], ctri_d, writes=[rC])
        EPSC = sb("EPSC", [128, 1], F32)
        for r_ in (rKTS, rC):
            r_.w = (ld, ld.cnt)
        cx.op(dve, lambda: V.memset(ONESF[:], 1.0), writes=[rC2])
        cx.op(dve, lambda: V.memset(EPSC[:], 1e-5), writes=[rC2])
        rVEC = cx.res()
        if NK * 96 >= 5120:
            INB = [VA[:].rearrange("p a b -> p (a b)").bitcast(F32)[:, 0:5120].rearrange("p (j n) -> p j n", j=5),
                   VN[:].rearrange("p a b -> p (a b)").bitcast(F32)[:, 0:5120].rearrange("p (j n) -> p j n", j=5)]
            kf = KTA[:].bitcast(F32)
            ACCB = [kf[:, 0:1024], kf[:, 1024:2048]]
            YTB2 = [KTA[:, 4096:5120].rearrange("p (c t) -> p c t", c=8), KTA[:, 5120:6144].rearrange("p (c t) -> p c t", c=8)]
            xf = XB[:].rearrange("p a b -> p (a b)").bitcast(F32)
            GTB, BTB = xf[:, 0:1024], xf[:, 1024:2048]
            ALIAS = True
        else:
            ALIAS = False
            INB = [sb(f"INB{i}", [128, 5, 1024], F32)[:] for i in range(2)]; rINB = [cx.res(), cx.res()]
            ACCB = [sb(f"ACCB{i}", [128, 1024], F32)[:] for i in range(2)]; rACCB = [cx.res(), cx.res()]
            YTB2 = [sb(f"YTB{i}", [128, 8, 128], BF16)[:] for i in range(2)]; rYTB2 = [cx.res(), cx.res()]
            GTB = sb("GTB", [128, 1024], F32)[:]; BTB = sb("BTB", [128, 1024], F32)[:]; rGB = cx.res()
        STT = sb("STT", [128, 2, 6], F32); rSTT = cx.res()
        MV = sb("MV", [128, 4], F32); rMV = cx.res()
        def load_weights(L):
            cx.dma(pool, ldw, WFM[:], wfm_d[L].rearrange("(c p) n -> p c n", p=128), writes=[rWFM])
            cx.dma(pool, ldw, WTM[:], wtm_d[L].rearrange("(c p) n -> p c n", p=128), writes=[rWTM])
            cx.dma(pool, ldw, WOUT[:], wout_d[L].rearrange("(c p) n -> p c n", p=128), writes=[rWOUT])
            cx.dma(pool, ldw, W1[0:64, :, :], w1_d[L, 0].rearrange("(l d) h -> d l h", d=64), writes=[rW1])
            cx.dma(pool, ldw, W1[64:128, :, :], w1_d[L, 1].rearrange("(l d) h -> d l h", d=64), writes=[rW1])
            cx.dma(pool, ldw, W2[:], w2_d[L], writes=[rW2])
            cx.dma(pool, ldw, POST[:], post_d[L], writes=[rPOST])
            cx.dma(pool, ldw, VEC[:], vec_d[L], writes=[rVEC])
            for r_ in (rWFM, rWTM, rWOUT, rW1, rW2, rPOST, rVEC):
                r_.w = (ldw, ldw.cnt)

        load_weights(0)
        for L in range(depth):
          if True:
              cx.op(dve, lambda: V.memset(VA[:, :, 64:128], 1.0), writes=[rVA])
              cx.op(pool, lambda: G.memset(VN[:, :, 64:128], 1.0), writes=[rVN])
              cx.op(dve, lambda: V.memset(VCA[:, :, 0:64], 0.0), writes=[rVCA])
              cx.op(dve, lambda: V.memset(VCA[:, :, 64:128], 1.0), writes=[rVCA])
              cx.op(dve, lambda: V.memset(KCT[:], 0.0), writes=[rKCT])
              cx.op(pool, lambda: G.memset(KTW[:], 0.0), writes=[rKTW])
              cx.op(pool, lambda: G.memset(QN[64:128, :, :], 0.0), writes=[rQN])
              cx.op(dve, lambda: V.memset(QF[:], 0.0), writes=[rQA])
              cx.op(dve, lambda: V.memset(QD0[:], 0.0), writes=[rQA])
              cx.op(pool, lambda: G.memset(QD1[:], 0.0), writes=[rQA])
              cx.op(dve, lambda: V.memset(VCTF[:], 0.0), writes=[rVCTF])
              cx.op(pool, lambda: G.memset(CMPIN[:], 0.0), writes=[rCMPIN])
              cx.op(dve, lambda: V.memset(CARRY[:], 0.0), writes=[rCARRY])
              for s_ in range(2):
                  pb, rpb = pj()
                  lo = 64 * s_
                  for l in range(32):
                      cx.op(pe, lambda l=l, lo=lo, pb=pb: T.matmul(pb[:, 0:1], lhsT=W1[lo:lo + 64, l, :], rhs=POST[lo:lo + 64, l:l + 1],
                                                                    start=(l == 0), stop=(l == 31)),
                            reads=[rW1, rPOST], writes=[rpb])
                  cx.op(dve, lambda pb=pb, s_=s_: V.tensor_copy(out=BH[:, s_:s_ + 1], in_=pb[:, 0:1]), reads=[rpb], writes=[rBH])
              cx.op(dve, lambda: V.tensor_tensor(out=LAM[:, 0:1], in0=VEC[:, 2:3], in1=VEC[:, 3:4], op=ALU.mult), reads=[rC, rVEC], writes=[rLAM])
              cx.op(dve, lambda: V.tensor_tensor(out=LAM[:, 1:2], in0=VEC[:, 4:5], in1=VEC[:, 5:6], op=ALU.mult), reads=[rC, rC2, rVEC, rLAM], writes=[rLAM])
              pb, rpb = pj()
              cx.op(pe, lambda: T.matmul(pb[:, 0:2], lhsT=ONESF[:], rhs=LAM[:, 0:2], start=True, stop=True), reads=[rC, rC2, rLAM], writes=[rpb])
              cx.op(act, lambda: A.activation(out=LAM[:, 2:4], in_=pb[:, 0:2], func=AF.Exp), reads=[rpb], writes=[rLAM])
              cx.op(dve, lambda: V.tensor_tensor(out=LAM[:, 0:1], in0=LAM[:, 3:4], in1=LAM[:, 2:3], op=ALU.subtract), reads=[rLAM], writes=[rLAM])
              cx.op(dve, lambda: V.tensor_tensor(out=LAM[:, 0:1], in0=LAM[:, 0:1], in1=VEC[:, 6:7], op=ALU.add),
                    reads=[rLAM, rC, rC2, rVEC], writes=[rLAM])
              cx.op(dve, lambda: V.tensor_tensor(out=LAM[:, 1:2], in0=VEC[:, 1:2], in1=VEC[:, 7:8], op=ALU.mult),
                    reads=[rC, rC2, rVEC, rLAM], writes=[rLAM])
              NEGLAM = LAM[:, 0:1]
              GCOL = LAM[:, 1:2]

              def load_tile(Tq):
                  if L == 0:
                      cx.dma(pool, ldx, XB[:], xT0_d[:, Tq * 512:(Tq + 1) * 512].rearrange("(c p) t -> p c t", p=128), writes=[rXB])
                  else:
                      for pc in range(4):
                          row = (Tq % TPC) * 512 + pc * 128
                          rk, ib = row // (TPC * 128), (Tq // TPC) * TPC + (row % (TPC * 128)) // 128
                          cx.dma(pool, ldx, XB[:, :, pc * 128:(pc + 1) * 128],
                                 xT1_d[ib // 2][rk * 1024:(rk + 1) * 1024, (ib % 2) * 128:(ib % 2 + 1) * 128].rearrange("(c p) t -> p c t", p=128),
                                 reads=[rXT1[ib // 2]], writes=[rXB])
                      rXB.w = (ldx, ldx.cnt)
                  cx.dma(sp, ldt, TAB[:], tab_d[:, :, Tq * 512:(Tq + 1) * 512], writes=[rTAB])

              load_tile(0)

              def fm_group(gi):
                  pb, rpb = pjx()
                  for c in range(8):
                      cx.op(pe, lambda c=c, pb=pb: T.matmul(pb[:], lhsT=WFM[:, c, gi * 128:(gi + 1) * 128], rhs=XB[:, c, :],
                                                             start=(c == 0), stop=(c == 7)), reads=[rWFM, rXB], writes=[rpb])
                  return pb, rpb

              def rope(pa, rpa, ps, rps, rows_a, rows_s, tabi, outs):
                  (a0, a1), (s0, s1) = rows_a, rows_s
                  n = a1 - a0
                  cx.op(dve, lambda: V.tensor_tensor(out=T0[0:n, :], in0=pa[a0:a1, :], in1=TAB[0:n, tabi, :], op=ALU.mult),
                        reads=[rpa, rTAB], writes=[rT0])
                  cx.op(dve, lambda: V.tensor_tensor(out=T1[0:n, :], in0=ps[s0:s1, :], in1=TAB[0:n, tabi + 1, :], op=ALU.mult),
                        reads=[rps, rTAB], writes=[rT1])
                  for (r0_, r1_, dst, rdst) in outs:
                      cx.op(pool, lambda r0_=r0_, r1_=r1_, dst=dst: G.tensor_tensor(out=dst, in0=T0[r0_:r1_, :], in1=T1[r0_:r1_, :], op=ALU.add),
                            reads=[rT0, rT1], writes=[rdst])

              def bcast_h(ap2d, h):
                  return bass.AP(tensor=ap2d.tensor, offset=ap2d.offset, ap=[list(ap2d.ap[0]), [0, h], list(ap2d.ap[1])])

              SGT, rSGT, ZN, rZN = FA, rFA, FB, rFB
              obc = [0]
              stc = [0]
              DSK = int(os.environ.get("DBG_DSK", "3"))

              def ob_next():
                  k = obc[0] % 3
                  obc[0] += 1
                  return OB[k], rOB[k]

              def st_next():
                  k = stc[0] % 4
                  stc[0] += 1
                  return STB[k], rST[k]

              for Tq in range(NT):
                  c0t = Tq * 512
                  pA, rA_ = fm_group(0)
                  pS, rS_ = fm_group(2)
                  rope(pA, rA_, pS, rS_, (0, 64), (0, 64), 2, [(0, 32, QD0[0:32, :], rQA), (32, 64, QD1[32:64, :], rQA)])
                  cx.op(act, lambda: A.copy(out=QF[64:128, :], in_=pA[64:128, :]), reads=[rA_], writes=[rQA])
                  pK, rK_ = fm_group(1)
                  rope(pK, rK_, pS, rS_, (0, 64), (64, 128), 2, [(0, 64, KTA[0:64, c0t:c0t + 512], rKTA)])
                  cx.op(act, lambda: A.copy(out=KTA[64:128, c0t:c0t + 512], in_=pK[64:128, :]), reads=[rK_], writes=[rKTA])
                  for (ga, gs, hq) in ((3, 4, 0), (5, 6, 2)):
                      p1, r1 = fm_group(ga)
                      p2, r2 = fm_group(gs)
                      rope(p1, r1, p2, r2, (0, 128), (0, 128), 0, [(0, 64, QN[0:64, hq, :], rQN), (64, 128, QN[0:64, hq + 1, :], rQN)])
                  p1, r1 = fm_group(7)
                  p2, r2 = fm_group(8)
                  rope(p1, r1, p2, r2, (0, 128), (0, 128), 0, [(0, 64, CMPIN[0:64, 16:528], rCMPIN), (64, 128, KTS[0:64, c0t:c0t + 512], rKTS)])
                  p1, r1 = fm_group(9)
                  p2, r2 = fm_group(10)
                  wslot = (Tq % 2) * 512
                  rope(p1, r1, p2, r2, (0, 64), (0, 64), 0, [(0, 64, KTW[0:64, wslot:wslot + 512], rKTW)])
                  cx.op(act, lambda: A.copy(out=CMPIN[64:128, 16:528], in_=p1[64:128, :]), reads=[r1], writes=[rCMPIN])
                  p1, r1 = fm_group(11)
                  cx.op(act, lambda: A.activation(out=Z1[:], in_=p1[:], func=AF.Sigmoid), reads=[r1], writes=[rZ1])
                  cx.op(dve, lambda: V.tensor_tensor(out=Z1[:], in0=p1[:], in1=Z1[:], op=ALU.mult), reads=[r1, rZ1], writes=[rZ1])
                  p1, r1 = fm_group(12)
                  cx.op(act, lambda: A.activation(out=ZN[:], in_=p1[:], func=AF.Sigmoid), reads=[r1], writes=[rZN])
                  cx.op(dve, lambda: V.tensor_tensor(out=ZN[:], in0=p1[:], in1=ZN[:], op=ALU.mult), reads=[r1, rZN], writes=[rZN])
                  sgbufs = ((FA, rFA), (T0, rT0), (T1, rT1))
                  for br in range(3):
                      p1, r1 = fm_group(13 + br)
                      SGb, rSGb = sgbufs[br]
                      cx.op(act, lambda: A.activation(out=SGb[:], in_=p1[:], func=AF.Sigmoid), reads=[r1], writes=[rSGb])
                      ob = 0 if br == 2 else 64
                      for h in range(2):
                          eng_, E_ = ((pool, G), (dve, V))[(2 * br + h) % 2]
                          cx.op(eng_, lambda h=h, ob=ob, br=br, E_=E_: E_.tensor_tensor(out=GZ[ob:ob + 64, br, h, :], in0=SGb[64 * h:64 * h + 64, :],
                                                                                       in1=ZN[64 * h:64 * h + 64, :], op=ALU.mult),
                                reads=[rSGb, rZN], writes=[rGZ])
                  for st_ in range(4):
                      kt = Tq * 4 + st_
                      pb, rpb = pjx()
                      for c in range(8):
                          cx.op(pe, lambda c=c, pb=pb, st_=st_: T.matmul(pb[:, 0:257], lhsT=XB[:, c, st_ * 128:(st_ + 1) * 128], rhs=WTM[:, c, :],
                                                                          start=(c == 0), stop=(c == 7)), reads=[rXB, rWTM], writes=[rpb])
                      cx.op(act, lambda pb=pb, kt=kt: A.copy(out=VA[:, kt, :].rearrange("p (a b) -> p a b", a=3)[:, 0::2, :],
                                                             in_=pb[:, 0:128].rearrange("p (a b) -> p a b", a=2)), reads=[rpb], writes=[rVA])
                      cx.op(dve, lambda pb=pb, kt=kt: V.tensor_copy(out=VN[:, kt, :].rearrange("p (a b) -> p a b", a=3)[:, 0::2, :],
                                                                    in_=pb[:, 128:256].rearrange("p (a b) -> p a b", a=2)), reads=[rpb], writes=[rVN])
                      cx.op(dve, lambda pb=pb, kt=kt: V.tensor_copy(out=LFRAW[:, kt:kt + 1], in_=pb[:, 256:257]), reads=[rpb], writes=[rLFRAW])
                  if Tq + 1 < NT:
                      load_tile(Tq + 1)
                  deferred = []

                  def defer(n, fn, tag):
                      deferred.append([n, fn, tag])

                  def run_deferred(upto_tag=None, everything=False, pred=None):
                      if pred is None and upto_tag is not None:
                          pred = lambda t: t == upto_tag
                      while deferred:
                          n, fn, tag = deferred[0]
                          if everything or n <= 0 or (pred is not None and any(pred(d[2]) for d in deferred)):
                              deferred.pop(0)
                              fn()
                          else:
                              break

                  j0 = 1 if Tq == 0 else 0
                  nj = 32 - j0
                  n0 = 32 * Tq - 1 + j0
                  for s_ in range(2):
                      lo = 64 * s_
                      pb, rpb = pjx()
                      for l in range(32):
                          cx.op(pe, lambda l=l, lo=lo, pb=pb: T.matmul(pb[:, 0:nj], lhsT=W1[lo:lo + 64, l, :],
                                                                        rhs=CMPIN[lo:lo + 64, 16 * j0 + l:16 * j0 + l + 16 * (nj - 1) + 1:16],
                                                                        start=(l == 0), stop=(l == 31)), reads=[rW1, rCMPIN], writes=[rpb])
                      cx.op(act, lambda pb=pb, s_=s_: A.activation(out=HSS[:, 32 * s_:32 * s_ + nj], in_=pb[:, 0:nj], func=AF.Sigmoid, bias=BH[:, s_:s_ + 1]),
                            reads=[rpb, rBH], writes=[rHSS])
                      cx.op(dve, lambda pb=pb, s_=s_: V.scalar_tensor_tensor(out=HS[:, 32 * s_:32 * s_ + nj], in0=pb[:, 0:nj], scalar=BH[:, s_:s_ + 1],
                                                                             in1=HSS[:, 32 * s_:32 * s_ + nj], op0=ALU.add, op1=ALU.mult),
                            reads=[rpb, rBH, rHSS], writes=[rHS])
                  cx.op(pool, lambda: G.tensor_copy(out=CMPIN[:, 0:16], in_=CMPIN[:, 512:528]), reads=[rCMPIN], writes=[rCMPIN])

                  def kc_stage2():
                      pb, rpb = st_next()
                      cx.op(pe, lambda: T.matmul(pb[0:64, 0:nj], lhsT=W2[:, 0:64], rhs=HS[:, 0:nj], start=True, stop=True), reads=[rW2, rHS], writes=[rpb])
                      cx.op(pe, lambda: T.matmul(pb[0:64, 64:64 + nj], lhsT=W2[:, 64:128], rhs=HS[:, 32:32 + nj], start=False, stop=True, skip_group_check=True),
                            reads=[rW2, rHS], writes=[rpb])
                      cx.op(dve, lambda: V.tensor_copy(out=KCT[0:64, n0:n0 + nj], in_=pb[0:64, 0:nj]), reads=[rpb], writes=[rKCT])
                      cx.op(dve, lambda: V.tensor_copy(out=VCTF[0:64, n0:n0 + nj], in_=pb[0:64, 64:64 + nj]), reads=[rpb], writes=[rVCTF])
                      for a_ in sorted(set([max(n0, 0) // 128, (n0 + nj - 1) // 128])):
                          pb2, rpb2 = st_next()
                          cx.op(pe, lambda a_=a_, pb2=pb2: T.transpose(out=pb2[:, 0:64], in_=VCTF[0:64, a_ * 128:(a_ + 1) * 128], identity=IDF[0:64, 0:64]),
                                reads=[rVCTF, rC, rC2], writes=[rpb2])
                          cx.op(dve, lambda a_=a_, pb2=pb2: V.tensor_copy(out=VCA[:, a_, 0:64], in_=pb2[:, 0:64]), reads=[rpb2], writes=[rVCA])
                  defer(4, kc_stage2, ("kc", Tq))

                  k0 = Tq * 4
                  cx.op(dve, lambda: V.tensor_scalar(out=SPL[:, k0:k0 + 4], in0=LFRAW[:, k0:k0 + 4], scalar1=VEC[:, 0:1], scalar2=None, op0=ALU.add),
                        reads=[rLFRAW, rC, rC2, rVEC], writes=[rSPL])
                  cx.op(act, lambda: A.activation(out=SPL[:, k0:k0 + 4], in_=SPL[:, k0:k0 + 4], func=AF.Exp, scale=-1.0), reads=[rSPL], writes=[rSPL])
                  cx.op(act, lambda: A.activation(out=SPL[:, k0:k0 + 4], in_=SPL[:, k0:k0 + 4], func=AF.Ln, bias=1.0), reads=[rSPL], writes=[rSPL])

                  def nb_stage():
                      pb, rpb = st_next()
                      cx.op(pe, lambda: T.matmul(pb[:, 0:4], lhsT=TRIF[:], rhs=SPL[:, k0:k0 + 4], start=True, stop=True), reads=[rC, rC2, rSPL], writes=[rpb])
                      cx.op(pe, lambda: T.matmul(pb[:, 8:12], lhsT=ONESF[:], rhs=SPL[:, k0:k0 + 4], start=False, stop=True, skip_group_check=True), reads=[rC, rC2, rSPL], writes=[rpb])
                      for j in range(4):
                          cx.op(dve, lambda j=j: V.tensor_tensor(out=CARRY[:, k0 + j + 1:k0 + j + 2], in0=CARRY[:, k0 + j:k0 + j + 1],
                                                                 in1=pb[:, 8 + j:9 + j], op=ALU.add), reads=[rpb, rCARRY], writes=[rCARRY])
                      cx.op(dve, lambda: V.tensor_tensor(out=CNEG[:, k0:k0 + 4], in0=pb[:, 0:4], in1=CARRY[:, k0:k0 + 4], op=ALU.add),
                            reads=[rpb, rCARRY], writes=[rCNEG])
                      cx.op(dve, lambda: V.tensor_scalar(out=NB[:, 0:k0 + 4], in0=CNEG[:, 0:k0 + 4], scalar1=CARRY[:, k0:k0 + 1], scalar2=None,
                                                         op0=ALU.subtract), reads=[rCNEG, rCARRY], writes=[rNB])
                  defer(8, nb_stage, ("nb", Tq))
                  chain = []


                  gidc = [0, 0]

                  def begin_group():
                      gidc[0] += 1
                      gidc[1] = 0

                  def add_item(smm, expf, maskf, pvf, after=None, pre=None):
                      chain.append([smm, expf, maskf, pvf, after, pre, gidc[0], gidc[1] == 0])
                      gidc[1] += 1

                  def outproj(st_):
                      for dc in range(2):
                          pb, rpb = st_next()
                          cx.op(pe, lambda dc=dc, pb=pb: T.matmul(pb[:], lhsT=CT0[:, st_ * 128:(st_ + 1) * 128], rhs=WOUT[:, 0, dc * 512:(dc + 1) * 512],
                                                                 start=True, stop=False), reads=[rCT0, rWOUT], writes=[rpb])
                          cx.op(pe, lambda dc=dc, pb=pb: T.matmul(pb[:], lhsT=CT1[:, st_ * 128:(st_ + 1) * 128], rhs=WOUT[:, 1, dc * 512:(dc + 1) * 512],
                                                                 start=False, stop=True), reads=[rCT1, rWOUT], writes=[rpb])
                          cx.op(dve, lambda dc=dc, pb=pb: V.tensor_copy(out=OUTS[:, dc * 512:(dc + 1) * 512], in_=pb[:]), reads=[rpb], writes=[rOUTS])
                      ch = Tq // TPC
                      r0 = (Tq % TPC) * 512 + st_ * 128
                      cx.dma(sp, sto, part_d[ch][r0:r0 + 128, :], OUTS[:], reads=[rOUTS, rPart[ch]])

                  def branch_out(ob_, rob_, br, qc, o_low, Y, rY):
                      so, oo = (64, 0) if o_low else (0, 64)
                      cx.op(dve, lambda: V.tensor_scalar(out=FN[so:so + 64, :], in0=ob_[so:so + 64, 0:256], scalar1=1e-30, scalar2=None, op0=ALU.max),
                            reads=[rob_], writes=[rFN])
                      cx.op(dve, lambda: V.reciprocal(out=FN[so:so + 64, :], in_=FN[so:so + 64, :]), reads=[rFN], writes=[rFN])
                      cx.op(dve, lambda: V.tensor_tensor(out=FN[so:so + 64, :].rearrange("p (h q) -> p h q", h=2),
                                                         in0=FN[so:so + 64, :].rearrange("p (h q) -> p h q", h=2),
                                                         in1=GZ[so:so + 64, br, :, qc:qc + 128], op=ALU.mult), reads=[rFN, rGZ], writes=[rFN])
                      cx.op(dve, lambda: V.tensor_tensor(out=Y, in0=ob_[oo:oo + 64, 0:256], in1=FN[so:so + 64, :], op=ALU.mult),
                            reads=[rob_, rFN], writes=[rY])

                  def cmp_group(bi):
                      begin_group()
                      i = 4 * Tq + bi
                      qc = 128 * bi
                      a_max = i // 16
                      r16 = i % 16
                      par = i % 2
                      ob_, rob_ = ob_next()
                      for a_ in range(a_max + 1):
                          def s_c(sb_, rsb, a_=a_):
                              cx.op(pe, lambda: T.matmul(sb_[:], lhsT=KCT[:, a_ * 128:(a_ + 1) * 128], rhs=QN[:, :, qc:qc + 128],
                                                         start=True, stop=True), reads=[rKCT, rQN], writes=[rsb])

                          def e_c(sb_, rsb, pt, rpt, a_=a_):
                              cx.op(act, lambda: A.activation(out=pt[:], in_=sb_[:], func=AF.Exp, scale=0.125), reads=[rsb], writes=[rpt])

                          def m_c(pt, rpt, a_=a_):
                              cx.op(dve, lambda: V.tensor_tensor(out=pt[:].rearrange("p (h q) -> p h q", h=4),
                                                                 in0=pt[:].rearrange("p (h q) -> p h q", h=4),
                                                                 in1=bcast_h(MASKC[:, r16 * 128:(r16 + 1) * 128], 4),
                                                                 op=ALU.mult), reads=[rpt, rC, rC2], writes=[rpt])

                          def p_c(pt, rpt, a_=a_):
                              cx.op(pe, lambda: T.matmul(ob_[:, 0:256], lhsT=VCA[:, a_, :], rhs=pt[:, 0:256], start=(a_ == 0), stop=(a_ == a_max)),
                                    reads=[rVCA, rpt], writes=[rob_])
                              for h in range(4):
                                  cx.op(pe, lambda h=h: T.matmul(IMPB[:, h * 128:(h + 1) * 128], lhsT=pt[:, h * 128:(h + 1) * 128], rhs=OV[:, a_, :],
                                                                 start=(a_ == 0 and h == 0), stop=(a_ == a_max), skip_group_check=True),
                                        reads=[rpt, rC, rC2], writes=[rIMPB])

                          def epi_cmp():
                              branch_out(ob_, rob_, 0, qc, True, YA[par], rYA[par])

                              def stage2():
                                  cx.op(dve, lambda: V.tensor_reduce(out=RS4[:], in_=IMPB[:].rearrange("p (h j) -> p h j", h=4), axis=AX.X, op=ALU.add),
                                        reads=[rIMPB], writes=[rRS4])
                                  cx.op(dve, lambda: V.tensor_scalar(out=RS4[:], in0=RS4[:], scalar1=1e-30, scalar2=None, op0=ALU.max), reads=[rRS4], writes=[rRS4])
                                  cx.op(dve, lambda: V.reciprocal(out=RI4[:], in_=RS4[:]), reads=[rRS4], writes=[rRI4])
                                  cx.op(dve, lambda: V.tensor_scalar(out=SC[:], in0=IMPB[:, 0:128], scalar1=RI4[:, 0:1], scalar2=None, op0=ALU.mult),
                                        reads=[rIMPB, rRI4], writes=[rSC])
                                  for h in range(1, 4):
                                      cx.op(dve, lambda h=h: V.scalar_tensor_tensor(out=SC[:], in0=IMPB[:, h * 128:(h + 1) * 128], scalar=RI4[:, h:h + 1], in1=SC[:],
                                                                                    op0=ALU.mult, op1=ALU.add), reads=[rIMPB, rRI4, rSC], writes=[rSC])

                              def stage3():
                                  cx.op(dve, lambda: V.memset(SC[:, 0:1], BIG), reads=[], writes=[rSC])
                                  lo0 = max(2 * i - 1, 0)
                                  cx.op(dve, lambda: V.memset(SC[0:64, lo0:2 * i + 1], BIG), writes=[rSC])
                                  cx.op(dve, lambda: V.memset(SC[64:128, 2 * i:2 * i + 2], BIG), writes=[rSC])
                                  cx.op(dve, lambda: V.max(out=M8[:], in_=SC[:]), reads=[rSC], writes=[rM8])
                                  cx.op(dve, lambda: V.match_replace(out=SC2[:], in_to_replace=M8[:], in_values=SC[:], imm_value=-BIG),
                                        reads=[rSC, rM8], writes=[rSC2])
                                  cx.op(dve, lambda: V.max(out=M8b[:], in_=SC2[:]), reads=[rSC2], writes=[rM8b])
                                  cx.op(dve, lambda: V.tensor_scalar(out=MNEG2[par][:], in0=SC[:], scalar1=M8b[:, 7:8], scalar2=-30000.0, op0=ALU.is_lt, op1=ALU.mult),
                                        reads=[rSC, rM8b], writes=[rMNEG2[par]])
                                  cx.op(pool, lambda: G.tensor_copy(out=QSA[par][0:64, :, :].rearrange("p a (h q) -> p a h q", h=2),
                                                                    in_=bass.AP(tensor=QN.tensor if hasattr(QN, "tensor") else QN[0:64, 0:2, qc:qc + 128].tensor,
                                                                                offset=QN[0:64, 0:2, qc:qc + 128].offset,
                                                                                ap=[list(QN[0:64, 0:2, qc:qc + 128].ap[0]), [0, 2]] + [list(a) for a in QN[0:64, 0:2, qc:qc + 128].ap[1:]])),
                                        reads=[rQN], writes=[rQSA[par][0], rQSA[par][1]])

                              def stage_b():
                                  mt, rmt = st_next()
                                  cx.op(pe, lambda: T.transpose(out=mt[:, 0:128], in_=MNEG2[par][:], identity=IDF[:]), reads=[rMNEG2[par], rC, rC2], writes=[rmt])
                                  for half in range(2):
                                      cx.op(dve, lambda half=half: V.tensor_copy(out=QSA[par][64:128, half, :].rearrange("p (h q) -> p h q", h=2),
                                                                                 in_=bcast_h(mt[64 * half:64 * half + 64, 0:128], 2)),
                                            reads=[rmt], writes=[rQSA[par][half]])
                              defer(6, stage2, ("c", i))
                              defer(12, stage3, ("c", i))
                              defer(28, stage_b, ("q", i))
                          add_item(s_c, e_c, m_c if a_ == a_max else None, p_c, epi_cmp if a_ == a_max else None,
                                   (lambda: run_deferred(pred=lambda t: t[0] in ("c", "kc"))) if a_ == 0 else None)

                  def win_group(bi):
                      begin_group()
                      i = 4 * Tq + bi
                      qc = 128 * bi
                      ob_, rob_ = ob_next()
                      kts = list(range(max(0, i - 4), i + 1))
                      for kt in kts:
                          def s_w(sb_, rsb, kt=kt):
                              wl = (kt % 8) * 128
                              cx.op(pe, lambda: T.matmul(sb_[:, 0:256], lhsT=KTW[:, wl:wl + 128], rhs=QN[:, 0:2, qc:qc + 128], start=True, stop=True),
                                    reads=[rKTW, rQN], writes=[rsb])

                          def e_w(sb_, rsb, pt, rpt):
                              cx.op(act, lambda: A.activation(out=pt[:, 0:256], in_=sb_[:, 0:256], func=AF.Exp, scale=0.125), reads=[rsb], writes=[rpt])
                          mk = None
                          if kt == i or kt == i - 4:
                              def mk(pt, rpt, kt=kt):
                                  msk = TRI if kt == i else TRIU
                                  cx.op(dve, lambda: V.tensor_tensor(out=pt[:, 0:256].rearrange("p (h q) -> p h q", h=2),
                                                                     in0=pt[:, 0:256].rearrange("p (h q) -> p h q", h=2),
                                                                     in1=bcast_h(msk[:], 2), op=ALU.mult), reads=[rpt, rC, rC2], writes=[rpt])

                          def p_w(pt, rpt, kt=kt):
                              cx.op(pe, lambda: T.matmul(ob_[:, 0:256], lhsT=VN[:, kt, 64:192], rhs=pt[:, 0:256], start=(kt == kts[0]), stop=(kt == kts[-1])),
                                    reads=[rVN, rpt], writes=[rob_])
                          add_item(s_w, e_w, mk, p_w, (lambda: branch_out(ob_, rob_, 2, qc, False, YB[:], rYB)) if kt == kts[-1] else None)

                  def sel_group(bi):
                      begin_group()
                      i = 4 * Tq + bi
                      qc = 128 * bi
                      par = i % 2
                      ob_, rob_ = ob_next()
                      for kt in range(i + 1):
                          def s_s(sb_, rsb, kt=kt):
                              half = kt // 32
                              cx.op(pe, lambda: T.matmul(sb_[:, 0:256], lhsT=KTS[:, kt * 128:(kt + 1) * 128], rhs=QSA[par][:, half, :], start=True, stop=True),
                                    reads=[rKTS, rQSA[par][half]], writes=[rsb])

                          def e_s(sb_, rsb, pt, rpt):
                              cx.op(act, lambda: A.activation(out=pt[:, 0:256], in_=sb_[:, 0:256], func=AF.Exp, scale=0.125), reads=[rsb], writes=[rpt])
                          mk = None
                          if kt == i:
                              def mk(pt, rpt):
                                  cx.op(dve, lambda: V.tensor_tensor(out=pt[:, 0:256].rearrange("p (h q) -> p h q", h=2),
                                                                     in0=pt[:, 0:256].rearrange("p (h q) -> p h q", h=2),
                                                                     in1=bcast_h(TRI[:], 2), op=ALU.mult), reads=[rpt, rC, rC2], writes=[rpt])

                          def p_s(pt, rpt, kt=kt):
                              cx.op(pe, lambda: T.matmul(ob_[:, 0:256], lhsT=VN[:, kt, 0:128], rhs=pt[:, 0:256], start=(kt == 0), stop=(kt == i)),
                                    reads=[rVN, rpt], writes=[rob_])

                          def epi_sel():
                              branch_out(ob_, rob_, 1, qc, True, YC[:], rYC)
                              cx.op(pool, lambda: G.tensor_tensor(out=YB[:], in0=YA[par], in1=YB[:], op=ALU.add), reads=[rYA[par], rYB], writes=[rYB])
                              for h in range(2):
                                  cx.op(pool, lambda h=h: G.tensor_tensor(out=CT1[64 * h:64 * h + 64, qc:qc + 128], in0=YB[:, 128 * h:128 * h + 128],
                                                                          in1=YC[:, 128 * h:128 * h + 128], op=ALU.add), reads=[rYB, rYC], writes=[rCT1])
                              defer(8, lambda: outproj(bi), ("o", i))
                          add_item(s_s, e_s, mk, p_s, epi_sel if kt == i else None, (lambda: run_deferred(upto_tag=("q", i))) if kt == 0 else None)

                  nkt = 4 * Tq + 4

                  def dense_group(kind):
                      begin_group()
                      ob_, rob_ = ob_next()
                      for kt in range(nkt):
                          jd = kt - 4 * Tq
                          cc = 128 * jd if jd >= 0 else 0
                          first, last = (kt == 0), (kt == nkt - 1)

                          def smm(sb_, rsb, kt=kt, cc=cc):
                              qsrc = (QF, QD0, QD1)[kind]
                              cx.op(pe, lambda: T.matmul(sb_[:, cc:512], lhsT=KTA[:, kt * 128:(kt + 1) * 128], rhs=qsrc[:, cc:512],
                                                         start=True, stop=True), reads=[rKTA, rQA], writes=[rsb])

                          def expf(sb_, rsb, pt, rpt, kt=kt, cc=cc):
                              if kind == 0:
                                  cx.op(act, lambda: A.activation(out=pt[:, cc:512], in_=sb_[:, cc:512], func=AF.Exp, bias=NB[:, kt:kt + 1], scale=0.125),
                                        reads=[rsb, rNB], writes=[rpt])
                              else:
                                  cx.op(act, lambda: A.activation(out=pt[:, cc:512], in_=sb_[:, cc:512], func=AF.Exp, scale=float(32 ** -0.5)),
                                        reads=[rsb], writes=[rpt])

                          def m_diag(pt, rpt, cc=cc):
                              cx.op(pool, lambda: G.tensor_tensor(out=pt[:, cc:cc + 128], in0=pt[:, cc:cc + 128], in1=TRI[:], op=ALU.mult),
                                    reads=[rpt, rC, rC2], writes=[rpt])

                          def pvf(pt, rpt, kt=kt, cc=cc, first=first, last=last):
                              lhs = VA[:, kt, 0:128] if kind == 0 else VA[:, kt, 64:192]
                              cx.op(pe, lambda: T.matmul(ob_[:, cc:512], lhsT=lhs, rhs=pt[:, cc:512], start=first, stop=last),
                                    reads=[rVA, rpt], writes=[rob_])

                          def epi():
                              if kind == 0:
                                  cx.op(dve, lambda: V.reciprocal(out=FA[64:128, :], in_=ob_[64:128, :]), reads=[rob_], writes=[rFA])
                                  cx.op(dve, lambda: V.tensor_tensor(out=FA[64:128, :], in0=FA[64:128, :], in1=Z1[64:128, :], op=ALU.mult),
                                        reads=[rFA, rZ1], writes=[rFA])
                                  cx.op(dve, lambda: V.tensor_tensor(out=CT0[0:64, :], in0=ob_[0:64, :], in1=FA[64:128, :], op=ALU.mult),
                                        reads=[rob_, rFA], writes=[rCT0])
                              elif kind == 1:
                                  cx.op(dve, lambda: V.reciprocal(out=FB[0:64, :], in_=ob_[0:64, :]), reads=[rob_], writes=[rFB])
                                  cx.op(dve, lambda: V.tensor_tensor(out=T0[0:64, :], in0=ob_[64:128, :], in1=FB[0:64, :], op=ALU.mult),
                                        reads=[rob_, rFB], writes=[rT0])
                              else:
                                  cx.op(dve, lambda: V.reciprocal(out=FB[0:64, :], in_=ob_[0:64, :]), reads=[rob_, rFB], writes=[rFB])
                                  cx.op(dve, lambda: V.tensor_scalar(out=FB[0:64, :], in0=FB[0:64, :], scalar1=NEGLAM[0:64, :], scalar2=None, op0=ALU.mult),
                                        reads=[rFB, rLAM], writes=[rFB])
                                  cx.op(dve, lambda: V.tensor_tensor(out=T1[0:64, :], in0=ob_[64:128, :], in1=FB[0:64, :], op=ALU.mult),
                                        reads=[rob_, rFB], writes=[rT1])
                                  cx.op(pool, lambda: G.tensor_tensor(out=T0[0:64, :], in0=T0[0:64, :], in1=T1[0:64, :], op=ALU.add),
                                        reads=[rT0, rT1], writes=[rT0])
                                  cx.op(pool, lambda: G.tensor_tensor(out=T1[0:64, :], in0=T0[0:64, :], in1=T0[0:64, :], op=ALU.mult),
                                        reads=[rT0, rT1], writes=[rT1])

                                  def stage_b():
                                      pb, rpb = st_next()
                                      cx.op(pe, lambda: T.matmul(pb[0:64, :], lhsT=ONESF[0:64, 0:64], rhs=T1[0:64, :], start=True, stop=True),
                                            reads=[rC, rC2, rT1], writes=[rpb])
                                      cx.op(act, lambda: A.activation(out=FB[0:64, :], in_=pb[0:64, :], func=AF.Ln, scale=1.0 / 64.0, bias=EPSC[0:64, :]),
                                            reads=[rpb, rC, rC2], writes=[rFB])
                                      cx.op(act, lambda: A.activation(out=FB[0:64, :], in_=FB[0:64, :], func=AF.Exp, scale=-0.5), reads=[rFB], writes=[rFB])
                                      cx.op(dve, lambda: V.scalar_tensor_tensor(out=T0[0:64, :], in0=T0[0:64, :], scalar=GCOL[0:64, :], in1=FB[0:64, :],
                                                                                op0=ALU.mult, op1=ALU.mult), reads=[rT0, rFB, rLAM], writes=[rT0])
                                      cx.op(pool, lambda: G.tensor_tensor(out=CT0[64:128, :], in0=T0[0:64, :], in1=Z1[0:64, :], op=ALU.mult),
                                            reads=[rT0, rZ1], writes=[rCT0])
                                  defer(14, stage_b, ("d", Tq))
                          add_item(smm, expf, m_diag if jd >= 0 else None, pvf, epi if last else None,
                                   (lambda: run_deferred(pred=lambda t: t[0] == "nb")) if (kind == 0 and kt == 0) else None)

                  if os.environ.get("DBG_ORDER") == "old":
                      dense_group(0)
                      dense_group(1)
                      dense_group(2)
                      for bi in range(4):
                          cmp_group(bi)
                          win_group(bi)
                          sel_group(bi)
                  else:
                      dense_group(1)
                      dense_group(2)
                      cmp_group(0)
                      dense_group(0)
                      for bi in range(4):
                          if bi + 1 < 4:
                              cmp_group(bi + 1)
                          win_group(bi)
                          sel_group(bi)

                  pend = []

                  def pop_pair():
                      X = pend.pop(0)
                      Y = pend.pop(0) if pend else None
                      order = [X] if Y is None else ([X, Y] if (X[5] and X[4] == Y[4]) else [Y, X])
                      for it in order:
                          it[0](it[1], it[2])
                      for it in ([X] if Y is None else [X, Y]):
                          if it[3] is not None:
                              it[3]()
                  ci = 0
                  while ci < len(chain):
                      pair = chain[ci:ci + 2]
                      ci += 2
                      for it in pair:
                          for d_ in deferred:
                              d_[0] -= 1
                      run_deferred()
                      for it in pair:
                          if it[5] is not None:
                              it[5]()
                      slots = []
                      for it in pair:
                          sb_, rsb = st_next()
                          pi = ptc[0] % NPT
                          ptc[0] += 1
                          slots.append((sb_, rsb, PT[pi], rPT[pi]))
                      for it, sl in reversed(list(zip(pair, slots))):
                          it[0](sl[0], sl[1])
                      for it, sl in reversed(list(zip(pair, slots))):
                          it[1](sl[0], sl[1], sl[2], sl[3])
                          if it[2] is not None:
                              it[2](sl[2], sl[3])
                      for it, sl in zip(pair, slots):
                          pend.append((it[3], sl[2], sl[3], it[4], it[6], it[7]))
                      if len(pend) > 2:
                          pop_pair()
                  while pend:
                      pop_pair()
                  run_deferred(everything=True)


                  if Tq % TPC == TPC - 1:
                      ch = Tq // TPC
                      cx._need(pool, [], [rPart[ch], rRSd[ch]])
                      ins = G.collective_compute("ReduceScatter", ALU.add, replica_groups=RG, ins=[part_d[ch].opt()], outs=[rs_d[ch].opt()])
                      rsc[ch].cnt += 1
                      ins.then_inc(rsc[ch].sem)
                      rPart[ch].w = (rsc[ch], rsc[ch].cnt); rPart[ch].r = []
                      rRSd[ch].w = (rsc[ch], rsc[ch].cnt); rRSd[ch].r = []

              if L + 1 < depth:
                  load_weights(L + 1)
              if ALIAS:
                  rINB = cx.fork(rVA, 1) + cx.fork(rVN, 1)
                  rACC0, rACC1, rYT0, rYT1 = cx.fork(rKTA, 4)
                  rACCB = [rACC0, rACC1]
                  rYTB2 = [rYT0, rYT1]
                  (rGB,) = cx.fork(rXB, 1)
              cx.dma(sp, ldg, GTB, lng_d[L], writes=[rGB])
              cx.dma(sp, ldg, BTB, lnb_d[L], writes=[rGB])
              rGB.w = (ldg, ldg.cnt)
              xres_d = xq_d if L == 0 else yq_d
              ntb = SQ // 128
              last = (L == depth - 1)

              cx._need(sp, [], [rYQ])

              cx._need(act, [], [rYQ])

              def loadb(i):
                  k = i % 2
                  r0 = i * 128
                  ch, jj = i // TPC, i % TPC
                  cx.dma(sp, ldb[k], INB[k][:, 0, :], rs_d[ch][jj * 128:(jj + 1) * 128, :], reads=[rRSd[ch]], writes=[rINB[k]])
                  cx.dma(sp, ldb[k], INB[k][:, 4, :], xres_d[r0:r0 + 128, :], writes=[rINB[k]])
                  rINB[k].w = (ldb[k], ldb[k].cnt)

              loadb(0)
              for i in range(ntb):
                  k = i % 2
                  if i + 1 < ntb:
                      loadb(i + 1)
                  I_, Ac, rI, rAc = INB[k], ACCB[k], rINB[k], rACCB[k]
                  cx.op(dve, lambda: V.scalar_tensor_tensor(out=Ac, in0=I_[:, 4, :], scalar=float(ALPHA), in1=I_[:, 0, :], op0=ALU.mult, op1=ALU.add),
                        reads=[rI], writes=[rAc])
                  for h in range(2):
                      cx.op(dve, lambda h=h: V.bn_stats(out=STT[:, h, :], in_=Ac[:, h * 512:(h + 1) * 512]), reads=[rAc], writes=[rSTT])
                  cx.op(dve, lambda: V.bn_aggr(out=MV[:, 0:2], in_=STT[:].rearrange("p a b -> p (a b)")), reads=[rSTT], writes=[rMV])
                  cx.op(dve, lambda: V.tensor_scalar(out=MV[:, 2:3], in0=MV[:, 1:2], scalar1=1e-5, scalar2=None, op0=ALU.add), reads=[rMV], writes=[rMV])
                  cx.op(act, lambda: A.activation(out=MV[:, 2:3], in_=MV[:, 2:3], func=AF.Ln), reads=[rMV], writes=[rMV])
                  cx.op(act, lambda: A.activation(out=MV[:, 3:4], in_=MV[:, 2:3], func=AF.Exp, scale=-0.5), reads=[rMV], writes=[rMV])
                  cx.op(dve, lambda: V.tensor_scalar(out=Ac, in0=Ac, scalar1=MV[:, 0:1], scalar2=MV[:, 3:4], op0=ALU.subtract, op1=ALU.mult),
                        reads=[rAc, rMV], writes=[rAc])
                  cx.op(dve, lambda: V.tensor_tensor(out=Ac, in0=Ac, in1=GTB, op=ALU.mult), reads=[rAc, rGB], writes=[rAc])
                  cx.op(dve, lambda: V.tensor_tensor(out=Ac, in0=Ac, in1=BTB, op=ALU.add), reads=[rAc, rGB], writes=[rAc])
                  if last:
                      cx.dma(act, sta[k], y_d[i * 128:(i + 1) * 128, :], Ac, reads=[rAc])
                  else:
                      cx.dma(act, sta[k], yq_d[i * 128:(i + 1) * 128, :], Ac, reads=[rAc, rYQ])
                      for hb in range(2):
                          pb, rpb = pj()
                          for c4 in range(4):
                              c = 4 * hb + c4
                              cx.op(pe, lambda c=c, c4=c4, pb=pb: T.transpose(out=pb[:, c4 * 128:(c4 + 1) * 128], in_=Ac[:, c * 128:(c + 1) * 128], identity=IDF[:]),
                                    reads=[rAc, rC], writes=[rpb])
                          cx.op(act if hb == 0 else dve, lambda hb=hb, pb=pb: (A.copy if hb == 0 else V.tensor_copy)(out=YTB2[k][:, 4 * hb:4 * hb + 4, :], in_=pb[:].rearrange("p (c t) -> p c t", c=4)),
                                reads=[rpb], writes=[rYTB2[k]])
                      i2 = i // 2
                      for c in range(8):
                          cx.dma(act, styt2[k], ytq_d[i2][c * 128:(c + 1) * 128, (i % 2) * 128:(i % 2 + 1) * 128], YTB2[k][:, c, :],
                                 reads=[rYTB2[k], rYTQ[i2]])
                      if i % 2 == 1:
                          cx._need(pool, [], [rYTQ[i2], rXT1[i2]])
                          ins = G.collective_compute("AllGather", ALU.bypass, replica_groups=RG, ins=[ytq_d[i2].opt()], outs=[xT1_d[i2].opt()])
                          ccs.cnt += 1
                          ins.then_inc(ccs.sem)
                          rYTQ[i2].w = (ccs, ccs.cnt); rYTQ[i2].r = []
                          rXT1[i2].w = (ccs, ccs.cnt); rXT1[i2].r = []
              if ALIAS:
                  cx.join(rVA, rINB[0:1]); cx.join(rVN, rINB[1:2]); cx.join(rKTA, [rACC0, rACC1, rYT0, rYT1]); cx.join(rXB, [rGB])
        cx.wait_all(sp, sta)
    return nc


def _consts(S):
    half = 32
    t = np.arange(S, dtype=np.float32)
    tab = np.zeros((128, 4, S), np.float32)
    inv = (10000.0 ** (-(np.arange(32, dtype=np.float32) * 2.0 / 64))).astype(np.float32)
    ang = t[None, :] * inv[:, None]
    cosn, sinn = np.cos(ang), np.sin(ang)
    for r in range(128):
        i = r % 64
        tab[r, 0] = cosn[i % 32]
        tab[r, 1] = -sinn[i % 32] if i < 32 else sinn[i % 32]
    invd = (10000.0 ** (-(np.arange(16, dtype=np.float32) * 2.0 / 32))).astype(np.float32)
    angd = t[None, :] * invd[:, None]
    cosd, sind = np.cos(angd), np.sin(angd)
    for r in range(64):
        w = r % 32
        tab[r, 2] = cosd[w % 16]
        tab[r, 3] = -sind[w % 16] if w < 16 else sind[w % 16]
    key = np.arange(S)
    ind = (((key[None, :] // 64) % 64) == np.arange(64)[:, None]).astype(np.float32)
    p = np.arange(128)
    c_tri = (p[:, None] <= p[None, :]).astype(np.float32)
    c_triu = (p[:, None] > p[None, :]).astype(np.float32)
    c_ident = np.eye(128, dtype=np.float32)
    u = np.arange(2048)
    c_maskc = ((16 * p[:, None] + 31) <= u[None, :]).astype(np.float32)
    n = np.arange(512)
    j = np.arange(128)
    ov = np.clip(np.minimum(16 * n[:, None] + 32, 64 * j[None, :] + 64) - np.maximum(16 * n[:, None], 64 * j[None, :]), 0, None)
    ov = (ov.astype(np.float32) / 32.0)
    ov[511] = 0.0
    c_ov = np.ascontiguousarray(ov.reshape(4, 128, 128).transpose(1, 0, 2))
    return dict(tab=tab, ind=ind, c_tri=c_tri, c_triu=c_triu, c_ident=c_ident, c_maskc=c_maskc, c_ov=c_ov)


def _core_cols(c):
    g, hp = c // 2, c % 2
    rng64 = np.arange(64)
    sw64 = (rng64 + 32) % 64
    d32 = np.arange(32)
    swd = np.concatenate([(d32 + 16) % 32, 32 + (d32 + 16) % 32])
    dq = OFF['diff_q'] + 64 * c + rng64
    dk = OFF['diff_k'] + 64 * c + rng64
    dq_sw = OFF['diff_q'] + 64 * c + swd
    dk_sw = OFF['diff_k'] + 64 * c + swd
    fq = OFF['fox_q'] + 64 * c + rng64
    fk = OFF['fox_k'] + 64 * c + rng64
    my = [4 * g + 2 * hp, 4 * g + 2 * hp + 1]
    oth = [4 * g + 2 * (1 - hp), 4 * g + 2 * (1 - hp) + 1]
    nq = lambda H, perm: OFF['nsa_q'] + 64 * H + perm
    kc = lambda perm: OFF['nsa_k_cmp'] + 64 * g + perm
    ks = lambda perm: OFF['nsa_k_sel'] + 64 * g + perm
    kw = lambda perm: OFF['nsa_k_win'] + 64 * g + perm
    vcmp = OFF['nsa_v_cmp'] + 64 * g + rng64
    gate = lambda H, br: np.full(64, OFF['nsa_gate'] + 3 * H + br)
    groups = [
        np.concatenate([dq, fq]), np.concatenate([dk, fk]), np.concatenate([dq_sw, dk_sw]),
        np.concatenate([nq(my[0], rng64), nq(my[1], rng64)]), np.concatenate([nq(my[0], sw64), nq(my[1], sw64)]),
        np.concatenate([nq(oth[0], rng64), nq(oth[1], rng64)]), np.concatenate([nq(oth[0], sw64), nq(oth[1], sw64)]),
        np.concatenate([kc(rng64), ks(rng64)]), np.concatenate([kc(sw64), ks(sw64)]),
        np.concatenate([kw(rng64), vcmp]), np.concatenate([kw(sw64), vcmp]),
        np.concatenate([OFF['diff_z'] + 64 * c + rng64, OFF['fox_z'] + 64 * c + rng64]),
        np.concatenate([OFF['nsa_z'] + 64 * my[0] + rng64, OFF['nsa_z'] + 64 * my[1] + rng64]),
        np.concatenate([gate(my[0], 0), gate(my[1], 0)]), np.concatenate([gate(my[0], 1), gate(my[1], 1)]),
        np.concatenate([gate(my[0], 2), gate(my[1], 2)]),
    ]
    fm = np.concatenate(groups)
    tm = np.concatenate([OFF['fox_v'] + 64 * c + rng64, OFF['diff_v'] + 64 * c + rng64,
                         OFF['nsa_v_sel'] + 64 * g + rng64, OFF['nsa_v_win'] + 64 * g + rng64,
                         np.array([OFF['fox_f'] + c])])
    wo = np.concatenate([64 * c + rng64, 256 + 512 + 64 * c + rng64, 256 + 64 * my[0] + rng64, 256 + 64 * my[1] + rng64])
    return fm, tm, wo


_PROG = {}


def _get(S, depth):
    k = (S, depth)
    if k not in _PROG:
        _PROG[k] = build_fused(S, depth)
    return _PROG[k]


def kernel(x, w_in, b_fox_f, cmp_pos_k, cmp_pos_v, cmp_w1_k, cmp_w2_k, cmp_w1_v, cmp_w2_v,
           lam_q1, lam_k1, lam_q2, lam_k2, diff_subln_g, w_out, ln_g, ln_b):
    f32 = lambda a: np.ascontiguousarray(np.asarray(a, dtype=np.float32))
    x = f32(x)
    B, S, D = x.shape
    depth = w_in.shape[0]
    SQ = S // 4
    w_in, w_out = f32(w_in), f32(w_out)
    cst = _consts(S)
    cols = [_core_cols(c) for c in range(4)]
    w1 = np.ascontiguousarray(np.stack([f32(cmp_w1_k), f32(cmp_w1_v)], axis=1))
    w2 = np.ascontiguousarray(np.concatenate([f32(cmp_w2_k), f32(cmp_w2_v)], axis=2))
    post = np.ascontiguousarray(np.concatenate([f32(cmp_pos_k).transpose(0, 2, 1), f32(cmp_pos_v).transpose(0, 2, 1)], axis=1))
    lng = np.ascontiguousarray(np.broadcast_to(f32(ln_g)[:, None, :], (depth, 128, D)))
    lnb = np.ascontiguousarray(np.broadcast_to(f32(ln_b)[:, None, :], (depth, 128, D)))
    NT = S // 512
    TPC = 1

    def tokmap(r):
        idx = []
        for i in range(SQ // 128):
            base = (i // TPC) * TPC * 512 + r * TPC * 128 + (i % TPC) * 128
            idx.append(np.arange(base, base + 128))
        return np.concatenate(idx)
    tmaps = [tokmap(r) for r in range(4)]
    xT = [np.ascontiguousarray(x[b].T) for b in range(B)]
    in_maps = []
    for core in range(8):
        b, c = core // 4, core % 4
        fm, tm, wo = cols[c]
        vec = np.zeros((depth, 128, 8), np.float32)
        for l in range(depth):
            lam_init = 0.8 - 0.6 * math.exp(-0.3 * l)
            vec[l, :, 0] = f32(b_fox_f)[l, c]
            vec[l, :, 1] = f32(diff_subln_g)[l][np.arange(128) % 64]
            vec[l, 0:32, 2] = f32(lam_q1)[l]; vec[l, 0:32, 3] = f32(lam_k1)[l]
            vec[l, 0:32, 4] = f32(lam_q2)[l]; vec[l, 0:32, 5] = f32(lam_k2)[l]
            vec[l, :, 6] = -lam_init
            vec[l, :, 7] = 1.0 - lam_init
        m = dict(xT=xT[b], xq=np.ascontiguousarray(x[b, tmaps[c]]),
                 wfm=np.ascontiguousarray(w_in[:, :, fm]), wtm=np.ascontiguousarray(w_in[:, :, tm]),
                 wout=np.ascontiguousarray(w_out[:, wo, :]), w1=w1, w2=w2, post=post, vecs=vec, lng=lng, lnb=lnb)
        m.update(cst)
        in_maps.append(m)
    res = run_bass_kernel_spmd(_get(S, depth), in_maps, core_ids=list(range(8)))
    out = np.empty((B, S, D), np.float32)
    for core in range(8):
        b, c = core // 4, core % 4
        out[b, tmaps[c]] = res.results[core]["y"]
    return out
```

```python
import math
import os
from contextlib import ExitStack
import numpy as np
import concourse.bass as bass
import concourse.mybir as mybir
from concourse.bass_utils import run_bass_kernel_spmd

F32 = mybir.dt.float32
BF16 = mybir.dt.bfloat16
ALU = mybir.AluOpType
AF = mybir.ActivationFunctionType
AX = mybir.AxisListType

D_MODEL = 1024
DEPTH = 2
HD = 64
IN_SPLITS = (('fox_q', 256), ('fox_k', 256), ('fox_v', 256), ('fox_f', 4), ('fox_z', 256), ('nsa_q', 512),
             ('nsa_k_cmp', 128), ('nsa_v_cmp', 128), ('nsa_k_sel', 128), ('nsa_v_sel', 128),
             ('nsa_k_win', 128), ('nsa_v_win', 128), ('nsa_gate', 24), ('nsa_z', 512),
             ('diff_q', 256), ('diff_k', 256), ('diff_v', 256), ('diff_z', 256))
OFF = {}
_a = 0
for _n, _w in IN_SPLITS:
    OFF[_n] = _a
    _a += _w
IN_WIDTH = _a
ALPHA = (2 * DEPTH) ** 0.25
NG = 16
BIG = 1.0e30


class Res:
    __slots__ = ("w", "r", "excl")

    def __init__(self, excl=False):
        self.excl = excl
        self.w = None
        self.r = []


class SemC:
    __slots__ = ("sem", "cnt", "key")
    _n = 0

    def __init__(self, sem):
        self.sem = sem
        self.cnt = 0
        SemC._n += 1
        self.key = SemC._n


class Eng:
    def __init__(self, h, semc, same_sync):
        self.h = h
        self.s = semc
        self.waited = {}
        self.same_sync = same_sync


class Ctx:
    def __init__(self, nc, stack):
        self.nc = nc
        self.stack = stack
        mk = lambda n: SemC(stack.enter_context(nc.semaphore(n)))
        self.pe = Eng(nc.tensor, mk("s_pe"), False)
        self.dve = Eng(nc.vector, mk("s_dve"), True)
        self.act = Eng(nc.scalar, mk("s_act"), True)
        self.pool = Eng(nc.gpsimd, mk("s_pool"), True)
        self.sp = Eng(nc.sync, mk("s_sp"), False)

    def res(self, excl=False):
        return Res(excl)

    def dsem(self, name):
        return SemC(self.stack.enter_context(self.nc.semaphore(name)))

    def sbuf(self, name, shape, dt):
        return self.stack.enter_context(self.nc.sbuf_tensor(name, list(shape), dt))

    def psum(self, name, shape, dt):
        return self.stack.enter_context(self.nc.psum_tensor(name, list(shape), dt))

    def _need(self, eng, reads, writes):
        need = {}

        def add(p):
            if p is None:
                return
            s, v = p
            if s is eng.s and not eng.same_sync:
                return
            if need.get(s.key, (None, -1))[1] < v:
                need[s.key] = (s, v)
        for r in reads:
            add(r.w)
            if r.excl:
                for p in r.r:
                    if p[0] is not eng.s:
                        add(p)
        for w in writes:
            add(w.w)
            for p in w.r:
                add(p)
        for k, (s, v) in need.items():
            if eng.waited.get(k, 0) < v:
                eng.h.wait_ge(s.sem, v)
                eng.waited[k] = v

    def op(self, eng, fn, reads=(), writes=()):
        self._need(eng, reads, writes)
        ins = fn()
        eng.s.cnt += 1
        ins.then_inc(eng.s.sem, 1)
        me = (eng.s, eng.s.cnt)
        for r in reads:
            r.r.append(me)
            if len(r.r) > 24:
                r.r = r.r[-24:] if False else _compact(r.r)
        for w in writes:
            w.w = me
            w.r = []
        return ins

    def dma(self, q, dsem, out, in_, reads=(), writes=()):
        self._need(q, reads, writes)
        ins = q.h.dma_start(out=out, in_=in_)
        dsem.cnt += 16
        ins.then_inc(dsem.sem, 16)
        me = (dsem, dsem.cnt)
        for r in reads:
            r.r.append(me)
        for w in writes:
            w.w = me
            w.r = []
        return ins

    def fork(self, parent, n):
        kids = [Res() for _ in range(n)]
        for k in kids:
            k.r = _compact(list(parent.r) + ([parent.w] if parent.w else []))
        return kids

    def join(self, parent, kids):
        for k in kids:
            parent.r = _compact(parent.r + k.r + ([k.w] if k.w else []))

    def wait_all(self, eng, semcs):
        for s in semcs:
            if s.cnt > 0 and eng.waited.get(s.key, 0) < s.cnt:
                eng.h.wait_ge(s.sem, s.cnt)
                eng.waited[s.key] = s.cnt


def _compact(lst):
    best = {}
    for s, v in lst:
        if best.get(s.key, (None, -1))[1] < v:
            best[s.key] = (s, v)
    return list(best.values())


def build_fused(S, depth=DEPTH):
    NT = S // 512
    NK = S // 128
    SQ = S // 4
    nc = bass.Bass("TRN2", target_bir_lowering=False)
    din = lambda n, sh, dt=F32: nc.dram_tensor(n, list(sh), dt, kind="ExternalInput").ap()
    xT0_d = din("xT", [1024, S])
    xq_d = din("xq", [SQ, 1024])
    wfm_d = din("wfm", [depth, 1024, NG * 128])
    wtm_d = din("wtm", [depth, 1024, 257])
    wout_d = din("wout", [depth, 256, 1024])
    w1_d = din("w1", [depth, 2, 2048, 128])
    w2_d = din("w2", [depth, 128, 128])
    post_d = din("post", [depth, 128, 32])
    vec_d = din("vecs", [depth, 128, 8])
    lng_d = din("lng", [depth, 128, 1024])
    lnb_d = din("lnb", [depth, 128, 1024])
    tab_d = din("tab", [128, 4, S])
    ind_d = din("ind", [64, S])
    ctri_d = din("c_tri", [128, 128])
    ctriu_d = din("c_triu", [128, 128])
    cid_d = din("c_ident", [128, 128])
    cmk_d = din("c_maskc", [128, 2048])
    cov_d = din("c_ov", [128, 4, 128])
    y_d = nc.dram_tensor("y", [SQ, 1024], F32, kind="ExternalOutput").ap()
    TPC = 1
    NCH = NT // TPC
    part_d = [nc.dram_tensor(f"part_i{c}", [TPC * 512, 1024], F32).ap() for c in range(NCH)]
    rs_d = [nc.dram_tensor(f"rs_i{c}", [TPC * 128, 1024], F32).ap() for c in range(NCH)]
    NTB = SQ // 128
    ytq_d = [nc.dram_tensor(f"ytq_i{i}", [1024, 256], BF16).ap() for i in range(NTB // 2)]
    xT1_d = [nc.dram_tensor(f"xT1_i{i}", [4 * 1024, 256], BF16).ap() for i in range(NTB // 2)]
    yq_d = nc.dram_tensor("yq_i", [SQ, 1024], F32).ap()
    RG = [[0, 1, 2, 3], [4, 5, 6, 7]]

    with ExitStack() as st:
        cx = Ctx(nc, st)
        pe, dve, act, pool, sp = cx.pe, cx.dve, cx.act, cx.pool, cx.sp
        V, G, T, A = nc.vector, nc.gpsimd, nc.tensor, nc.scalar
        sb = cx.sbuf
        WFM = sb("WFM", [128, 8, NG * 128], BF16); rWFM = cx.res()
        WTM = sb("WTM", [128, 8, 257], BF16); rWTM = cx.res()
        WOUT = sb("WOUT", [128, 2, 1024], BF16); rWOUT = cx.res()
        W1 = sb("W1", [128, 32, 128], BF16); rW1 = cx.res()
        W2 = sb("W2", [128, 128], BF16); rW2 = cx.res()
        POST = sb("POST", [128, 32], BF16); rPOST = cx.res()
        BH = sb("BH", [128, 2], F32); rBH = cx.res()
        KTA = sb("KTA", [128, S], BF16); rKTA = cx.res()
        KTS = sb("KTS", [128, S], BF16); rKTS = cx.res()
        KTW = sb("KTW", [128, 1024], BF16); rKTW = cx.res()
        VA = sb("VA", [128, NK, 192], BF16); rVA = cx.res()
        VN = sb("VN", [128, NK, 192], BF16); rVN = cx.res()
        VCA = sb("VCA", [128, 4, 128], BF16); rVCA = cx.res()
        KCT = sb("KCT", [128, 512], BF16); rKCT = cx.res()
        VCTF = sb("VCTF", [64, 512], F32); rVCTF = cx.res()
        CMPIN = sb("CMPIN", [128, 528], BF16); rCMPIN = cx.res()
        XB = sb("XB", [128, 8, 512], BF16); rXB = cx.res()
        TAB = sb("TAB", [128, 4, 512], F32); rTAB = cx.res()
        TRI = sb("TRI", [128, 128], BF16); TRIU = sb("TRIU", [128, 128], BF16)
        MASKC = sb("MASKC", [128, 2048], BF16); OV = sb("OV", [128, 4, 128], BF16)
        IDF = sb("IDF", [128, 128], F32); TRIF = sb("TRIF", [128, 128], F32); ONESF = sb("ONESF", [128, 128], F32)
        VEC = sb("VEC", [128, 8], F32)
        rC = cx.res()
        rC2 = cx.res()
        QF = sb("QF", [128, 512], BF16); QD0 = sb("QD0", [128, 512], BF16); QD1 = sb("QD1", [128, 512], BF16); rQA = cx.res()
        QN = sb("QN", [128, 4, 512], BF16); rQN = cx.res()
        QSA = [sb(f"QSA{i}", [128, 2, 256], BF16) for i in range(2)]; rQSA = [[cx.res(), cx.res()] for _ in range(2)]
        T0 = sb("T0", [128, 512], F32); rT0 = cx.res()
        T1 = sb("T1", [128, 512], F32); rT1 = cx.res()
        Z1 = sb("Z1", [128, 512], F32); rZ1 = cx.res()
        GZ = sb("GZ", [128, 3, 2, 512], BF16); rGZ = cx.res()
        NPT = 5
        ptc = [0]
        PT = [sb(f"PT{i}", [128, 512], BF16) for i in range(NPT)]; rPT = [cx.res() for _ in range(NPT)]
        FA = sb("FA", [128, 512], F32); rFA = cx.res()
        FB = sb("FB", [128, 512], F32); rFB = cx.res()
        CT0 = sb("CT0", [128, 512], BF16); rCT0 = cx.res()
        CT1 = sb("CT1", [128, 512], BF16); rCT1 = cx.res()
        OUTS = sb("OUTS", [128, 1024], F32); rOUTS = cx.res()
        CNEG = sb("CNEG", [128, NK], F32); rCNEG = cx.res()
        CARRY = sb("CARRY", [128, NK + 4], F32); rCARRY = cx.res()
        SPL = sb("SPL", [128, NK], F32); rSPL = cx.res()
        NB = sb("NB", [128, NK], F32); rNB = cx.res()
        LFRAW = sb("LFRAW", [128, NK], F32); rLFRAW = cx.res()
        SC = sb("SC", [128, 128], F32); rSC = cx.res()
        SC2 = sb("SC2", [128, 128], F32); rSC2 = cx.res()
        M8 = sb("M8", [128, 8], F32); M8b = sb("M8b", [128, 8], F32); rM8 = cx.res(); rM8b = cx.res()
        MNEG2 = [sb(f"MNEG{i}", [128, 128], F32) for i in range(2)]; rMNEG2 = [cx.res(), cx.res()]
        RS4 = sb("RS4", [128, 4], F32); rRS4 = cx.res()
        RI4 = sb("RI4", [128, 4], F32); rRI4 = cx.res()
        YA = [sb(f"YA{i}", [64, 256], F32)[:] for i in range(2)]; rYA = [cx.res(), cx.res()]
        YB = sb("YB", [64, 256], F32); rYB = cx.res()
        YC = sb("YC", [64, 256], F32); rYC = cx.res()
        FN = sb("FN", [128, 256], F32); rFN = cx.res()
        HS = sb("HS", [128, 64], BF16); rHS = cx.res()
        HSS = sb("HSS", [128, 64], F32); rHSS = cx.res()
        LAM = sb("LAM", [128, 4], F32); rLAM = cx.res()
        IMPB = cx.psum("IMPB", [128, 512], F32); rIMPB = cx.res(True)
        STB = [cx.psum(f"ST{i}", [128, 512], F32) for i in range(4)]; rST = [cx.res(True) for _ in range(4)]
        OB = [cx.psum(f"OB{i}", [128, 512], F32) for i in range(3)]; rOB = [cx.res(True) for _ in range(3)]
        ALLB = [(IMPB, rIMPB)] + list(zip(STB, rST)) + list(zip(OB, rOB))
        ld = cx.dsem("ld_const"); ldx = cx.dsem("ld_x"); ldt = cx.dsem("ld_tab"); sto = cx.dsem("st_out")
        pjc = [0]

        def pjx():
            i = pjc[0] % 8
            pjc[0] += 1
            return ALLB[i]

        pj = pjx

        ldw = cx.dsem("ld_w"); ccs = cx.dsem("cc_sem"); ldb = [cx.dsem("ld_b0"), cx.dsem("ld_b1")]; sta = [cx.dsem("st_a0"), cx.dsem("st_a1")]; ldg = cx.dsem("ld_g")
        rYQ = cx.res()
        rYTQ = [cx.res() for _ in range(NTB // 2)]; rXT1 = [cx.res() for _ in range(NTB // 2)]
        rPart = [cx.res() for _ in range(NCH)]; rRSd = [cx.res() for _ in range(NCH)]
        styt2 = [cx.dsem("st_yt0"), cx.dsem("st_yt1")]
        rsc = [cx.dsem(f"rs_sem{c}") for c in range(NCH)]
        cx.dma(pool, ld, KTS[64:128, :], ind_d, writes=[rKTS])
        cx.dma(pool, ld, TRI[:], ctri_d, writes=[rC])
        cx.dma(pool, ld, TRIU[:], ctriu_d, writes=[rC])
        cx.dma(pool, ld, MASKC[:], cmk_d, writes=[rC])
        cx.dma(pool, ld, OV[:], cov_d, writes=[rC])
        cx.dma(sp, ld, IDF[:], cid_d, writes=[rC])
        cx.dma(sp, ld, TRIF[:], ctri_d, writes=[rC])
        EPSC = sb("EPSC", [128, 1], F32)
        for r_ in (rKTS, rC):
            r_.w = (ld, ld.cnt)
        cx.op(dve, lambda: V.memset(ONESF[:], 1.0), writes=[rC2])
        cx.op(dve, lambda: V.memset(EPSC[:], 1e-5), writes=[rC2])
        rVEC = cx.res()
        if NK * 96 >= 5120:
            INB = [VA[:].rearrange("p a b -> p (a b)").bitcast(F32)[:, 0:5120].rearrange("p (j n) -> p j n", j=5),
                   VN[:].rearrange("p a b -> p (a b)").bitcast(F32)[:, 0:5120].rearrange("p (j n) -> p j n", j=5)]
            kf = KTA[:].bitcast(F32)
            ACCB = [kf[:, 0:1024], kf[:, 1024:2048]]
            YTB2 = [KTA[:, 4096:5120].rearrange("p (c t) -> p c t", c=8), KTA[:, 5120:6144].rearrange("p (c t) -> p c t", c=8)]
            xf = XB[:].rearrange("p a b -> p (a b)").bitcast(F32)
            GTB, BTB = xf[:, 0:1024], xf[:, 1024:2048]
            ALIAS = True
        else:
            ALIAS = False
            INB = [sb(f"INB{i}", [128, 5, 1024], F32)[:] for i in range(2)]; rINB = [cx.res(), cx.res()]
            ACCB = [sb(f"ACCB{i}", [128, 1024], F32)[:] for i in range(2)]; rACCB = [cx.res(), cx.res()]
            YTB2 = [sb(f"YTB{i}", [128, 8, 128], BF16)[:] for i in range(2)]; rYTB2 = [cx.res(), cx.res()]
            GTB = sb("GTB", [128, 1024], F32)[:]; BTB = sb("BTB", [128, 1024], F32)[:]; rGB = cx.res()
        STT = sb("STT", [128, 2, 6], F32); rSTT = cx.res()
        MV = sb("MV", [128, 4], F32); rMV = cx.res()
        def load_weights(L):
            cx.dma(pool, ldw, WFM[:], wfm_d[L].rearrange("(c p) n -> p c n", p=128), writes=[rWFM])
            cx.dma(pool, ldw, WTM[:], wtm_d[L].rearrange("(c p) n -> p c n", p=128), writes=[rWTM])
            cx.dma(pool, ldw, WOUT[:], wout_d[L].rearrange("(c p) n -> p c n", p=128), writes=[rWOUT])
            cx.dma(pool, ldw, W1[0:64, :, :], w1_d[L, 0].rearrange("(l d) h -> d l h", d=64), writes=[rW1])
            cx.dma(pool, ldw, W1[64:128, :, :], w1_d[L, 1].rearrange("(l d) h -> d l h", d=64), writes=[rW1])
            cx.dma(pool, ldw, W2[:], w2_d[L], writes=[rW2])
            cx.dma(pool, ldw, POST[:], post_d[L], writes=[rPOST])
            cx.dma(pool, ldw, VEC[:], vec_d[L], writes=[rVEC])
            for r_ in (rWFM, rWTM, rWOUT, rW1, rW2, rPOST, rVEC):
                r_.w = (ldw, ldw.cnt)

        load_weights(0)
        for L in range(depth):
          if True:
              cx.op(dve, lambda: V.memset(VA[:, :, 64:128], 1.0), writes=[rVA])
              cx.op(pool, lambda: G.memset(VN[:, :, 64:128], 1.0), writes=[rVN])
              cx.op(dve, lambda: V.memset(VCA[:, :, 0:64], 0.0), writes=[rVCA])
              cx.op(dve, lambda: V.memset(VCA[:, :, 64:128], 1.0), writes=[rVCA])
              cx.op(dve, lambda: V.memset(KCT[:], 0.0), writes=[rKCT])
              cx.op(pool, lambda: G.memset(KTW[:], 0.0), writes=[rKTW])
              cx.op(pool, lambda: G.memset(QN[64:128, :, :], 0.0), writes=[rQN])
              cx.op(dve, lambda: V.memset(QF[:], 0.0), writes=[rQA])
              cx.op(dve, lambda: V.memset(QD0[:], 0.0), writes=[rQA])
              cx.op(pool, lambda: G.memset(QD1[:], 0.0), writes=[rQA])
              cx.op(dve, lambda: V.memset(VCTF[:], 0.0), writes=[rVCTF])
              cx.op(pool, lambda: G.memset(CMPIN[:], 0.0), writes=[rCMPIN])
              cx.op(dve, lambda: V.memset(CARRY[:], 0.0), writes=[rCARRY])
              for s_ in range(2):
                  pb, rpb = pj()
                  lo = 64 * s_
                  for l in range(32):
                      cx.op(pe, lambda l=l, lo=lo, pb=pb: T.matmul(pb[:, 0:1], lhsT=W1[lo:lo + 64, l, :], rhs=POST[lo:lo + 64, l:l + 1],
                                                                    start=(l == 0), stop=(l == 31)),
                            reads=[rW1, rPOST], writes=[rpb])
                  cx.op(dve, lambda pb=pb, s_=s_: V.tensor_copy(out=BH[:, s_:s_ + 1], in_=pb[:, 0:1]), reads=[rpb], writes=[rBH])
              cx.op(dve, lambda: V.tensor_tensor(out=LAM[:, 0:1], in0=VEC[:, 2:3], in1=VEC[:, 3:4], op=ALU.mult), reads=[rC, rVEC], writes=[rLAM])
              cx.op(dve, lambda: V.tensor_tensor(out=LAM[:, 1:2], in0=VEC[:, 4:5], in1=VEC[:, 5:6], op=ALU.mult), reads=[rC, rC2, rVEC, rLAM], writes=[rLAM])
              pb, rpb = pj()
              cx.op(pe, lambda: T.matmul(pb[:, 0:2], lhsT=ONESF[:], rhs=LAM[:, 0:2], start=True, stop=True), reads=[rC, rC2, rLAM], writes=[rpb])
              cx.op(act, lambda: A.activation(out=LAM[:, 2:4], in_=pb[:, 0:2], func=AF.Exp), reads=[rpb], writes=[rLAM])
              cx.op(dve, lambda: V.tensor_tensor(out=LAM[:, 0:1], in0=LAM[:, 3:4], in1=LAM[:, 2:3], op=ALU.subtract), reads=[rLAM], writes=[rLAM])
              cx.op(dve, lambda: V.tensor_tensor(out=LAM[:, 0:1], in0=LAM[:, 0:1], in1=VEC[:, 6:7], op=ALU.add),
                    reads=[rLAM, rC, rC2, rVEC], writes=[rLAM])
              cx.op(dve, lambda: V.tensor_tensor(out=LAM[:, 1:2], in0=VEC[:, 1:2], in1=VEC[:, 7:8], op=ALU.mult),
                    reads=[rC, rC2, rVEC, rLAM], writes=[rLAM])
              NEGLAM = LAM[:, 0:1]
              GCOL = LAM[:, 1:2]

              def load_tile(Tq):
                  if L == 0:
                      cx.dma(pool, ldx, XB[:], xT0_d[:, Tq * 512:(Tq + 1) * 512].rearrange("(c p) t -> p c t", p=128), writes=[rXB])
                  else:
                      for pc in range(4):
                          row = (Tq % TPC) * 512 + pc * 128
                          rk, ib = row // (TPC * 128), (Tq // TPC) * TPC + (row % (TPC * 128)) // 128
                          cx.dma(pool, ldx, XB[:, :, pc * 128:(pc + 1) * 128],
                                 xT1_d[ib // 2][rk * 1024:(rk + 1) * 1024, (ib % 2) * 128:(ib % 2 + 1) * 128].rearrange("(c p) t -> p c t", p=128),
                                 reads=[rXT1[ib // 2]], writes=[rXB])
                      rXB.w = (ldx, ldx.cnt)
                  cx.dma(sp, ldt, TAB[:], tab_d[:, :, Tq * 512:(Tq + 1) * 512], writes=[rTAB])

              load_tile(0)

              def fm_group(gi):
                  pb, rpb = pjx()
                  for c in range(8):
                      cx.op(pe, lambda c=c, pb=pb: T.matmul(pb[:], lhsT=WFM[:, c, gi * 128:(gi + 1) * 128], rhs=XB[:, c, :],
                                                             start=(c == 0), stop=(c == 7)), reads=[rWFM, rXB], writes=[rpb])
                  return pb, rpb

              def rope(pa, rpa, ps, rps, rows_a, rows_s, tabi, outs):
                  (a0, a1), (s0, s1) = rows_a, rows_s
                  n = a1 - a0
                  cx.op(dve, lambda: V.tensor_tensor(out=T0[0:n, :], in0=pa[a0:a1, :], in1=TAB[0:n, tabi, :], op=ALU.mult),
                        reads=[rpa, rTAB], writes=[rT0])
                  cx.op(dve, lambda: V.tensor_tensor(out=T1[0:n, :], in0=ps[s0:s1, :], in1=TAB[0:n, tabi + 1, :], op=ALU.mult),
                        reads=[rps, rTAB], writes=[rT1])
                  for (r0_, r1_, dst, rdst) in outs:
                      cx.op(pool, lambda r0_=r0_, r1_=r1_, dst=dst: G.tensor_tensor(out=dst, in0=T0[r0_:r1_, :], in1=T1[r0_:r1_, :], op=ALU.add),
                            reads=[rT0, rT1], writes=[rdst])

              def bcast_h(ap2d, h):
                  return bass.AP(tensor=ap2d.tensor, offset=ap2d.offset, ap=[list(ap2d.ap[0]), [0, h], list(ap2d.ap[1])])

              SGT, rSGT, ZN, rZN = FA, rFA, FB, rFB
              obc = [0]
              stc = [0]
              DSK = int(os.environ.get("DBG_DSK", "3"))

              def ob_next():
                  k = obc[0] % 3
                  obc[0] += 1
                  return OB[k], rOB[k]

              def st_next():
                  k = stc[0] % 4
                  stc[0] += 1
                  return STB[k], rST[k]

              carry = []
              for Tq in range(NT):
                  c0t = Tq * 512
                  pA, rA_ = fm_group(0)
                  pS, rS_ = fm_group(2)
                  rope(pA, rA_, pS, rS_, (0, 64), (0, 64), 2, [(0, 32, QD0[0:32, :], rQA), (32, 64, QD1[32:64, :], rQA)])
                  cx.op(act, lambda: A.copy(out=QF[64:128, :], in_=pA[64:128, :]), reads=[rA_], writes=[rQA])
                  pK, rK_ = fm_group(1)
                  rope(pK, rK_, pS, rS_, (0, 64), (64, 128), 2, [(0, 64, KTA[0:64, c0t:c0t + 512], rKTA)])
                  cx.op(act, lambda: A.copy(out=KTA[64:128, c0t:c0t + 512], in_=pK[64:128, :]), reads=[rK_], writes=[rKTA])
                  for f_ in carry:
                      f_()
                  del carry[:]
                  for (ga, gs, hq) in ((3, 4, 0), (5, 6, 2)):
                      p1, r1 = fm_group(ga)
                      p2, r2 = fm_group(gs)
                      rope(p1, r1, p2, r2, (0, 128), (0, 128), 0, [(0, 64, QN[0:64, hq, :], rQN), (64, 128, QN[0:64, hq + 1, :], rQN)])
                  p1, r1 = fm_group(7)
                  p2, r2 = fm_group(8)
                  rope(p1, r1, p2, r2, (0, 128), (0, 128), 0, [(0, 64, CMPIN[0:64, 16:528], rCMPIN), (64, 128, KTS[0:64, c0t:c0t + 512], rKTS)])
                  p1, r1 = fm_group(9)
                  p2, r2 = fm_group(10)
                  wslot = (Tq % 2) * 512
                  rope(p1, r1, p2, r2, (0, 64), (0, 64), 0, [(0, 64, KTW[0:64, wslot:wslot + 512], rKTW)])
                  cx.op(act, lambda: A.copy(out=CMPIN[64:128, 16:528], in_=p1[64:128, :]), reads=[r1], writes=[rCMPIN])
                  p1, r1 = fm_group(11)
                  cx.op(act, lambda: A.activation(out=Z1[:], in_=p1[:], func=AF.Sigmoid), reads=[r1], writes=[rZ1])
                  cx.op(dve, lambda: V.tensor_tensor(out=Z1[:], in0=p1[:], in1=Z1[:], op=ALU.mult), reads=[r1, rZ1], writes=[rZ1])
                  p1, r1 = fm_group(12)
                  cx.op(act, lambda: A.activation(out=ZN[:], in_=p1[:], func=AF.Sigmoid), reads=[r1], writes=[rZN])
                  cx.op(dve, lambda: V.tensor_tensor(out=ZN[:], in0=p1[:], in1=ZN[:], op=ALU.mult), reads=[r1, rZN], writes=[rZN])
                  sgbufs = ((FA, rFA), (T0, rT0), (T1, rT1))
                  for br in range(3):
                      p1, r1 = fm_group(13 + br)
                      SGb, rSGb = sgbufs[br]
                      cx.op(act, lambda: A.activation(out=SGb[:], in_=p1[:], func=AF.Sigmoid), reads=[r1], writes=[rSGb])
                      ob = 0 if br == 2 else 64
                      for h in range(2):
                          eng_, E_ = ((pool, G), (dve, V))[(2 * br + h) % 2]
                          cx.op(eng_, lambda h=h, ob=ob, br=br, E_=E_: E_.tensor_tensor(out=GZ[ob:ob + 64, br, h, :], in0=SGb[64 * h:64 * h + 64, :],
                                                                                       in1=ZN[64 * h:64 * h + 64, :], op=ALU.mult),
                                reads=[rSGb, rZN], writes=[rGZ])
                  for st_ in range(4):
                      kt = Tq * 4 + st_
                      pb, rpb = pjx()
                      for c in range(8):
                          cx.op(pe, lambda c=c, pb=pb, st_=st_: T.matmul(pb[:, 0:257], lhsT=XB[:, c, st_ * 128:(st_ + 1) * 128], rhs=WTM[:, c, :],
                                                                          start=(c == 0), stop=(c == 7)), reads=[rXB, rWTM], writes=[rpb])
                      cx.op(act, lambda pb=pb, kt=kt: A.copy(out=VA[:, kt, :].rearrange("p (a b) -> p a b", a=3)[:, 0::2, :],
                                                             in_=pb[:, 0:128].rearrange("p (a b) -> p a b", a=2)), reads=[rpb], writes=[rVA])
                      cx.op(dve, lambda pb=pb, kt=kt: V.tensor_copy(out=VN[:, kt, :].rearrange("p (a b) -> p a b", a=3)[:, 0::2, :],
                                                                    in_=pb[:, 128:256].rearrange("p (a b) -> p a b", a=2)), reads=[rpb], writes=[rVN])
                      cx.op(dve, lambda pb=pb, kt=kt: V.tensor_copy(out=LFRAW[:, kt:kt + 1], in_=pb[:, 256:257]), reads=[rpb], writes=[rLFRAW])
                  if Tq + 1 < NT:
                      load_tile(Tq + 1)
                  deferred = []

                  def defer(n, fn, tag):
                      deferred.append([n, fn, tag])

                  def run_deferred(upto_tag=None, everything=False, pred=None):
                      if pred is None and upto_tag is not None:
                          pred = lambda t: t == upto_tag
                      while deferred:
                          n, fn, tag = deferred[0]
                          if everything or n <= 0 or (pred is not None and any(pred(d[2]) for d in deferred)):
                              deferred.pop(0)
                              fn()
                          else:
                              break

                  j0 = 1 if Tq == 0 else 0
                  nj = 32 - j0
                  n0 = 32 * Tq - 1 + j0
                  for s_ in range(2):
                      lo = 64 * s_
                      pb, rpb = pjx()
                      for l in range(32):
                          cx.op(pe, lambda l=l, lo=lo, pb=pb: T.matmul(pb[:, 0:nj], lhsT=W1[lo:lo + 64, l, :],
                                                                        rhs=CMPIN[lo:lo + 64, 16 * j0 + l:16 * j0 + l + 16 * (nj - 1) + 1:16],
                                                                        start=(l == 0), stop=(l == 31)), reads=[rW1, rCMPIN], writes=[rpb])
                      cx.op(act, lambda pb=pb, s_=s_: A.activation(out=HSS[:, 32 * s_:32 * s_ + nj], in_=pb[:, 0:nj], func=AF.Sigmoid, bias=BH[:, s_:s_ + 1]),
                            reads=[rpb, rBH], writes=[rHSS])
                      cx.op(dve, lambda pb=pb, s_=s_: V.scalar_tensor_tensor(out=HS[:, 32 * s_:32 * s_ + nj], in0=pb[:, 0:nj], scalar=BH[:, s_:s_ + 1],
                                                                             in1=HSS[:, 32 * s_:32 * s_ + nj], op0=ALU.add, op1=ALU.mult),
                            reads=[rpb, rBH, rHSS], writes=[rHS])
                  cx.op(pool, lambda: G.tensor_copy(out=CMPIN[:, 0:16], in_=CMPIN[:, 512:528]), reads=[rCMPIN], writes=[rCMPIN])

                  def kc_stage2():
                      pb, rpb = st_next()
                      cx.op(pe, lambda: T.matmul(pb[0:64, 0:nj], lhsT=W2[:, 0:64], rhs=HS[:, 0:nj], start=True, stop=True), reads=[rW2, rHS], writes=[rpb])
                      cx.op(pe, lambda: T.matmul(pb[0:64, 64:64 + nj], lhsT=W2[:, 64:128], rhs=HS[:, 32:32 + nj], start=False, stop=True, skip_group_check=True),
                            reads=[rW2, rHS], writes=[rpb])
                      cx.op(dve, lambda: V.tensor_copy(out=KCT[0:64, n0:n0 + nj], in_=pb[0:64, 0:nj]), reads=[rpb], writes=[rKCT])
                      cx.op(dve, lambda: V.tensor_copy(out=VCTF[0:64, n0:n0 + nj], in_=pb[0:64, 64:64 + nj]), reads=[rpb], writes=[rVCTF])
                      for a_ in sorted(set([max(n0, 0) // 128, (n0 + nj - 1) // 128])):
                          pb2, rpb2 = st_next()
                          cx.op(pe, lambda a_=a_, pb2=pb2: T.transpose(out=pb2[:, 0:64], in_=VCTF[0:64, a_ * 128:(a_ + 1) * 128], identity=IDF[0:64, 0:64]),
                                reads=[rVCTF, rC, rC2], writes=[rpb2])
                          cx.op(dve, lambda a_=a_, pb2=pb2: V.tensor_copy(out=VCA[:, a_, 0:64], in_=pb2[:, 0:64]), reads=[rpb2], writes=[rVCA])
                  defer(4, kc_stage2, ("kc", Tq))

                  k0 = Tq * 4
                  cx.op(dve, lambda: V.tensor_scalar(out=SPL[:, k0:k0 + 4], in0=LFRAW[:, k0:k0 + 4], scalar1=VEC[:, 0:1], scalar2=None, op0=ALU.add),
                        reads=[rLFRAW, rC, rC2, rVEC], writes=[rSPL])
                  cx.op(act, lambda: A.activation(out=SPL[:, k0:k0 + 4], in_=SPL[:, k0:k0 + 4], func=AF.Exp, scale=-1.0), reads=[rSPL], writes=[rSPL])
                  cx.op(act, lambda: A.activation(out=SPL[:, k0:k0 + 4], in_=SPL[:, k0:k0 + 4], func=AF.Ln, bias=1.0), reads=[rSPL], writes=[rSPL])

                  def nb_stage():
                      pb, rpb = st_next()
                      cx.op(pe, lambda: T.matmul(pb[:, 0:4], lhsT=TRIF[:], rhs=SPL[:, k0:k0 + 4], start=True, stop=True), reads=[rC, rC2, rSPL], writes=[rpb])
                      cx.op(pe, lambda: T.matmul(pb[:, 8:12], lhsT=ONESF[:], rhs=SPL[:, k0:k0 + 4], start=False, stop=True, skip_group_check=True), reads=[rC, rC2, rSPL], writes=[rpb])
                      for j in range(4):
                          cx.op(dve, lambda j=j: V.tensor_tensor(out=CARRY[:, k0 + j + 1:k0 + j + 2], in0=CARRY[:, k0 + j:k0 + j + 1],
                                                                 in1=pb[:, 8 + j:9 + j], op=ALU.add), reads=[rpb, rCARRY], writes=[rCARRY])
                      cx.op(dve, lambda: V.tensor_tensor(out=CNEG[:, k0:k0 + 4], in0=pb[:, 0:4], in1=CARRY[:, k0:k0 + 4], op=ALU.add),
                            reads=[rpb, rCARRY], writes=[rCNEG])
                      cx.op(dve, lambda: V.tensor_scalar(out=NB[:, 0:k0 + 4], in0=CNEG[:, 0:k0 + 4], scalar1=CARRY[:, k0:k0 + 1], scalar2=None,
                                                         op0=ALU.subtract), reads=[rCNEG, rCARRY], writes=[rNB])
                  defer(8, nb_stage, ("nb", Tq))
                  chain = []


                  gidc = [0, 0]

                  def begin_group():
                      gidc[0] += 1
                      gidc[1] = 0

                  def add_item(smm, expf, maskf, pvf, after=None, pre=None):
                      chain.append([smm, expf, maskf, pvf, after, pre, gidc[0], gidc[1] == 0])
                      gidc[1] += 1

                  def outproj(st_, Tq=Tq):
                      for dc in range(2):
                          pb, rpb = st_next()
                          cx.op(pe, lambda dc=dc, pb=pb: T.matmul(pb[:], lhsT=CT0[:, st_ * 128:(st_ + 1) * 128], rhs=WOUT[:, 0, dc * 512:(dc + 1) * 512],
                                                                 start=True, stop=False), reads=[rCT0, rWOUT], writes=[rpb])
                          cx.op(pe, lambda dc=dc, pb=pb: T.matmul(pb[:], lhsT=CT1[:, st_ * 128:(st_ + 1) * 128], rhs=WOUT[:, 1, dc * 512:(dc + 1) * 512],
                                                                 start=False, stop=True), reads=[rCT1, rWOUT], writes=[rpb])
                          cx.op(dve, lambda dc=dc, pb=pb: V.tensor_copy(out=OUTS[:, dc * 512:(dc + 1) * 512], in_=pb[:]), reads=[rpb], writes=[rOUTS])
                      ch = Tq // TPC
                      r0 = (Tq % TPC) * 512 + st_ * 128
                      cx.dma(sp, sto, part_d[ch][r0:r0 + 128, :], OUTS[:], reads=[rOUTS, rPart[ch]])

                  def branch_out(ob_, rob_, br, qc, o_low, Y, rY):
                      so, oo = (64, 0) if o_low else (0, 64)
                      cx.op(dve, lambda: V.tensor_scalar(out=FN[so:so + 64, :], in0=ob_[so:so + 64, 0:256], scalar1=1e-30, scalar2=None, op0=ALU.max),
                            reads=[rob_], writes=[rFN])
                      cx.op(dve, lambda: V.reciprocal(out=FN[so:so + 64, :], in_=FN[so:so + 64, :]), reads=[rFN], writes=[rFN])
                      cx.op(dve, lambda: V.tensor_tensor(out=FN[so:so + 64, :].rearrange("p (h q) -> p h q", h=2),
                                                         in0=FN[so:so + 64, :].rearrange("p (h q) -> p h q", h=2),
                                                         in1=GZ[so:so + 64, br, :, qc:qc + 128], op=ALU.mult), reads=[rFN, rGZ], writes=[rFN])
                      cx.op(dve, lambda: V.tensor_tensor(out=Y, in0=ob_[oo:oo + 64, 0:256], in1=FN[so:so + 64, :], op=ALU.mult),
                            reads=[rob_, rFN], writes=[rY])

                  def cmp_group(bi):
                      begin_group()
                      i = 4 * Tq + bi
                      qc = 128 * bi
                      a_max = i // 16
                      r16 = i % 16
                      par = i % 2
                      ob_, rob_ = ob_next()
                      for a_ in range(a_max + 1):
                          def s_c(sb_, rsb, a_=a_):
                              cx.op(pe, lambda: T.matmul(sb_[:], lhsT=KCT[:, a_ * 128:(a_ + 1) * 128], rhs=QN[:, :, qc:qc + 128],
                                                         start=True, stop=True), reads=[rKCT, rQN], writes=[rsb])

                          def e_c(sb_, rsb, pt, rpt, a_=a_):
                              cx.op(act, lambda: A.activation(out=pt[:], in_=sb_[:], func=AF.Exp, scale=0.125), reads=[rsb], writes=[rpt])

                          def m_c(pt, rpt, a_=a_):
                              cx.op(dve, lambda: V.tensor_tensor(out=pt[:].rearrange("p (h q) -> p h q", h=4),
                                                                 in0=pt[:].rearrange("p (h q) -> p h q", h=4),
                                                                 in1=bcast_h(MASKC[:, r16 * 128:(r16 + 1) * 128], 4),
                                                                 op=ALU.mult), reads=[rpt, rC, rC2], writes=[rpt])

                          def p_c(pt, rpt, a_=a_):
                              cx.op(pe, lambda: T.matmul(ob_[:, 0:256], lhsT=VCA[:, a_, :], rhs=pt[:, 0:256], start=(a_ == 0), stop=(a_ == a_max)),
                                    reads=[rVCA, rpt], writes=[rob_])
                              for h in range(4):
                                  cx.op(pe, lambda h=h: T.matmul(IMPB[:, h * 128:(h + 1) * 128], lhsT=pt[:, h * 128:(h + 1) * 128], rhs=OV[:, a_, :],
                                                                 start=(a_ == 0 and h == 0), stop=(a_ == a_max), skip_group_check=True),
                                        reads=[rpt, rC, rC2], writes=[rIMPB])

                          def epi_cmp():
                              branch_out(ob_, rob_, 0, qc, True, YA[par], rYA[par])

                              def stage2():
                                  cx.op(dve, lambda: V.tensor_reduce(out=RS4[:], in_=IMPB[:].rearrange("p (h j) -> p h j", h=4), axis=AX.X, op=ALU.add),
                                        reads=[rIMPB], writes=[rRS4])
                                  cx.op(dve, lambda: V.tensor_scalar(out=RS4[:], in0=RS4[:], scalar1=1e-30, scalar2=None, op0=ALU.max), reads=[rRS4], writes=[rRS4])
                                  cx.op(dve, lambda: V.reciprocal(out=RI4[:], in_=RS4[:]), reads=[rRS4], writes=[rRI4])
                                  cx.op(dve, lambda: V.tensor_scalar(out=SC[:], in0=IMPB[:, 0:128], scalar1=RI4[:, 0:1], scalar2=None, op0=ALU.mult),
                                        reads=[rIMPB, rRI4], writes=[rSC])
                                  for h in range(1, 4):
                                      cx.op(dve, lambda h=h: V.scalar_tensor_tensor(out=SC[:], in0=IMPB[:, h * 128:(h + 1) * 128], scalar=RI4[:, h:h + 1], in1=SC[:],
                                                                                    op0=ALU.mult, op1=ALU.add), reads=[rIMPB, rRI4, rSC], writes=[rSC])

                              def stage3():
                                  cx.op(dve, lambda: V.memset(SC[:, 0:1], BIG), reads=[], writes=[rSC])
                                  lo0 = max(2 * i - 1, 0)
                                  cx.op(dve, lambda: V.memset(SC[0:64, lo0:2 * i + 1], BIG), writes=[rSC])
                                  cx.op(dve, lambda: V.memset(SC[64:128, 2 * i:2 * i + 2], BIG), writes=[rSC])
                                  cx.op(dve, lambda: V.max(out=M8[:], in_=SC[:]), reads=[rSC], writes=[rM8])
                                  cx.op(dve, lambda: V.match_replace(out=SC2[:], in_to_replace=M8[:], in_values=SC[:], imm_value=-BIG),
                                        reads=[rSC, rM8], writes=[rSC2])
                                  cx.op(dve, lambda: V.max(out=M8b[:], in_=SC2[:]), reads=[rSC2], writes=[rM8b])
                                  cx.op(dve, lambda: V.tensor_scalar(out=MNEG2[par][:], in0=SC[:], scalar1=M8b[:, 7:8], scalar2=-30000.0, op0=ALU.is_lt, op1=ALU.mult),
                                        reads=[rSC, rM8b], writes=[rMNEG2[par]])
                                  cx.op(pool, lambda: G.tensor_copy(out=QSA[par][0:64, :, :].rearrange("p a (h q) -> p a h q", h=2),
                                                                    in_=bass.AP(tensor=QN.tensor if hasattr(QN, "tensor") else QN[0:64, 0:2, qc:qc + 128].tensor,
                                                                                offset=QN[0:64, 0:2, qc:qc + 128].offset,
                                                                                ap=[list(QN[0:64, 0:2, qc:qc + 128].ap[0]), [0, 2]] + [list(a) for a in QN[0:64, 0:2, qc:qc + 128].ap[1:]])),
                                        reads=[rQN], writes=[rQSA[par][0], rQSA[par][1]])

                              def stage_b():
                                  mt, rmt = st_next()
                                  cx.op(pe, lambda: T.transpose(out=mt[:, 0:128], in_=MNEG2[par][:], identity=IDF[:]), reads=[rMNEG2[par], rC, rC2], writes=[rmt])
                                  for half in range(2):
                                      cx.op(dve, lambda half=half: V.tensor_copy(out=QSA[par][64:128, half, :].rearrange("p (h q) -> p h q", h=2),
                                                                                 in_=bcast_h(mt[64 * half:64 * half + 64, 0:128], 2)),
                                            reads=[rmt], writes=[rQSA[par][half]])
                              defer(6, stage2, ("c", i))
                              defer(12, stage3, ("c", i))
                              defer(28, stage_b, ("q", i))
                          add_item(s_c, e_c, m_c if a_ == a_max else None, p_c, epi_cmp if a_ == a_max else None,
                                   (lambda: run_deferred(pred=lambda t: t[0] in ("c", "kc"))) if a_ == 0 else None)

                  def win_group(bi):
                      begin_group()
                      i = 4 * Tq + bi
                      qc = 128 * bi
                      ob_, rob_ = ob_next()
                      kts = list(range(max(0, i - 4), i + 1))
                      for kt in kts:
                          def s_w(sb_, rsb, kt=kt):
                              wl = (kt % 8) * 128
                              cx.op(pe, lambda: T.matmul(sb_[:, 0:256], lhsT=KTW[:, wl:wl + 128], rhs=QN[:, 0:2, qc:qc + 128], start=True, stop=True),
                                    reads=[rKTW, rQN], writes=[rsb])

                          def e_w(sb_, rsb, pt, rpt):
                              cx.op(act, lambda: A.activation(out=pt[:, 0:256], in_=sb_[:, 0:256], func=AF.Exp, scale=0.125), reads=[rsb], writes=[rpt])
                          mk = None
                          if kt == i or kt == i - 4:
                              def mk(pt, rpt, kt=kt):
                                  msk = TRI if kt == i else TRIU
                                  cx.op(dve, lambda: V.tensor_tensor(out=pt[:, 0:256].rearrange("p (h q) -> p h q", h=2),
                                                                     in0=pt[:, 0:256].rearrange("p (h q) -> p h q", h=2),
                                                                     in1=bcast_h(msk[:], 2), op=ALU.mult), reads=[rpt, rC, rC2], writes=[rpt])

                          def p_w(pt, rpt, kt=kt):
                              cx.op(pe, lambda: T.matmul(ob_[:, 0:256], lhsT=VN[:, kt, 64:192], rhs=pt[:, 0:256], start=(kt == kts[0]), stop=(kt == kts[-1])),
                                    reads=[rVN, rpt], writes=[rob_])
                          add_item(s_w, e_w, mk, p_w, (lambda: branch_out(ob_, rob_, 2, qc, False, YB[:], rYB)) if kt == kts[-1] else None)

                  def sel_group(bi):
                      begin_group()
                      i = 4 * Tq + bi
                      qc = 128 * bi
                      par = i % 2
                      ob_, rob_ = ob_next()
                      for kt in range(i + 1):
                          def s_s(sb_, rsb, kt=kt):
                              half = kt // 32
                              cx.op(pe, lambda: T.matmul(sb_[:, 0:256], lhsT=KTS[:, kt * 128:(kt + 1) * 128], rhs=QSA[par][:, half, :], start=True, stop=True),
                                    reads=[rKTS, rQSA[par][half]], writes=[rsb])

                          def e_s(sb_, rsb, pt, rpt):
                              cx.op(act, lambda: A.activation(out=pt[:, 0:256], in_=sb_[:, 0:256], func=AF.Exp, scale=0.125), reads=[rsb], writes=[rpt])
                          mk = None
                          if kt == i:
                              def mk(pt, rpt):
                                  cx.op(dve, lambda: V.tensor_tensor(out=pt[:, 0:256].rearrange("p (h q) -> p h q", h=2),
                                                                     in0=pt[:, 0:256].rearrange("p (h q) -> p h q", h=2),
                                                                     in1=bcast_h(TRI[:], 2), op=ALU.mult), reads=[rpt, rC, rC2], writes=[rpt])

                          def p_s(pt, rpt, kt=kt):
                              cx.op(pe, lambda: T.matmul(ob_[:, 0:256], lhsT=VN[:, kt, 0:128], rhs=pt[:, 0:256], start=(kt == 0), stop=(kt == i)),
                                    reads=[rVN, rpt], writes=[rob_])

                          def epi_sel():
                              branch_out(ob_, rob_, 1, qc, True, YC[:], rYC)
                              cx.op(pool, lambda: G.tensor_tensor(out=YB[:], in0=YA[par], in1=YB[:], op=ALU.add), reads=[rYA[par], rYB], writes=[rYB])
                              for h in range(2):
                                  cx.op(pool, lambda h=h: G.tensor_tensor(out=CT1[64 * h:64 * h + 64, qc:qc + 128], in0=YB[:, 128 * h:128 * h + 128],
                                                                          in1=YC[:, 128 * h:128 * h + 128], op=ALU.add), reads=[rYB, rYC], writes=[rCT1])
                              defer(8, lambda: outproj(bi), ("o", i))
                          add_item(s_s, e_s, mk, p_s, epi_sel if kt == i else None, (lambda: run_deferred(upto_tag=("q", i))) if kt == 0 else None)

                  nkt = 4 * Tq + 4

                  def dense_group(kind):
                      begin_group()
                      ob_, rob_ = ob_next()
                      for kt in range(nkt):
                          jd = kt - 4 * Tq
                          cc = 128 * jd if jd >= 0 else 0
                          first, last = (kt == 0), (kt == nkt - 1)

                          def smm(sb_, rsb, kt=kt, cc=cc):
                              qsrc = (QF, QD0, QD1)[kind]
                              cx.op(pe, lambda: T.matmul(sb_[:, cc:512], lhsT=KTA[:, kt * 128:(kt + 1) * 128], rhs=qsrc[:, cc:512],
                                                         start=True, stop=True), reads=[rKTA, rQA], writes=[rsb])

                          def expf(sb_, rsb, pt, rpt, kt=kt, cc=cc):
                              if kind == 0:
                                  cx.op(act, lambda: A.activation(out=pt[:, cc:512], in_=sb_[:, cc:512], func=AF.Exp, bias=NB[:, kt:kt + 1], scale=0.125),
                                        reads=[rsb, rNB], writes=[rpt])
                              else:
                                  cx.op(act, lambda: A.activation(out=pt[:, cc:512], in_=sb_[:, cc:512], func=AF.Exp, scale=float(32 ** -0.5)),
                                        reads=[rsb], writes=[rpt])

                          def m_diag(pt, rpt, cc=cc):
                              cx.op(pool, lambda: G.tensor_tensor(out=pt[:, cc:cc + 128], in0=pt[:, cc:cc + 128], in1=TRI[:], op=ALU.mult),
                                    reads=[rpt, rC, rC2], writes=[rpt])

                          def pvf(pt, rpt, kt=kt, cc=cc, first=first, last=last):
                              lhs = VA[:, kt, 0:128] if kind == 0 else VA[:, kt, 64:192]
                              cx.op(pe, lambda: T.matmul(ob_[:, cc:512], lhsT=lhs, rhs=pt[:, cc:512], start=first, stop=last),
                                    reads=[rVA, rpt], writes=[rob_])

                          def epi():
                              if kind == 0:
                                  cx.op(dve, lambda: V.reciprocal(out=FA[64:128, :], in_=ob_[64:128, :]), reads=[rob_], writes=[rFA])
                                  cx.op(dve, lambda: V.tensor_tensor(out=FA[64:128, :], in0=FA[64:128, :], in1=Z1[64:128, :], op=ALU.mult),
                                        reads=[rFA, rZ1], writes=[rFA])
                                  cx.op(dve, lambda: V.tensor_tensor(out=CT0[0:64, :], in0=ob_[0:64, :], in1=FA[64:128, :], op=ALU.mult),
                                        reads=[rob_, rFA], writes=[rCT0])
                              elif kind == 1:
                                  cx.op(dve, lambda: V.reciprocal(out=FB[0:64, :], in_=ob_[0:64, :]), reads=[rob_], writes=[rFB])
                                  cx.op(dve, lambda: V.tensor_tensor(out=T0[0:64, :], in0=ob_[64:128, :], in1=FB[0:64, :], op=ALU.mult),
                                        reads=[rob_, rFB], writes=[rT0])
                              else:
                                  cx.op(dve, lambda: V.reciprocal(out=FB[0:64, :], in_=ob_[0:64, :]), reads=[rob_, rFB], writes=[rFB])
                                  cx.op(dve, lambda: V.tensor_scalar(out=FB[0:64, :], in0=FB[0:64, :], scalar1=NEGLAM[0:64, :], scalar2=None, op0=ALU.mult),
                                        reads=[rFB, rLAM], writes=[rFB])
                                  cx.op(dve, lambda: V.tensor_tensor(out=T1[0:64, :], in0=ob_[64:128, :], in1=FB[0:64, :], op=ALU.mult),
                                        reads=[rob_, rFB], writes=[rT1])
                                  cx.op(pool, lambda: G.tensor_tensor(out=T0[0:64, :], in0=T0[0:64, :], in1=T1[0:64, :], op=ALU.add),
                                        reads=[rT0, rT1], writes=[rT0])
                                  cx.op(pool, lambda: G.tensor_tensor(out=T1[0:64, :], in0=T0[0:64, :], in1=T0[0:64, :], op=ALU.mult),
                                        reads=[rT0, rT1], writes=[rT1])

                                  def stage_b():
                                      pb, rpb = st_next()
                                      cx.op(pe, lambda: T.matmul(pb[0:64, :], lhsT=ONESF[0:64, 0:64], rhs=T1[0:64, :], start=True, stop=True),
                                            reads=[rC, rC2, rT1], writes=[rpb])
                                      cx.op(act, lambda: A.activation(out=FB[0:64, :], in_=pb[0:64, :], func=AF.Ln, scale=1.0 / 64.0, bias=EPSC[0:64, :]),
                                            reads=[rpb, rC, rC2], writes=[rFB])
                                      cx.op(act, lambda: A.activation(out=FB[0:64, :], in_=FB[0:64, :], func=AF.Exp, scale=-0.5), reads=[rFB], writes=[rFB])
                                      cx.op(dve, lambda: V.scalar_tensor_tensor(out=T0[0:64, :], in0=T0[0:64, :], scalar=GCOL[0:64, :], in1=FB[0:64, :],
                                                                                op0=ALU.mult, op1=ALU.mult), reads=[rT0, rFB, rLAM], writes=[rT0])
                                      cx.op(pool, lambda: G.tensor_tensor(out=CT0[64:128, :], in0=T0[0:64, :], in1=Z1[0:64, :], op=ALU.mult),
                                            reads=[rT0, rZ1], writes=[rCT0])
                                  defer(14, stage_b, ("d", Tq))
                          add_item(smm, expf, m_diag if jd >= 0 else None, pvf, epi if last else None,
                                   (lambda: run_deferred(pred=lambda t: t[0] == "nb")) if (kind == 0 and kt == 0) else None)

                  if os.environ.get("DBG_ORDER") == "old":
                      dense_group(0)
                      dense_group(1)
                      dense_group(2)
                      for bi in range(4):
                          cmp_group(bi)
                          win_group(bi)
                          sel_group(bi)
                  else:
                      dense_group(1)
                      dense_group(2)
                      cmp_group(0)
                      dense_group(0)
                      for bi in range(4):
                          if bi + 1 < 4:
                              cmp_group(bi + 1)
                          win_group(bi)
                          sel_group(bi)

                  pend = []

                  def pop_pair():
                      X = pend.pop(0)
                      Y = pend.pop(0) if pend else None
                      order = [X] if Y is None else ([X, Y] if (X[5] and X[4] == Y[4]) else [Y, X])
                      for it in order:
                          it[0](it[1], it[2])
                      for it in ([X] if Y is None else [X, Y]):
                          if it[3] is not None:
                              it[3]()
                  ci = 0
                  while ci < len(chain):
                      pair = chain[ci:ci + 2]
                      ci += 2
                      for it in pair:
                          for d_ in deferred:
                              d_[0] -= 1
                      run_deferred()
                      for it in pair:
                          if it[5] is not None:
                              it[5]()
                      slots = []
                      for it in pair:
                          sb_, rsb = st_next()
                          pi = ptc[0] % NPT
                          ptc[0] += 1
                          slots.append((sb_, rsb, PT[pi], rPT[pi]))
                      for it, sl in reversed(list(zip(pair, slots))):
                          it[0](sl[0], sl[1])
                      for it, sl in reversed(list(zip(pair, slots))):
                          it[1](sl[0], sl[1], sl[2], sl[3])
                          if it[2] is not None:
                              it[2](sl[2], sl[3])
                      for it, sl in zip(pair, slots):
                          pend.append((it[3], sl[2], sl[3], it[4], it[6], it[7]))
                      if len(pend) > 2:
                          pop_pair()
                  while pend:
                      pop_pair()
                  rest_ = [d_ for d_ in deferred if d_[2][0] == "o"]
                  deferred[:] = [d_ for d_ in deferred if d_[2][0] != "o"]
                  run_deferred(everything=True)

                  def tile_tail(Tq=Tq, rest_=rest_):
                      for d_ in rest_:
                          d_[1]()
                      if Tq % TPC == TPC - 1:
                          ch = Tq // TPC
                          cx._need(pool, [], [rPart[ch], rRSd[ch]])
                          ins = G.collective_compute("ReduceScatter", ALU.add, replica_groups=RG, ins=[part_d[ch].opt()], outs=[rs_d[ch].opt()])
                          rsc[ch].cnt += 1
                          ins.then_inc(rsc[ch].sem)
                          rPart[ch].w = (rsc[ch], rsc[ch].cnt); rPart[ch].r = []
                          rRSd[ch].w = (rsc[ch], rsc[ch].cnt); rRSd[ch].r = []
                  if Tq + 1 < NT:
                      carry.append(tile_tail)
                  else:
                      tile_tail()

              if L + 1 < depth:
                  load_weights(L + 1)
              if ALIAS:
                  rINB = cx.fork(rVA, 1) + cx.fork(rVN, 1)
                  rACC0, rACC1, rYT0, rYT1 = cx.fork(rKTA, 4)
                  rACCB = [rACC0, rACC1]
                  rYTB2 = [rYT0, rYT1]
                  (rGB,) = cx.fork(rXB, 1)
              cx.dma(sp, ldg, GTB, lng_d[L], writes=[rGB])
              cx.dma(sp, ldg, BTB, lnb_d[L], writes=[rGB])
              rGB.w = (ldg, ldg.cnt)
              xres_d = xq_d if L == 0 else yq_d
              ntb = SQ // 128
              last = (L == depth - 1)

              cx._need(sp, [], [rYQ])

              cx._need(act, [], [rYQ])

              def loadb(i):
                  k = i % 2
                  r0 = i * 128
                  ch, jj = i // TPC, i % TPC
                  cx.dma(sp, ldb[k], INB[k][:, 0, :], rs_d[ch][jj * 128:(jj + 1) * 128, :], reads=[rRSd[ch]], writes=[rINB[k]])
                  cx.dma(sp, ldb[k], INB[k][:, 4, :], xres_d[r0:r0 + 128, :], writes=[rINB[k]])
                  rINB[k].w = (ldb[k], ldb[k].cnt)

              loadb(0)
              for i in range(ntb):
                  k = i % 2
                  if i + 1 < ntb:
                      loadb(i + 1)
                  I_, Ac, rI, rAc = INB[k], ACCB[k], rINB[k], rACCB[k]
                  cx.op(dve, lambda: V.scalar_tensor_tensor(out=Ac, in0=I_[:, 4, :], scalar=float(ALPHA), in1=I_[:, 0, :], op0=ALU.mult, op1=ALU.add),
                        reads=[rI], writes=[rAc])
                  for h in range(2):
                      cx.op(dve, lambda h=h: V.bn_stats(out=STT[:, h, :], in_=Ac[:, h * 512:(h + 1) * 512]), reads=[rAc], writes=[rSTT])
                  cx.op(dve, lambda: V.bn_aggr(out=MV[:, 0:2], in_=STT[:].rearrange("p a b -> p (a b)")), reads=[rSTT], writes=[rMV])
                  cx.op(dve, lambda: V.tensor_scalar(out=MV[:, 2:3], in0=MV[:, 1:2], scalar1=1e-5, scalar2=None, op0=ALU.add), reads=[rMV], writes=[rMV])
                  cx.op(act, lambda: A.activation(out=MV[:, 2:3], in_=MV[:, 2:3], func=AF.Ln), reads=[rMV], writes=[rMV])
                  cx.op(act, lambda: A.activation(out=MV[:, 3:4], in_=MV[:, 2:3], func=AF.Exp, scale=-0.5), reads=[rMV], writes=[rMV])
                  cx.op(dve, lambda: V.tensor_scalar(out=Ac, in0=Ac, scalar1=MV[:, 0:1], scalar2=MV[:, 3:4], op0=ALU.subtract, op1=ALU.mult),
                        reads=[rAc, rMV], writes=[rAc])
                  cx.op(dve, lambda: V.tensor_tensor(out=Ac, in0=Ac, in1=GTB, op=ALU.mult), reads=[rAc, rGB], writes=[rAc])
                  cx.op(dve, lambda: V.tensor_tensor(out=Ac, in0=Ac, in1=BTB, op=ALU.add), reads=[rAc, rGB], writes=[rAc])
                  if last:
                      cx.dma(act, sta[k], y_d[i * 128:(i + 1) * 128, :], Ac, reads=[rAc])
                  else:
                      cx.dma(act, sta[k], yq_d[i * 128:(i + 1) * 128, :], Ac, reads=[rAc, rYQ])
                      for hb in range(2):
                          pb, rpb = pj()
                          for c4 in range(4):
                              c = 4 * hb + c4
                              cx.op(pe, lambda c=c, c4=c4, pb=pb: T.transpose(out=pb[:, c4 * 128:(c4 + 1) * 128], in_=Ac[:, c * 128:(c + 1) * 128], identity=IDF[:]),
                                    reads=[rAc, rC], writes=[rpb])
                          cx.op(act if hb == 0 else dve, lambda hb=hb, pb=pb: (A.copy if hb == 0 else V.tensor_copy)(out=YTB2[k][:, 4 * hb:4 * hb + 4, :], in_=pb[:].rearrange("p (c t) -> p c t", c=4)),
                                reads=[rpb], writes=[rYTB2[k]])
                      i2 = i // 2
                      for c in range(8):
                          cx.dma(act, styt2[k], ytq_d[i2][c * 128:(c + 1) * 128, (i % 2) * 128:(i % 2 + 1) * 128], YTB2[k][:, c, :],
                                 reads=[rYTB2[k], rYTQ[i2]])
                      if i % 2 == 1:
                          cx._need(pool, [], [rYTQ[i2], rXT1[i2]])
                          ins = G.collective_compute("AllGather", ALU.bypass, replica_groups=RG, ins=[ytq_d[i2].opt()], outs=[xT1_d[i2].opt()])
                          ccs.cnt += 1
                          ins.then_inc(ccs.sem)
                          rYTQ[i2].w = (ccs, ccs.cnt); rYTQ[i2].r = []
                          rXT1[i2].w = (ccs, ccs.cnt); rXT1[i2].r = []
              if ALIAS:
                  cx.join(rVA, rINB[0:1]); cx.join(rVN, rINB[1:2]); cx.join(rKTA, [rACC0, rACC1, rYT0, rYT1]); cx.join(rXB, [rGB])
        cx.wait_all(sp, sta)
    return nc


def _consts(S):
    half = 32
    t = np.arange(S, dtype=np.float32)
    tab = np.zeros((128, 4, S), np.float32)
    inv = (10000.0 ** (-(np.arange(32, dtype=np.float32) * 2.0 / 64))).astype(np.float32)
    ang = t[None, :] * inv[:, None]
    cosn, sinn = np.cos(ang), np.sin(ang)
    for r in range(128):
        i = r % 64
        tab[r, 0] = cosn[i % 32]
        tab[r, 1] = -sinn[i % 32] if i < 32 else sinn[i % 32]
    invd = (10000.0 ** (-(np.arange(16, dtype=np.float32) * 2.0 / 32))).astype(np.float32)
    angd = t[None, :] * invd[:, None]
    cosd, sind = np.cos(angd), np.sin(angd)
    for r in range(64):
        w = r % 32
        tab[r, 2] = cosd[w % 16]
        tab[r, 3] = -sind[w % 16] if w < 16 else sind[w % 16]
    key = np.arange(S)
    ind = (((key[None, :] // 64) % 64) == np.arange(64)[:, None]).astype(np.float32)
    p = np.arange(128)
    c_tri = (p[:, None] <= p[None, :]).astype(np.float32)
    c_triu = (p[:, None] > p[None, :]).astype(np.float32)
    c_ident = np.eye(128, dtype=np.float32)
    u = np.arange(2048)
    c_maskc = ((16 * p[:, None] + 31) <= u[None, :]).astype(np.float32)
    n = np.arange(512)
    j = np.arange(128)
    ov = np.clip(np.minimum(16 * n[:, None] + 32, 64 * j[None, :] + 64) - np.maximum(16 * n[:, None], 64 * j[None, :]), 0, None)
    ov = (ov.astype(np.float32) / 32.0)
    ov[511] = 0.0
    c_ov = np.ascontiguousarray(ov.reshape(4, 128, 128).transpose(1, 0, 2))
    return dict(tab=tab, ind=ind, c_tri=c_tri, c_triu=c_triu, c_ident=c_ident, c_maskc=c_maskc, c_ov=c_ov)


def _core_cols(c):
    g, hp = c // 2, c % 2
    rng64 = np.arange(64)
    sw64 = (rng64 + 32) % 64
    d32 = np.arange(32)
    swd = np.concatenate([(d32 + 16) % 32, 32 + (d32 + 16) % 32])
    dq = OFF['diff_q'] + 64 * c + rng64
    dk = OFF['diff_k'] + 64 * c + rng64
    dq_sw = OFF['diff_q'] + 64 * c + swd
    dk_sw = OFF['diff_k'] + 64 * c + swd
    fq = OFF['fox_q'] + 64 * c + rng64
    fk = OFF['fox_k'] + 64 * c + rng64
    my = [4 * g + 2 * hp, 4 * g + 2 * hp + 1]
    oth = [4 * g + 2 * (1 - hp), 4 * g + 2 * (1 - hp) + 1]
    nq = lambda H, perm: OFF['nsa_q'] + 64 * H + perm
    kc = lambda perm: OFF['nsa_k_cmp'] + 64 * g + perm
    ks = lambda perm: OFF['nsa_k_sel'] + 64 * g + perm
    kw = lambda perm: OFF['nsa_k_win'] + 64 * g + perm
    vcmp = OFF['nsa_v_cmp'] + 64 * g + rng64
    gate = lambda H, br: np.full(64, OFF['nsa_gate'] + 3 * H + br)
    groups = [
        np.concatenate([dq, fq]), np.concatenate([dk, fk]), np.concatenate([dq_sw, dk_sw]),
        np.concatenate([nq(my[0], rng64), nq(my[1], rng64)]), np.concatenate([nq(my[0], sw64), nq(my[1], sw64)]),
        np.concatenate([nq(oth[0], rng64), nq(oth[1], rng64)]), np.concatenate([nq(oth[0], sw64), nq(oth[1], sw64)]),
        np.concatenate([kc(rng64), ks(rng64)]), np.concatenate([kc(sw64), ks(sw64)]),
        np.concatenate([kw(rng64), vcmp]), np.concatenate([kw(sw64), vcmp]),
        np.concatenate([OFF['diff_z'] + 64 * c + rng64, OFF['fox_z'] + 64 * c + rng64]),
        np.concatenate([OFF['nsa_z'] + 64 * my[0] + rng64, OFF['nsa_z'] + 64 * my[1] + rng64]),
        np.concatenate([gate(my[0], 0), gate(my[1], 0)]), np.concatenate([gate(my[0], 1), gate(my[1], 1)]),
        np.concatenate([gate(my[0], 2), gate(my[1], 2)]),
    ]
    fm = np.concatenate(groups)
    tm = np.concatenate([OFF['fox_v'] + 64 * c + rng64, OFF['diff_v'] + 64 * c + rng64,
                         OFF['nsa_v_sel'] + 64 * g + rng64, OFF['nsa_v_win'] + 64 * g + rng64,
                         np.array([OFF['fox_f'] + c])])
    wo = np.concatenate([64 * c + rng64, 256 + 512 + 64 * c + rng64, 256 + 64 * my[0] + rng64, 256 + 64 * my[1] + rng64])
    return fm, tm, wo


_PROG = {}


def _get(S, depth):
    k = (S, depth)
    if k not in _PROG:
        _PROG[k] = build_fused(S, depth)
    return _PROG[k]


def kernel(x, w_in, b_fox_f, cmp_pos_k, cmp_pos_v, cmp_w1_k, cmp_w2_k, cmp_w1_v, cmp_w2_v,
           lam_q1, lam_k1, lam_q2, lam_k2, diff_subln_g, w_out, ln_g, ln_b):
    f32 = lambda a: np.ascontiguousarray(np.asarray(a, dtype=np.float32))
    x = f32(x)
    B, S, D = x.shape
    depth = w_in.shape[0]
    SQ = S // 4
    w_in, w_out = f32(w_in), f32(w_out)
    cst = _consts(S)
    cols = [_core_cols(c) for c in range(4)]
    w1 = np.ascontiguousarray(np.stack([f32(cmp_w1_k), f32(cmp_w1_v)], axis=1))
    w2 = np.ascontiguousarray(np.concatenate([f32(cmp_w2_k), f32(cmp_w2_v)], axis=2))
    post = np.ascontiguousarray(np.concatenate([f32(cmp_pos_k).transpose(0, 2, 1), f32(cmp_pos_v).transpose(0, 2, 1)], axis=1))
    lng = np.ascontiguousarray(np.broadcast_to(f32(ln_g)[:, None, :], (depth, 128, D)))
    lnb = np.ascontiguousarray(np.broadcast_to(f32(ln_b)[:, None, :], (depth, 128, D)))
    NT = S // 512
    TPC = 1

    def tokmap(r):
        idx = []
        for i in range(SQ // 128):
            base = (i // TPC) * TPC * 512 + r * TPC * 128 + (i % TPC) * 128
            idx.append(np.arange(base, base + 128))
        return np.concatenate(idx)
    tmaps = [tokmap(r) for r in range(4)]
    xT = [np.ascontiguousarray(x[b].T) for b in range(B)]
    in_maps = []
    for core in range(8):
        b, c = core // 4, core % 4
        fm, tm, wo = cols[c]
        vec = np.zeros((depth, 128, 8), np.float32)
        for l in range(depth):
            lam_init = 0.8 - 0.6 * math.exp(-0.3 * l)
            vec[l, :, 0] = f32(b_fox_f)[l, c]
            vec[l, :, 1] = f32(diff_subln_g)[l][np.arange(128) % 64]
            vec[l, 0:32, 2] = f32(lam_q1)[l]; vec[l, 0:32, 3] = f32(lam_k1)[l]
            vec[l, 0:32, 4] = f32(lam_q2)[l]; vec[l, 0:32, 5] = f32(lam_k2)[l]
            vec[l, :, 6] = -lam_init
            vec[l, :, 7] = 1.0 - lam_init
        m = dict(xT=xT[b], xq=np.ascontiguousarray(x[b, tmaps[c]]),
                 wfm=np.ascontiguousarray(w_in[:, :, fm]), wtm=np.ascontiguousarray(w_in[:, :, tm]),
                 wout=np.ascontiguousarray(w_out[:, wo, :]), w1=w1, w2=w2, post=post, vecs=vec, lng=lng, lnb=lnb)
        m.update(cst)
        in_maps.append(m)
    res = run_bass_kernel_spmd(_get(S, depth), in_maps, core_ids=list(range(8)))
    out = np.empty((B, S, D), np.float32)
    for core in range(8):
        b, c = core // 4, core % 4
        out[b, tmaps[c]] = res.results[core]["y"]
    return out
```

```python
import math
import os
from contextlib import ExitStack
import numpy as np
import concourse.bass as bass
import concourse.mybir as mybir
from concourse.bass_utils import run_bass_kernel_spmd

F32 = mybir.dt.float32
BF16 = mybir.dt.bfloat16
ALU = mybir.AluOpType
AF = mybir.ActivationFunctionType
AX = mybir.AxisListType

D_MODEL = 1024
DEPTH = 2
HD = 64
IN_SPLITS = (('fox_q', 256), ('fox_k', 256), ('fox_v', 256), ('fox_f', 4), ('fox_z', 256), ('nsa_q', 512),
             ('nsa_k_cmp', 128), ('nsa_v_cmp', 128), ('nsa_k_sel', 128), ('nsa_v_sel', 128),
             ('nsa_k_win', 128), ('nsa_v_win', 128), ('nsa_gate', 24), ('nsa_z', 512),
             ('diff_q', 256), ('diff_k', 256), ('diff_v', 256), ('diff_z', 256))
OFF = {}
_a = 0
for _n, _w in IN_SPLITS:
    OFF[_n] = _a
    _a += _w
IN_WIDTH = _a
ALPHA = (2 * DEPTH) ** 0.25
NG = 16
BIG = 1.0e30


class Res:
    __slots__ = ("w", "r", "excl")

    def __init__(self, excl=False):
        self.excl = excl
        self.w = None
        self.r = []


class SemC:
    __slots__ = ("sem", "cnt", "key")
    _n = 0

    def __init__(self, sem):
        self.sem = sem
        self.cnt = 0
        SemC._n += 1
        self.key = SemC._n


class Eng:
    def __init__(self, h, semc, same_sync):
        self.h = h
        self.s = semc
        self.waited = {}
        self.same_sync = same_sync


class Ctx:
    def __init__(self, nc, stack):
        self.nc = nc
        self.stack = stack
        mk = lambda n: SemC(stack.enter_context(nc.semaphore(n)))
        self.pe = Eng(nc.tensor, mk("s_pe"), False)
        self.dve = Eng(nc.vector, mk("s_dve"), True)
        self.act = Eng(nc.scalar, mk("s_act"), True)
        self.pool = Eng(nc.gpsimd, mk("s_pool"), True)
        self.sp = Eng(nc.sync, mk("s_sp"), False)

    def res(self, excl=False):
        return Res(excl)

    def dsem(self, name):
        return SemC(self.stack.enter_context(self.nc.semaphore(name)))

    def sbuf(self, name, shape, dt):
        return self.stack.enter_context(self.nc.sbuf_tensor(name, list(shape), dt))

    def psum(self, name, shape, dt):
        return self.stack.enter_context(self.nc.psum_tensor(name, list(shape), dt))

    def _need(self, eng, reads, writes):
        need = {}

        def add(p):
            if p is None:
                return
            s, v = p
            if s is eng.s and not eng.same_sync:
                return
            if need.get(s.key, (None, -1))[1] < v:
                need[s.key] = (s, v)
        for r in reads:
            add(r.w)
            if r.excl:
                for p in r.r:
                    if p[0] is not eng.s:
                        add(p)
        for w in writes:
            add(w.w)
            for p in w.r:
                add(p)
        for k, (s, v) in need.items():
            if eng.waited.get(k, 0) < v:
                eng.h.wait_ge(s.sem, v)
                eng.waited[k] = v

    def op(self, eng, fn, reads=(), writes=()):
        self._need(eng, reads, writes)
        ins = fn()
        eng.s.cnt += 1
        ins.then_inc(eng.s.sem, 1)
        me = (eng.s, eng.s.cnt)
        for r in reads:
            r.r.append(me)
            if len(r.r) > 24:
                r.r = r.r[-24:] if False else _compact(r.r)
        for w in writes:
            w.w = me
            w.r = []
        return ins

    def dma(self, q, dsem, out, in_, reads=(), writes=()):
        self._need(q, reads, writes)
        ins = q.h.dma_start(out=out, in_=in_)
        dsem.cnt += 16
        ins.then_inc(dsem.sem, 16)
        me = (dsem, dsem.cnt)
        for r in reads:
            r.r.append(me)
        for w in writes:
            w.w = me
            w.r = []
        return ins

    def fork(self, parent, n):
        kids = [Res() for _ in range(n)]
        for k in kids:
            k.r = _compact(list(parent.r) + ([parent.w] if parent.w else []))
        return kids

    def join(self, parent, kids):
        for k in kids:
            parent.r = _compact(parent.r + k.r + ([k.w] if k.w else []))

    def wait_all(self, eng, semcs):
        for s in semcs:
            if s.cnt > 0 and eng.waited.get(s.key, 0) < s.cnt:
                eng.h.wait_ge(s.sem, s.cnt)
                eng.waited[s.key] = s.cnt


def _compact(lst):
    best = {}
    for s, v in lst:
        if best.get(s.key, (None, -1))[1] < v:
            best[s.key] = (s, v)
    return list(best.values())


def build_fused(S, depth=DEPTH):
    NT = S // 512
    NK = S // 128
    SQ = S // 4
    nc = bass.Bass("TRN2", target_bir_lowering=False)
    din = lambda n, sh, dt=F32: nc.dram_tensor(n, list(sh), dt, kind="ExternalInput").ap()
    xT0_d = din("xT", [1024, S])
    xq_d = din("xq", [SQ, 1024])
    wfm_d = din("wfm", [depth, 1024, NG * 128])
    wtm_d = din("wtm", [depth, 1024, 257])
    wout_d = din("wout", [depth, 256, 1024])
    w1_d = din("w1", [depth, 2, 2048, 128])
    w2_d = din("w2", [depth, 128, 128])
    post_d = din("post", [depth, 128, 32])
    vec_d = din("vecs", [depth, 128, 8])
    lng_d = din("lng", [depth, 128, 1024])
    lnb_d = din("lnb", [depth, 128, 1024])
    tab_d = din("tab", [128, 4, S])
    ind_d = din("ind", [64, S])
    ctri_d = din("c_tri", [128, 128])
    ctriu_d = din("c_triu", [128, 128])
    cid_d = din("c_ident", [128, 128])
    cmk_d = din("c_maskc", [128, 2048])
    cov_d = din("c_ov", [128, 4, 128])
    y_d = nc.dram_tensor("y", [SQ, 1024], F32, kind="ExternalOutput").ap()
    TPC = 1
    NCH = NT // TPC
    part_d = [nc.dram_tensor(f"part_i{c}", [TPC * 512, 1024], F32).ap() for c in range(NCH)]
    rs_d = [nc.dram_tensor(f"rs_i{c}", [TPC * 128, 1024], F32).ap() for c in range(NCH)]
    NTB = SQ // 128
    ytq_d = [nc.dram_tensor(f"ytq_i{i}", [1024, 256], BF16).ap() for i in range(NTB // 2)]
    xT1_d = [nc.dram_tensor(f"xT1_i{i}", [4 * 1024, 256], BF16).ap() for i in range(NTB // 2)]
    yq_d = nc.dram_tensor("yq_i", [SQ, 1024], F32).ap()
    RG = [[0, 1, 2, 3], [4, 5, 6, 7]]

    with ExitStack() as st:
        cx = Ctx(nc, st)
        pe, dve, act, pool, sp = cx.pe, cx.dve, cx.act, cx.pool, cx.sp
        V, G, T, A = nc.vector, nc.gpsimd, nc.tensor, nc.scalar
        sb = cx.sbuf
        WFM = sb("WFM", [128, 8, NG * 128], BF16); rWFM = cx.res()
        WTM = sb("WTM", [128, 8, 257], BF16); rWTM = cx.res()
        WOUT = sb("WOUT", [128, 2, 1024], BF16); rWOUT = cx.res()
        W1 = sb("W1", [128, 32, 128], BF16); rW1 = cx.res()
        W2 = sb("W2", [128, 128], BF16); rW2 = cx.res()
        POST = sb("POST", [128, 32], BF16); rPOST = cx.res()
        BH = sb("BH", [128, 2], F32); rBH = cx.res()
        KTA = sb("KTA", [128, S], BF16); rKTA = cx.res()
        KTS = sb("KTS", [128, S], BF16); rKTS = cx.res()
        KTW = sb("KTW", [128, 1024], BF16); rKTW = cx.res()
        VA = sb("VA", [128, NK, 192], BF16); rVA = cx.res()
        VN = sb("VN", [128, NK, 192], BF16); rVN = cx.res()
        VCA = sb("VCA", [128, 4, 128], BF16); rVCA = cx.res()
        KCT = sb("KCT", [128, 512], BF16); rKCT = cx.res()
        VCTF = sb("VCTF", [64, 512], F32); rVCTF = cx.res()
        CMPIN = sb("CMPIN", [128, 528], BF16); rCMPIN = cx.res()
        XB = sb("XB", [128, 8, 512], BF16); rXB = cx.res()
        TAB = sb("TAB", [128, 4, 512], F32); rTAB = cx.res()
        TRI = sb("TRI", [128, 128], BF16); TRIU = sb("TRIU", [128, 128], BF16)
        MASKC = sb("MASKC", [128, 2048], BF16); OV = sb("OV", [128, 4, 128], BF16)
        IDF = sb("IDF", [128, 128], F32); TRIF = sb("TRIF", [128, 128], F32); ONESF = sb("ONESF", [128, 128], F32)
        VEC = sb("VEC", [128, 8], F32)
        rC = cx.res()
        rC2 = cx.res()
        QF = sb("QF", [128, 512], BF16); QD0 = sb("QD0", [128, 512], BF16); QD1 = sb("QD1", [128, 512], BF16); rQA = cx.res()
        QN = sb("QN", [128, 4, 512], BF16); rQN = cx.res()
        QSA = [sb(f"QSA{i}", [128, 2, 256], BF16) for i in range(2)]; rQSA = [[cx.res(), cx.res()] for _ in range(2)]
        T0 = sb("T0", [128, 512], F32); rT0 = cx.res()
        T1 = sb("T1", [128, 512], F32); rT1 = cx.res()
        Z1 = sb("Z1", [128, 512], F32); rZ1 = cx.res()
        GZ = sb("GZ", [128, 3, 2, 512], BF16); rGZ = cx.res()
        NPT = 5
        ptc = [0]
        PT = [sb(f"PT{i}", [128, 512], BF16) for i in range(NPT)]; rPT = [cx.res() for _ in range(NPT)]
        FA = sb("FA", [128, 512], F32); rFA = cx.res()
        FB = sb("FB", [128, 512], F32); rFB = cx.res()
        CT0 = sb("CT0", [128, 512], BF16); rCT0 = cx.res()
        CT1 = sb("CT1", [128, 512], BF16); rCT1 = cx.res()
        OUTS = sb("OUTS", [128, 1024], F32); rOUTS = cx.res()
        CNEG = sb("CNEG", [128, NK], F32); rCNEG = cx.res()
        CARRY = sb("CARRY", [128, NK + 4], F32); rCARRY = cx.res()
        SPL = sb("SPL", [128, NK], F32); rSPL = cx.res()
        NB = sb("NB", [128, NK], F32); rNB = cx.res()
        LFRAW = sb("LFRAW", [128, NK], F32); rLFRAW = cx.res()
        SC = sb("SC", [128, 128], F32); rSC = cx.res()
        SC2 = sb("SC2", [128, 128], F32); rSC2 = cx.res()
        M8 = sb("M8", [128, 8], F32); M8b = sb("M8b", [128, 8], F32); rM8 = cx.res(); rM8b = cx.res()
        MNEG2 = [sb(f"MNEG{i}", [128, 128], F32) for i in range(2)]; rMNEG2 = [cx.res(), cx.res()]
        RS4 = sb("RS4", [128, 4], F32); rRS4 = cx.res()
        RI4 = sb("RI4", [128, 4], F32); rRI4 = cx.res()
        YA = [sb(f"YA{i}", [64, 256], F32)[:] for i in range(2)]; rYA = [cx.res(), cx.res()]
        YB = sb("YB", [64, 256], F32); rYB = cx.res()
        YC = sb("YC", [64, 256], F32); rYC = cx.res()
        FN = sb("FN", [128, 256], F32); rFN = cx.res()
        HS = sb("HS", [128, 64], BF16); rHS = cx.res()
        HSS = sb("HSS", [128, 64], F32); rHSS = cx.res()
        LAM = sb("LAM", [128, 4], F32); rLAM = cx.res()
        IMPB = cx.psum("IMPB", [128, 512], F32); rIMPB = cx.res(True)
        STB = [cx.psum(f"ST{i}", [128, 512], F32) for i in range(4)]; rST = [cx.res(True) for _ in range(4)]
        OB = [cx.psum(f"OB{i}", [128, 512], F32) for i in range(3)]; rOB = [cx.res(True) for _ in range(3)]
        ALLB = [(IMPB, rIMPB)] + list(zip(STB, rST)) + list(zip(OB, rOB))
        ld = cx.dsem("ld_const"); ldx = cx.dsem("ld_x"); ldt = cx.dsem("ld_tab"); sto = cx.dsem("st_out")
        pjc = [0]

        def pjx():
            i = pjc[0] % 8
            pjc[0] += 1
            return ALLB[i]

        pj = pjx

        ldw = cx.dsem("ld_w"); ccs = cx.dsem("cc_sem"); ldb = [cx.dsem("ld_b0"), cx.dsem("ld_b1")]; sta = [cx.dsem("st_a0"), cx.dsem("st_a1")]; ldg = cx.dsem("ld_g")
        rYQ = cx.res()
        rYTQ = [cx.res() for _ in range(NTB // 2)]; rXT1 = [cx.res() for _ in range(NTB // 2)]
        rPart = [cx.res() for _ in range(NCH)]; rRSd = [cx.res() for _ in range(NCH)]
        styt2 = [cx.dsem("st_yt0"), cx.dsem("st_yt1")]
        rsc = [cx.dsem(f"rs_sem{c}") for c in range(NCH)]
        cx.dma(pool, ld, KTS[64:128, :], ind_d, writes=[rKTS])
        cx.dma(pool, ld, TRI[:], ctri_d, writes=[rC])
        cx.dma(pool, ld, TRIU[:], ctriu_d, writes=[rC])
        cx.dma(pool, ld, MASKC[:], cmk_d, writes=[rC])
        cx.dma(pool, ld, OV[:], cov_d, writes=[rC])
        cx.dma(sp, ld, IDF[:], cid_d, writes=[rC])
        cx.dma(sp, ld, TRIF[:], ctri_d, writes=[rC])
        EPSC = sb("EPSC", [128, 1], F32)
        for r_ in (rKTS, rC):
            r_.w = (ld, ld.cnt)
        cx.op(dve, lambda: V.memset(ONESF[:], 1.0), writes=[rC2])
        IDB = sb("IDB", [128, 128], BF16); TRIN = sb("TRIN", [128, 128], BF16); TRIUN = sb("TRIUN", [128, 128], BF16)
        cx.op(dve, lambda: V.tensor_copy(out=IDB[:], in_=IDF[:]), reads=[rC], writes=[rC2])
        cx.op(dve, lambda: V.tensor_scalar(out=TRIN[:], in0=TRI[:], scalar1=-1.0, scalar2=30000.0, op0=ALU.add, op1=ALU.mult), reads=[rC], writes=[rC2])
        cx.op(dve, lambda: V.tensor_scalar(out=TRIUN[:], in0=TRIU[:], scalar1=-1.0, scalar2=30000.0, op0=ALU.add, op1=ALU.mult), reads=[rC], writes=[rC2])
        cx.op(dve, lambda: V.tensor_scalar(out=MASKC[:], in0=MASKC[:], scalar1=-1.0, scalar2=30000.0, op0=ALU.add, op1=ALU.mult), reads=[rC], writes=[rC])
        cx.op(dve, lambda: V.memset(EPSC[:], 1e-5), writes=[rC2])
        rVEC = cx.res()
        if NK * 96 >= 5120:
            INB = [VA[:].rearrange("p a b -> p (a b)").bitcast(F32)[:, 0:5120].rearrange("p (j n) -> p j n", j=5),
                   VN[:].rearrange("p a b -> p (a b)").bitcast(F32)[:, 0:5120].rearrange("p (j n) -> p j n", j=5)]
            kf = KTA[:].bitcast(F32)
            ACCB = [kf[:, 0:1024], kf[:, 1024:2048]]
            YTB2 = [KTA[:, 4096:5120].rearrange("p (c t) -> p c t", c=8), KTA[:, 5120:6144].rearrange("p (c t) -> p c t", c=8)]
            xf = XB[:].rearrange("p a b -> p (a b)").bitcast(F32)
            GTB, BTB = xf[:, 0:1024], xf[:, 1024:2048]
            ALIAS = True
        else:
            ALIAS = False
            INB = [sb(f"INB{i}", [128, 5, 1024], F32)[:] for i in range(2)]; rINB = [cx.res(), cx.res()]
            ACCB = [sb(f"ACCB{i}", [128, 1024], F32)[:] for i in range(2)]; rACCB = [cx.res(), cx.res()]
            YTB2 = [sb(f"YTB{i}", [128, 8, 128], BF16)[:] for i in range(2)]; rYTB2 = [cx.res(), cx.res()]
            GTB = sb("GTB", [128, 1024], F32)[:]; BTB = sb("BTB", [128, 1024], F32)[:]; rGB = cx.res()
        STT = sb("STT", [128, 2, 6], F32); rSTT = cx.res()
        MV = sb("MV", [128, 4], F32); rMV = cx.res()
        def load_weights(L):
            cx.dma(pool, ldw, WFM[:], wfm_d[L].rearrange("(c p) n -> p c n", p=128), writes=[rWFM])
            cx.dma(pool, ldw, WTM[:], wtm_d[L].rearrange("(c p) n -> p c n", p=128), writes=[rWTM])
            cx.dma(pool, ldw, WOUT[:], wout_d[L].rearrange("(c p) n -> p c n", p=128), writes=[rWOUT])
            cx.dma(pool, ldw, W1[0:64, :, :], w1_d[L, 0].rearrange("(l d) h -> d l h", d=64), writes=[rW1])
            cx.dma(pool, ldw, W1[64:128, :, :], w1_d[L, 1].rearrange("(l d) h -> d l h", d=64), writes=[rW1])
            cx.dma(pool, ldw, W2[:], w2_d[L], writes=[rW2])
            cx.dma(pool, ldw, POST[:], post_d[L], writes=[rPOST])
            cx.dma(pool, ldw, VEC[:], vec_d[L], writes=[rVEC])
            for r_ in (rWFM, rWTM, rWOUT, rW1, rW2, rPOST, rVEC):
                r_.w = (ldw, ldw.cnt)

        load_weights(0)
        for L in range(depth):
          if True:
              cx.op(dve, lambda: V.memset(VA[:, :, 64:128], 1.0), writes=[rVA])
              cx.op(pool, lambda: G.memset(VN[:, :, 64:128], 1.0), writes=[rVN])
              cx.op(dve, lambda: V.memset(VCA[:, :, 0:64], 0.0), writes=[rVCA])
              cx.op(dve, lambda: V.memset(VCA[:, :, 64:128], 1.0), writes=[rVCA])
              cx.op(dve, lambda: V.memset(KCT[:], 0.0), writes=[rKCT])
              cx.op(pool, lambda: G.memset(KTW[:], 0.0), writes=[rKTW])
              cx.op(pool, lambda: G.memset(QN[64:128, :, :], 0.0), writes=[rQN])
              cx.op(dve, lambda: V.memset(QF[:], 0.0), writes=[rQA])
              cx.op(dve, lambda: V.memset(QD0[:], 0.0), writes=[rQA])
              cx.op(pool, lambda: G.memset(QD1[:], 0.0), writes=[rQA])
              cx.op(dve, lambda: V.memset(VCTF[:], 0.0), writes=[rVCTF])
              cx.op(pool, lambda: G.memset(CMPIN[:], 0.0), writes=[rCMPIN])
              cx.op(dve, lambda: V.memset(CARRY[:], 0.0), writes=[rCARRY])
              for s_ in range(2):
                  pb, rpb = pj()
                  lo = 64 * s_
                  for l in range(32):
                      cx.op(pe, lambda l=l, lo=lo, pb=pb: T.matmul(pb[:, 0:1], lhsT=W1[lo:lo + 64, l, :], rhs=POST[lo:lo + 64, l:l + 1],
                                                                    start=(l == 0), stop=(l == 31)),
                            reads=[rW1, rPOST], writes=[rpb])
                  cx.op(dve, lambda pb=pb, s_=s_: V.tensor_copy(out=BH[:, s_:s_ + 1], in_=pb[:, 0:1]), reads=[rpb], writes=[rBH])
              cx.op(dve, lambda: V.tensor_tensor(out=LAM[:, 0:1], in0=VEC[:, 2:3], in1=VEC[:, 3:4], op=ALU.mult), reads=[rC, rVEC], writes=[rLAM])
              cx.op(dve, lambda: V.tensor_tensor(out=LAM[:, 1:2], in0=VEC[:, 4:5], in1=VEC[:, 5:6], op=ALU.mult), reads=[rC, rC2, rVEC, rLAM], writes=[rLAM])
              pb, rpb = pj()
              cx.op(pe, lambda: T.matmul(pb[:, 0:2], lhsT=ONESF[:], rhs=LAM[:, 0:2], start=True, stop=True), reads=[rC, rC2, rLAM], writes=[rpb])
              cx.op(act, lambda: A.activation(out=LAM[:, 2:4], in_=pb[:, 0:2], func=AF.Exp), reads=[rpb], writes=[rLAM])
              cx.op(dve, lambda: V.tensor_tensor(out=LAM[:, 0:1], in0=LAM[:, 3:4], in1=LAM[:, 2:3], op=ALU.subtract), reads=[rLAM], writes=[rLAM])
              cx.op(dve, lambda: V.tensor_tensor(out=LAM[:, 0:1], in0=LAM[:, 0:1], in1=VEC[:, 6:7], op=ALU.add),
                    reads=[rLAM, rC, rC2, rVEC], writes=[rLAM])
              cx.op(dve, lambda: V.tensor_tensor(out=LAM[:, 1:2], in0=VEC[:, 1:2], in1=VEC[:, 7:8], op=ALU.mult),
                    reads=[rC, rC2, rVEC, rLAM], writes=[rLAM])
              NEGLAM = LAM[:, 0:1]
              GCOL = LAM[:, 1:2]

              def load_tile(Tq):
                  if L == 0:
                      cx.dma(pool, ldx, XB[:], xT0_d[:, Tq * 512:(Tq + 1) * 512].rearrange("(c p) t -> p c t", p=128), writes=[rXB])
                  else:
                      for pc in range(4):
                          row = (Tq % TPC) * 512 + pc * 128
                          rk, ib = row // (TPC * 128), (Tq // TPC) * TPC + (row % (TPC * 128)) // 128
                          cx.dma(pool, ldx, XB[:, :, pc * 128:(pc + 1) * 128],
                                 xT1_d[ib // 2][rk * 1024:(rk + 1) * 1024, (ib % 2) * 128:(ib % 2 + 1) * 128].rearrange("(c p) t -> p c t", p=128),
                                 reads=[rXT1[ib // 2]], writes=[rXB])
                      rXB.w = (ldx, ldx.cnt)
                  cx.dma(sp, ldt, TAB[:], tab_d[:, :, Tq * 512:(Tq + 1) * 512], writes=[rTAB])

              load_tile(0)

              def fm_group(gi):
                  pb, rpb = pjx()
                  for c in range(8):
                      cx.op(pe, lambda c=c, pb=pb: T.matmul(pb[:], lhsT=WFM[:, c, gi * 128:(gi + 1) * 128], rhs=XB[:, c, :],
                                                             start=(c == 0), stop=(c == 7)), reads=[rWFM, rXB], writes=[rpb])
                  return pb, rpb

              def rope(pa, rpa, ps, rps, rows_a, rows_s, tabi, outs):
                  (a0, a1), (s0, s1) = rows_a, rows_s
                  n = a1 - a0
                  cx.op(dve, lambda: V.tensor_tensor(out=T0[0:n, :], in0=pa[a0:a1, :], in1=TAB[0:n, tabi, :], op=ALU.mult),
                        reads=[rpa, rTAB], writes=[rT0])
                  cx.op(dve, lambda: V.tensor_tensor(out=T1[0:n, :], in0=ps[s0:s1, :], in1=TAB[0:n, tabi + 1, :], op=ALU.mult),
                        reads=[rps, rTAB], writes=[rT1])
                  for (r0_, r1_, dst, rdst) in outs:
                      cx.op(pool, lambda r0_=r0_, r1_=r1_, dst=dst: G.tensor_tensor(out=dst, in0=T0[r0_:r1_, :], in1=T1[r0_:r1_, :], op=ALU.add),
                            reads=[rT0, rT1], writes=[rdst])

              def bcast_h(ap2d, h):
                  return bass.AP(tensor=ap2d.tensor, offset=ap2d.offset, ap=[list(ap2d.ap[0]), [0, h], list(ap2d.ap[1])])

              SGT, rSGT, ZN, rZN = FA, rFA, FB, rFB
              obc = [0]
              stc = [0]
              DSK = int(os.environ.get("DBG_DSK", "3"))

              def ob_next():
                  k = obc[0] % 3
                  obc[0] += 1
                  return OB[k], rOB[k]

              def st_next():
                  k = stc[0] % 4
                  stc[0] += 1
                  return STB[k], rST[k]

              for Tq in range(NT):
                  c0t = Tq * 512
                  pA, rA_ = fm_group(0)
                  pS, rS_ = fm_group(2)
                  rope(pA, rA_, pS, rS_, (0, 64), (0, 64), 2, [(0, 32, QD0[0:32, :], rQA), (32, 64, QD1[32:64, :], rQA)])
                  cx.op(act, lambda: A.copy(out=QF[64:128, :], in_=pA[64:128, :]), reads=[rA_], writes=[rQA])
                  pK, rK_ = fm_group(1)
                  rope(pK, rK_, pS, rS_, (0, 64), (64, 128), 2, [(0, 64, KTA[0:64, c0t:c0t + 512], rKTA)])
                  cx.op(act, lambda: A.copy(out=KTA[64:128, c0t:c0t + 512], in_=pK[64:128, :]), reads=[rK_], writes=[rKTA])
                  for (ga, gs, hq) in ((3, 4, 0), (5, 6, 2)):
                      p1, r1 = fm_group(ga)
                      p2, r2 = fm_group(gs)
                      rope(p1, r1, p2, r2, (0, 128), (0, 128), 0, [(0, 64, QN[0:64, hq, :], rQN), (64, 128, QN[0:64, hq + 1, :], rQN)])
                  p1, r1 = fm_group(7)
                  p2, r2 = fm_group(8)
                  rope(p1, r1, p2, r2, (0, 128), (0, 128), 0, [(0, 64, CMPIN[0:64, 16:528], rCMPIN), (64, 128, KTS[0:64, c0t:c0t + 512], rKTS)])
                  p1, r1 = fm_group(9)
                  p2, r2 = fm_group(10)
                  wslot = (Tq % 2) * 512
                  rope(p1, r1, p2, r2, (0, 64), (0, 64), 0, [(0, 64, KTW[0:64, wslot:wslot + 512], rKTW)])
                  cx.op(act, lambda: A.copy(out=CMPIN[64:128, 16:528], in_=p1[64:128, :]), reads=[r1], writes=[rCMPIN])
                  p1, r1 = fm_group(11)
                  cx.op(act, lambda: A.activation(out=Z1[:], in_=p1[:], func=AF.Sigmoid), reads=[r1], writes=[rZ1])
                  cx.op(dve, lambda: V.tensor_tensor(out=Z1[:], in0=p1[:], in1=Z1[:], op=ALU.mult), reads=[r1, rZ1], writes=[rZ1])
                  p1, r1 = fm_group(12)
                  cx.op(act, lambda: A.activation(out=ZN[:], in_=p1[:], func=AF.Sigmoid), reads=[r1], writes=[rZN])
                  cx.op(dve, lambda: V.tensor_tensor(out=ZN[:], in0=p1[:], in1=ZN[:], op=ALU.mult), reads=[r1, rZN], writes=[rZN])
                  sgbufs = ((FA, rFA), (T0, rT0), (T1, rT1))
                  for br in range(3):
                      p1, r1 = fm_group(13 + br)
                      SGb, rSGb = sgbufs[br]
                      cx.op(act, lambda: A.activation(out=SGb[:], in_=p1[:], func=AF.Sigmoid), reads=[r1], writes=[rSGb])
                      ob = 0 if br == 2 else 64
                      for h in range(2):
                          eng_, E_ = ((pool, G), (dve, V))[(2 * br + h) % 2]
                          cx.op(eng_, lambda h=h, ob=ob, br=br, E_=E_: E_.tensor_tensor(out=GZ[ob:ob + 64, br, h, :], in0=SGb[64 * h:64 * h + 64, :],
                                                                                       in1=ZN[64 * h:64 * h + 64, :], op=ALU.mult),
                                reads=[rSGb, rZN], writes=[rGZ])
                  for st_ in range(4):
                      kt = Tq * 4 + st_
                      pb, rpb = pjx()
                      for c in range(8):
                          cx.op(pe, lambda c=c, pb=pb, st_=st_: T.matmul(pb[:, 0:257], lhsT=XB[:, c, st_ * 128:(st_ + 1) * 128], rhs=WTM[:, c, :],
                                                                          start=(c == 0), stop=(c == 7)), reads=[rXB, rWTM], writes=[rpb])
                      cx.op(act, lambda pb=pb, kt=kt: A.copy(out=VA[:, kt, :].rearrange("p (a b) -> p a b", a=3)[:, 0::2, :],
                                                             in_=pb[:, 0:128].rearrange("p (a b) -> p a b", a=2)), reads=[rpb], writes=[rVA])
                      cx.op(dve, lambda pb=pb, kt=kt: V.tensor_copy(out=VN[:, kt, :].rearrange("p (a b) -> p a b", a=3)[:, 0::2, :],
                                                                    in_=pb[:, 128:256].rearrange("p (a b) -> p a b", a=2)), reads=[rpb], writes=[rVN])
                      cx.op(dve, lambda pb=pb, kt=kt: V.tensor_copy(out=LFRAW[:, kt:kt + 1], in_=pb[:, 256:257]), reads=[rpb], writes=[rLFRAW])
                  if Tq + 1 < NT:
                      load_tile(Tq + 1)
                  deferred = []

                  def defer(n, fn, tag):
                      deferred.append([n, fn, tag])

                  def run_deferred(upto_tag=None, everything=False, pred=None):
                      if pred is None and upto_tag is not None:
                          pred = lambda t: t == upto_tag
                      while deferred:
                          n, fn, tag = deferred[0]
                          if everything or n <= 0 or (pred is not None and any(pred(d[2]) for d in deferred)):
                              deferred.pop(0)
                              fn()
                          else:
                              break

                  j0 = 1 if Tq == 0 else 0
                  nj = 32 - j0
                  n0 = 32 * Tq - 1 + j0
                  for s_ in range(2):
                      lo = 64 * s_
                      pb, rpb = pjx()
                      for l in range(32):
                          cx.op(pe, lambda l=l, lo=lo, pb=pb: T.matmul(pb[:, 0:nj], lhsT=W1[lo:lo + 64, l, :],
                                                                        rhs=CMPIN[lo:lo + 64, 16 * j0 + l:16 * j0 + l + 16 * (nj - 1) + 1:16],
                                                                        start=(l == 0), stop=(l == 31)), reads=[rW1, rCMPIN], writes=[rpb])
                      cx.op(act, lambda pb=pb, s_=s_: A.activation(out=HSS[:, 32 * s_:32 * s_ + nj], in_=pb[:, 0:nj], func=AF.Sigmoid, bias=BH[:, s_:s_ + 1]),
                            reads=[rpb, rBH], writes=[rHSS])
                      cx.op(dve, lambda pb=pb, s_=s_: V.scalar_tensor_tensor(out=HS[:, 32 * s_:32 * s_ + nj], in0=pb[:, 0:nj], scalar=BH[:, s_:s_ + 1],
                                                                             in1=HSS[:, 32 * s_:32 * s_ + nj], op0=ALU.add, op1=ALU.mult),
                            reads=[rpb, rBH, rHSS], writes=[rHS])
                  cx.op(pool, lambda: G.tensor_copy(out=CMPIN[:, 0:16], in_=CMPIN[:, 512:528]), reads=[rCMPIN], writes=[rCMPIN])

                  def kc_stage2():
                      pb, rpb = st_next()
                      cx.op(pe, lambda: T.matmul(pb[0:64, 0:nj], lhsT=W2[:, 0:64], rhs=HS[:, 0:nj], start=True, stop=True), reads=[rW2, rHS], writes=[rpb])
                      cx.op(pe, lambda: T.matmul(pb[0:64, 64:64 + nj], lhsT=W2[:, 64:128], rhs=HS[:, 32:32 + nj], start=False, stop=True, skip_group_check=True),
                            reads=[rW2, rHS], writes=[rpb])
                      cx.op(dve, lambda: V.tensor_copy(out=KCT[0:64, n0:n0 + nj], in_=pb[0:64, 0:nj]), reads=[rpb], writes=[rKCT])
                      cx.op(dve, lambda: V.tensor_copy(out=VCTF[0:64, n0:n0 + nj], in_=pb[0:64, 64:64 + nj]), reads=[rpb], writes=[rVCTF])
                      for a_ in sorted(set([max(n0, 0) // 128, (n0 + nj - 1) // 128])):
                          pb2, rpb2 = st_next()
                          cx.op(pe, lambda a_=a_, pb2=pb2: T.transpose(out=pb2[:, 0:64], in_=VCTF[0:64, a_ * 128:(a_ + 1) * 128], identity=IDF[0:64, 0:64]),
                                reads=[rVCTF, rC, rC2], writes=[rpb2])
                          cx.op(dve, lambda a_=a_, pb2=pb2: V.tensor_copy(out=VCA[:, a_, 0:64], in_=pb2[:, 0:64]), reads=[rpb2], writes=[rVCA])
                  defer(4, kc_stage2, ("kc", Tq))

                  k0 = Tq * 4
                  cx.op(dve, lambda: V.tensor_scalar(out=SPL[:, k0:k0 + 4], in0=LFRAW[:, k0:k0 + 4], scalar1=VEC[:, 0:1], scalar2=None, op0=ALU.add),
                        reads=[rLFRAW, rC, rC2, rVEC], writes=[rSPL])
                  cx.op(act, lambda: A.activation(out=SPL[:, k0:k0 + 4], in_=SPL[:, k0:k0 + 4], func=AF.Exp, scale=-1.0), reads=[rSPL], writes=[rSPL])
                  cx.op(act, lambda: A.activation(out=SPL[:, k0:k0 + 4], in_=SPL[:, k0:k0 + 4], func=AF.Ln, bias=1.0), reads=[rSPL], writes=[rSPL])

                  def nb_stage():
                      pb, rpb = st_next()
                      cx.op(pe, lambda: T.matmul(pb[:, 0:4], lhsT=TRIF[:], rhs=SPL[:, k0:k0 + 4], start=True, stop=True), reads=[rC, rC2, rSPL], writes=[rpb])
                      cx.op(pe, lambda: T.matmul(pb[:, 8:12], lhsT=ONESF[:], rhs=SPL[:, k0:k0 + 4], start=False, stop=True, skip_group_check=True), reads=[rC, rC2, rSPL], writes=[rpb])
                      for j in range(4):
                          cx.op(dve, lambda j=j: V.tensor_tensor(out=CARRY[:, k0 + j + 1:k0 + j + 2], in0=CARRY[:, k0 + j:k0 + j + 1],
                                                                 in1=pb[:, 8 + j:9 + j], op=ALU.add), reads=[rpb, rCARRY], writes=[rCARRY])
                      cx.op(dve, lambda: V.tensor_tensor(out=CNEG[:, k0:k0 + 4], in0=pb[:, 0:4], in1=CARRY[:, k0:k0 + 4], op=ALU.add),
                            reads=[rpb, rCARRY], writes=[rCNEG])
                      cx.op(dve, lambda: V.tensor_scalar(out=NB[:, 0:k0 + 4], in0=CNEG[:, 0:k0 + 4], scalar1=CARRY[:, k0:k0 + 1], scalar2=None,
                                                         op0=ALU.subtract), reads=[rCNEG, rCARRY], writes=[rNB])
                  defer(8, nb_stage, ("nb", Tq))
                  chain = []


                  gidc = [0, 0]

                  def begin_group():
                      gidc[0] += 1
                      gidc[1] = 0

                  def add_item(smm, expf, maskf, pvf, after=None, pre=None):
                      chain.append([smm, expf, maskf, pvf, after, pre, gidc[0], gidc[1] == 0])
                      gidc[1] += 1

                  def outproj(st_):
                      for dc in range(2):
                          pb, rpb = st_next()
                          cx.op(pe, lambda dc=dc, pb=pb: T.matmul(pb[:], lhsT=CT0[:, st_ * 128:(st_ + 1) * 128], rhs=WOUT[:, 0, dc * 512:(dc + 1) * 512],
                                                                 start=True, stop=False), reads=[rCT0, rWOUT], writes=[rpb])
                          cx.op(pe, lambda dc=dc, pb=pb: T.matmul(pb[:], lhsT=CT1[:, st_ * 128:(st_ + 1) * 128], rhs=WOUT[:, 1, dc * 512:(dc + 1) * 512],
                                                                 start=False, stop=True), reads=[rCT1, rWOUT], writes=[rpb])
                          cx.op(dve, lambda dc=dc, pb=pb: V.tensor_copy(out=OUTS[:, dc * 512:(dc + 1) * 512], in_=pb[:]), reads=[rpb], writes=[rOUTS])
                      ch = Tq // TPC
                      r0 = (Tq % TPC) * 512 + st_ * 128
                      cx.dma(sp, sto, part_d[ch][r0:r0 + 128, :], OUTS[:], reads=[rOUTS, rPart[ch]])

                  def branch_out(ob_, rob_, br, qc, o_low, Y, rY):
                      so, oo = (64, 0) if o_low else (0, 64)
                      cx.op(dve, lambda: V.tensor_scalar(out=FN[so:so + 64, :], in0=ob_[so:so + 64, 0:256], scalar1=1e-30, scalar2=None, op0=ALU.max),
                            reads=[rob_], writes=[rFN])
                      cx.op(dve, lambda: V.reciprocal(out=FN[so:so + 64, :], in_=FN[so:so + 64, :]), reads=[rFN], writes=[rFN])
                      cx.op(dve, lambda: V.tensor_tensor(out=FN[so:so + 64, :].rearrange("p (h q) -> p h q", h=2),
                                                         in0=FN[so:so + 64, :].rearrange("p (h q) -> p h q", h=2),
                                                         in1=GZ[so:so + 64, br, :, qc:qc + 128], op=ALU.mult), reads=[rFN, rGZ], writes=[rFN])
                      cx.op(dve, lambda: V.tensor_tensor(out=Y, in0=ob_[oo:oo + 64, 0:256], in1=FN[so:so + 64, :], op=ALU.mult),
                            reads=[rob_, rFN], writes=[rY])

                  def cmp_group(bi):
                      begin_group()
                      i = 4 * Tq + bi
                      qc = 128 * bi
                      a_max = i // 16
                      r16 = i % 16
                      par = i % 2
                      ob_, rob_ = ob_next()
                      for a_ in range(a_max + 1):
                          def s_c(sb_, rsb, a_=a_):
                              cx.op(pe, lambda: T.matmul(sb_[:], lhsT=KCT[:, a_ * 128:(a_ + 1) * 128], rhs=QN[:, :, qc:qc + 128],
                                                         start=True, stop=(a_ != a_max)), reads=[rKCT, rQN], writes=[rsb])
                              if a_ == a_max:
                                  cx.op(pe, lambda: T.matmul(sb_[:], lhsT=IDB[:], rhs=bcast_h(MASKC[:, r16 * 128:(r16 + 1) * 128], 4),
                                                             start=False, stop=True), reads=[rC, rC2], writes=[rsb])

                          def e_c(sb_, rsb, pt, rpt, a_=a_):
                              cx.op(act, lambda: A.activation(out=pt[:], in_=sb_[:], func=AF.Exp, scale=0.125), reads=[rsb], writes=[rpt])

                          def m_c(pt, rpt, a_=a_):
                              cx.op(dve, lambda: V.tensor_tensor(out=pt[:].rearrange("p (h q) -> p h q", h=4),
                                                                 in0=pt[:].rearrange("p (h q) -> p h q", h=4),
                                                                 in1=bcast_h(MASKC[:, r16 * 128:(r16 + 1) * 128], 4),
                                                                 op=ALU.mult), reads=[rpt, rC, rC2], writes=[rpt])

                          def p_c(pt, rpt, a_=a_):
                              cx.op(pe, lambda: T.matmul(ob_[:, 0:256], lhsT=VCA[:, a_, :], rhs=pt[:, 0:256], start=(a_ == 0), stop=(a_ == a_max)),
                                    reads=[rVCA, rpt], writes=[rob_])
                              for h in range(4):
                                  cx.op(pe, lambda h=h: T.matmul(IMPB[:, h * 128:(h + 1) * 128], lhsT=pt[:, h * 128:(h + 1) * 128], rhs=OV[:, a_, :],
                                                                 start=(a_ == 0 and h == 0), stop=(a_ == a_max), skip_group_check=True),
                                        reads=[rpt, rC, rC2], writes=[rIMPB])

                          def epi_cmp():
                              branch_out(ob_, rob_, 0, qc, True, YA[par], rYA[par])

                              def stage2():
                                  cx.op(dve, lambda: V.tensor_reduce(out=RS4[:], in_=IMPB[:].rearrange("p (h j) -> p h j", h=4), axis=AX.X, op=ALU.add),
                                        reads=[rIMPB], writes=[rRS4])
                                  cx.op(dve, lambda: V.tensor_scalar(out=RS4[:], in0=RS4[:], scalar1=1e-30, scalar2=None, op0=ALU.max), reads=[rRS4], writes=[rRS4])
                                  cx.op(dve, lambda: V.reciprocal(out=RI4[:], in_=RS4[:]), reads=[rRS4], writes=[rRI4])
                                  cx.op(dve, lambda: V.tensor_scalar(out=SC[:], in0=IMPB[:, 0:128], scalar1=RI4[:, 0:1], scalar2=None, op0=ALU.mult),
                                        reads=[rIMPB, rRI4], writes=[rSC])
                                  for h in range(1, 4):
                                      cx.op(dve, lambda h=h: V.scalar_tensor_tensor(out=SC[:], in0=IMPB[:, h * 128:(h + 1) * 128], scalar=RI4[:, h:h + 1], in1=SC[:],
                                                                                    op0=ALU.mult, op1=ALU.add), reads=[rIMPB, rRI4, rSC], writes=[rSC])

                              def stage3():
                                  cx.op(dve, lambda: V.memset(SC[:, 0:1], BIG), reads=[], writes=[rSC])
                                  lo0 = max(2 * i - 1, 0)
                                  cx.op(dve, lambda: V.memset(SC[0:64, lo0:2 * i + 1], BIG), writes=[rSC])
                                  cx.op(dve, lambda: V.memset(SC[64:128, 2 * i:2 * i + 2], BIG), writes=[rSC])
                                  cx.op(dve, lambda: V.max(out=M8[:], in_=SC[:]), reads=[rSC], writes=[rM8])
                                  cx.op(dve, lambda: V.match_replace(out=SC2[:], in_to_replace=M8[:], in_values=SC[:], imm_value=-BIG),
                                        reads=[rSC, rM8], writes=[rSC2])
                                  cx.op(dve, lambda: V.max(out=M8b[:], in_=SC2[:]), reads=[rSC2], writes=[rM8b])
                                  cx.op(dve, lambda: V.tensor_scalar(out=MNEG2[par][:], in0=SC[:], scalar1=M8b[:, 7:8], scalar2=-30000.0, op0=ALU.is_lt, op1=ALU.mult),
                                        reads=[rSC, rM8b], writes=[rMNEG2[par]])
                                  cx.op(pool, lambda: G.tensor_copy(out=QSA[par][0:64, :, :].rearrange("p a (h q) -> p a h q", h=2),
                                                                    in_=bass.AP(tensor=QN.tensor if hasattr(QN, "tensor") else QN[0:64, 0:2, qc:qc + 128].tensor,
                                                                                offset=QN[0:64, 0:2, qc:qc + 128].offset,
                                                                                ap=[list(QN[0:64, 0:2, qc:qc + 128].ap[0]), [0, 2]] + [list(a) for a in QN[0:64, 0:2, qc:qc + 128].ap[1:]])),
                                        reads=[rQN], writes=[rQSA[par][0], rQSA[par][1]])

                              def stage_b():
                                  mt, rmt = st_next()
                                  cx.op(pe, lambda: T.transpose(out=mt[:, 0:128], in_=MNEG2[par][:], identity=IDF[:]), reads=[rMNEG2[par], rC, rC2], writes=[rmt])
                                  for half in range(2):
                                      cx.op(dve, lambda half=half: V.tensor_copy(out=QSA[par][64:128, half, :].rearrange("p (h q) -> p h q", h=2),
                                                                                 in_=bcast_h(mt[64 * half:64 * half + 64, 0:128], 2)),
                                            reads=[rmt], writes=[rQSA[par][half]])
                              defer(6, stage2, ("c", i))
                              defer(12, stage3, ("c", i))
                              defer(28, stage_b, ("q", i))
                          add_item(s_c, e_c, None, p_c, epi_cmp if a_ == a_max else None,
                                   (lambda: run_deferred(pred=lambda t: t[0] in ("c", "kc"))) if a_ == 0 else None)

                  def win_group(bi):
                      begin_group()
                      i = 4 * Tq + bi
                      qc = 128 * bi
                      ob_, rob_ = ob_next()
                      kts = list(range(max(0, i - 4), i + 1))
                      for kt in kts:
                          def s_w(sb_, rsb, kt=kt):
                              wl = (kt % 8) * 128
                              msk_ = TRIN if kt == i else (TRIUN if kt == i - 4 else None)
                              cx.op(pe, lambda: T.matmul(sb_[:, 0:256], lhsT=KTW[:, wl:wl + 128], rhs=QN[:, 0:2, qc:qc + 128], start=True, stop=(msk_ is None)),
                                    reads=[rKTW, rQN], writes=[rsb])
                              if msk_ is not None:
                                  cx.op(pe, lambda: T.matmul(sb_[:, 0:256], lhsT=IDB[:], rhs=bcast_h(msk_[:], 2), start=False, stop=True),
                                        reads=[rC, rC2], writes=[rsb])

                          def e_w(sb_, rsb, pt, rpt):
                              cx.op(act, lambda: A.activation(out=pt[:, 0:256], in_=sb_[:, 0:256], func=AF.Exp, scale=0.125), reads=[rsb], writes=[rpt])
                          mk = None
                          if kt == i or kt == i - 4:
                              def mk(pt, rpt, kt=kt):
                                  msk = TRI if kt == i else TRIU
                                  cx.op(dve, lambda: V.tensor_tensor(out=pt[:, 0:256].rearrange("p (h q) -> p h q", h=2),
                                                                     in0=pt[:, 0:256].rearrange("p (h q) -> p h q", h=2),
                                                                     in1=bcast_h(msk[:], 2), op=ALU.mult), reads=[rpt, rC, rC2], writes=[rpt])

                          def p_w(pt, rpt, kt=kt):
                              cx.op(pe, lambda: T.matmul(ob_[:, 0:256], lhsT=VN[:, kt, 64:192], rhs=pt[:, 0:256], start=(kt == kts[0]), stop=(kt == kts[-1])),
                                    reads=[rVN, rpt], writes=[rob_])
                          add_item(s_w, e_w, None, p_w, (lambda: branch_out(ob_, rob_, 2, qc, False, YB[:], rYB)) if kt == kts[-1] else None)

                  def sel_group(bi):
                      begin_group()
                      i = 4 * Tq + bi
                      qc = 128 * bi
                      par = i % 2
                      ob_, rob_ = ob_next()
                      for kt in range(i + 1):
                          def s_s(sb_, rsb, kt=kt):
                              half = kt // 32
                              cx.op(pe, lambda: T.matmul(sb_[:, 0:256], lhsT=KTS[:, kt * 128:(kt + 1) * 128], rhs=QSA[par][:, half, :], start=True, stop=(kt != i)),
                                    reads=[rKTS, rQSA[par][half]], writes=[rsb])
                              if kt == i:
                                  cx.op(pe, lambda: T.matmul(sb_[:, 0:256], lhsT=IDB[:], rhs=bcast_h(TRIN[:], 2), start=False, stop=True),
                                        reads=[rC, rC2], writes=[rsb])

                          def e_s(sb_, rsb, pt, rpt):
                              cx.op(act, lambda: A.activation(out=pt[:, 0:256], in_=sb_[:, 0:256], func=AF.Exp, scale=0.125), reads=[rsb], writes=[rpt])
                          mk = None
                          if kt == i:
                              def mk(pt, rpt):
                                  cx.op(dve, lambda: V.tensor_tensor(out=pt[:, 0:256].rearrange("p (h q) -> p h q", h=2),
                                                                     in0=pt[:, 0:256].rearrange("p (h q) -> p h q", h=2),
                                                                     in1=bcast_h(TRI[:], 2), op=ALU.mult), reads=[rpt, rC, rC2], writes=[rpt])

                          def p_s(pt, rpt, kt=kt):
                              cx.op(pe, lambda: T.matmul(ob_[:, 0:256], lhsT=VN[:, kt, 0:128], rhs=pt[:, 0:256], start=(kt == 0), stop=(kt == i)),
                                    reads=[rVN, rpt], writes=[rob_])

                          def epi_sel():
                              branch_out(ob_, rob_, 1, qc, True, YC[:], rYC)
                              cx.op(pool, lambda: G.tensor_tensor(out=YB[:], in0=YA[par], in1=YB[:], op=ALU.add), reads=[rYA[par], rYB], writes=[rYB])
                              for h in range(2):
                                  cx.op(pool, lambda h=h: G.tensor_tensor(out=CT1[64 * h:64 * h + 64, qc:qc + 128], in0=YB[:, 128 * h:128 * h + 128],
                                                                          in1=YC[:, 128 * h:128 * h + 128], op=ALU.add), reads=[rYB, rYC], writes=[rCT1])
                              defer(8, lambda: outproj(bi), ("o", i))
                          add_item(s_s, e_s, None, p_s, epi_sel if kt == i else None, (lambda: run_deferred(upto_tag=("q", i))) if kt == 0 else None)

                  nkt = 4 * Tq + 4

                  def dense_group(kind):
                      begin_group()
                      ob_, rob_ = ob_next()
                      for kt in range(nkt):
                          jd = kt - 4 * Tq
                          cc = 128 * jd if jd >= 0 else 0
                          first, last = (kt == 0), (kt == nkt - 1)

                          def smm(sb_, rsb, kt=kt, cc=cc):
                              qsrc = (QF, QD0, QD1)[kind]
                              dg_ = (kt - 4 * Tq) >= 0
                              cx.op(pe, lambda: T.matmul(sb_[:, cc:512], lhsT=KTA[:, kt * 128:(kt + 1) * 128], rhs=qsrc[:, cc:512],
                                                         start=True, stop=(not dg_)), reads=[rKTA, rQA], writes=[rsb])
                              if dg_:
                                  cx.op(pe, lambda: T.matmul(sb_[:, cc:cc + 128], lhsT=IDB[:], rhs=TRIN[:], start=False, stop=True),
                                        reads=[rC, rC2], writes=[rsb])

                          def expf(sb_, rsb, pt, rpt, kt=kt, cc=cc):
                              if kind == 0:
                                  cx.op(act, lambda: A.activation(out=pt[:, cc:512], in_=sb_[:, cc:512], func=AF.Exp, bias=NB[:, kt:kt + 1], scale=0.125),
                                        reads=[rsb, rNB], writes=[rpt])
                              else:
                                  cx.op(act, lambda: A.activation(out=pt[:, cc:512], in_=sb_[:, cc:512], func=AF.Exp, scale=float(32 ** -0.5)),
                                        reads=[rsb], writes=[rpt])

                          def m_diag(pt, rpt, cc=cc):
                              cx.op(pool, lambda: G.tensor_tensor(out=pt[:, cc:cc + 128], in0=pt[:, cc:cc + 128], in1=TRI[:], op=ALU.mult),
                                    reads=[rpt, rC, rC2], writes=[rpt])

                          def pvf(pt, rpt, kt=kt, cc=cc, first=first, last=last):
                              lhs = VA[:, kt, 0:128] if kind == 0 else VA[:, kt, 64:192]
                              cx.op(pe, lambda: T.matmul(ob_[:, cc:512], lhsT=lhs, rhs=pt[:, cc:512], start=first, stop=last),
                                    reads=[rVA, rpt], writes=[rob_])

                          def epi():
                              if kind == 0:
                                  cx.op(dve, lambda: V.reciprocal(out=FA[64:128, :], in_=ob_[64:128, :]), reads=[rob_], writes=[rFA])
                                  cx.op(dve, lambda: V.tensor_tensor(out=FA[64:128, :], in0=FA[64:128, :], in1=Z1[64:128, :], op=ALU.mult),
                                        reads=[rFA, rZ1], writes=[rFA])
                                  cx.op(dve, lambda: V.tensor_tensor(out=CT0[0:64, :], in0=ob_[0:64, :], in1=FA[64:128, :], op=ALU.mult),
                                        reads=[rob_, rFA], writes=[rCT0])
                              elif kind == 1:
                                  cx.op(dve, lambda: V.reciprocal(out=FB[0:64, :], in_=ob_[0:64, :]), reads=[rob_], writes=[rFB])
                                  cx.op(dve, lambda: V.tensor_tensor(out=T0[0:64, :], in0=ob_[64:128, :], in1=FB[0:64, :], op=ALU.mult),
                                        reads=[rob_, rFB], writes=[rT0])
                              else:
                                  cx.op(dve, lambda: V.reciprocal(out=FB[0:64, :], in_=ob_[0:64, :]), reads=[rob_, rFB], writes=[rFB])
                                  cx.op(dve, lambda: V.tensor_scalar(out=FB[0:64, :], in0=FB[0:64, :], scalar1=NEGLAM[0:64, :], scalar2=None, op0=ALU.mult),
                                        reads=[rFB, rLAM], writes=[rFB])
                                  cx.op(dve, lambda: V.tensor_tensor(out=T1[0:64, :], in0=ob_[64:128, :], in1=FB[0:64, :], op=ALU.mult),
                                        reads=[rob_, rFB], writes=[rT1])
                                  cx.op(pool, lambda: G.tensor_tensor(out=T0[0:64, :], in0=T0[0:64, :], in1=T1[0:64, :], op=ALU.add),
                                        reads=[rT0, rT1], writes=[rT0])
                                  cx.op(pool, lambda: G.tensor_tensor(out=T1[0:64, :], in0=T0[0:64, :], in1=T0[0:64, :], op=ALU.mult),
                                        reads=[rT0, rT1], writes=[rT1])

                                  def stage_b():
                                      pb, rpb = st_next()
                                      cx.op(pe, lambda: T.matmul(pb[0:64, :], lhsT=ONESF[0:64, 0:64], rhs=T1[0:64, :], start=True, stop=True),
                                            reads=[rC, rC2, rT1], writes=[rpb])
                                      cx.op(act, lambda: A.activation(out=FB[0:64, :], in_=pb[0:64, :], func=AF.Ln, scale=1.0 / 64.0, bias=EPSC[0:64, :]),
                                            reads=[rpb, rC, rC2], writes=[rFB])
                                      cx.op(act, lambda: A.activation(out=FB[0:64, :], in_=FB[0:64, :], func=AF.Exp, scale=-0.5), reads=[rFB], writes=[rFB])
                                      cx.op(dve, lambda: V.scalar_tensor_tensor(out=T0[0:64, :], in0=T0[0:64, :], scalar=GCOL[0:64, :], in1=FB[0:64, :],
                                                                                op0=ALU.mult, op1=ALU.mult), reads=[rT0, rFB, rLAM], writes=[rT0])
                                      cx.op(pool, lambda: G.tensor_tensor(out=CT0[64:128, :], in0=T0[0:64, :], in1=Z1[0:64, :], op=ALU.mult),
                                            reads=[rT0, rZ1], writes=[rCT0])
                                  defer(14, stage_b, ("d", Tq))
                          add_item(smm, expf, None, pvf, epi if last else None,
                                   (lambda: run_deferred(pred=lambda t: t[0] == "nb")) if (kind == 0 and kt == 0) else None)

                  if os.environ.get("DBG_ORDER") == "old":
                      dense_group(0)
                      dense_group(1)
                      dense_group(2)
                      for bi in range(4):
                          cmp_group(bi)
                          win_group(bi)
                          sel_group(bi)
                  else:
                      dense_group(1)
                      dense_group(2)
                      cmp_group(0)
                      dense_group(0)
                      for bi in range(4):
                          if bi + 1 < 4:
                              cmp_group(bi + 1)
                          win_group(bi)
                          sel_group(bi)

                  pend = []

                  def pop_pair():
                      X = pend.pop(0)
                      Y = pend.pop(0) if pend else None
                      order = [X] if Y is None else ([X, Y] if (X[5] and X[4] == Y[4]) else [Y, X])
                      for it in order:
                          it[0](it[1], it[2])
                      for it in ([X] if Y is None else [X, Y]):
                          if it[3] is not None:
                              it[3]()
                  ci = 0
                  while ci < len(chain):
                      pair = chain[ci:ci + 2]
                      ci += 2
                      for it in pair:
                          for d_ in deferred:
                              d_[0] -= 1
                      run_deferred()
                      for it in pair:
                          if it[5] is not None:
                              it[5]()
                      slots = []
                      for it in pair:
                          sb_, rsb = st_next()
                          pi = ptc[0] % NPT
                          ptc[0] += 1
                          slots.append((sb_, rsb, PT[pi], rPT[pi]))
                      for it, sl in reversed(list(zip(pair, slots))):
                          it[0](sl[0], sl[1])
                      for it, sl in reversed(list(zip(pair, slots))):
                          it[1](sl[0], sl[1], sl[2], sl[3])
                          if it[2] is not None:
                              it[2](sl[2], sl[3])
                      for it, sl in zip(pair, slots):
                          pend.append((it[3], sl[2], sl[3], it[4], it[6], it[7]))
                      if len(pend) > 2:
                          pop_pair()
                  while pend:
                      pop_pair()
                  run_deferred(everything=True)


                  if Tq % TPC == TPC - 1:
                      ch = Tq // TPC
                      cx._need(pool, [], [rPart[ch], rRSd[ch]])
                      ins = G.collective_compute("ReduceScatter", ALU.add, replica_groups=RG, ins=[part_d[ch].opt()], outs=[rs_d[ch].opt()])
                      rsc[ch].cnt += 1
                      ins.then_inc(rsc[ch].sem)
                      rPart[ch].w = (rsc[ch], rsc[ch].cnt); rPart[ch].r = []
                      rRSd[ch].w = (rsc[ch], rsc[ch].cnt); rRSd[ch].r = []

              if L + 1 < depth:
                  load_weights(L + 1)
              if ALIAS:
                  rINB = cx.fork(rVA, 1) + cx.fork(rVN, 1)
                  rACC0, rACC1, rYT0, rYT1 = cx.fork(rKTA, 4)
                  rACCB = [rACC0, rACC1]
                  rYTB2 = [rYT0, rYT1]
                  (rGB,) = cx.fork(rXB, 1)
              cx.dma(sp, ldg, GTB, lng_d[L], writes=[rGB])
              cx.dma(sp, ldg, BTB, lnb_d[L], writes=[rGB])
              rGB.w = (ldg, ldg.cnt)
              xres_d = xq_d if L == 0 else yq_d
              ntb = SQ // 128
              last = (L == depth - 1)

              cx._need(sp, [], [rYQ])

              cx._need(act, [], [rYQ])

              def loadb(i):
                  k = i % 2
                  r0 = i * 128
                  ch, jj = i // TPC, i % TPC
                  cx.dma(sp, ldb[k], INB[k][:, 0, :], rs_d[ch][jj * 128:(jj + 1) * 128, :], reads=[rRSd[ch]], writes=[rINB[k]])
                  cx.dma(sp, ldb[k], INB[k][:, 4, :], xres_d[r0:r0 + 128, :], writes=[rINB[k]])
                  rINB[k].w = (ldb[k], ldb[k].cnt)

              loadb(0)
              for i in range(ntb):
                  k = i % 2
                  if i + 1 < ntb:
                      loadb(i + 1)
                  I_, Ac, rI, rAc = INB[k], ACCB[k], rINB[k], rACCB[k]
                  cx.op(dve, lambda: V.scalar_tensor_tensor(out=Ac, in0=I_[:, 4, :], scalar=float(ALPHA), in1=I_[:, 0, :], op0=ALU.mult, op1=ALU.add),
                        reads=[rI], writes=[rAc])
                  for h in range(2):
                      cx.op(dve, lambda h=h: V.bn_stats(out=STT[:, h, :], in_=Ac[:, h * 512:(h + 1) * 512]), reads=[rAc], writes=[rSTT])
                  cx.op(dve, lambda: V.bn_aggr(out=MV[:, 0:2], in_=STT[:].rearrange("p a b -> p (a b)")), reads=[rSTT], writes=[rMV])
                  cx.op(dve, lambda: V.tensor_scalar(out=MV[:, 2:3], in0=MV[:, 1:2], scalar1=1e-5, scalar2=None, op0=ALU.add), reads=[rMV], writes=[rMV])
                  cx.op(act, lambda: A.activation(out=MV[:, 2:3], in_=MV[:, 2:3], func=AF.Ln), reads=[rMV], writes=[rMV])
                  cx.op(act, lambda: A.activation(out=MV[:, 3:4], in_=MV[:, 2:3], func=AF.Exp, scale=-0.5), reads=[rMV], writes=[rMV])
                  cx.op(dve, lambda: V.tensor_scalar(out=Ac, in0=Ac, scalar1=MV[:, 0:1], scalar2=MV[:, 3:4], op0=ALU.subtract, op1=ALU.mult),
                        reads=[rAc, rMV], writes=[rAc])
                  cx.op(dve, lambda: V.tensor_tensor(out=Ac, in0=Ac, in1=GTB, op=ALU.mult), reads=[rAc, rGB], writes=[rAc])
                  cx.op(dve, lambda: V.tensor_tensor(out=Ac, in0=Ac, in1=BTB, op=ALU.add), reads=[rAc, rGB], writes=[rAc])
                  if last:
                      cx.dma(act, sta[k], y_d[i * 128:(i + 1) * 128, :], Ac, reads=[rAc])
                  else:
                      cx.dma(act, sta[k], yq_d[i * 128:(i + 1) * 128, :], Ac, reads=[rAc, rYQ])
                      for hb in range(2):
                          pb, rpb = pj()
                          for c4 in range(4):
                              c = 4 * hb + c4
                              cx.op(pe, lambda c=c, c4=c4, pb=pb: T.transpose(out=pb[:, c4 * 128:(c4 + 1) * 128], in_=Ac[:, c * 128:(c + 1) * 128], identity=IDF[:]),
                                    reads=[rAc, rC], writes=[rpb])
                          cx.op(act if hb == 0 else dve, lambda hb=hb, pb=pb: (A.copy if hb == 0 else V.tensor_copy)(out=YTB2[k][:, 4 * hb:4 * hb + 4, :], in_=pb[:].rearrange("p (c t) -> p c t", c=4)),
                                reads=[rpb], writes=[rYTB2[k]])
                      i2 = i // 2
                      for c in range(8):
                          cx.dma(act, styt2[k], ytq_d[i2][c * 128:(c + 1) * 128, (i % 2) * 128:(i % 2 + 1) * 128], YTB2[k][:, c, :],
                                 reads=[rYTB2[k], rYTQ[i2]])
                      if i % 2 == 1:
                          cx._need(pool, [], [rYTQ[i2], rXT1[i2]])
                          ins = G.collective_compute("AllGather", ALU.bypass, replica_groups=RG, ins=[ytq_d[i2].opt()], outs=[xT1_d[i2].opt()])
                          ccs.cnt += 1
                          ins.then_inc(ccs.sem)
                          rYTQ[i2].w = (ccs, ccs.cnt); rYTQ[i2].r = []
                          rXT1[i2].w = (ccs, ccs.cnt); rXT1[i2].r = []
              if ALIAS:
                  cx.join(rVA, rINB[0:1]); cx.join(rVN, rINB[1:2]); cx.join(rKTA, [rACC0, rACC1, rYT0, rYT1]); cx.join(rXB, [rGB])
        cx.wait_all(sp, sta)
    return nc


def _consts(S):
    half = 32
    t = np.arange(S, dtype=np.float32)
    tab = np.zeros((128, 4, S), np.float32)
    inv = (10000.0 ** (-(np.arange(32, dtype=np.float32) * 2.0 / 64))).astype(np.float32)
    ang = t[None, :] * inv[:, None]
    cosn, sinn = np.cos(ang), np.sin(ang)
    for r in range(128):
        i = r % 64
        tab[r, 0] = cosn[i % 32]
        tab[r, 1] = -sinn[i % 32] if i < 32 else sinn[i % 32]
    invd = (10000.0 ** (-(np.arange(16, dtype=np.float32) * 2.0 / 32))).astype(np.float32)
    angd = t[None, :] * invd[:, None]
    cosd, sind = np.cos(angd), np.sin(angd)
    for r in range(64):
        w = r % 32
        tab[r, 2] = cosd[w % 16]
        tab[r, 3] = -sind[w % 16] if w < 16 else sind[w % 16]
    key = np.arange(S)
    ind = (((key[None, :] // 64) % 64) == np.arange(64)[:, None]).astype(np.float32)
    p = np.arange(128)
    c_tri = (p[:, None] <= p[None, :]).astype(np.float32)
    c_triu = (p[:, None] > p[None, :]).astype(np.float32)
    c_ident = np.eye(128, dtype=np.float32)
    u = np.arange(2048)
    c_maskc = ((16 * p[:, None] + 31) <= u[None, :]).astype(np.float32)
    n = np.arange(512)
    j = np.arange(128)
    ov = np.clip(np.minimum(16 * n[:, None] + 32, 64 * j[None, :] + 64) - np.maximum(16 * n[:, None], 64 * j[None, :]), 0, None)
    ov = (ov.astype(np.float32) / 32.0)
    ov[511] = 0.0
    c_ov = np.ascontiguousarray(ov.reshape(4, 128, 128).transpose(1, 0, 2))
    return dict(tab=tab, ind=ind, c_tri=c_tri, c_triu=c_triu, c_ident=c_ident, c_maskc=c_maskc, c_ov=c_ov)


def _core_cols(c):
    g, hp = c // 2, c % 2
    rng64 = np.arange(64)
    sw64 = (rng64 + 32) % 64
    d32 = np.arange(32)
    swd = np.concatenate([(d32 + 16) % 32, 32 + (d32 + 16) % 32])
    dq = OFF['diff_q'] + 64 * c + rng64
    dk = OFF['diff_k'] + 64 * c + rng64
    dq_sw = OFF['diff_q'] + 64 * c + swd
    dk_sw = OFF['diff_k'] + 64 * c + swd
    fq = OFF['fox_q'] + 64 * c + rng64
    fk = OFF['fox_k'] + 64 * c + rng64
    my = [4 * g + 2 * hp, 4 * g + 2 * hp + 1]
    oth = [4 * g + 2 * (1 - hp), 4 * g + 2 * (1 - hp) + 1]
    nq = lambda H, perm: OFF['nsa_q'] + 64 * H + perm
    kc = lambda perm: OFF['nsa_k_cmp'] + 64 * g + perm
    ks = lambda perm: OFF['nsa_k_sel'] + 64 * g + perm
    kw = lambda perm: OFF['nsa_k_win'] + 64 * g + perm
    vcmp = OFF['nsa_v_cmp'] + 64 * g + rng64
    gate = lambda H, br: np.full(64, OFF['nsa_gate'] + 3 * H + br)
    groups = [
        np.concatenate([dq, fq]), np.concatenate([dk, fk]), np.concatenate([dq_sw, dk_sw]),
        np.concatenate([nq(my[0], rng64), nq(my[1], rng64)]), np.concatenate([nq(my[0], sw64), nq(my[1], sw64)]),
        np.concatenate([nq(oth[0], rng64), nq(oth[1], rng64)]), np.concatenate([nq(oth[0], sw64), nq(oth[1], sw64)]),
        np.concatenate([kc(rng64), ks(rng64)]), np.concatenate([kc(sw64), ks(sw64)]),
        np.concatenate([kw(rng64), vcmp]), np.concatenate([kw(sw64), vcmp]),
        np.concatenate([OFF['diff_z'] + 64 * c + rng64, OFF['fox_z'] + 64 * c + rng64]),
        np.concatenate([OFF['nsa_z'] + 64 * my[0] + rng64, OFF['nsa_z'] + 64 * my[1] + rng64]),
        np.concatenate([gate(my[0], 0), gate(my[1], 0)]), np.concatenate([gate(my[0], 1), gate(my[1], 1)]),
        np.concatenate([gate(my[0], 2), gate(my[1], 2)]),
    ]
    fm = np.concatenate(groups)
    tm = np.concatenate([OFF['fox_v'] + 64 * c + rng64, OFF['diff_v'] + 64 * c + rng64,
                         OFF['nsa_v_sel'] + 64 * g + rng64, OFF['nsa_v_win'] + 64 * g + rng64,
                         np.array([OFF['fox_f'] + c])])
    wo = np.concatenate([64 * c + rng64, 256 + 512 + 64 * c + rng64, 256 + 64 * my[0] + rng64, 256 + 64 * my[1] + rng64])
    return fm, tm, wo


_PROG = {}


def _get(S, depth):
    k = (S, depth)
    if k not in _PROG:
        _PROG[k] = build_fused(S, depth)
    return _PROG[k]


def kernel(x, w_in, b_fox_f, cmp_pos_k, cmp_pos_v, cmp_w1_k, cmp_w2_k, cmp_w1_v, cmp_w2_v,
           lam_q1, lam_k1, lam_q2, lam_k2, diff_subln_g, w_out, ln_g, ln_b):
    f32 = lambda a: np.ascontiguousarray(np.asarray(a, dtype=np.float32))
    x = f32(x)
    B, S, D = x.shape
    depth = w_in.shape[0]
    SQ = S // 4
    w_in, w_out = f32(w_in), f32(w_out)
    cst = _consts(S)
    cols = [_core_cols(c) for c in range(4)]
    w1 = np.ascontiguousarray(np.stack([f32(cmp_w1_k), f32(cmp_w1_v)], axis=1))
    w2 = np.ascontiguousarray(np.concatenate([f32(cmp_w2_k), f32(cmp_w2_v)], axis=2))
    post = np.ascontiguousarray(np.concatenate([f32(cmp_pos_k).transpose(0, 2, 1), f32(cmp_pos_v).transpose(0, 2, 1)], axis=1))
    lng = np.ascontiguousarray(np.broadcast_to(f32(ln_g)[:, None, :], (depth, 128, D)))
    lnb = np.ascontiguousarray(np.broadcast_to(f32(ln_b)[:, None, :], (depth, 128, D)))
    NT = S // 512
    TPC = 1

    def tokmap(r):
        idx = []
        for i in range(SQ // 128):
            base = (i // TPC) * TPC * 512 + r * TPC * 128 + (i % TPC) * 128
            idx.append(np.arange(base, base + 128))
        return np.concatenate(idx)
    tmaps = [tokmap(r) for r in range(4)]
    xT = [np.ascontiguousarray(x[b].T) for b in range(B)]
    in_maps = []
    for core in range(8):
        b, c = core // 4, core % 4
        fm, tm, wo = cols[c]
        vec = np.zeros((depth, 128, 8), np.float32)
        for l in range(depth):
            lam_init = 0.8 - 0.6 * math.exp(-0.3 * l)
            vec[l, :, 0] = f32(b_fox_f)[l, c]
            vec[l, :, 1] = f32(diff_subln_g)[l][np.arange(128) % 64]
            vec[l, 0:32, 2] = f32(lam_q1)[l]; vec[l, 0:32, 3] = f32(lam_k1)[l]
            vec[l, 0:32, 4] = f32(lam_q2)[l]; vec[l, 0:32, 5] = f32(lam_k2)[l]
            vec[l, :, 6] = -lam_init
            vec[l, :, 7] = 1.0 - lam_init
        m = dict(xT=xT[b], xq=np.ascontiguousarray(x[b, tmaps[c]]),
                 wfm=np.ascontiguousarray(w_in[:, :, fm]), wtm=np.ascontiguousarray(w_in[:, :, tm]),
                 wout=np.ascontiguousarray(w_out[:, wo, :]), w1=w1, w2=w2, post=post, vecs=vec, lng=lng, lnb=lnb)
        m.update(cst)
        in_maps.append(m)
    res = run_bass_kernel_spmd(_get(S, depth), in_maps, core_ids=list(range(8)))
    out = np.empty((B, S, D), np.float32)
    for core in range(8):
        b, c = core // 4, core % 4
        out[b, tmaps[c]] = res.results[core]["y"]
    return out
```

```python
import math
import os
from contextlib import ExitStack
import numpy as np
import concourse.bass as bass
import concourse.mybir as mybir
from concourse.bass_utils import run_bass_kernel_spmd

F32 = mybir.dt.float32
BF16 = mybir.dt.bfloat16
ALU = mybir.AluOpType
AF = mybir.ActivationFunctionType
AX = mybir.AxisListType

D_MODEL = 1024
DEPTH = 2
HD = 64
IN_SPLITS = (('fox_q', 256), ('fox_k', 256), ('fox_v', 256), ('fox_f', 4), ('fox_z', 256), ('nsa_q', 512),
             ('nsa_k_cmp', 128), ('nsa_v_cmp', 128), ('nsa_k_sel', 128), ('nsa_v_sel', 128),
             ('nsa_k_win', 128), ('nsa_v_win', 128), ('nsa_gate', 24), ('nsa_z', 512),
             ('diff_q', 256), ('diff_k', 256), ('diff_v', 256), ('diff_z', 256))
OFF = {}
_a = 0
for _n, _w in IN_SPLITS:
    OFF[_n] = _a
    _a += _w
IN_WIDTH = _a
ALPHA = (2 * DEPTH) ** 0.25
NG = 16
BIG = 1.0e30


class Res:
    __slots__ = ("w", "r", "excl")

    def __init__(self, excl=False):
        self.excl = excl
        self.w = None
        self.r = []


class SemC:
    __slots__ = ("sem", "cnt", "key")
    _n = 0

    def __init__(self, sem):
        self.sem = sem
        self.cnt = 0
        SemC._n += 1
        self.key = SemC._n


class Eng:
    def __init__(self, h, semc, same_sync):
        self.h = h
        self.s = semc
        self.waited = {}
        self.same_sync = same_sync


class Ctx:
    def __init__(self, nc, stack):
        self.nc = nc
        self.stack = stack
        mk = lambda n: SemC(stack.enter_context(nc.semaphore(n)))
        self.pe = Eng(nc.tensor, mk("s_pe"), False)
        self.dve = Eng(nc.vector, mk("s_dve"), True)
        self.act = Eng(nc.scalar, mk("s_act"), True)
        self.pool = Eng(nc.gpsimd, mk("s_pool"), True)
        self.sp = Eng(nc.sync, mk("s_sp"), False)

    def res(self, excl=False):
        return Res(excl)

    def dsem(self, name):
        return SemC(self.stack.enter_context(self.nc.semaphore(name)))

    def sbuf(self, name, shape, dt):
        return self.stack.enter_context(self.nc.sbuf_tensor(name, list(shape), dt))

    def psum(self, name, shape, dt):
        return self.stack.enter_context(self.nc.psum_tensor(name, list(shape), dt))

    def _need(self, eng, reads, writes):
        need = {}

        def add(p):
            if p is None:
                return
            s, v = p
            if s is eng.s and not eng.same_sync:
                return
            if need.get(s.key, (None, -1))[1] < v:
                need[s.key] = (s, v)
        for r in reads:
            add(r.w)
            if r.excl:
                for p in r.r:
                    if p[0] is not eng.s:
                        add(p)
        for w in writes:
            add(w.w)
            for p in w.r:
                add(p)
        for k, (s, v) in need.items():
            if eng.waited.get(k, 0) < v:
                eng.h.wait_ge(s.sem, v)
                eng.waited[k] = v

    def op(self, eng, fn, reads=(), writes=()):
        self._need(eng, reads, writes)
        ins = fn()
        eng.s.cnt += 1
        ins.then_inc(eng.s.sem, 1)
        me = (eng.s, eng.s.cnt)
        for r in reads:
            r.r.append(me)
            if len(r.r) > 24:
                r.r = r.r[-24:] if False else _compact(r.r)
        for w in writes:
            w.w = me
            w.r = []
        return ins

    def dma(self, q, dsem, out, in_, reads=(), writes=()):
        self._need(q, reads, writes)
        ins = q.h.dma_start(out=out, in_=in_)
        dsem.cnt += 16
        ins.then_inc(dsem.sem, 16)
        me = (dsem, dsem.cnt)
        for r in reads:
            r.r.append(me)
        for w in writes:
            w.w = me
            w.r = []
        return ins

    def fork(self, parent, n):
        kids = [Res() for _ in range(n)]
        for k in kids:
            k.r = _compact(list(parent.r) + ([parent.w] if parent.w else []))
        return kids

    def join(self, parent, kids):
        for k in kids:
            parent.r = _compact(parent.r + k.r + ([k.w] if k.w else []))

    def wait_all(self, eng, semcs):
        for s in semcs:
            if s.cnt > 0 and eng.waited.get(s.key, 0) < s.cnt:
                eng.h.wait_ge(s.sem, s.cnt)
                eng.waited[s.key] = s.cnt


def _compact(lst):
    best = {}
    for s, v in lst:
        if best.get(s.key, (None, -1))[1] < v:
            best[s.key] = (s, v)
    return list(best.values())


def build_fused(S, depth=DEPTH):
    NT = S // 512
    NK = S // 128
    SQ = S // 4
    nc = bass.Bass("TRN2", target_bir_lowering=False)
    din = lambda n, sh, dt=F32: nc.dram_tensor(n, list(sh), dt, kind="ExternalInput").ap()
    xT0_d = din("xT", [1024, S])
    xq_d = din("xq", [SQ, 1024])
    wfm_d = din("wfm", [depth, 1024, NG * 128])
    wtm_d = din("wtm", [depth, 1024, 257])
    wout_d = din("wout", [depth, 256, 1024])
    w1_d = din("w1", [depth, 2, 2048, 128])
    w2_d = din("w2", [depth, 128, 128])
    post_d = din("post", [depth, 128, 32])
    vec_d = din("vecs", [depth, 128, 8])
    lng_d = din("lng", [depth, 128, 1024])
    lnb_d = din("lnb", [depth, 128, 1024])
    tab_d = din("tab", [128, 4, S])
    ind_d = din("ind", [64, S])
    ctri_d = din("c_tri", [128, 128])
    ctriu_d = din("c_triu", [128, 128])
    cid_d = din("c_ident", [128, 128])
    cmk_d = din("c_maskc", [128, 2048])
    cov_d = din("c_ov", [128, 4, 128])
    y_d = nc.dram_tensor("y", [SQ, 1024], F32, kind="ExternalOutput").ap()
    TPC = 1
    NCH = NT // TPC
    part_d = [nc.dram_tensor(f"part_i{c}", [TPC * 512, 1024], F32).ap() for c in range(NCH)]
    rs_d = [nc.dram_tensor(f"rs_i{c}", [TPC * 128, 1024], F32).ap() for c in range(NCH)]
    NTB = SQ // 128
    ytq_d = [nc.dram_tensor(f"ytq_i{i}", [1024, 256], BF16).ap() for i in range(NTB // 2)]
    xT1_d = [nc.dram_tensor(f"xT1_i{i}", [4 * 1024, 256], BF16).ap() for i in range(NTB // 2)]
    yq_d = nc.dram_tensor("yq_i", [SQ, 1024], F32).ap()
    RG = [[0, 1, 2, 3], [4, 5, 6, 7]]

    with ExitStack() as st:
        cx = Ctx(nc, st)
        pe, dve, act, pool, sp = cx.pe, cx.dve, cx.act, cx.pool, cx.sp
        V, G, T, A = nc.vector, nc.gpsimd, nc.tensor, nc.scalar
        sb = cx.sbuf
        WFM = sb("WFM", [128, 8, NG * 128], BF16); rWFM = cx.res()
        WTM = sb("WTM", [128, 8, 257], BF16); rWTM = cx.res()
        WOUT = sb("WOUT", [128, 2, 1024], BF16); rWOUT = cx.res()
        W1 = sb("W1", [128, 32, 128], BF16); rW1 = cx.res()
        W2 = sb("W2", [128, 128], BF16); rW2 = cx.res()
        POST = sb("POST", [128, 32], BF16); rPOST = cx.res()
        BH = sb("BH", [128, 2], F32); rBH = cx.res()
        KTA = sb("KTA", [128, S], BF16); rKTA = cx.res()
        KTS = sb("KTS", [128, S], BF16); rKTS = cx.res()
        KTW = sb("KTW", [128, 1024], BF16); rKTW = cx.res()
        VA = sb("VA", [128, NK, 192], BF16); rVA = cx.res()
        VN = sb("VN", [128, NK, 192], BF16); rVN = cx.res()
        VCA = sb("VCA", [128, 4, 128], BF16); rVCA = cx.res()
        KCT = sb("KCT", [128, 512], BF16); rKCT = cx.res()
        VCTF = sb("VCTF", [64, 512], F32); rVCTF = cx.res()
        CMPIN = sb("CMPIN", [128, 528], BF16); rCMPIN = cx.res()
        XB = sb("XB", [128, 8, 512], BF16); rXB = cx.res()
        TAB = sb("TAB", [128, 4, 512], F32); rTAB = cx.res()
        TRI = sb("TRI", [128, 128], BF16); TRIU = sb("TRIU", [128, 128], BF16)
        MASKC = sb("MASKC", [128, 2048], BF16); OV = sb("OV", [128, 4, 128], BF16)
        IDF = sb("IDF", [128, 128], F32); TRIF = sb("TRIF", [128, 128], F32); ONESF = sb("ONESF", [128, 128], F32)
        VEC = sb("VEC", [128, 8], F32)
        rC = cx.res()
        rC2 = cx.res()
        QF = sb("QF", [128, 512], BF16); QD0 = sb("QD0", [128, 512], BF16); QD1 = sb("QD1", [128, 512], BF16); rQA = cx.res()
        QN = sb("QN", [128, 4, 512], BF16); rQN = cx.res()
        QSA = [sb(f"QSA{i}", [128, 2, 256], BF16) for i in range(2)]; rQSA = [[cx.res(), cx.res()] for _ in range(2)]
        T0 = sb("T0", [128, 512], F32); rT0 = cx.res()
        T1 = sb("T1", [128, 512], F32); rT1 = cx.res()
        Z1 = sb("Z1", [128, 512], F32); rZ1 = cx.res()
        GZ = sb("GZ", [128, 3, 2, 512], BF16); rGZ = cx.res()
        NPT = 5
        ptc = [0]
        PT = [sb(f"PT{i}", [128, 512], BF16) for i in range(NPT)]; rPT = [cx.res() for _ in range(NPT)]
        FA = sb("FA", [128, 512], F32); rFA = cx.res()
        FB = sb("FB", [128, 512], F32); rFB = cx.res()
        CT0 = sb("CT0", [128, 512], BF16); rCT0 = cx.res()
        CT1 = sb("CT1", [128, 512], BF16); rCT1 = cx.res()
        OUTS = sb("OUTS", [128, 1024], F32); rOUTS = cx.res()
        CNEG = sb("CNEG", [128, NK], F32); rCNEG = cx.res()
        CARRY = sb("CARRY", [128, NK + 4], F32); rCARRY = cx.res()
        SPL = sb("SPL", [128, NK], F32); rSPL = cx.res()
        NB = sb("NB", [128, NK], F32); rNB = cx.res()
        LFRAW = sb("LFRAW", [128, NK], F32); rLFRAW = cx.res()
        SC = sb("SC", [128, 128], F32); rSC = cx.res()
        SC2 = sb("SC2", [128, 128], F32); rSC2 = cx.res()
        M8 = sb("M8", [128, 8], F32); M8b = sb("M8b", [128, 8], F32); rM8 = cx.res(); rM8b = cx.res()
        MNEG2 = [sb(f"MNEG{i}", [128, 128], F32) for i in range(2)]; rMNEG2 = [cx.res(), cx.res()]
        RS4 = sb("RS4", [128, 4], F32); rRS4 = cx.res()
        RI4 = sb("RI4", [128, 4], F32); rRI4 = cx.res()
        YA = [sb(f"YA{i}", [64, 256], F32)[:] for i in range(2)]; rYA = [cx.res(), cx.res()]
        YB = sb("YB", [64, 256], F32); rYB = cx.res()
        YC = sb("YC", [64, 256], F32); rYC = cx.res()
        FN = sb("FN", [128, 256], F32); rFN = cx.res()
        HS = sb("HS", [128, 64], BF16); rHS = cx.res()
        HSS = sb("HSS", [128, 64], F32); rHSS = cx.res()
        LAM = sb("LAM", [128, 4], F32); rLAM = cx.res()
        IMPB = cx.psum("IMPB", [128, 512], F32); rIMPB = cx.res(True)
        STB = [cx.psum(f"ST{i}", [128, 512], F32) for i in range(4)]; rST = [cx.res(True) for _ in range(4)]
        OB = [cx.psum(f"OB{i}", [128, 512], F32) for i in range(3)]; rOB = [cx.res(True) for _ in range(3)]
        ALLB = [(IMPB, rIMPB)] + list(zip(STB, rST)) + list(zip(OB, rOB))
        ld = cx.dsem("ld_const"); ldx = cx.dsem("ld_x"); ldt = cx.dsem("ld_tab"); sto = cx.dsem("st_out")
        pjc = [0]

        def pjx():
            i = pjc[0] % 8
            pjc[0] += 1
            return ALLB[i]

        pj = pjx

        ldw = cx.dsem("ld_w"); ccs = cx.dsem("cc_sem"); ldb = [cx.dsem("ld_b0"), cx.dsem("ld_b1")]; sta = [cx.dsem("st_a0"), cx.dsem("st_a1")]; ldg = cx.dsem("ld_g")
        rYQ = cx.res()
        rYTQ = [cx.res() for _ in range(NTB // 2)]; rXT1 = [cx.res() for _ in range(NTB // 2)]
        rPart = [cx.res() for _ in range(NCH)]; rRSd = [cx.res() for _ in range(NCH)]
        styt2 = [cx.dsem("st_yt0"), cx.dsem("st_yt1")]
        rsc = [cx.dsem(f"rs_sem{c}") for c in range(NCH)]
        cx.dma(pool, ld, KTS[64:128, :], ind_d, writes=[rKTS])
        cx.dma(pool, ld, TRI[:], ctri_d, writes=[rC])
        cx.dma(pool, ld, TRIU[:], ctriu_d, writes=[rC])
        cx.dma(pool, ld, MASKC[:], cmk_d, writes=[rC])
        cx.dma(pool, ld, OV[:], cov_d, writes=[rC])
        cx.dma(sp, ld, IDF[:], cid_d, writes=[rC])
        cx.dma(sp, ld, TRIF[:], ctri_d, writes=[rC])
        EPSC = sb("EPSC", [128, 1], F32)
        for r_ in (rKTS, rC):
            r_.w = (ld, ld.cnt)
        cx.op(dve, lambda: V.memset(ONESF[:], 1.0), writes=[rC2])
        IDB = sb("IDB", [128, 128], BF16); TRIN = sb("TRIN", [128, 128], BF16); TRIUN = sb("TRIUN", [128, 128], BF16)
        cx.op(dve, lambda: V.tensor_copy(out=IDB[:], in_=IDF[:]), reads=[rC], writes=[rC2])
        cx.op(dve, lambda: V.tensor_scalar(out=TRIN[:], in0=TRI[:], scalar1=-1.0, scalar2=30000.0, op0=ALU.add, op1=ALU.mult), reads=[rC], writes=[rC2])
        cx.op(dve, lambda: V.tensor_scalar(out=TRIUN[:], in0=TRIU[:], scalar1=-1.0, scalar2=30000.0, op0=ALU.add, op1=ALU.mult), reads=[rC], writes=[rC2])
        cx.op(dve, lambda: V.tensor_scalar(out=MASKC[:], in0=MASKC[:], scalar1=-1.0, scalar2=30000.0, op0=ALU.add, op1=ALU.mult), reads=[rC], writes=[rC])
        cx.op(dve, lambda: V.memset(EPSC[:], 1e-5), writes=[rC2])
        rVEC = cx.res()
        if NK * 96 >= 5120:
            INB = [VA[:].rearrange("p a b -> p (a b)").bitcast(F32)[:, 0:5120].rearrange("p (j n) -> p j n", j=5),
                   VN[:].rearrange("p a b -> p (a b)").bitcast(F32)[:, 0:5120].rearrange("p (j n) -> p j n", j=5)]
            kf = KTA[:].bitcast(F32)
            ACCB = [kf[:, 0:1024], kf[:, 1024:2048]]
            YTB2 = [KTA[:, 4096:5120].rearrange("p (c t) -> p c t", c=8), KTA[:, 5120:6144].rearrange("p (c t) -> p c t", c=8)]
            xf = XB[:].rearrange("p a b -> p (a b)").bitcast(F32)
            GTB, BTB = xf[:, 0:1024], xf[:, 1024:2048]
            ALIAS = True
        else:
            ALIAS = False
            INB = [sb(f"INB{i}", [128, 5, 1024], F32)[:] for i in range(2)]; rINB = [cx.res(), cx.res()]
            ACCB = [sb(f"ACCB{i}", [128, 1024], F32)[:] for i in range(2)]; rACCB = [cx.res(), cx.res()]
            YTB2 = [sb(f"YTB{i}", [128, 8, 128], BF16)[:] for i in range(2)]; rYTB2 = [cx.res(), cx.res()]
            GTB = sb("GTB", [128, 1024], F32)[:]; BTB = sb("BTB", [128, 1024], F32)[:]; rGB = cx.res()
        STT = sb("STT", [128, 2, 6], F32); rSTT = cx.res()
        MV = sb("MV", [128, 4], F32); rMV = cx.res()
        def load_weights(L):
            cx.dma(pool, ldw, WFM[:], wfm_d[L].rearrange("(c p) n -> p c n", p=128), writes=[rWFM])
            cx.dma(pool, ldw, WTM[:], wtm_d[L].rearrange("(c p) n -> p c n", p=128), writes=[rWTM])
            cx.dma(pool, ldw, WOUT[:], wout_d[L].rearrange("(c p) n -> p c n", p=128), writes=[rWOUT])
            cx.dma(pool, ldw, W1[0:64, :, :], w1_d[L, 0].rearrange("(l d) h -> d l h", d=64), writes=[rW1])
            cx.dma(pool, ldw, W1[64:128, :, :], w1_d[L, 1].rearrange("(l d) h -> d l h", d=64), writes=[rW1])
            cx.dma(pool, ldw, W2[:], w2_d[L], writes=[rW2])
            cx.dma(pool, ldw, POST[:], post_d[L], writes=[rPOST])
            cx.dma(pool, ldw, VEC[:], vec_d[L], writes=[rVEC])
            for r_ in (rWFM, rWTM, rWOUT, rW1, rW2, rPOST, rVEC):
                r_.w = (ldw, ldw.cnt)

        load_weights(0)
        for L in range(depth):
          if True:
              cx.op(dve, lambda: V.memset(VA[:, :, 64:128], 1.0), writes=[rVA])
              cx.op(pool, lambda: G.memset(VN[:, :, 64:128], 1.0), writes=[rVN])
              cx.op(dve, lambda: V.memset(VCA[:, :, 0:64], 0.0), writes=[rVCA])
              cx.op(dve, lambda: V.memset(VCA[:, :, 64:128], 1.0), writes=[rVCA])
              cx.op(dve, lambda: V.memset(KCT[:], 0.0), writes=[rKCT])
              cx.op(pool, lambda: G.memset(KTW[:], 0.0), writes=[rKTW])
              cx.op(pool, lambda: G.memset(QN[64:128, :, :], 0.0), writes=[rQN])
              cx.op(dve, lambda: V.memset(QF[:], 0.0), writes=[rQA])
              cx.op(dve, lambda: V.memset(QD0[:], 0.0), writes=[rQA])
              cx.op(pool, lambda: G.memset(QD1[:], 0.0), writes=[rQA])
              cx.op(dve, lambda: V.memset(VCTF[:], 0.0), writes=[rVCTF])
              cx.op(pool, lambda: G.memset(CMPIN[:], 0.0), writes=[rCMPIN])
              cx.op(dve, lambda: V.memset(CARRY[:], 0.0), writes=[rCARRY])
              for s_ in range(2):
                  pb, rpb = pj()
                  lo = 64 * s_
                  for l in range(32):
                      cx.op(pe, lambda l=l, lo=lo, pb=pb: T.matmul(pb[:, 0:1], lhsT=W1[lo:lo + 64, l, :], rhs=POST[lo:lo + 64, l:l + 1],
                                                                    start=(l == 0), stop=(l == 31)),
                            reads=[rW1, rPOST], writes=[rpb])
                  cx.op(dve, lambda pb=pb, s_=s_: V.tensor_copy(out=BH[:, s_:s_ + 1], in_=pb[:, 0:1]), reads=[rpb], writes=[rBH])
              cx.op(dve, lambda: V.tensor_tensor(out=LAM[:, 0:1], in0=VEC[:, 2:3], in1=VEC[:, 3:4], op=ALU.mult), reads=[rC, rVEC], writes=[rLAM])
              cx.op(dve, lambda: V.tensor_tensor(out=LAM[:, 1:2], in0=VEC[:, 4:5], in1=VEC[:, 5:6], op=ALU.mult), reads=[rC, rC2, rVEC, rLAM], writes=[rLAM])
              pb, rpb = pj()
              cx.op(pe, lambda: T.matmul(pb[:, 0:2], lhsT=ONESF[:], rhs=LAM[:, 0:2], start=True, stop=True), reads=[rC, rC2, rLAM], writes=[rpb])
              cx.op(act, lambda: A.activation(out=LAM[:, 2:4], in_=pb[:, 0:2], func=AF.Exp), reads=[rpb], writes=[rLAM])
              cx.op(dve, lambda: V.tensor_tensor(out=LAM[:, 0:1], in0=LAM[:, 3:4], in1=LAM[:, 2:3], op=ALU.subtract), reads=[rLAM], writes=[rLAM])
              cx.op(dve, lambda: V.tensor_tensor(out=LAM[:, 0:1], in0=LAM[:, 0:1], in1=VEC[:, 6:7], op=ALU.add),
                    reads=[rLAM, rC, rC2, rVEC], writes=[rLAM])
              cx.op(dve, lambda: V.tensor_tensor(out=LAM[:, 1:2], in0=VEC[:, 1:2], in1=VEC[:, 7:8], op=ALU.mult),
                    reads=[rC, rC2, rVEC, rLAM], writes=[rLAM])
              NEGLAM = LAM[:, 0:1]
              GCOL = LAM[:, 1:2]

              def load_tile(Tq):
                  if L == 0:
                      cx.dma(pool, ldx, XB[:], xT0_d[:, Tq * 512:(Tq + 1) * 512].rearrange("(c p) t -> p c t", p=128), writes=[rXB])
                  else:
                      for pc in range(4):
                          row = (Tq % TPC) * 512 + pc * 128
                          rk, ib = row // (TPC * 128), (Tq // TPC) * TPC + (row % (TPC * 128)) // 128
                          cx.dma(pool, ldx, XB[:, :, pc * 128:(pc + 1) * 128],
                                 xT1_d[ib // 2][rk * 1024:(rk + 1) * 1024, (ib % 2) * 128:(ib % 2 + 1) * 128].rearrange("(c p) t -> p c t", p=128),
                                 reads=[rXT1[ib // 2]], writes=[rXB])
                      rXB.w = (ldx, ldx.cnt)
                  cx.dma(sp, ldt, TAB[:], tab_d[:, :, Tq * 512:(Tq + 1) * 512], writes=[rTAB])

              load_tile(0)

              def fm_group(gi):
                  pb, rpb = pjx()
                  for c in range(8):
                      cx.op(pe, lambda c=c, pb=pb: T.matmul(pb[:], lhsT=WFM[:, c, gi * 128:(gi + 1) * 128], rhs=XB[:, c, :],
                                                             start=(c == 0), stop=(c == 7)), reads=[rWFM, rXB], writes=[rpb])
                  return pb, rpb

              def rope(pa, rpa, ps, rps, rows_a, rows_s, tabi, outs):
                  (a0, a1), (s0, s1) = rows_a, rows_s
                  n = a1 - a0
                  cx.op(dve, lambda: V.tensor_tensor(out=T0[0:n, :], in0=pa[a0:a1, :], in1=TAB[0:n, tabi, :], op=ALU.mult),
                        reads=[rpa, rTAB], writes=[rT0])
                  cx.op(dve, lambda: V.tensor_tensor(out=T1[0:n, :], in0=ps[s0:s1, :], in1=TAB[0:n, tabi + 1, :], op=ALU.mult),
                        reads=[rps, rTAB], writes=[rT1])
                  for (r0_, r1_, dst, rdst) in outs:
                      cx.op(pool, lambda r0_=r0_, r1_=r1_, dst=dst: G.tensor_tensor(out=dst, in0=T0[r0_:r1_, :], in1=T1[r0_:r1_, :], op=ALU.add),
                            reads=[rT0, rT1], writes=[rdst])

              def bcast_h(ap2d, h):
                  return bass.AP(tensor=ap2d.tensor, offset=ap2d.offset, ap=[list(ap2d.ap[0]), [0, h], list(ap2d.ap[1])])

              SGT, rSGT, ZN, rZN = FA, rFA, FB, rFB
              obc = [0]
              stc = [0]
              DSK = int(os.environ.get("DBG_DSK", "3"))

              def ob_next():
                  k = obc[0] % 3
                  obc[0] += 1
                  return OB[k], rOB[k]

              def st_next():
                  k = stc[0] % 4
                  stc[0] += 1
                  return STB[k], rST[k]

              for Tq in range(NT):
                  c0t = Tq * 512
                  pA, rA_ = fm_group(0)
                  pS, rS_ = fm_group(2)
                  rope(pA, rA_, pS, rS_, (0, 64), (0, 64), 2, [(0, 32, QD0[0:32, :], rQA), (32, 64, QD1[32:64, :], rQA)])
                  cx.op(act, lambda: A.copy(out=QF[64:128, :], in_=pA[64:128, :]), reads=[rA_], writes=[rQA])
                  pK, rK_ = fm_group(1)
                  rope(pK, rK_, pS, rS_, (0, 64), (64, 128), 2, [(0, 64, KTA[0:64, c0t:c0t + 512], rKTA)])
                  cx.op(act, lambda: A.copy(out=KTA[64:128, c0t:c0t + 512], in_=pK[64:128, :]), reads=[rK_], writes=[rKTA])
                  for (ga, gs, hq) in ((3, 4, 0), (5, 6, 2)):
                      p1, r1 = fm_group(ga)
                      p2, r2 = fm_group(gs)
                      rope(p1, r1, p2, r2, (0, 128), (0, 128), 0, [(0, 64, QN[0:64, hq, :], rQN), (64, 128, QN[0:64, hq + 1, :], rQN)])
                  p1, r1 = fm_group(7)
                  p2, r2 = fm_group(8)
                  rope(p1, r1, p2, r2, (0, 128), (0, 128), 0, [(0, 64, CMPIN[0:64, 16:528], rCMPIN), (64, 128, KTS[0:64, c0t:c0t + 512], rKTS)])
                  p1, r1 = fm_group(9)
                  p2, r2 = fm_group(10)
                  wslot = (Tq % 2) * 512
                  rope(p1, r1, p2, r2, (0, 64), (0, 64), 0, [(0, 64, KTW[0:64, wslot:wslot + 512], rKTW)])
                  cx.op(act, lambda: A.copy(out=CMPIN[64:128, 16:528], in_=p1[64:128, :]), reads=[r1], writes=[rCMPIN])
                  p1, r1 = fm_group(11)
                  cx.op(act, lambda: A.activation(out=Z1[:], in_=p1[:], func=AF.Sigmoid), reads=[r1], writes=[rZ1])
                  cx.op(dve, lambda: V.tensor_tensor(out=Z1[:], in0=p1[:], in1=Z1[:], op=ALU.mult), reads=[r1, rZ1], writes=[rZ1])
                  p1, r1 = fm_group(12)
                  cx.op(act, lambda: A.activation(out=ZN[:], in_=p1[:], func=AF.Sigmoid), reads=[r1], writes=[rZN])
                  cx.op(dve, lambda: V.tensor_tensor(out=ZN[:], in0=p1[:], in1=ZN[:], op=ALU.mult), reads=[r1, rZN], writes=[rZN])
                  sgbufs = ((FA, rFA), (T0, rT0), (T1, rT1))
                  for br in range(3):
                      p1, r1 = fm_group(13 + br)
                      SGb, rSGb = sgbufs[br]
                      cx.op(act, lambda: A.activation(out=SGb[:], in_=p1[:], func=AF.Sigmoid), reads=[r1], writes=[rSGb])
                      ob = 0 if br == 2 else 64
                      for h in range(2):
                          eng_, E_ = ((pool, G), (dve, V))[(2 * br + h) % 2]
                          cx.op(eng_, lambda h=h, ob=ob, br=br, E_=E_: E_.tensor_tensor(out=GZ[ob:ob + 64, br, h, :], in0=SGb[64 * h:64 * h + 64, :],
                                                                                       in1=ZN[64 * h:64 * h + 64, :], op=ALU.mult),
                                reads=[rSGb, rZN], writes=[rGZ])
                  for st_ in range(4):
                      kt = Tq * 4 + st_
                      pb, rpb = pjx()
                      for c in range(8):
                          cx.op(pe, lambda c=c, pb=pb, st_=st_: T.matmul(pb[:, 0:257], lhsT=XB[:, c, st_ * 128:(st_ + 1) * 128], rhs=WTM[:, c, :],
                                                                          start=(c == 0), stop=(c == 7)), reads=[rXB, rWTM], writes=[rpb])
                      cx.op(act, lambda pb=pb, kt=kt: A.copy(out=VA[:, kt, :].rearrange("p (a b) -> p a b", a=3)[:, 0::2, :],
                                                             in_=pb[:, 0:128].rearrange("p (a b) -> p a b", a=2)), reads=[rpb], writes=[rVA])
                      cx.op(dve, lambda pb=pb, kt=kt: V.tensor_copy(out=VN[:, kt, :].rearrange("p (a b) -> p a b", a=3)[:, 0::2, :],
                                                                    in_=pb[:, 128:256].rearrange("p (a b) -> p a b", a=2)), reads=[rpb], writes=[rVN])
                      cx.op(dve, lambda pb=pb, kt=kt: V.tensor_copy(out=LFRAW[:, kt:kt + 1], in_=pb[:, 256:257]), reads=[rpb], writes=[rLFRAW])
                  if Tq + 1 < NT:
                      load_tile(Tq + 1)
                  deferred = []

                  def defer(n, fn, tag):
                      deferred.append([n, fn, tag])

                  def run_deferred(upto_tag=None, everything=False, pred=None):
                      if pred is None and upto_tag is not None:
                          pred = lambda t: t == upto_tag
                      while deferred:
                          n, fn, tag = deferred[0]
                          if everything or n <= 0 or (pred is not None and any(pred(d[2]) for d in deferred)):
                              deferred.pop(0)
                              fn()
                          else:
                              break

                  j0 = 1 if Tq == 0 else 0
                  nj = 32 - j0
                  n0 = 32 * Tq - 1 + j0
                  for s_ in range(2):
                      lo = 64 * s_
                      pb, rpb = pjx()
                      for l in range(32):
                          cx.op(pe, lambda l=l, lo=lo, pb=pb: T.matmul(pb[:, 0:nj], lhsT=W1[lo:lo + 64, l, :],
                                                                        rhs=CMPIN[lo:lo + 64, 16 * j0 + l:16 * j0 + l + 16 * (nj - 1) + 1:16],
                                                                        start=(l == 0), stop=(l == 31)), reads=[rW1, rCMPIN], writes=[rpb])
                      cx.op(act, lambda pb=pb, s_=s_: A.activation(out=HSS[:, 32 * s_:32 * s_ + nj], in_=pb[:, 0:nj], func=AF.Sigmoid, bias=BH[:, s_:s_ + 1]),
                            reads=[rpb, rBH], writes=[rHSS])
                      cx.op(dve, lambda pb=pb, s_=s_: V.scalar_tensor_tensor(out=HS[:, 32 * s_:32 * s_ + nj], in0=pb[:, 0:nj], scalar=BH[:, s_:s_ + 1],
                                                                             in1=HSS[:, 32 * s_:32 * s_ + nj], op0=ALU.add, op1=ALU.mult),
                            reads=[rpb, rBH, rHSS], writes=[rHS])
                  cx.op(pool, lambda: G.tensor_copy(out=CMPIN[:, 0:16], in_=CMPIN[:, 512:528]), reads=[rCMPIN], writes=[rCMPIN])

                  def kc_stage2():
                      pb, rpb = st_next()
                      cx.op(pe, lambda: T.matmul(pb[0:64, 0:nj], lhsT=W2[:, 0:64], rhs=HS[:, 0:nj], start=True, stop=True), reads=[rW2, rHS], writes=[rpb])
                      cx.op(pe, lambda: T.matmul(pb[0:64, 64:64 + nj], lhsT=W2[:, 64:128], rhs=HS[:, 32:32 + nj], start=False, stop=True, skip_group_check=True),
                            reads=[rW2, rHS], writes=[rpb])
                      cx.op(dve, lambda: V.tensor_copy(out=KCT[0:64, n0:n0 + nj], in_=pb[0:64, 0:nj]), reads=[rpb], writes=[rKCT])
                      cx.op(dve, lambda: V.tensor_copy(out=VCTF[0:64, n0:n0 + nj], in_=pb[0:64, 64:64 + nj]), reads=[rpb], writes=[rVCTF])
                      for a_ in sorted(set([max(n0, 0) // 128, (n0 + nj - 1) // 128])):
                          pb2, rpb2 = st_next()
                          cx.op(pe, lambda a_=a_, pb2=pb2: T.transpose(out=pb2[:, 0:64], in_=VCTF[0:64, a_ * 128:(a_ + 1) * 128], identity=IDF[0:64, 0:64]),
                                reads=[rVCTF, rC, rC2], writes=[rpb2])
                          cx.op(dve, lambda a_=a_, pb2=pb2: V.tensor_copy(out=VCA[:, a_, 0:64], in_=pb2[:, 0:64]), reads=[rpb2], writes=[rVCA])
                  defer(4, kc_stage2, ("kc", Tq))

                  k0 = Tq * 4
                  cx.op(dve, lambda: V.tensor_scalar(out=SPL[:, k0:k0 + 4], in0=LFRAW[:, k0:k0 + 4], scalar1=VEC[:, 0:1], scalar2=None, op0=ALU.add),
                        reads=[rLFRAW, rC, rC2, rVEC], writes=[rSPL])
                  cx.op(act, lambda: A.activation(out=SPL[:, k0:k0 + 4], in_=SPL[:, k0:k0 + 4], func=AF.Exp, scale=-1.0), reads=[rSPL], writes=[rSPL])
                  cx.op(act, lambda: A.activation(out=SPL[:, k0:k0 + 4], in_=SPL[:, k0:k0 + 4], func=AF.Ln, bias=1.0), reads=[rSPL], writes=[rSPL])

                  def nb_stage():
                      pb, rpb = st_next()
                      cx.op(pe, lambda: T.matmul(pb[:, 0:4], lhsT=TRIF[:], rhs=SPL[:, k0:k0 + 4], start=True, stop=True), reads=[rC, rC2, rSPL], writes=[rpb])
                      cx.op(pe, lambda: T.matmul(pb[:, 8:12], lhsT=ONESF[:], rhs=SPL[:, k0:k0 + 4], start=False, stop=True, skip_group_check=True), reads=[rC, rC2, rSPL], writes=[rpb])
                      for j in range(4):
                          cx.op(dve, lambda j=j: V.tensor_tensor(out=CARRY[:, k0 + j + 1:k0 + j + 2], in0=CARRY[:, k0 + j:k0 + j + 1],
                                                                 in1=pb[:, 8 + j:9 + j], op=ALU.add), reads=[rpb, rCARRY], writes=[rCARRY])
                      cx.op(dve, lambda: V.tensor_tensor(out=CNEG[:, k0:k0 + 4], in0=pb[:, 0:4], in1=CARRY[:, k0:k0 + 4], op=ALU.add),
                            reads=[rpb, rCARRY], writes=[rCNEG])
                      cx.op(dve, lambda: V.tensor_scalar(out=NB[:, 0:k0 + 4], in0=CNEG[:, 0:k0 + 4], scalar1=CARRY[:, k0:k0 + 1], scalar2=None,
                                                         op0=ALU.subtract), reads=[rCNEG, rCARRY], writes=[rNB])
                  defer(8, nb_stage, ("nb", Tq))
                  chain = []


                  gidc = [0, 0]

                  def begin_group():
                      gidc[0] += 1
                      gidc[1] = 0

                  def add_item(smm, expf, maskf, pvf, after=None, pre=None):
                      chain.append([smm, expf, maskf, pvf, after, pre, gidc[0], gidc[1] == 0])
                      gidc[1] += 1

                  def outproj(st_):
                      for dc in range(2):
                          pb, rpb = st_next()
                          cx.op(pe, lambda dc=dc, pb=pb: T.matmul(pb[:], lhsT=CT0[:, st_ * 128:(st_ + 1) * 128], rhs=WOUT[:, 0, dc * 512:(dc + 1) * 512],
                                                                 start=True, stop=False), reads=[rCT0, rWOUT], writes=[rpb])
                          cx.op(pe, lambda dc=dc, pb=pb: T.matmul(pb[:], lhsT=CT1[:, st_ * 128:(st_ + 1) * 128], rhs=WOUT[:, 1, dc * 512:(dc + 1) * 512],
                                                                 start=False, stop=True), reads=[rCT1, rWOUT], writes=[rpb])
                          cx.op(dve, lambda dc=dc, pb=pb: V.tensor_copy(out=OUTS[:, dc * 512:(dc + 1) * 512], in_=pb[:]), reads=[rpb], writes=[rOUTS])
                      ch = Tq // TPC
                      r0 = (Tq % TPC) * 512 + st_ * 128
                      cx.dma(sp, sto, part_d[ch][r0:r0 + 128, :], OUTS[:], reads=[rOUTS, rPart[ch]])

                  def branch_out(ob_, rob_, br, qc, o_low, Y, rY):
                      so, oo = (64, 0) if o_low else (0, 64)
                      cx.op(dve, lambda: V.tensor_scalar(out=FN[so:so + 64, :], in0=ob_[so:so + 64, 0:256], scalar1=1e-30, scalar2=None, op0=ALU.max),
                            reads=[rob_], writes=[rFN])
                      cx.op(dve, lambda: V.reciprocal(out=FN[so:so + 64, :], in_=FN[so:so + 64, :]), reads=[rFN], writes=[rFN])
                      cx.op(dve, lambda: V.tensor_tensor(out=FN[so:so + 64, :].rearrange("p (h q) -> p h q", h=2),
                                                         in0=FN[so:so + 64, :].rearrange("p (h q) -> p h q", h=2),
                                                         in1=GZ[so:so + 64, br, :, qc:qc + 128], op=ALU.mult), reads=[rFN, rGZ], writes=[rFN])
                      cx.op(dve, lambda: V.tensor_tensor(out=Y, in0=ob_[oo:oo + 64, 0:256], in1=FN[so:so + 64, :], op=ALU.mult),
                            reads=[rob_, rFN], writes=[rY])

                  def cmp_group(bi):
                      begin_group()
                      i = 4 * Tq + bi
                      qc = 128 * bi
                      a_max = i // 16
                      r16 = i % 16
                      par = i % 2
                      ob_, rob_ = ob_next()
                      for a_ in range(a_max + 1):
                          def s_c(sb_, rsb, a_=a_):
                              cx.op(pe, lambda: T.matmul(sb_[:], lhsT=KCT[:, a_ * 128:(a_ + 1) * 128], rhs=QN[:, :, qc:qc + 128],
                                                         start=True, stop=(a_ != a_max)), reads=[rKCT, rQN], writes=[rsb])
                              if a_ == a_max:
                                  cx.op(pe, lambda: T.matmul(sb_[:], lhsT=IDB[:], rhs=bcast_h(MASKC[:, r16 * 128:(r16 + 1) * 128], 4),
                                                             start=False, stop=True), reads=[rC, rC2], writes=[rsb])

                          def e_c(sb_, rsb, pt, rpt, a_=a_):
                              cx.op(act, lambda: A.activation(out=pt[:], in_=sb_[:], func=AF.Exp, scale=0.125), reads=[rsb], writes=[rpt])

                          def m_c(pt, rpt, a_=a_):
                              cx.op(dve, lambda: V.tensor_tensor(out=pt[:].rearrange("p (h q) -> p h q", h=4),
                                                                 in0=pt[:].rearrange("p (h q) -> p h q", h=4),
                                                                 in1=bcast_h(MASKC[:, r16 * 128:(r16 + 1) * 128], 4),
                                                                 op=ALU.mult), reads=[rpt, rC, rC2], writes=[rpt])

                          def p_c(pt, rpt, a_=a_):
                              cx.op(pe, lambda: T.matmul(ob_[:, 0:256], lhsT=VCA[:, a_, :], rhs=pt[:, 0:256], start=(a_ == 0), stop=(a_ == a_max)),
                                    reads=[rVCA, rpt], writes=[rob_])
                              for h in range(4):
                                  cx.op(pe, lambda h=h: T.matmul(IMPB[:, h * 128:(h + 1) * 128], lhsT=pt[:, h * 128:(h + 1) * 128], rhs=OV[:, a_, :],
                                                                 start=(a_ == 0 and h == 0), stop=(a_ == a_max), skip_group_check=True),
                                        reads=[rpt, rC, rC2], writes=[rIMPB])

                          def epi_cmp():
                              branch_out(ob_, rob_, 0, qc, True, YA[par], rYA[par])

                              def stage2():
                                  cx.op(dve, lambda: V.tensor_reduce(out=RS4[:], in_=IMPB[:].rearrange("p (h j) -> p h j", h=4), axis=AX.X, op=ALU.add),
                                        reads=[rIMPB], writes=[rRS4])
                                  cx.op(dve, lambda: V.tensor_scalar(out=RS4[:], in0=RS4[:], scalar1=1e-30, scalar2=None, op0=ALU.max), reads=[rRS4], writes=[rRS4])
                                  cx.op(dve, lambda: V.reciprocal(out=RI4[:], in_=RS4[:]), reads=[rRS4], writes=[rRI4])
                                  cx.op(dve, lambda: V.tensor_scalar(out=SC[:], in0=IMPB[:, 0:128], scalar1=RI4[:, 0:1], scalar2=None, op0=ALU.mult),
                                        reads=[rIMPB, rRI4], writes=[rSC])
                                  for h in range(1, 4):
                                      cx.op(dve, lambda h=h: V.scalar_tensor_tensor(out=SC[:], in0=IMPB[:, h * 128:(h + 1) * 128], scalar=RI4[:, h:h + 1], in1=SC[:],
                                                                                    op0=ALU.mult, op1=ALU.add), reads=[rIMPB, rRI4, rSC], writes=[rSC])

                              def stage3():
                                  cx.op(dve, lambda: V.memset(SC[:, 0:1], BIG), reads=[], writes=[rSC])
                                  lo0 = max(2 * i - 1, 0)
                                  cx.op(dve, lambda: V.memset(SC[0:64, lo0:2 * i + 1], BIG), writes=[rSC])
                                  cx.op(dve, lambda: V.memset(SC[64:128, 2 * i:2 * i + 2], BIG), writes=[rSC])
                                  cx.op(dve, lambda: V.max(out=M8[:], in_=SC[:]), reads=[rSC], writes=[rM8])
                                  cx.op(dve, lambda: V.match_replace(out=SC2[:], in_to_replace=M8[:], in_values=SC[:], imm_value=-BIG),
                                        reads=[rSC, rM8], writes=[rSC2])
                                  cx.op(dve, lambda: V.max(out=M8b[:], in_=SC2[:]), reads=[rSC2], writes=[rM8b])
                                  cx.op(dve, lambda: V.tensor_scalar(out=MNEG2[par][:], in0=SC[:], scalar1=M8b[:, 7:8], scalar2=-30000.0, op0=ALU.is_lt, op1=ALU.mult),
                                        reads=[rSC, rM8b], writes=[rMNEG2[par]])
                                  cx.op(pool, lambda: G.tensor_copy(out=QSA[par][0:64, :, :].rearrange("p a (h q) -> p a h q", h=2),
                                                                    in_=bass.AP(tensor=QN.tensor if hasattr(QN, "tensor") else QN[0:64, 0:2, qc:qc + 128].tensor,
                                                                                offset=QN[0:64, 0:2, qc:qc + 128].offset,
                                                                                ap=[list(QN[0:64, 0:2, qc:qc + 128].ap[0]), [0, 2]] + [list(a) for a in QN[0:64, 0:2, qc:qc + 128].ap[1:]])),
                                        reads=[rQN], writes=[rQSA[par][0], rQSA[par][1]])

                              def stage_b():
                                  mt, rmt = st_next()
                                  cx.op(pe, lambda: T.transpose(out=mt[:, 0:128], in_=MNEG2[par][:], identity=IDF[:]), reads=[rMNEG2[par], rC, rC2], writes=[rmt])
                                  for half in range(2):
                                      cx.op(dve, lambda half=half: V.tensor_copy(out=QSA[par][64:128, half, :].rearrange("p (h q) -> p h q", h=2),
                                                                                 in_=bcast_h(mt[64 * half:64 * half + 64, 0:128], 2)),
                                            reads=[rmt], writes=[rQSA[par][half]])
                              defer(4, stage2, ("c", i))
                              defer(9, stage3, ("c", i))
                              defer(28, stage_b, ("q", i))
                          add_item(s_c, e_c, None, p_c, epi_cmp if a_ == a_max else None,
                                   (lambda: run_deferred(pred=lambda t: t[0] in ("c", "kc"))) if a_ == 0 else None)

                  def win_group(bi):
                      begin_group()
                      i = 4 * Tq + bi
                      qc = 128 * bi
                      ob_, rob_ = ob_next()
                      kts = list(range(max(0, i - 4), i + 1))
                      for kt in kts:
                          def s_w(sb_, rsb, kt=kt):
                              wl = (kt % 8) * 128
                              msk_ = TRIN if kt == i else (TRIUN if kt == i - 4 else None)
                              cx.op(pe, lambda: T.matmul(sb_[:, 0:256], lhsT=KTW[:, wl:wl + 128], rhs=QN[:, 0:2, qc:qc + 128], start=True, stop=(msk_ is None)),
                                    reads=[rKTW, rQN], writes=[rsb])
                              if msk_ is not None:
                                  cx.op(pe, lambda: T.matmul(sb_[:, 0:256], lhsT=IDB[:], rhs=bcast_h(msk_[:], 2), start=False, stop=True),
                                        reads=[rC, rC2], writes=[rsb])

                          def e_w(sb_, rsb, pt, rpt):
                              cx.op(act, lambda: A.activation(out=pt[:, 0:256], in_=sb_[:, 0:256], func=AF.Exp, scale=0.125), reads=[rsb], writes=[rpt])
                          mk = None
                          if kt == i or kt == i - 4:
                              def mk(pt, rpt, kt=kt):
                                  msk = TRI if kt == i else TRIU
                                  cx.op(dve, lambda: V.tensor_tensor(out=pt[:, 0:256].rearrange("p (h q) -> p h q", h=2),
                                                                     in0=pt[:, 0:256].rearrange("p (h q) -> p h q", h=2),
                                                                     in1=bcast_h(msk[:], 2), op=ALU.mult), reads=[rpt, rC, rC2], writes=[rpt])

                          def p_w(pt, rpt, kt=kt):
                              cx.op(pe, lambda: T.matmul(ob_[:, 0:256], lhsT=VN[:, kt, 64:192], rhs=pt[:, 0:256], start=(kt == kts[0]), stop=(kt == kts[-1])),
                                    reads=[rVN, rpt], writes=[rob_])
                          add_item(s_w, e_w, None, p_w, (lambda: branch_out(ob_, rob_, 2, qc, False, YB[:], rYB)) if kt == kts[-1] else None)

                  def sel_group(bi):
                      begin_group()
                      i = 4 * Tq + bi
                      qc = 128 * bi
                      par = i % 2
                      ob_, rob_ = ob_next()
                      for kt in range(i + 1):
                          def s_s(sb_, rsb, kt=kt):
                              half = kt // 32
                              cx.op(pe, lambda: T.matmul(sb_[:, 0:256], lhsT=KTS[:, kt * 128:(kt + 1) * 128], rhs=QSA[par][:, half, :], start=True, stop=(kt != i)),
                                    reads=[rKTS, rQSA[par][half]], writes=[rsb])
                              if kt == i:
                                  cx.op(pe, lambda: T.matmul(sb_[:, 0:256], lhsT=IDB[:], rhs=bcast_h(TRIN[:], 2), start=False, stop=True),
                                        reads=[rC, rC2], writes=[rsb])

                          def e_s(sb_, rsb, pt, rpt):
                              cx.op(act, lambda: A.activation(out=pt[:, 0:256], in_=sb_[:, 0:256], func=AF.Exp, scale=0.125), reads=[rsb], writes=[rpt])
                          mk = None
                          if kt == i:
                              def mk(pt, rpt):
                                  cx.op(dve, lambda: V.tensor_tensor(out=pt[:, 0:256].rearrange("p (h q) -> p h q", h=2),
                                                                     in0=pt[:, 0:256].rearrange("p (h q) -> p h q", h=2),
                                                                     in1=bcast_h(TRI[:], 2), op=ALU.mult), reads=[rpt, rC, rC2], writes=[rpt])

                          def p_s(pt, rpt, kt=kt):
                              cx.op(pe, lambda: T.matmul(ob_[:, 0:256], lhsT=VN[:, kt, 0:128], rhs=pt[:, 0:256], start=(kt == 0), stop=(kt == i)),
                                    reads=[rVN, rpt], writes=[rob_])

                          def epi_sel():
                              branch_out(ob_, rob_, 1, qc, True, YC[:], rYC)
                              cx.op(pool, lambda: G.tensor_tensor(out=YB[:], in0=YA[par], in1=YB[:], op=ALU.add), reads=[rYA[par], rYB], writes=[rYB])
                              for h in range(2):
                                  cx.op(pool, lambda h=h: G.tensor_tensor(out=CT1[64 * h:64 * h + 64, qc:qc + 128], in0=YB[:, 128 * h:128 * h + 128],
                                                                          in1=YC[:, 128 * h:128 * h + 128], op=ALU.add), reads=[rYB, rYC], writes=[rCT1])
                              defer(20, lambda: outproj(bi), ("o", i))
                          add_item(s_s, e_s, None, p_s, epi_sel if kt == i else None, (lambda: run_deferred(upto_tag=("q", i))) if kt == 0 else None)

                  nkt = 4 * Tq + 4

                  def dense_group(kind):
                      begin_group()
                      ob_, rob_ = ob_next()
                      for kt in range(nkt):
                          jd = kt - 4 * Tq
                          cc = 128 * jd if jd >= 0 else 0
                          first, last = (kt == 0), (kt == nkt - 1)

                          def smm(sb_, rsb, kt=kt, cc=cc):
                              qsrc = (QF, QD0, QD1)[kind]
                              dg_ = (kt - 4 * Tq) >= 0
                              cx.op(pe, lambda: T.matmul(sb_[:, cc:512], lhsT=KTA[:, kt * 128:(kt + 1) * 128], rhs=qsrc[:, cc:512],
                                                         start=True, stop=(not dg_)), reads=[rKTA, rQA], writes=[rsb])
                              if dg_:
                                  cx.op(pe, lambda: T.matmul(sb_[:, cc:cc + 128], lhsT=IDB[:], rhs=TRIN[:], start=False, stop=True),
                                        reads=[rC, rC2], writes=[rsb])

                          def expf(sb_, rsb, pt, rpt, kt=kt, cc=cc):
                              if kind == 0:
                                  cx.op(act, lambda: A.activation(out=pt[:, cc:512], in_=sb_[:, cc:512], func=AF.Exp, bias=NB[:, kt:kt + 1], scale=0.125),
                                        reads=[rsb, rNB], writes=[rpt])
                              else:
                                  cx.op(act, lambda: A.activation(out=pt[:, cc:512], in_=sb_[:, cc:512], func=AF.Exp, scale=float(32 ** -0.5)),
                                        reads=[rsb], writes=[rpt])

                          def m_diag(pt, rpt, cc=cc):
                              cx.op(pool, lambda: G.tensor_tensor(out=pt[:, cc:cc + 128], in0=pt[:, cc:cc + 128], in1=TRI[:], op=ALU.mult),
                                    reads=[rpt, rC, rC2], writes=[rpt])

                          def pvf(pt, rpt, kt=kt, cc=cc, first=first, last=last):
                              lhs = VA[:, kt, 0:128] if kind == 0 else VA[:, kt, 64:192]
                              cx.op(pe, lambda: T.matmul(ob_[:, cc:512], lhsT=lhs, rhs=pt[:, cc:512], start=first, stop=last),
                                    reads=[rVA, rpt], writes=[rob_])

                          def epi():
                              if kind == 0:
                                  cx.op(dve, lambda: V.reciprocal(out=FA[64:128, :], in_=ob_[64:128, :]), reads=[rob_], writes=[rFA])
                                  cx.op(dve, lambda: V.tensor_tensor(out=FA[64:128, :], in0=FA[64:128, :], in1=Z1[64:128, :], op=ALU.mult),
                                        reads=[rFA, rZ1], writes=[rFA])
                                  cx.op(dve, lambda: V.tensor_tensor(out=CT0[0:64, :], in0=ob_[0:64, :], in1=FA[64:128, :], op=ALU.mult),
                                        reads=[rob_, rFA], writes=[rCT0])
                              elif kind == 1:
                                  cx.op(dve, lambda: V.reciprocal(out=FB[0:64, :], in_=ob_[0:64, :]), reads=[rob_], writes=[rFB])
                                  cx.op(dve, lambda: V.tensor_tensor(out=T0[0:64, :], in0=ob_[64:128, :], in1=FB[0:64, :], op=ALU.mult),
                                        reads=[rob_, rFB], writes=[rT0])
                              else:
                                  cx.op(dve, lambda: V.reciprocal(out=FB[0:64, :], in_=ob_[0:64, :]), reads=[rob_, rFB], writes=[rFB])
                                  cx.op(dve, lambda: V.tensor_scalar(out=FB[0:64, :], in0=FB[0:64, :], scalar1=NEGLAM[0:64, :], scalar2=None, op0=ALU.mult),
                                        reads=[rFB, rLAM], writes=[rFB])
                                  cx.op(dve, lambda: V.tensor_tensor(out=T1[0:64, :], in0=ob_[64:128, :], in1=FB[0:64, :], op=ALU.mult),
                                        reads=[rob_, rFB], writes=[rT1])
                                  cx.op(pool, lambda: G.tensor_tensor(out=T0[0:64, :], in0=T0[0:64, :], in1=T1[0:64, :], op=ALU.add),
                                        reads=[rT0, rT1], writes=[rT0])
                                  cx.op(pool, lambda: G.tensor_tensor(out=T1[0:64, :], in0=T0[0:64, :], in1=T0[0:64, :], op=ALU.mult),
                                        reads=[rT0, rT1], writes=[rT1])

                                  def stage_b():
                                      pb, rpb = st_next()
                                      cx.op(pe, lambda: T.matmul(pb[0:64, :], lhsT=ONESF[0:64, 0:64], rhs=T1[0:64, :], start=True, stop=True),
                                            reads=[rC, rC2, rT1], writes=[rpb])
                                      cx.op(act, lambda: A.activation(out=FB[0:64, :], in_=pb[0:64, :], func=AF.Ln, scale=1.0 / 64.0, bias=EPSC[0:64, :]),
                                            reads=[rpb, rC, rC2], writes=[rFB])
                                      cx.op(act, lambda: A.activation(out=FB[0:64, :], in_=FB[0:64, :], func=AF.Exp, scale=-0.5), reads=[rFB], writes=[rFB])
                                      cx.op(dve, lambda: V.scalar_tensor_tensor(out=T0[0:64, :], in0=T0[0:64, :], scalar=GCOL[0:64, :], in1=FB[0:64, :],
                                                                                op0=ALU.mult, op1=ALU.mult), reads=[rT0, rFB, rLAM], writes=[rT0])
                                      cx.op(pool, lambda: G.tensor_tensor(out=CT0[64:128, :], in0=T0[0:64, :], in1=Z1[0:64, :], op=ALU.mult),
                                            reads=[rT0, rZ1], writes=[rCT0])
                                  defer(14, stage_b, ("d", Tq))
                          add_item(smm, expf, None, pvf, epi if last else None,
                                   (lambda: run_deferred(pred=lambda t: t[0] == "nb")) if (kind == 0 and kt == 0) else None)

                  if os.environ.get("DBG_ORDER") == "old":
                      dense_group(0)
                      dense_group(1)
                      dense_group(2)
                      for bi in range(4):
                          cmp_group(bi)
                          win_group(bi)
                          sel_group(bi)
                  else:
                      dense_group(1)
                      dense_group(2)
                      cmp_group(0)
                      dense_group(0)
                      for bi in range(4):
                          if bi + 1 < 4:
                              cmp_group(bi + 1)
                          win_group(bi)
                          sel_group(bi)

                  pend = []

                  def pop_pair():
                      X = pend.pop(0)
                      Y = pend.pop(0) if pend else None
                      order = [X] if Y is None else ([X, Y] if (X[5] and X[4] == Y[4]) else [Y, X])
                      for it in order:
                          it[0](it[1], it[2])
                      for it in ([X] if Y is None else [X, Y]):
                          if it[3] is not None:
                              it[3]()
                  ci = 0
                  while ci < len(chain):
                      pair = chain[ci:ci + 2]
                      ci += 2
                      for it in pair:
                          for d_ in deferred:
                              d_[0] -= 1
                      run_deferred()
                      for it in pair:
                          if it[5] is not None:
                              it[5]()
                      slots = []
                      for it in pair:
                          sb_, rsb = st_next()
                          pi = ptc[0] % NPT
                          ptc[0] += 1
                          slots.append((sb_, rsb, PT[pi], rPT[pi]))
                      for it, sl in reversed(list(zip(pair, slots))):
                          it[0](sl[0], sl[1])
                      for it, sl in reversed(list(zip(pair, slots))):
                          it[1](sl[0], sl[1], sl[2], sl[3])
                          if it[2] is not None:
                              it[2](sl[2], sl[3])
                      for it, sl in zip(pair, slots):
                          pend.append((it[3], sl[2], sl[3], it[4], it[6], it[7]))
                      if len(pend) > 2:
                          pop_pair()
                  while pend:
                      pop_pair()
                  run_deferred(everything=True)


                  if Tq % TPC == TPC - 1:
                      ch = Tq // TPC
                      cx._need(pool, [], [rPart[ch], rRSd[ch]])
                      ins = G.collective_compute("ReduceScatter", ALU.add, replica_groups=RG, ins=[part_d[ch].opt()], outs=[rs_d[ch].opt()])
                      rsc[ch].cnt += 1
                      ins.then_inc(rsc[ch].sem)
                      rPart[ch].w = (rsc[ch], rsc[ch].cnt); rPart[ch].r = []
                      rRSd[ch].w = (rsc[ch], rsc[ch].cnt); rRSd[ch].r = []

              if L + 1 < depth:
                  load_weights(L + 1)
              if ALIAS:
                  rINB = cx.fork(rVA, 1) + cx.fork(rVN, 1)
                  rACC0, rACC1, rYT0, rYT1 = cx.fork(rKTA, 4)
                  rACCB = [rACC0, rACC1]
                  rYTB2 = [rYT0, rYT1]
                  (rGB,) = cx.fork(rXB, 1)
              cx.dma(sp, ldg, GTB, lng_d[L], writes=[rGB])
              cx.dma(sp, ldg, BTB, lnb_d[L], writes=[rGB])
              rGB.w = (ldg, ldg.cnt)
              xres_d = xq_d if L == 0 else yq_d
              ntb = SQ // 128
              last = (L == depth - 1)

              cx._need(sp, [], [rYQ])

              cx._need(act, [], [rYQ])

              def loadb(i):
                  k = i % 2
                  r0 = i * 128
                  ch, jj = i // TPC, i % TPC
                  cx.dma(sp, ldb[k], INB[k][:, 0, :], rs_d[ch][jj * 128:(jj + 1) * 128, :], reads=[rRSd[ch]], writes=[rINB[k]])
                  cx.dma(sp, ldb[k], INB[k][:, 4, :], xres_d[r0:r0 + 128, :], writes=[rINB[k]])
                  rINB[k].w = (ldb[k], ldb[k].cnt)

              loadb(0)
              for i in range(ntb):
                  k = i % 2
                  if i + 1 < ntb:
                      loadb(i + 1)
                  I_, Ac, rI, rAc = INB[k], ACCB[k], rINB[k], rACCB[k]
                  cx.op(dve, lambda: V.scalar_tensor_tensor(out=Ac, in0=I_[:, 4, :], scalar=float(ALPHA), in1=I_[:, 0, :], op0=ALU.mult, op1=ALU.add),
                        reads=[rI], writes=[rAc])
                  for h in range(2):
                      cx.op(dve, lambda h=h: V.bn_stats(out=STT[:, h, :], in_=Ac[:, h * 512:(h + 1) * 512]), reads=[rAc], writes=[rSTT])
                  cx.op(dve, lambda: V.bn_aggr(out=MV[:, 0:2], in_=STT[:].rearrange("p a b -> p (a b)")), reads=[rSTT], writes=[rMV])
                  cx.op(dve, lambda: V.tensor_scalar(out=MV[:, 2:3], in0=MV[:, 1:2], scalar1=1e-5, scalar2=None, op0=ALU.add), reads=[rMV], writes=[rMV])
                  cx.op(act, lambda: A.activation(out=MV[:, 2:3], in_=MV[:, 2:3], func=AF.Ln), reads=[rMV], writes=[rMV])
                  cx.op(act, lambda: A.activation(out=MV[:, 3:4], in_=MV[:, 2:3], func=AF.Exp, scale=-0.5), reads=[rMV], writes=[rMV])
                  cx.op(dve, lambda: V.tensor_scalar(out=Ac, in0=Ac, scalar1=MV[:, 0:1], scalar2=MV[:, 3:4], op0=ALU.subtract, op1=ALU.mult),
                        reads=[rAc, rMV], writes=[rAc])
                  cx.op(dve, lambda: V.tensor_tensor(out=Ac, in0=Ac, in1=GTB, op=ALU.mult), reads=[rAc, rGB], writes=[rAc])
                  cx.op(dve, lambda: V.tensor_tensor(out=Ac, in0=Ac, in1=BTB, op=ALU.add), reads=[rAc, rGB], writes=[rAc])
                  if last:
                      cx.dma(act, sta[k], y_d[i * 128:(i + 1) * 128, :], Ac, reads=[rAc])
                  else:
                      cx.dma(act, sta[k], yq_d[i * 128:(i + 1) * 128, :], Ac, reads=[rAc, rYQ])
                      for hb in range(2):
                          pb, rpb = pj()
                          for c4 in range(4):
                              c = 4 * hb + c4
                              cx.op(pe, lambda c=c, c4=c4, pb=pb: T.transpose(out=pb[:, c4 * 128:(c4 + 1) * 128], in_=Ac[:, c * 128:(c + 1) * 128], identity=IDF[:]),
                                    reads=[rAc, rC], writes=[rpb])
                          cx.op(act if hb == 0 else dve, lambda hb=hb, pb=pb: (A.copy if hb == 0 else V.tensor_copy)(out=YTB2[k][:, 4 * hb:4 * hb + 4, :], in_=pb[:].rearrange("p (c t) -> p c t", c=4)),
                                reads=[rpb], writes=[rYTB2[k]])
                      i2 = i // 2
                      for c in range(8):
                          cx.dma(act, styt2[k], ytq_d[i2][c * 128:(c + 1) * 128, (i % 2) * 128:(i % 2 + 1) * 128], YTB2[k][:, c, :],
                                 reads=[rYTB2[k], rYTQ[i2]])
                      if i % 2 == 1:
                          cx._need(pool, [], [rYTQ[i2], rXT1[i2]])
                          ins = G.collective_compute("AllGather", ALU.bypass, replica_groups=RG, ins=[ytq_d[i2].opt()], outs=[xT1_d[i2].opt()])
                          ccs.cnt += 1
                          ins.then_inc(ccs.sem)
                          rYTQ[i2].w = (ccs, ccs.cnt); rYTQ[i2].r = []
                          rXT1[i2].w = (ccs, ccs.cnt); rXT1[i2].r = []
              if ALIAS:
                  cx.join(rVA, rINB[0:1]); cx.join(rVN, rINB[1:2]); cx.join(rKTA, [rACC0, rACC1, rYT0, rYT1]); cx.join(rXB, [rGB])
        cx.wait_all(sp, sta)
    return nc


def _consts(S):
    half = 32
    t = np.arange(S, dtype=np.float32)
    tab = np.zeros((128, 4, S), np.float32)
    inv = (10000.0 ** (-(np.arange(32, dtype=np.float32) * 2.0 / 64))).astype(np.float32)
    ang = t[None, :] * inv[:, None]
    cosn, sinn = np.cos(ang), np.sin(ang)
    for r in range(128):
        i = r % 64
        tab[r, 0] = cosn[i % 32]
        tab[r, 1] = -sinn[i % 32] if i < 32 else sinn[i % 32]
    invd = (10000.0 ** (-(np.arange(16, dtype=np.float32) * 2.0 / 32))).astype(np.float32)
    angd = t[None, :] * invd[:, None]
    cosd, sind = np.cos(angd), np.sin(angd)
    for r in range(64):
        w = r % 32
        tab[r, 2] = cosd[w % 16]
        tab[r, 3] = -sind[w % 16] if w < 16 else sind[w % 16]
    key = np.arange(S)
    ind = (((key[None, :] // 64) % 64) == np.arange(64)[:, None]).astype(np.float32)
    p = np.arange(128)
    c_tri = (p[:, None] <= p[None, :]).astype(np.float32)
    c_triu = (p[:, None] > p[None, :]).astype(np.float32)
    c_ident = np.eye(128, dtype=np.float32)
    u = np.arange(2048)
    c_maskc = ((16 * p[:, None] + 31) <= u[None, :]).astype(np.float32)
    n = np.arange(512)
    j = np.arange(128)
    ov = np.clip(np.minimum(16 * n[:, None] + 32, 64 * j[None, :] + 64) - np.maximum(16 * n[:, None], 64 * j[None, :]), 0, None)
    ov = (ov.astype(np.float32) / 32.0)
    ov[511] = 0.0
    c_ov = np.ascontiguousarray(ov.reshape(4, 128, 128).transpose(1, 0, 2))
    return dict(tab=tab, ind=ind, c_tri=c_tri, c_triu=c_triu, c_ident=c_ident, c_maskc=c_maskc, c_ov=c_ov)


def _core_cols(c):
    g, hp = c // 2, c % 2
    rng64 = np.arange(64)
    sw64 = (rng64 + 32) % 64
    d32 = np.arange(32)
    swd = np.concatenate([(d32 + 16) % 32, 32 + (d32 + 16) % 32])
    dq = OFF['diff_q'] + 64 * c + rng64
    dk = OFF['diff_k'] + 64 * c + rng64
    dq_sw = OFF['diff_q'] + 64 * c + swd
    dk_sw = OFF['diff_k'] + 64 * c + swd
    fq = OFF['fox_q'] + 64 * c + rng64
    fk = OFF['fox_k'] + 64 * c + rng64
    my = [4 * g + 2 * hp, 4 * g + 2 * hp + 1]
    oth = [4 * g + 2 * (1 - hp), 4 * g + 2 * (1 - hp) + 1]
    nq = lambda H, perm: OFF['nsa_q'] + 64 * H + perm
    kc = lambda perm: OFF['nsa_k_cmp'] + 64 * g + perm
    ks = lambda perm: OFF['nsa_k_sel'] + 64 * g + perm
    kw = lambda perm: OFF['nsa_k_win'] + 64 * g + perm
    vcmp = OFF['nsa_v_cmp'] + 64 * g + rng64
    gate = lambda H, br: np.full(64, OFF['nsa_gate'] + 3 * H + br)
    groups = [
        np.concatenate([dq, fq]), np.concatenate([dk, fk]), np.concatenate([dq_sw, dk_sw]),
        np.concatenate([nq(my[0], rng64), nq(my[1], rng64)]), np.concatenate([nq(my[0], sw64), nq(my[1], sw64)]),
        np.concatenate([nq(oth[0], rng64), nq(oth[1], rng64)]), np.concatenate([nq(oth[0], sw64), nq(oth[1], sw64)]),
        np.concatenate([kc(rng64), ks(rng64)]), np.concatenate([kc(sw64), ks(sw64)]),
        np.concatenate([kw(rng64), vcmp]), np.concatenate([kw(sw64), vcmp]),
        np.concatenate([OFF['diff_z'] + 64 * c + rng64, OFF['fox_z'] + 64 * c + rng64]),
        np.concatenate([OFF['nsa_z'] + 64 * my[0] + rng64, OFF['nsa_z'] + 64 * my[1] + rng64]),
        np.concatenate([gate(my[0], 0), gate(my[1], 0)]), np.concatenate([gate(my[0], 1), gate(my[1], 1)]),
        np.concatenate([gate(my[0], 2), gate(my[1], 2)]),
    ]
    fm = np.concatenate(groups)
    tm = np.concatenate([OFF['fox_v'] + 64 * c + rng64, OFF['diff_v'] + 64 * c + rng64,
                         OFF['nsa_v_sel'] + 64 * g + rng64, OFF['nsa_v_win'] + 64 * g + rng64,
                         np.array([OFF['fox_f'] + c])])
    wo = np.concatenate([64 * c + rng64, 256 + 512 + 64 * c + rng64, 256 + 64 * my[0] + rng64, 256 + 64 * my[1] + rng64])
    return fm, tm, wo


_PROG = {}


def _get(S, depth):
    k = (S, depth)
    if k not in _PROG:
        _PROG[k] = build_fused(S, depth)
    return _PROG[k]


def kernel(x, w_in, b_fox_f, cmp_pos_k, cmp_pos_v, cmp_w1_k, cmp_w2_k, cmp_w1_v, cmp_w2_v,
           lam_q1, lam_k1, lam_q2, lam_k2, diff_subln_g, w_out, ln_g, ln_b):
    f32 = lambda a: np.ascontiguousarray(np.asarray(a, dtype=np.float32))
    x = f32(x)
    B, S, D = x.shape
    depth = w_in.shape[0]
    SQ = S // 4
    w_in, w_out = f32(w_in), f32(w_out)
    cst = _consts(S)
    cols = [_core_cols(c) for c in range(4)]
    w1 = np.ascontiguousarray(np.stack([f32(cmp_w1_k), f32(cmp_w1_v)], axis=1))
    w2 = np.ascontiguousarray(np.concatenate([f32(cmp_w2_k), f32(cmp_w2_v)], axis=2))
    post = np.ascontiguousarray(np.concatenate([f32(cmp_pos_k).transpose(0, 2, 1), f32(cmp_pos_v).transpose(0, 2, 1)], axis=1))
    lng = np.ascontiguousarray(np.broadcast_to(f32(ln_g)[:, None, :], (depth, 128, D)))
    lnb = np.ascontiguousarray(np.broadcast_to(f32(ln_b)[:, None, :], (depth, 128, D)))
    NT = S // 512
    TPC = 1

    def tokmap(r):
        idx = []
        for i in range(SQ // 128):
            base = (i // TPC) * TPC * 512 + r * TPC * 128 + (i % TPC) * 128
            idx.append(np.arange(base, base + 128))
        return np.concatenate(idx)
    tmaps = [tokmap(r) for r in range(4)]
    xT = [np.ascontiguousarray(x[b].T) for b in range(B)]
    in_maps = []
    for core in range(8):
        b, c = core // 4, core % 4
        fm, tm, wo = cols[c]
        vec = np.zeros((depth, 128, 8), np.float32)
        for l in range(depth):
            lam_init = 0.8 - 0.6 * math.exp(-0.3 * l)
            vec[l, :, 0] = f32(b_fox_f)[l, c]
            vec[l, :, 1] = f32(diff_subln_g)[l][np.arange(128) % 64]
            vec[l, 0:32, 2] = f32(lam_q1)[l]; vec[l, 0:32, 3] = f32(lam_k1)[l]
            vec[l, 0:32, 4] = f32(lam_q2)[l]; vec[l, 0:32, 5] = f32(lam_k2)[l]
            vec[l, :, 6] = -lam_init
            vec[l, :, 7] = 1.0 - lam_init
        m = dict(xT=xT[b], xq=np.ascontiguousarray(x[b, tmaps[c]]),
                 wfm=np.ascontiguousarray(w_in[:, :, fm]), wtm=np.ascontiguousarray(w_in[:, :, tm]),
                 wout=np.ascontiguousarray(w_out[:, wo, :]), w1=w1, w2=w2, post=post, vecs=vec, lng=lng, lnb=lnb)
        m.update(cst)
        in_maps.append(m)
    res = run_bass_kernel_spmd(_get(S, depth), in_maps, core_ids=list(range(8)))
    out = np.empty((B, S, D), np.float32)
    for core in range(8):
        b, c = core // 4, core % 4
        out[b, tmaps[c]] = res.results[core]["y"]
    return out
```

```python
import math
import os
from contextlib import ExitStack
import numpy as np
import concourse.bass as bass
import concourse.mybir as mybir
from concourse.bass_utils import run_bass_kernel_spmd

F32 = mybir.dt.float32
BF16 = mybir.dt.bfloat16
ALU = mybir.AluOpType
AF = mybir.ActivationFunctionType
AX = mybir.AxisListType

D_MODEL = 1024
DEPTH = 2
HD = 64
IN_SPLITS = (('fox_q', 256), ('fox_k', 256), ('fox_v', 256), ('fox_f', 4), ('fox_z', 256), ('nsa_q', 512),
             ('nsa_k_cmp', 128), ('nsa_v_cmp', 128), ('nsa_k_sel', 128), ('nsa_v_sel', 128),
             ('nsa_k_win', 128), ('nsa_v_win', 128), ('nsa_gate', 24), ('nsa_z', 512),
             ('diff_q', 256), ('diff_k', 256), ('diff_v', 256), ('diff_z', 256))
OFF = {}
_a = 0
for _n, _w in IN_SPLITS:
    OFF[_n] = _a
    _a += _w
IN_WIDTH = _a
ALPHA = (2 * DEPTH) ** 0.25
NG = 16
BIG = 1.0e30


class Res:
    __slots__ = ("w", "r", "excl")

    def __init__(self, excl=False):
        self.excl = excl
        self.w = None
        self.r = []


class SemC:
    __slots__ = ("sem", "cnt", "key")
    _n = 0

    def __init__(self, sem):
        self.sem = sem
        self.cnt = 0
        SemC._n += 1
        self.key = SemC._n


class Eng:
    def __init__(self, h, semc, same_sync):
        self.h = h
        self.s = semc
        self.waited = {}
        self.same_sync = same_sync


class Ctx:
    def __init__(self, nc, stack):
        self.nc = nc
        self.stack = stack
        mk = lambda n: SemC(stack.enter_context(nc.semaphore(n)))
        self.pe = Eng(nc.tensor, mk("s_pe"), False)
        self.dve = Eng(nc.vector, mk("s_dve"), True)
        self.act = Eng(nc.scalar, mk("s_act"), True)
        self.pool = Eng(nc.gpsimd, mk("s_pool"), True)
        self.sp = Eng(nc.sync, mk("s_sp"), False)

    def res(self, excl=False):
        return Res(excl)

    def dsem(self, name):
        return SemC(self.stack.enter_context(self.nc.semaphore(name)))

    def sbuf(self, name, shape, dt):
        return self.stack.enter_context(self.nc.sbuf_tensor(name, list(shape), dt))

    def psum(self, name, shape, dt):
        return self.stack.enter_context(self.nc.psum_tensor(name, list(shape), dt))

    def _need(self, eng, reads, writes):
        need = {}

        def add(p):
            if p is None:
                return
            s, v = p
            if s is eng.s and not eng.same_sync:
                return
            if need.get(s.key, (None, -1))[1] < v:
                need[s.key] = (s, v)
        for r in reads:
            add(r.w)
            if r.excl:
                for p in r.r:
                    if p[0] is not eng.s:
                        add(p)
        for w in writes:
            add(w.w)
            for p in w.r:
                add(p)
        for k, (s, v) in need.items():
            if eng.waited.get(k, 0) < v:
                eng.h.wait_ge(s.sem, v)
                eng.waited[k] = v

    def op(self, eng, fn, reads=(), writes=()):
        self._need(eng, reads, writes)
        ins = fn()
        eng.s.cnt += 1
        ins.then_inc(eng.s.sem, 1)
        me = (eng.s, eng.s.cnt)
        for r in reads:
            r.r.append(me)
            if len(r.r) > 24:
                r.r = r.r[-24:] if False else _compact(r.r)
        for w in writes:
            w.w = me
            w.r = []
        return ins

    def dma(self, q, dsem, out, in_, reads=(), writes=()):
        self._need(q, reads, writes)
        ins = q.h.dma_start(out=out, in_=in_)
        dsem.cnt += 16
        ins.then_inc(dsem.sem, 16)
        me = (dsem, dsem.cnt)
        for r in reads:
            r.r.append(me)
        for w in writes:
            w.w = me
            w.r = []
        return ins

    def fork(self, parent, n):
        kids = [Res() for _ in range(n)]
        for k in kids:
            k.r = _compact(list(parent.r) + ([parent.w] if parent.w else []))
        return kids

    def join(self, parent, kids):
        for k in kids:
            parent.r = _compact(parent.r + k.r + ([k.w] if k.w else []))

    def wait_all(self, eng, semcs):
        for s in semcs:
            if s.cnt > 0 and eng.waited.get(s.key, 0) < s.cnt:
                eng.h.wait_ge(s.sem, s.cnt)
                eng.waited[s.key] = s.cnt


def _compact(lst):
    best = {}
    for s, v in lst:
        if best.get(s.key, (None, -1))[1] < v:
            best[s.key] = (s, v)
    return list(best.values())


def build_fused(S, depth=DEPTH):
    NT = S // 512
    NK = S // 128
    SQ = S // 4
    nc = bass.Bass("TRN2", target_bir_lowering=False)
    din = lambda n, sh, dt=F32: nc.dram_tensor(n, list(sh), dt, kind="ExternalInput").ap()
    xT0_d = din("xT", [1024, S])
    xq_d = din("xq", [SQ, 1024])
    wfm_d = din("wfm", [depth, 1024, NG * 128])
    wtm_d = din("wtm", [depth, 1024, 257])
    wout_d = din("wout", [depth, 256, 1024])
    w1_d = din("w1", [depth, 2, 2048, 128])
    w2_d = din("w2", [depth, 128, 128])
    post_d = din("post", [depth, 128, 32])
    vec_d = din("vecs", [depth, 128, 8])
    lng_d = din("lng", [depth, 128, 1024])
    lnb_d = din("lnb", [depth, 128, 1024])
    tab_d = din("tab", [128, 4, S])
    ind_d = din("ind", [64, S])
    ctri_d = din("c_tri", [128, 128])
    ctriu_d = din("c_triu", [128, 128])
    cid_d = din("c_ident", [128, 128])
    cmk_d = din("c_maskc", [128, 2048])
    cov_d = din("c_ov", [128, 4, 128])
    y_d = nc.dram_tensor("y", [SQ, 1024], F32, kind="ExternalOutput").ap()
    TPC = 1
    NCH = NT // TPC
    part_d = [nc.dram_tensor(f"part_i{c}", [TPC * 512, 1024], F32).ap() for c in range(NCH)]
    rs_d = [nc.dram_tensor(f"rs_i{c}", [TPC * 128, 1024], F32).ap() for c in range(NCH)]
    NTB = SQ // 128
    ytq_d = [nc.dram_tensor(f"ytq_i{i}", [1024, 256], BF16).ap() for i in range(NTB // 2)]
    xT1_d = [nc.dram_tensor(f"xT1_i{i}", [4 * 1024, 256], BF16).ap() for i in range(NTB // 2)]
    yq_d = nc.dram_tensor("yq_i", [SQ, 1024], F32).ap()
    RG = [[0, 1, 2, 3], [4, 5, 6, 7]]

    with ExitStack() as st:
        cx = Ctx(nc, st)
        pe, dve, act, pool, sp = cx.pe, cx.dve, cx.act, cx.pool, cx.sp
        V, G, T, A = nc.vector, nc.gpsimd, nc.tensor, nc.scalar
        sb = cx.sbuf
        WFM = sb("WFM", [128, 8, NG * 128], BF16); rWFM = cx.res()
        WTM = sb("WTM", [128, 8, 257], BF16); rWTM = cx.res()
        WOUT = sb("WOUT", [128, 2, 1024], BF16); rWOUT = cx.res()
        W1 = sb("W1", [128, 32, 128], BF16); rW1 = cx.res()
        W2 = sb("W2", [128, 128], BF16); rW2 = cx.res()
        POST = sb("POST", [128, 32], BF16); rPOST = cx.res()
        BH = sb("BH", [128, 2], F32); rBH = cx.res()
        KTA = sb("KTA", [128, S], BF16); rKTA = cx.res()
        KTS = sb("KTS", [128, S], BF16); rKTS = cx.res()
        KTW = sb("KTW", [128, 1024], BF16); rKTW = cx.res()
        VA = sb("VA", [128, NK, 192], BF16); rVA = cx.res()
        VN = sb("VN", [128, NK, 192], BF16); rVN = cx.res()
        VCA = sb("VCA", [128, 4, 128], BF16); rVCA = cx.res()
        KCT = sb("KCT", [128, 512], BF16); rKCT = cx.res()
        VCTF = sb("VCTF", [64, 512], F32); rVCTF = cx.res()
        CMPIN = sb("CMPIN", [128, 528], BF16); rCMPIN = cx.res()
        XB = sb("XB", [128, 8, 512], BF16); rXB = cx.res()
        TAB = sb("TAB", [128, 4, 512], F32); rTAB = cx.res()
        TRI = sb("TRI", [128, 128], BF16); TRIU = sb("TRIU", [128, 128], BF16)
        MASKC = sb("MASKC", [128, 2048], BF16); OV = sb("OV", [128, 4, 128], BF16)
        IDF = sb("IDF", [128, 128], F32); TRIF = sb("TRIF", [128, 128], F32); ONESF = sb("ONESF", [128, 128], F32)
        VEC = sb("VEC", [128, 8], F32)
        rC = cx.res()
        rC2 = cx.res()
        QF = sb("QF", [128, 512], BF16); QD0 = sb("QD0", [128, 512], BF16); QD1 = sb("QD1", [128, 512], BF16); rQA = cx.res()
        QN = sb("QN", [128, 4, 512], BF16); rQN = cx.res()
        QSA = [sb(f"QSA{i}", [128, 2, 256], BF16) for i in range(2)]; rQSA = [[cx.res(), cx.res()] for _ in range(2)]
        T0 = sb("T0", [128, 512], F32); rT0 = cx.res()
        T1 = sb("T1", [128, 512], F32); rT1 = cx.res()
        Z1 = sb("Z1", [128, 512], F32); rZ1 = cx.res()
        GZ = sb("GZ", [128, 3, 2, 512], BF16); rGZ = cx.res()
        NPT = 5
        ptc = [0]
        PT = [sb(f"PT{i}", [128, 512], BF16) for i in range(NPT)]; rPT = [cx.res() for _ in range(NPT)]
        FA = sb("FA", [128, 512], F32); rFA = cx.res()
        FB = sb("FB", [128, 512], F32); rFB = cx.res()
        CT0 = sb("CT0", [128, 512], BF16); rCT0 = cx.res()
        CT1 = sb("CT1", [128, 512], BF16); rCT1 = cx.res()
        OUTS = sb("OUTS", [128, 1024], F32); rOUTS = cx.res()
        CNEG = sb("CNEG", [128, NK], F32); rCNEG = cx.res()
        CARRY = sb("CARRY", [128, NK + 4], F32); rCARRY = cx.res()
        SPL = sb("SPL", [128, NK], F32); rSPL = cx.res()
        NB = sb("NB", [128, NK], F32); rNB = cx.res()
        LFRAW = sb("LFRAW", [128, NK], F32); rLFRAW = cx.res()
        SC = sb("SC", [128, 128], F32); rSC = cx.res()
        SC2 = sb("SC2", [128, 128], F32); rSC2 = cx.res()
        M8 = sb("M8", [128, 8], F32); M8b = sb("M8b", [128, 8], F32); rM8 = cx.res(); rM8b = cx.res()
        MNEG2 = [sb(f"MNEG{i}", [128, 128], F32) for i in range(2)]; rMNEG2 = [cx.res(), cx.res()]
        RS4 = sb("RS4", [128, 4], F32); rRS4 = cx.res()
        RI4 = sb("RI4", [128, 4], F32); rRI4 = cx.res()
        YA = [sb(f"YA{i}", [64, 256], F32)[:] for i in range(2)]; rYA = [cx.res(), cx.res()]
        YB = sb("YB", [64, 256], F32); rYB = cx.res()
        YC = sb("YC", [64, 256], F32); rYC = cx.res()
        FN = sb("FN", [128, 256], F32); rFN = cx.res()
        HS = sb("HS", [128, 64], BF16); rHS = cx.res()
        HSS = sb("HSS", [128, 64], F32); rHSS = cx.res()
        LAM = sb("LAM", [128, 4], F32); rLAM = cx.res()
        IMPB = cx.psum("IMPB", [128, 512], F32); rIMPB = cx.res(True)
        STB = [cx.psum(f"ST{i}", [128, 512], F32) for i in range(4)]; rST = [cx.res(True) for _ in range(4)]
        OB = [cx.psum(f"OB{i}", [128, 512], F32) for i in range(3)]; rOB = [cx.res(True) for _ in range(3)]
        ALLB = [(IMPB, rIMPB)] + list(zip(STB, rST)) + list(zip(OB, rOB))
        ld = cx.dsem("ld_const"); ldx = cx.dsem("ld_x"); ldt = cx.dsem("ld_tab"); sto = cx.dsem("st_out")
        pjc = [0]

        def pjx():
            i = pjc[0] % 8
            pjc[0] += 1
            return ALLB[i]

        pj = pjx

        ldw = cx.dsem("ld_w"); ccs = cx.dsem("cc_sem"); ldb = [cx.dsem("ld_b0"), cx.dsem("ld_b1")]; sta = [cx.dsem("st_a0"), cx.dsem("st_a1")]; ldg = cx.dsem("ld_g")
        rYQ = cx.res()
        rYTQ = [cx.res() for _ in range(NTB // 2)]; rXT1 = [cx.res() for _ in range(NTB // 2)]
        rPart = [cx.res() for _ in range(NCH)]; rRSd = [cx.res() for _ in range(NCH)]
        styt2 = [cx.dsem("st_yt0"), cx.dsem("st_yt1")]
        rsc = [cx.dsem(f"rs_sem{c}") for c in range(NCH)]
        cx.dma(pool, ld, KTS[64:128, :], ind_d, writes=[rKTS])
        cx.dma(pool, ld, TRI[:], ctri_d, writes=[rC])
        cx.dma(pool, ld, TRIU[:], ctriu_d, writes=[rC])
        cx.dma(pool, ld, MASKC[:], cmk_d, writes=[rC])
        cx.dma(pool, ld, OV[:], cov_d, writes=[rC])
        cx.dma(pool, ld, IDF[:], cid_d, writes=[rC])
        cx.dma(pool, ld, TRIF[:], ctri_d, writes=[rC])
        EPSC = sb("EPSC", [128, 1], F32)
        for r_ in (rKTS, rC):
            r_.w = (ld, ld.cnt)
        cx.op(dve, lambda: V.memset(ONESF[:], 1.0), writes=[rC2])
        IDB = sb("IDB", [128, 128], BF16); TRIN = sb("TRIN", [128, 128], BF16); TRIUN = sb("TRIUN", [128, 128], BF16)
        cx.op(dve, lambda: V.tensor_copy(out=IDB[:], in_=IDF[:]), reads=[rC], writes=[rC2])
        cx.op(dve, lambda: V.tensor_scalar(out=TRIN[:], in0=TRI[:], scalar1=-1.0, scalar2=30000.0, op0=ALU.add, op1=ALU.mult), reads=[rC], writes=[rC2])
        cx.op(dve, lambda: V.tensor_scalar(out=TRIUN[:], in0=TRIU[:], scalar1=-1.0, scalar2=30000.0, op0=ALU.add, op1=ALU.mult), reads=[rC], writes=[rC2])
        cx.op(dve, lambda: V.tensor_scalar(out=MASKC[:], in0=MASKC[:], scalar1=-1.0, scalar2=30000.0, op0=ALU.add, op1=ALU.mult), reads=[rC], writes=[rC])
        cx.op(dve, lambda: V.memset(EPSC[:], 1e-5), writes=[rC2])
        rVEC = cx.res()
        if NK * 96 >= 5120:
            INB = [VA[:].rearrange("p a b -> p (a b)").bitcast(F32)[:, 0:5120].rearrange("p (j n) -> p j n", j=5),
                   VN[:].rearrange("p a b -> p (a b)").bitcast(F32)[:, 0:5120].rearrange("p (j n) -> p j n", j=5)]
            kf = KTA[:].bitcast(F32)
            ACCB = [kf[:, 0:1024], kf[:, 1024:2048]]
            YTB2 = [KTA[:, 4096:5120].rearrange("p (c t) -> p c t", c=8), KTA[:, 5120:6144].rearrange("p (c t) -> p c t", c=8)]
            xf = XB[:].rearrange("p a b -> p (a b)").bitcast(F32)
            GTB, BTB = xf[:, 0:1024], xf[:, 1024:2048]
            ALIAS = True
        else:
            ALIAS = False
            INB = [sb(f"INB{i}", [128, 5, 1024], F32)[:] for i in range(2)]; rINB = [cx.res(), cx.res()]
            ACCB = [sb(f"ACCB{i}", [128, 1024], F32)[:] for i in range(2)]; rACCB = [cx.res(), cx.res()]
            YTB2 = [sb(f"YTB{i}", [128, 8, 128], BF16)[:] for i in range(2)]; rYTB2 = [cx.res(), cx.res()]
            GTB = sb("GTB", [128, 1024], F32)[:]; BTB = sb("BTB", [128, 1024], F32)[:]; rGB = cx.res()
        STT = sb("STT", [128, 2, 6], F32); rSTT = cx.res()
        MV = sb("MV", [128, 4], F32); rMV = cx.res()
        def load_weights(L):
            cx.dma(pool, ldw, WFM[:], wfm_d[L].rearrange("(c p) n -> p c n", p=128), writes=[rWFM])
            cx.dma(pool, ldw, WTM[:], wtm_d[L].rearrange("(c p) n -> p c n", p=128), writes=[rWTM])
            cx.dma(pool, ldw, WOUT[:], wout_d[L].rearrange("(c p) n -> p c n", p=128), writes=[rWOUT])
            cx.dma(pool, ldw, W1[0:64, :, :], w1_d[L, 0].rearrange("(l d) h -> d l h", d=64), writes=[rW1])
            cx.dma(pool, ldw, W1[64:128, :, :], w1_d[L, 1].rearrange("(l d) h -> d l h", d=64), writes=[rW1])
            cx.dma(pool, ldw, W2[:], w2_d[L], writes=[rW2])
            cx.dma(pool, ldw, POST[:], post_d[L], writes=[rPOST])
            cx.dma(pool, ldw, VEC[:], vec_d[L], writes=[rVEC])
            for r_ in (rWFM, rWTM, rWOUT, rW1, rW2, rPOST, rVEC):
                r_.w = (ldw, ldw.cnt)

        load_weights(0)
        for L in range(depth):
          if True:
              cx.op(dve, lambda: V.memset(VA[:, :, 64:128], 1.0), writes=[rVA])
              cx.op(pool, lambda: G.memset(VN[:, :, 64:128], 1.0), writes=[rVN])
              cx.op(dve, lambda: V.memset(VCA[:, :, 0:64], 0.0), writes=[rVCA])
              cx.op(dve, lambda: V.memset(VCA[:, :, 64:128], 1.0), writes=[rVCA])
              cx.op(dve, lambda: V.memset(KCT[:], 0.0), writes=[rKCT])
              cx.op(pool, lambda: G.memset(KTW[:], 0.0), writes=[rKTW])
              cx.op(pool, lambda: G.memset(QN[64:128, :, :], 0.0), writes=[rQN])
              cx.op(dve, lambda: V.memset(QF[:], 0.0), writes=[rQA])
              cx.op(dve, lambda: V.memset(QD0[:], 0.0), writes=[rQA])
              cx.op(pool, lambda: G.memset(QD1[:], 0.0), writes=[rQA])
              cx.op(dve, lambda: V.memset(VCTF[:], 0.0), writes=[rVCTF])
              cx.op(pool, lambda: G.memset(CMPIN[:], 0.0), writes=[rCMPIN])
              cx.op(dve, lambda: V.memset(CARRY[:], 0.0), writes=[rCARRY])
              for s_ in range(2):
                  pb, rpb = pj()
                  lo = 64 * s_
                  for l in range(32):
                      cx.op(pe, lambda l=l, lo=lo, pb=pb: T.matmul(pb[:, 0:1], lhsT=W1[lo:lo + 64, l, :], rhs=POST[lo:lo + 64, l:l + 1],
                                                                    start=(l == 0), stop=(l == 31)),
                            reads=[rW1, rPOST], writes=[rpb])
                  cx.op(dve, lambda pb=pb, s_=s_: V.tensor_copy(out=BH[:, s_:s_ + 1], in_=pb[:, 0:1]), reads=[rpb], writes=[rBH])
              cx.op(dve, lambda: V.tensor_tensor(out=LAM[:, 0:1], in0=VEC[:, 2:3], in1=VEC[:, 3:4], op=ALU.mult), reads=[rC, rVEC], writes=[rLAM])
              cx.op(dve, lambda: V.tensor_tensor(out=LAM[:, 1:2], in0=VEC[:, 4:5], in1=VEC[:, 5:6], op=ALU.mult), reads=[rC, rC2, rVEC, rLAM], writes=[rLAM])
              pb, rpb = pj()
              cx.op(pe, lambda: T.matmul(pb[:, 0:2], lhsT=ONESF[:], rhs=LAM[:, 0:2], start=True, stop=True), reads=[rC, rC2, rLAM], writes=[rpb])
              cx.op(act, lambda: A.activation(out=LAM[:, 2:4], in_=pb[:, 0:2], func=AF.Exp), reads=[rpb], writes=[rLAM])
              cx.op(dve, lambda: V.tensor_tensor(out=LAM[:, 0:1], in0=LAM[:, 3:4], in1=LAM[:, 2:3], op=ALU.subtract), reads=[rLAM], writes=[rLAM])
              cx.op(dve, lambda: V.tensor_tensor(out=LAM[:, 0:1], in0=LAM[:, 0:1], in1=VEC[:, 6:7], op=ALU.add),
                    reads=[rLAM, rC, rC2, rVEC], writes=[rLAM])
              cx.op(dve, lambda: V.tensor_tensor(out=LAM[:, 1:2], in0=VEC[:, 1:2], in1=VEC[:, 7:8], op=ALU.mult),
                    reads=[rC, rC2, rVEC, rLAM], writes=[rLAM])
              NEGLAM = LAM[:, 0:1]
              GCOL = LAM[:, 1:2]

              def load_tile(Tq):
                  if L == 0:
                      cx.dma(pool, ldx, XB[:], xT0_d[:, Tq * 512:(Tq + 1) * 512].rearrange("(c p) t -> p c t", p=128), writes=[rXB])
                  else:
                      for pc in range(4):
                          row = (Tq % TPC) * 512 + pc * 128
                          rk, ib = row // (TPC * 128), (Tq // TPC) * TPC + (row % (TPC * 128)) // 128
                          cx.dma(pool, ldx, XB[:, :, pc * 128:(pc + 1) * 128],
                                 xT1_d[ib // 2][rk * 1024:(rk + 1) * 1024, (ib % 2) * 128:(ib % 2 + 1) * 128].rearrange("(c p) t -> p c t", p=128),
                                 reads=[rXT1[ib // 2]], writes=[rXB])
                      rXB.w = (ldx, ldx.cnt)
                  cx.dma(sp, ldt, TAB[:], tab_d[:, :, Tq * 512:(Tq + 1) * 512], writes=[rTAB])

              load_tile(0)

              def fm_group(gi):
                  pb, rpb = pjx()
                  for c in range(8):
                      cx.op(pe, lambda c=c, pb=pb: T.matmul(pb[:], lhsT=WFM[:, c, gi * 128:(gi + 1) * 128], rhs=XB[:, c, :],
                                                             start=(c == 0), stop=(c == 7)), reads=[rWFM, rXB], writes=[rpb])
                  return pb, rpb

              def rope(pa, rpa, ps, rps, rows_a, rows_s, tabi, outs):
                  (a0, a1), (s0, s1) = rows_a, rows_s
                  n = a1 - a0
                  cx.op(dve, lambda: V.tensor_tensor(out=T0[0:n, :], in0=pa[a0:a1, :], in1=TAB[0:n, tabi, :], op=ALU.mult),
                        reads=[rpa, rTAB], writes=[rT0])
                  cx.op(dve, lambda: V.tensor_tensor(out=T1[0:n, :], in0=ps[s0:s1, :], in1=TAB[0:n, tabi + 1, :], op=ALU.mult),
                        reads=[rps, rTAB], writes=[rT1])
                  for (r0_, r1_, dst, rdst) in outs:
                      cx.op(pool, lambda r0_=r0_, r1_=r1_, dst=dst: G.tensor_tensor(out=dst, in0=T0[r0_:r1_, :], in1=T1[r0_:r1_, :], op=ALU.add),
                            reads=[rT0, rT1], writes=[rdst])

              def bcast_h(ap2d, h):
                  return bass.AP(tensor=ap2d.tensor, offset=ap2d.offset, ap=[list(ap2d.ap[0]), [0, h], list(ap2d.ap[1])])

              SGT, rSGT, ZN, rZN = FA, rFA, FB, rFB
              obc = [0]
              stc = [0]
              DSK = int(os.environ.get("DBG_DSK", "3"))

              def ob_next():
                  k = obc[0] % 3
                  obc[0] += 1
                  return OB[k], rOB[k]

              def st_next():
                  k = stc[0] % 4
                  stc[0] += 1
                  return STB[k], rST[k]

              for Tq in range(NT):
                  c0t = Tq * 512
                  pA, rA_ = fm_group(0)
                  pS, rS_ = fm_group(2)
                  rope(pA, rA_, pS, rS_, (0, 64), (0, 64), 2, [(0, 32, QD0[0:32, :], rQA), (32, 64, QD1[32:64, :], rQA)])
                  cx.op(act, lambda: A.copy(out=QF[64:128, :], in_=pA[64:128, :]), reads=[rA_], writes=[rQA])
                  pK, rK_ = fm_group(1)
                  rope(pK, rK_, pS, rS_, (0, 64), (64, 128), 2, [(0, 64, KTA[0:64, c0t:c0t + 512], rKTA)])
                  cx.op(act, lambda: A.copy(out=KTA[64:128, c0t:c0t + 512], in_=pK[64:128, :]), reads=[rK_], writes=[rKTA])
                  for (ga, gs, hq) in ((3, 4, 0), (5, 6, 2)):
                      p1, r1 = fm_group(ga)
                      p2, r2 = fm_group(gs)
                      rope(p1, r1, p2, r2, (0, 128), (0, 128), 0, [(0, 64, QN[0:64, hq, :], rQN), (64, 128, QN[0:64, hq + 1, :], rQN)])
                  p1, r1 = fm_group(7)
                  p2, r2 = fm_group(8)
                  rope(p1, r1, p2, r2, (0, 128), (0, 128), 0, [(0, 64, CMPIN[0:64, 16:528], rCMPIN), (64, 128, KTS[0:64, c0t:c0t + 512], rKTS)])
                  p1, r1 = fm_group(9)
                  p2, r2 = fm_group(10)
                  wslot = (Tq % 2) * 512
                  rope(p1, r1, p2, r2, (0, 64), (0, 64), 0, [(0, 64, KTW[0:64, wslot:wslot + 512], rKTW)])
                  cx.op(act, lambda: A.copy(out=CMPIN[64:128, 16:528], in_=p1[64:128, :]), reads=[r1], writes=[rCMPIN])
                  p1, r1 = fm_group(11)
                  cx.op(act, lambda: A.activation(out=Z1[:], in_=p1[:], func=AF.Sigmoid), reads=[r1], writes=[rZ1])
                  cx.op(dve, lambda: V.tensor_tensor(out=Z1[:], in0=p1[:], in1=Z1[:], op=ALU.mult), reads=[r1, rZ1], writes=[rZ1])
                  p1, r1 = fm_group(12)
                  cx.op(act, lambda: A.activation(out=ZN[:], in_=p1[:], func=AF.Sigmoid), reads=[r1], writes=[rZN])
                  cx.op(dve, lambda: V.tensor_tensor(out=ZN[:], in0=p1[:], in1=ZN[:], op=ALU.mult), reads=[r1, rZN], writes=[rZN])
                  sgbufs = ((FA, rFA), (T0, rT0), (T1, rT1))
                  for br in range(3):
                      p1, r1 = fm_group(13 + br)
                      SGb, rSGb = sgbufs[br]
                      cx.op(act, lambda: A.activation(out=SGb[:], in_=p1[:], func=AF.Sigmoid), reads=[r1], writes=[rSGb])
                      ob = 0 if br == 2 else 64
                      for h in range(2):
                          eng_, E_ = ((pool, G), (dve, V))[(2 * br + h) % 2]
                          cx.op(eng_, lambda h=h, ob=ob, br=br, E_=E_: E_.tensor_tensor(out=GZ[ob:ob + 64, br, h, :], in0=SGb[64 * h:64 * h + 64, :],
                                                                                       in1=ZN[64 * h:64 * h + 64, :], op=ALU.mult),
                                reads=[rSGb, rZN], writes=[rGZ])
                  for st_ in range(4):
                      kt = Tq * 4 + st_
                      pb, rpb = pjx()
                      for c in range(8):
                          cx.op(pe, lambda c=c, pb=pb, st_=st_: T.matmul(pb[:, 0:257], lhsT=XB[:, c, st_ * 128:(st_ + 1) * 128], rhs=WTM[:, c, :],
                                                                          start=(c == 0), stop=(c == 7)), reads=[rXB, rWTM], writes=[rpb])
                      cx.op(act, lambda pb=pb, kt=kt: A.copy(out=VA[:, kt, :].rearrange("p (a b) -> p a b", a=3)[:, 0::2, :],
                                                             in_=pb[:, 0:128].rearrange("p (a b) -> p a b", a=2)), reads=[rpb], writes=[rVA])
                      cx.op(dve, lambda pb=pb, kt=kt: V.tensor_copy(out=VN[:, kt, :].rearrange("p (a b) -> p a b", a=3)[:, 0::2, :],
                                                                    in_=pb[:, 128:256].rearrange("p (a b) -> p a b", a=2)), reads=[rpb], writes=[rVN])
                      cx.op(dve, lambda pb=pb, kt=kt: V.tensor_copy(out=LFRAW[:, kt:kt + 1], in_=pb[:, 256:257]), reads=[rpb], writes=[rLFRAW])
                  if Tq + 1 < NT:
                      load_tile(Tq + 1)
                  deferred = []

                  def defer(n, fn, tag):
                      deferred.append([n, fn, tag])

                  def run_deferred(upto_tag=None, everything=False, pred=None):
                      if pred is None and upto_tag is not None:
                          pred = lambda t: t == upto_tag
                      while deferred:
                          n, fn, tag = deferred[0]
                          if everything or n <= 0 or (pred is not None and any(pred(d[2]) for d in deferred)):
                              deferred.pop(0)
                              fn()
                          else:
                              break

                  j0 = 1 if Tq == 0 else 0
                  nj = 32 - j0
                  n0 = 32 * Tq - 1 + j0
                  for s_ in range(2):
                      lo = 64 * s_
                      pb, rpb = pjx()
                      for l in range(32):
                          cx.op(pe, lambda l=l, lo=lo, pb=pb: T.matmul(pb[:, 0:nj], lhsT=W1[lo:lo + 64, l, :],
                                                                        rhs=CMPIN[lo:lo + 64, 16 * j0 + l:16 * j0 + l + 16 * (nj - 1) + 1:16],
                                                                        start=(l == 0), stop=(l == 31)), reads=[rW1, rCMPIN], writes=[rpb])
                      cx.op(act, lambda pb=pb, s_=s_: A.activation(out=HSS[:, 32 * s_:32 * s_ + nj], in_=pb[:, 0:nj], func=AF.Sigmoid, bias=BH[:, s_:s_ + 1]),
                            reads=[rpb, rBH], writes=[rHSS])
                      cx.op(dve, lambda pb=pb, s_=s_: V.scalar_tensor_tensor(out=HS[:, 32 * s_:32 * s_ + nj], in0=pb[:, 0:nj], scalar=BH[:, s_:s_ + 1],
                                                                             in1=HSS[:, 32 * s_:32 * s_ + nj], op0=ALU.add, op1=ALU.mult),
                            reads=[rpb, rBH, rHSS], writes=[rHS])
                  cx.op(pool, lambda: G.tensor_copy(out=CMPIN[:, 0:16], in_=CMPIN[:, 512:528]), reads=[rCMPIN], writes=[rCMPIN])

                  def kc_stage2():
                      pb, rpb = st_next()
                      cx.op(pe, lambda: T.matmul(pb[0:64, 0:nj], lhsT=W2[:, 0:64], rhs=HS[:, 0:nj], start=True, stop=True), reads=[rW2, rHS], writes=[rpb])
                      cx.op(pe, lambda: T.matmul(pb[0:64, 64:64 + nj], lhsT=W2[:, 64:128], rhs=HS[:, 32:32 + nj], start=False, stop=True, skip_group_check=True),
                            reads=[rW2, rHS], writes=[rpb])
                      cx.op(dve, lambda: V.tensor_copy(out=KCT[0:64, n0:n0 + nj], in_=pb[0:64, 0:nj]), reads=[rpb], writes=[rKCT])
                      cx.op(dve, lambda: V.tensor_copy(out=VCTF[0:64, n0:n0 + nj], in_=pb[0:64, 64:64 + nj]), reads=[rpb], writes=[rVCTF])
                      for a_ in sorted(set([max(n0, 0) // 128, (n0 + nj - 1) // 128])):
                          pb2, rpb2 = st_next()
                          cx.op(pe, lambda a_=a_, pb2=pb2: T.transpose(out=pb2[:, 0:64], in_=VCTF[0:64, a_ * 128:(a_ + 1) * 128], identity=IDF[0:64, 0:64]),
                                reads=[rVCTF, rC, rC2], writes=[rpb2])
                          cx.op(dve, lambda a_=a_, pb2=pb2: V.tensor_copy(out=VCA[:, a_, 0:64], in_=pb2[:, 0:64]), reads=[rpb2], writes=[rVCA])
                  defer(4, kc_stage2, ("kc", Tq))

                  k0 = Tq * 4
                  cx.op(dve, lambda: V.tensor_scalar(out=SPL[:, k0:k0 + 4], in0=LFRAW[:, k0:k0 + 4], scalar1=VEC[:, 0:1], scalar2=None, op0=ALU.add),
                        reads=[rLFRAW, rC, rC2, rVEC], writes=[rSPL])
                  cx.op(act, lambda: A.activation(out=SPL[:, k0:k0 + 4], in_=SPL[:, k0:k0 + 4], func=AF.Exp, scale=-1.0), reads=[rSPL], writes=[rSPL])
                  cx.op(act, lambda: A.activation(out=SPL[:, k0:k0 + 4], in_=SPL[:, k0:k0 + 4], func=AF.Ln, bias=1.0), reads=[rSPL], writes=[rSPL])

                  def nb_stage():
                      pb, rpb = st_next()
                      cx.op(pe, lambda: T.matmul(pb[:, 0:4], lhsT=TRIF[:], rhs=SPL[:, k0:k0 + 4], start=True, stop=True), reads=[rC, rC2, rSPL], writes=[rpb])
                      cx.op(pe, lambda: T.matmul(pb[:, 8:12], lhsT=ONESF[:], rhs=SPL[:, k0:k0 + 4], start=False, stop=True, skip_group_check=True), reads=[rC, rC2, rSPL], writes=[rpb])
                      for j in range(4):
                          cx.op(dve, lambda j=j: V.tensor_tensor(out=CARRY[:, k0 + j + 1:k0 + j + 2], in0=CARRY[:, k0 + j:k0 + j + 1],
                                                                 in1=pb[:, 8 + j:9 + j], op=ALU.add), reads=[rpb, rCARRY], writes=[rCARRY])
                      cx.op(dve, lambda: V.tensor_tensor(out=CNEG[:, k0:k0 + 4], in0=pb[:, 0:4], in1=CARRY[:, k0:k0 + 4], op=ALU.add),
                            reads=[rpb, rCARRY], writes=[rCNEG])
                      cx.op(dve, lambda: V.tensor_scalar(out=NB[:, 0:k0 + 4], in0=CNEG[:, 0:k0 + 4], scalar1=CARRY[:, k0:k0 + 1], scalar2=None,
                                                         op0=ALU.subtract), reads=[rCNEG, rCARRY], writes=[rNB])
                  defer(8, nb_stage, ("nb", Tq))
                  chain = []


                  gidc = [0, 0]

                  def begin_group():
                      gidc[0] += 1
                      gidc[1] = 0

                  def add_item(smm, expf, maskf, pvf, after=None, pre=None):
                      chain.append([smm, expf, maskf, pvf, after, pre, gidc[0], gidc[1] == 0])
                      gidc[1] += 1

                  def outproj(st_):
                      for dc in range(2):
                          pb, rpb = st_next()
                          cx.op(pe, lambda dc=dc, pb=pb: T.matmul(pb[:], lhsT=CT0[:, st_ * 128:(st_ + 1) * 128], rhs=WOUT[:, 0, dc * 512:(dc + 1) * 512],
                                                                 start=True, stop=False), reads=[rCT0, rWOUT], writes=[rpb])
                          cx.op(pe, lambda dc=dc, pb=pb: T.matmul(pb[:], lhsT=CT1[:, st_ * 128:(st_ + 1) * 128], rhs=WOUT[:, 1, dc * 512:(dc + 1) * 512],
                                                                 start=False, stop=True), reads=[rCT1, rWOUT], writes=[rpb])
                          cx.op(dve, lambda dc=dc, pb=pb: V.tensor_copy(out=OUTS[:, dc * 512:(dc + 1) * 512], in_=pb[:]), reads=[rpb], writes=[rOUTS])
                      ch = Tq // TPC
                      r0 = (Tq % TPC) * 512 + st_ * 128
                      cx.dma(sp, sto, part_d[ch][r0:r0 + 128, :], OUTS[:], reads=[rOUTS, rPart[ch]])

                  def branch_out(ob_, rob_, br, qc, o_low, Y, rY):
                      so, oo = (64, 0) if o_low else (0, 64)
                      cx.op(dve, lambda: V.tensor_scalar(out=FN[so:so + 64, :], in0=ob_[so:so + 64, 0:256], scalar1=1e-30, scalar2=None, op0=ALU.max),
                            reads=[rob_], writes=[rFN])
                      cx.op(dve, lambda: V.reciprocal(out=FN[so:so + 64, :], in_=FN[so:so + 64, :]), reads=[rFN], writes=[rFN])
                      cx.op(dve, lambda: V.tensor_tensor(out=FN[so:so + 64, :].rearrange("p (h q) -> p h q", h=2),
                                                         in0=FN[so:so + 64, :].rearrange("p (h q) -> p h q", h=2),
                                                         in1=GZ[so:so + 64, br, :, qc:qc + 128], op=ALU.mult), reads=[rFN, rGZ], writes=[rFN])
                      cx.op(dve, lambda: V.tensor_tensor(out=Y, in0=ob_[oo:oo + 64, 0:256], in1=FN[so:so + 64, :], op=ALU.mult),
                            reads=[rob_, rFN], writes=[rY])

                  def cmp_group(bi):
                      begin_group()
                      i = 4 * Tq + bi
                      qc = 128 * bi
                      a_max = i // 16
                      r16 = i % 16
                      par = i % 2
                      ob_, rob_ = ob_next()
                      for a_ in range(a_max + 1):
                          def s_c(sb_, rsb, a_=a_):
                              cx.op(pe, lambda: T.matmul(sb_[:], lhsT=KCT[:, a_ * 128:(a_ + 1) * 128], rhs=QN[:, :, qc:qc + 128],
                                                         start=True, stop=(a_ != a_max)), reads=[rKCT, rQN], writes=[rsb])
                              if a_ == a_max:
                                  cx.op(pe, lambda: T.matmul(sb_[:], lhsT=IDB[:], rhs=bcast_h(MASKC[:, r16 * 128:(r16 + 1) * 128], 4),
                                                             start=False, stop=True), reads=[rC, rC2], writes=[rsb])

                          def e_c(sb_, rsb, pt, rpt, a_=a_):
                              cx.op(act, lambda: A.activation(out=pt[:], in_=sb_[:], func=AF.Exp, scale=0.125), reads=[rsb], writes=[rpt])

                          def m_c(pt, rpt, a_=a_):
                              cx.op(dve, lambda: V.tensor_tensor(out=pt[:].rearrange("p (h q) -> p h q", h=4),
                                                                 in0=pt[:].rearrange("p (h q) -> p h q", h=4),
                                                                 in1=bcast_h(MASKC[:, r16 * 128:(r16 + 1) * 128], 4),
                                                                 op=ALU.mult), reads=[rpt, rC, rC2], writes=[rpt])

                          def p_c(pt, rpt, a_=a_):
                              cx.op(pe, lambda: T.matmul(ob_[:, 0:256], lhsT=VCA[:, a_, :], rhs=pt[:, 0:256], start=(a_ == 0), stop=(a_ == a_max)),
                                    reads=[rVCA, rpt], writes=[rob_])
                              for h in range(4):
                                  cx.op(pe, lambda h=h: T.matmul(IMPB[:, h * 128:(h + 1) * 128], lhsT=pt[:, h * 128:(h + 1) * 128], rhs=OV[:, a_, :],
                                                                 start=(a_ == 0 and h == 0), stop=(a_ == a_max), skip_group_check=True),
                                        reads=[rpt, rC, rC2], writes=[rIMPB])

                          def epi_cmp():
                              branch_out(ob_, rob_, 0, qc, True, YA[par], rYA[par])

                              def stage2():
                                  cx.op(dve, lambda: V.tensor_reduce(out=RS4[:], in_=IMPB[:].rearrange("p (h j) -> p h j", h=4), axis=AX.X, op=ALU.add),
                                        reads=[rIMPB], writes=[rRS4])
                                  cx.op(dve, lambda: V.tensor_scalar(out=RS4[:], in0=RS4[:], scalar1=1e-30, scalar2=None, op0=ALU.max), reads=[rRS4], writes=[rRS4])
                                  cx.op(dve, lambda: V.reciprocal(out=RI4[:], in_=RS4[:]), reads=[rRS4], writes=[rRI4])
                                  cx.op(dve, lambda: V.tensor_scalar(out=SC[:], in0=IMPB[:, 0:128], scalar1=RI4[:, 0:1], scalar2=None, op0=ALU.mult),
                                        reads=[rIMPB, rRI4], writes=[rSC])
                                  for h in range(1, 4):
                                      cx.op(dve, lambda h=h: V.scalar_tensor_tensor(out=SC[:], in0=IMPB[:, h * 128:(h + 1) * 128], scalar=RI4[:, h:h + 1], in1=SC[:],
                                                                                    op0=ALU.mult, op1=ALU.add), reads=[rIMPB, rRI4, rSC], writes=[rSC])

                              def stage3():
                                  cx.op(dve, lambda: V.memset(SC[:, 0:1], BIG), reads=[], writes=[rSC])
                                  lo0 = max(2 * i - 1, 0)
                                  cx.op(dve, lambda: V.memset(SC[0:64, lo0:2 * i + 1], BIG), writes=[rSC])
                                  cx.op(dve, lambda: V.memset(SC[64:128, 2 * i:2 * i + 2], BIG), writes=[rSC])
                                  cx.op(dve, lambda: V.max(out=M8[:], in_=SC[:]), reads=[rSC], writes=[rM8])
                                  cx.op(dve, lambda: V.match_replace(out=SC2[:], in_to_replace=M8[:], in_values=SC[:], imm_value=-BIG),
                                        reads=[rSC, rM8], writes=[rSC2])
                                  cx.op(dve, lambda: V.max(out=M8b[:], in_=SC2[:]), reads=[rSC2], writes=[rM8b])
                                  cx.op(dve, lambda: V.tensor_scalar(out=MNEG2[par][:], in0=SC[:], scalar1=M8b[:, 7:8], scalar2=-30000.0, op0=ALU.is_lt, op1=ALU.mult),
                                        reads=[rSC, rM8b], writes=[rMNEG2[par]])
                                  cx.op(pool, lambda: G.tensor_copy(out=QSA[par][0:64, :, :].rearrange("p a (h q) -> p a h q", h=2),
                                                                    in_=bass.AP(tensor=QN.tensor if hasattr(QN, "tensor") else QN[0:64, 0:2, qc:qc + 128].tensor,
                                                                                offset=QN[0:64, 0:2, qc:qc + 128].offset,
                                                                                ap=[list(QN[0:64, 0:2, qc:qc + 128].ap[0]), [0, 2]] + [list(a) for a in QN[0:64, 0:2, qc:qc + 128].ap[1:]])),
                                        reads=[rQN], writes=[rQSA[par][0], rQSA[par][1]])

                              def stage_b():
                                  mt, rmt = st_next()
                                  cx.op(pe, lambda: T.transpose(out=mt[:, 0:128], in_=MNEG2[par][:], identity=IDF[:]), reads=[rMNEG2[par], rC, rC2], writes=[rmt])
                                  for half in range(2):
                                      cx.op(dve, lambda half=half: V.tensor_copy(out=QSA[par][64:128, half, :].rearrange("p (h q) -> p h q", h=2),
                                                                                 in_=bcast_h(mt[64 * half:64 * half + 64, 0:128], 2)),
                                            reads=[rmt], writes=[rQSA[par][half]])
                              defer(4, stage2, ("c", i))
                              defer(9, stage3, ("c", i))
                              defer(28, stage_b, ("q", i))
                          add_item(s_c, e_c, None, p_c, epi_cmp if a_ == a_max else None,
                                   (lambda: run_deferred(pred=lambda t: t[0] in ("c", "kc"))) if a_ == 0 else None)

                  def win_group(bi):
                      begin_group()
                      i = 4 * Tq + bi
                      qc = 128 * bi
                      ob_, rob_ = ob_next()
                      kts = list(range(max(0, i - 4), i + 1))
                      for kt in kts:
                          def s_w(sb_, rsb, kt=kt):
                              wl = (kt % 8) * 128
                              msk_ = TRIN if kt == i else (TRIUN if kt == i - 4 else None)
                              cx.op(pe, lambda: T.matmul(sb_[:, 0:256], lhsT=KTW[:, wl:wl + 128], rhs=QN[:, 0:2, qc:qc + 128], start=True, stop=(msk_ is None)),
                                    reads=[rKTW, rQN], writes=[rsb])
                              if msk_ is not None:
                                  cx.op(pe, lambda: T.matmul(sb_[:, 0:256], lhsT=IDB[:], rhs=bcast_h(msk_[:], 2), start=False, stop=True),
                                        reads=[rC, rC2], writes=[rsb])

                          def e_w(sb_, rsb, pt, rpt):
                              cx.op(act, lambda: A.activation(out=pt[:, 0:256], in_=sb_[:, 0:256], func=AF.Exp, scale=0.125), reads=[rsb], writes=[rpt])
                          mk = None
                          if kt == i or kt == i - 4:
                              def mk(pt, rpt, kt=kt):
                                  msk = TRI if kt == i else TRIU
                                  cx.op(dve, lambda: V.tensor_tensor(out=pt[:, 0:256].rearrange("p (h q) -> p h q", h=2),
                                                                     in0=pt[:, 0:256].rearrange("p (h q) -> p h q", h=2),
                                                                     in1=bcast_h(msk[:], 2), op=ALU.mult), reads=[rpt, rC, rC2], writes=[rpt])

                          def p_w(pt, rpt, kt=kt):
                              cx.op(pe, lambda: T.matmul(ob_[:, 0:256], lhsT=VN[:, kt, 64:192], rhs=pt[:, 0:256], start=(kt == kts[0]), stop=(kt == kts[-1])),
                                    reads=[rVN, rpt], writes=[rob_])
                          add_item(s_w, e_w, None, p_w, (lambda: branch_out(ob_, rob_, 2, qc, False, YB[:], rYB)) if kt == kts[-1] else None)

                  def sel_group(bi):
                      begin_group()
                      i = 4 * Tq + bi
                      qc = 128 * bi
                      par = i % 2
                      ob_, rob_ = ob_next()
                      for kt in range(i + 1):
                          def s_s(sb_, rsb, kt=kt):
                              half = kt // 32
                              cx.op(pe, lambda: T.matmul(sb_[:, 0:256], lhsT=KTS[:, kt * 128:(kt + 1) * 128], rhs=QSA[par][:, half, :], start=True, stop=(kt != i)),
                                    reads=[rKTS, rQSA[par][half]], writes=[rsb])
                              if kt == i:
                                  cx.op(pe, lambda: T.matmul(sb_[:, 0:256], lhsT=IDB[:], rhs=bcast_h(TRIN[:], 2), start=False, stop=True),
                                        reads=[rC, rC2], writes=[rsb])

                          def e_s(sb_, rsb, pt, rpt):
                              cx.op(act, lambda: A.activation(out=pt[:, 0:256], in_=sb_[:, 0:256], func=AF.Exp, scale=0.125), reads=[rsb], writes=[rpt])
                          mk = None
                          if kt == i:
                              def mk(pt, rpt):
                                  cx.op(dve, lambda: V.tensor_tensor(out=pt[:, 0:256].rearrange("p (h q) -> p h q", h=2),
                                                                     in0=pt[:, 0:256].rearrange("p (h q) -> p h q", h=2),
                                                                     in1=bcast_h(TRI[:], 2), op=ALU.mult), reads=[rpt, rC, rC2], writes=[rpt])

                          def p_s(pt, rpt, kt=kt):
                              cx.op(pe, lambda: T.matmul(ob_[:, 0:256], lhsT=VN[:, kt, 0:128], rhs=pt[:, 0:256], start=(kt == 0), stop=(kt == i)),
                                    reads=[rVN, rpt], writes=[rob_])

                          def epi_sel():
                              branch_out(ob_, rob_, 1, qc, True, YC[:], rYC)
                              cx.op(pool, lambda: G.tensor_tensor(out=YB[:], in0=YA[par], in1=YB[:], op=ALU.add), reads=[rYA[par], rYB], writes=[rYB])
                              for h in range(2):
                                  cx.op(pool, lambda h=h: G.tensor_tensor(out=CT1[64 * h:64 * h + 64, qc:qc + 128], in0=YB[:, 128 * h:128 * h + 128],
                                                                          in1=YC[:, 128 * h:128 * h + 128], op=ALU.add), reads=[rYB, rYC], writes=[rCT1])
                              defer(20, lambda: outproj(bi), ("o", i))
                          add_item(s_s, e_s, None, p_s, epi_sel if kt == i else None, (lambda: run_deferred(upto_tag=("q", i))) if kt == 0 else None)

                  nkt = 4 * Tq + 4

                  def dense_group(kind):
                      begin_group()
                      ob_, rob_ = ob_next()
                      for kt in range(nkt):
                          jd = kt - 4 * Tq
                          cc = 128 * jd if jd >= 0 else 0
                          first, last = (kt == 0), (kt == nkt - 1)

                          def smm(sb_, rsb, kt=kt, cc=cc):
                              qsrc = (QF, QD0, QD1)[kind]
                              dg_ = (kt - 4 * Tq) >= 0
                              cx.op(pe, lambda: T.matmul(sb_[:, cc:512], lhsT=KTA[:, kt * 128:(kt + 1) * 128], rhs=qsrc[:, cc:512],
                                                         start=True, stop=(not dg_)), reads=[rKTA, rQA], writes=[rsb])
                              if dg_:
                                  cx.op(pe, lambda: T.matmul(sb_[:, cc:cc + 128], lhsT=IDB[:], rhs=TRIN[:], start=False, stop=True),
                                        reads=[rC, rC2], writes=[rsb])

                          def expf(sb_, rsb, pt, rpt, kt=kt, cc=cc):
                              if kind == 0:
                                  cx.op(act, lambda: A.activation(out=pt[:, cc:512], in_=sb_[:, cc:512], func=AF.Exp, bias=NB[:, kt:kt + 1], scale=0.125),
                                        reads=[rsb, rNB], writes=[rpt])
                              else:
                                  cx.op(act, lambda: A.activation(out=pt[:, cc:512], in_=sb_[:, cc:512], func=AF.Exp, scale=float(32 ** -0.5)),
                                        reads=[rsb], writes=[rpt])

                          def m_diag(pt, rpt, cc=cc):
                              cx.op(pool, lambda: G.tensor_tensor(out=pt[:, cc:cc + 128], in0=pt[:, cc:cc + 128], in1=TRI[:], op=ALU.mult),
                                    reads=[rpt, rC, rC2], writes=[rpt])

                          def pvf(pt, rpt, kt=kt, cc=cc, first=first, last=last):
                              lhs = VA[:, kt, 0:128] if kind == 0 else VA[:, kt, 64:192]
                              cx.op(pe, lambda: T.matmul(ob_[:, cc:512], lhsT=lhs, rhs=pt[:, cc:512], start=first, stop=last),
                                    reads=[rVA, rpt], writes=[rob_])

                          def epi():
                              if kind == 0:
                                  cx.op(dve, lambda: V.reciprocal(out=FA[64:128, :], in_=ob_[64:128, :]), reads=[rob_], writes=[rFA])
                                  cx.op(dve, lambda: V.tensor_tensor(out=FA[64:128, :], in0=FA[64:128, :], in1=Z1[64:128, :], op=ALU.mult),
                                        reads=[rFA, rZ1], writes=[rFA])
                                  cx.op(dve, lambda: V.tensor_tensor(out=CT0[0:64, :], in0=ob_[0:64, :], in1=FA[64:128, :], op=ALU.mult),
                                        reads=[rob_, rFA], writes=[rCT0])
                              elif kind == 1:
                                  cx.op(dve, lambda: V.reciprocal(out=FB[0:64, :], in_=ob_[0:64, :]), reads=[rob_], writes=[rFB])
                                  cx.op(dve, lambda: V.tensor_tensor(out=T0[0:64, :], in0=ob_[64:128, :], in1=FB[0:64, :], op=ALU.mult),
                                        reads=[rob_, rFB], writes=[rT0])
                              else:
                                  cx.op(dve, lambda: V.reciprocal(out=FB[0:64, :], in_=ob_[0:64, :]), reads=[rob_, rFB], writes=[rFB])
                                  cx.op(dve, lambda: V.tensor_scalar(out=FB[0:64, :], in0=FB[0:64, :], scalar1=NEGLAM[0:64, :], scalar2=None, op0=ALU.mult),
                                        reads=[rFB, rLAM], writes=[rFB])
                                  cx.op(dve, lambda: V.tensor_tensor(out=T1[0:64, :], in0=ob_[64:128, :], in1=FB[0:64, :], op=ALU.mult),
                                        reads=[rob_, rFB], writes=[rT1])
                                  cx.op(pool, lambda: G.tensor_tensor(out=T0[0:64, :], in0=T0[0:64, :], in1=T1[0:64, :], op=ALU.add),
                                        reads=[rT0, rT1], writes=[rT0])
                                  cx.op(pool, lambda: G.tensor_tensor(out=T1[0:64, :], in0=T0[0:64, :], in1=T0[0:64, :], op=ALU.mult),
                                        reads=[rT0, rT1], writes=[rT1])

                                  def stage_b():
                                      pb, rpb = st_next()
                                      cx.op(pe, lambda: T.matmul(pb[0:64, :], lhsT=ONESF[0:64, 0:64], rhs=T1[0:64, :], start=True, stop=True),
                                            reads=[rC, rC2, rT1], writes=[rpb])
                                      cx.op(act, lambda: A.activation(out=FB[0:64, :], in_=pb[0:64, :], func=AF.Ln, scale=1.0 / 64.0, bias=EPSC[0:64, :]),
                                            reads=[rpb, rC, rC2], writes=[rFB])
                                      cx.op(act, lambda: A.activation(out=FB[0:64, :], in_=FB[0:64, :], func=AF.Exp, scale=-0.5), reads=[rFB], writes=[rFB])
                                      cx.op(dve, lambda: V.scalar_tensor_tensor(out=T0[0:64, :], in0=T0[0:64, :], scalar=GCOL[0:64, :], in1=FB[0:64, :],
                                                                                op0=ALU.mult, op1=ALU.mult), reads=[rT0, rFB, rLAM], writes=[rT0])
                                      cx.op(pool, lambda: G.tensor_tensor(out=CT0[64:128, :], in0=T0[0:64, :], in1=Z1[0:64, :], op=ALU.mult),
                                            reads=[rT0, rZ1], writes=[rCT0])
                                  defer(14, stage_b, ("d", Tq))
                          add_item(smm, expf, None, pvf, epi if last else None,
                                   (lambda: run_deferred(pred=lambda t: t[0] == "nb")) if (kind == 0 and kt == 0) else None)

                  if os.environ.get("DBG_ORDER") == "old":
                      dense_group(0)
                      dense_group(1)
                      dense_group(2)
                      for bi in range(4):
                          cmp_group(bi)
                          win_group(bi)
                          sel_group(bi)
                  else:
                      dense_group(1)
                      dense_group(2)
                      cmp_group(0)
                      dense_group(0)
                      for bi in range(4):
                          if bi + 1 < 4:
                              cmp_group(bi + 1)
                          win_group(bi)
                          sel_group(bi)

                  pend = []

                  def pop_pair():
                      X = pend.pop(0)
                      Y = pend.pop(0) if pend else None
                      order = [X] if Y is None else ([X, Y] if (X[4] == Y[4] and (X[5] or Y[3] is not None)) else [Y, X])
                      for it in order:
                          it[0](it[1], it[2])
                      for it in ([X] if Y is None else [X, Y]):
                          if it[3] is not None:
                              it[3]()
                  ci = 0
                  while ci < len(chain):
                      pair = chain[ci:ci + 2]
                      ci += 2
                      for it in pair:
                          for d_ in deferred:
                              d_[0] -= 1
                      run_deferred()
                      for it in pair:
                          if it[5] is not None:
                              it[5]()
                      slots = []
                      for it in pair:
                          sb_, rsb = st_next()
                          pi = ptc[0] % NPT
                          ptc[0] += 1
                          slots.append((sb_, rsb, PT[pi], rPT[pi]))
                      for it, sl in reversed(list(zip(pair, slots))):
                          it[0](sl[0], sl[1])
                      for it, sl in reversed(list(zip(pair, slots))):
                          it[1](sl[0], sl[1], sl[2], sl[3])
                          if it[2] is not None:
                              it[2](sl[2], sl[3])
                      for it, sl in zip(pair, slots):
                          pend.append((it[3], sl[2], sl[3], it[4], it[6], it[7]))
                      if len(pend) > 2:
                          pop_pair()
                  while pend:
                      pop_pair()
                  run_deferred(everything=True)


                  if Tq % TPC == TPC - 1:
                      ch = Tq // TPC
                      cx._need(pool, [], [rPart[ch], rRSd[ch]])
                      ins = G.collective_compute("ReduceScatter", ALU.add, replica_groups=RG, ins=[part_d[ch].opt()], outs=[rs_d[ch].opt()])
                      rsc[ch].cnt += 1
                      ins.then_inc(rsc[ch].sem)
                      rPart[ch].w = (rsc[ch], rsc[ch].cnt); rPart[ch].r = []
                      rRSd[ch].w = (rsc[ch], rsc[ch].cnt); rRSd[ch].r = []

              if L + 1 < depth:
                  load_weights(L + 1)
              if ALIAS:
                  rINB = cx.fork(rVA, 1) + cx.fork(rVN, 1)
                  rACC0, rACC1, rYT0, rYT1 = cx.fork(rKTA, 4)
                  rACCB = [rACC0, rACC1]
                  rYTB2 = [rYT0, rYT1]
                  (rGB,) = cx.fork(rXB, 1)
              cx.dma(sp, ldg, GTB, lng_d[L], writes=[rGB])
              cx.dma(sp, ldg, BTB, lnb_d[L], writes=[rGB])
              rGB.w = (ldg, ldg.cnt)
              xres_d = xq_d if L == 0 else yq_d
              ntb = SQ // 128
              last = (L == depth - 1)

              cx._need(sp, [], [rYQ])

              cx._need(act, [], [rYQ])

              def loadb(i):
                  k = i % 2
                  r0 = i * 128
                  ch, jj = i // TPC, i % TPC
                  cx.dma(sp, ldb[k], INB[k][:, 0, :], rs_d[ch][jj * 128:(jj + 1) * 128, :], reads=[rRSd[ch]], writes=[rINB[k]])
                  cx.dma(sp, ldb[k], INB[k][:, 4, :], xres_d[r0:r0 + 128, :], writes=[rINB[k]])
                  rINB[k].w = (ldb[k], ldb[k].cnt)

              loadb(0)
              for i in range(ntb):
                  k = i % 2
                  if i + 1 < ntb:
                      loadb(i + 1)
                  I_, Ac, rI, rAc = INB[k], ACCB[k], rINB[k], rACCB[k]
                  cx.op(dve, lambda: V.scalar_tensor_tensor(out=Ac, in0=I_[:, 4, :], scalar=float(ALPHA), in1=I_[:, 0, :], op0=ALU.mult, op1=ALU.add),
                        reads=[rI], writes=[rAc])
                  for h in range(2):
                      cx.op(dve, lambda h=h: V.bn_stats(out=STT[:, h, :], in_=Ac[:, h * 512:(h + 1) * 512]), reads=[rAc], writes=[rSTT])
                  cx.op(dve, lambda: V.bn_aggr(out=MV[:, 0:2], in_=STT[:].rearrange("p a b -> p (a b)")), reads=[rSTT], writes=[rMV])
                  cx.op(dve, lambda: V.tensor_scalar(out=MV[:, 2:3], in0=MV[:, 1:2], scalar1=1e-5, scalar2=None, op0=ALU.add), reads=[rMV], writes=[rMV])
                  cx.op(act, lambda: A.activation(out=MV[:, 2:3], in_=MV[:, 2:3], func=AF.Ln), reads=[rMV], writes=[rMV])
                  cx.op(act, lambda: A.activation(out=MV[:, 3:4], in_=MV[:, 2:3], func=AF.Exp, scale=-0.5), reads=[rMV], writes=[rMV])
                  cx.op(dve, lambda: V.tensor_scalar(out=Ac, in0=Ac, scalar1=MV[:, 0:1], scalar2=MV[:, 3:4], op0=ALU.subtract, op1=ALU.mult),
                        reads=[rAc, rMV], writes=[rAc])
                  cx.op(dve, lambda: V.tensor_tensor(out=Ac, in0=Ac, in1=GTB, op=ALU.mult), reads=[rAc, rGB], writes=[rAc])
                  cx.op(dve, lambda: V.tensor_tensor(out=Ac, in0=Ac, in1=BTB, op=ALU.add), reads=[rAc, rGB], writes=[rAc])
                  if last:
                      cx.dma(act, sta[k], y_d[i * 128:(i + 1) * 128, :], Ac, reads=[rAc])
                  else:
                      cx.dma(act, sta[k], yq_d[i * 128:(i + 1) * 128, :], Ac, reads=[rAc, rYQ])
                      for hb in range(2):
                          pb, rpb = pj()
                          for c4 in range(4):
                              c = 4 * hb + c4
                              cx.op(pe, lambda c=c, c4=c4, pb=pb: T.transpose(out=pb[:, c4 * 128:(c4 + 1) * 128], in_=Ac[:, c * 128:(c + 1) * 128], identity=IDF[:]),
                                    reads=[rAc, rC], writes=[rpb])
                          cx.op(act if hb == 0 else dve, lambda hb=hb, pb=pb: (A.copy if hb == 0 else V.tensor_copy)(out=YTB2[k][:, 4 * hb:4 * hb + 4, :], in_=pb[:].rearrange("p (c t) -> p c t", c=4)),
                                reads=[rpb], writes=[rYTB2[k]])
                      i2 = i // 2
                      for c in range(8):
                          cx.dma(act, styt2[k], ytq_d[i2][c * 128:(c + 1) * 128, (i % 2) * 128:(i % 2 + 1) * 128], YTB2[k][:, c, :],
                                 reads=[rYTB2[k], rYTQ[i2]])
                      if i % 2 == 1:
                          cx._need(pool, [], [rYTQ[i2], rXT1[i2]])
                          ins = G.collective_compute("AllGather", ALU.bypass, replica_groups=RG, ins=[ytq_d[i2].opt()], outs=[xT1_d[i2].opt()])
                          ccs.cnt += 1
                          ins.then_inc(ccs.sem)
                          rYTQ[i2].w = (ccs, ccs.cnt); rYTQ[i2].r = []
                          rXT1[i2].w = (ccs, ccs.cnt); rXT1[i2].r = []
              if ALIAS:
                  cx.join(rVA, rINB[0:1]); cx.join(rVN, rINB[1:2]); cx.join(rKTA, [rACC0, rACC1, rYT0, rYT1]); cx.join(rXB, [rGB])
        cx.wait_all(sp, sta)
    return nc


def _consts(S):
    half = 32
    t = np.arange(S, dtype=np.float32)
    tab = np.zeros((128, 4, S), np.float32)
    inv = (10000.0 ** (-(np.arange(32, dtype=np.float32) * 2.0 / 64))).astype(np.float32)
    ang = t[None, :] * inv[:, None]
    cosn, sinn = np.cos(ang), np.sin(ang)
    for r in range(128):
        i = r % 64
        tab[r, 0] = cosn[i % 32]
        tab[r, 1] = -sinn[i % 32] if i < 32 else sinn[i % 32]
    invd = (10000.0 ** (-(np.arange(16, dtype=np.float32) * 2.0 / 32))).astype(np.float32)
    angd = t[None, :] * invd[:, None]
    cosd, sind = np.cos(angd), np.sin(angd)
    for r in range(64):
        w = r % 32
        tab[r, 2] = cosd[w % 16]
        tab[r, 3] = -sind[w % 16] if w < 16 else sind[w % 16]
    key = np.arange(S)
    ind = (((key[None, :] // 64) % 64) == np.arange(64)[:, None]).astype(np.float32)
    p = np.arange(128)
    c_tri = (p[:, None] <= p[None, :]).astype(np.float32)
    c_triu = (p[:, None] > p[None, :]).astype(np.float32)
    c_ident = np.eye(128, dtype=np.float32)
    u = np.arange(2048)
    c_maskc = ((16 * p[:, None] + 31) <= u[None, :]).astype(np.float32)
    n = np.arange(512)
    j = np.arange(128)
    ov = np.clip(np.minimum(16 * n[:, None] + 32, 64 * j[None, :] + 64) - np.maximum(16 * n[:, None], 64 * j[None, :]), 0, None)
    ov = (ov.astype(np.float32) / 32.0)
    ov[511] = 0.0
    c_ov = np.ascontiguousarray(ov.reshape(4, 128, 128).transpose(1, 0, 2))
    return dict(tab=tab, ind=ind, c_tri=c_tri, c_triu=c_triu, c_ident=c_ident, c_maskc=c_maskc, c_ov=c_ov)


def _core_cols(c):
    g, hp = c // 2, c % 2
    rng64 = np.arange(64)
    sw64 = (rng64 + 32) % 64
    d32 = np.arange(32)
    swd = np.concatenate([(d32 + 16) % 32, 32 + (d32 + 16) % 32])
    dq = OFF['diff_q'] + 64 * c + rng64
    dk = OFF['diff_k'] + 64 * c + rng64
    dq_sw = OFF['diff_q'] + 64 * c + swd
    dk_sw = OFF['diff_k'] + 64 * c + swd
    fq = OFF['fox_q'] + 64 * c + rng64
    fk = OFF['fox_k'] + 64 * c + rng64
    my = [4 * g + 2 * hp, 4 * g + 2 * hp + 1]
    oth = [4 * g + 2 * (1 - hp), 4 * g + 2 * (1 - hp) + 1]
    nq = lambda H, perm: OFF['nsa_q'] + 64 * H + perm
    kc = lambda perm: OFF['nsa_k_cmp'] + 64 * g + perm
    ks = lambda perm: OFF['nsa_k_sel'] + 64 * g + perm
    kw = lambda perm: OFF['nsa_k_win'] + 64 * g + perm
    vcmp = OFF['nsa_v_cmp'] + 64 * g + rng64
    gate = lambda H, br: np.full(64, OFF['nsa_gate'] + 3 * H + br)
    groups = [
        np.concatenate([dq, fq]), np.concatenate([dk, fk]), np.concatenate([dq_sw, dk_sw]),
        np.concatenate([nq(my[0], rng64), nq(my[1], rng64)]), np.concatenate([nq(my[0], sw64), nq(my[1], sw64)]),
        np.concatenate([nq(oth[0], rng64), nq(oth[1], rng64)]), np.concatenate([nq(oth[0], sw64), nq(oth[1], sw64)]),
        np.concatenate([kc(rng64), ks(rng64)]), np.concatenate([kc(sw64), ks(sw64)]),
        np.concatenate([kw(rng64), vcmp]), np.concatenate([kw(sw64), vcmp]),
        np.concatenate([OFF['diff_z'] + 64 * c + rng64, OFF['fox_z'] + 64 * c + rng64]),
        np.concatenate([OFF['nsa_z'] + 64 * my[0] + rng64, OFF['nsa_z'] + 64 * my[1] + rng64]),
        np.concatenate([gate(my[0], 0), gate(my[1], 0)]), np.concatenate([gate(my[0], 1), gate(my[1], 1)]),
        np.concatenate([gate(my[0], 2), gate(my[1], 2)]),
    ]
    fm = np.concatenate(groups)
    tm = np.concatenate([OFF['fox_v'] + 64 * c + rng64, OFF['diff_v'] + 64 * c + rng64,
                         OFF['nsa_v_sel'] + 64 * g + rng64, OFF['nsa_v_win'] + 64 * g + rng64,
                         np.array([OFF['fox_f'] + c])])
    wo = np.concatenate([64 * c + rng64, 256 + 512 + 64 * c + rng64, 256 + 64 * my[0] + rng64, 256 + 64 * my[1] + rng64])
    return fm, tm, wo


_PROG = {}


def _get(S, depth):
    k = (S, depth)
    if k not in _PROG:
        _PROG[k] = build_fused(S, depth)
    return _PROG[k]


def kernel(x, w_in, b_fox_f, cmp_pos_k, cmp_pos_v, cmp_w1_k, cmp_w2_k, cmp_w1_v, cmp_w2_v,
           lam_q1, lam_k1, lam_q2, lam_k2, diff_subln_g, w_out, ln_g, ln_b):
    f32 = lambda a: np.ascontiguousarray(np.asarray(a, dtype=np.float32))
    x = f32(x)
    B, S, D = x.shape
    depth = w_in.shape[0]
    SQ = S // 4
    w_in, w_out = f32(w_in), f32(w_out)
    cst = _consts(S)
    cols = [_core_cols(c) for c in range(4)]
    w1 = np.ascontiguousarray(np.stack([f32(cmp_w1_k), f32(cmp_w1_v)], axis=1))
    w2 = np.ascontiguousarray(np.concatenate([f32(cmp_w2_k), f32(cmp_w2_v)], axis=2))
    post = np.ascontiguousarray(np.concatenate([f32(cmp_pos_k).transpose(0, 2, 1), f32(cmp_pos_v).transpose(0, 2, 1)], axis=1))
    lng = np.ascontiguousarray(np.broadcast_to(f32(ln_g)[:, None, :], (depth, 128, D)))
    lnb = np.ascontiguousarray(np.broadcast_to(f32(ln_b)[:, None, :], (depth, 128, D)))
    NT = S // 512
    TPC = 1

    def tokmap(r):
        idx = []
        for i in range(SQ // 128):
            base = (i // TPC) * TPC * 512 + r * TPC * 128 + (i % TPC) * 128
            idx.append(np.arange(base, base + 128))
        return np.concatenate(idx)
    tmaps = [tokmap(r) for r in range(4)]
    xT = [np.ascontiguousarray(x[b].T) for b in range(B)]
    in_maps = []
    for core in range(8):
        b, c = core // 4, core % 4
        fm, tm, wo = cols[c]
        vec = np.zeros((depth, 128, 8), np.float32)
        for l in range(depth):
            lam_init = 0.8 - 0.6 * math.exp(-0.3 * l)
            vec[l, :, 0] = f32(b_fox_f)[l, c]
            vec[l, :, 1] = f32(diff_subln_g)[l][np.arange(128) % 64]
            vec[l, 0:32, 2] = f32(lam_q1)[l]; vec[l, 0:32, 3] = f32(lam_k1)[l]
            vec[l, 0:32, 4] = f32(lam_q2)[l]; vec[l, 0:32, 5] = f32(lam_k2)[l]
            vec[l, :, 6] = -lam_init
            vec[l, :, 7] = 1.0 - lam_init
        m = dict(xT=xT[b], xq=np.ascontiguousarray(x[b, tmaps[c]]),
                 wfm=np.ascontiguousarray(w_in[:, :, fm]), wtm=np.ascontiguousarray(w_in[:, :, tm]),
                 wout=np.ascontiguousarray(w_out[:, wo, :]), w1=w1, w2=w2, post=post, vecs=vec, lng=lng, lnb=lnb)
        m.update(cst)
        in_maps.append(m)
    res = run_bass_kernel_spmd(_get(S, depth), in_maps, core_ids=list(range(8)))
    out = np.empty((B, S, D), np.float32)
    for core in range(8):
        b, c = core // 4, core % 4
        out[b, tmaps[c]] = res.results[core]["y"]
    return out
```
